# Optimizing a Trainium2 kernel written in Bass

```python
import math
import jax, jax.numpy as jnp
from jax import lax
import numpy as np

D_MODEL = 1024
BATCH = 2
SEQ = 8192
DEPTH = 1
DEC_BATCH = 16
DEC_SEQ = 16
PAST_LEN = 4096

CHUNK = 64
N_PAST_CHUNKS = 8
KV_BAND = N_PAST_CHUNKS * CHUNK
N_HEADS = 8
HEAD_DIM = 64
ATTN_WIDTH = N_HEADS * HEAD_DIM
SSM_WIDTH = D_MODEL // 2
GROUP_CH = 16
N_GROUPS = SSM_WIDTH // GROUP_CH
STATE_DIM = 64
REL_CLIP = 128
EPS = 1e-6
NEG_INF = -1e30
DT_MIN = 1e-3
DT_MAX = 1e-1
IN_SPLITS = (ATTN_WIDTH, ATTN_WIDTH, ATTN_WIDTH, ATTN_WIDTH, SSM_WIDTH, SSM_WIDTH, D_MODEL, D_MODEL)
IN_WIDTH = 4 * ATTN_WIDTH + 2 * SSM_WIDTH + 2 * D_MODEL

kernel_name = "chunk_band_attn_s5_gated_parallel_step"


def rms_norm(x, g):
    xf = x.astype(jnp.float32)
    y = xf * lax.rsqrt(jnp.mean(xf * xf, axis=-1, keepdims=True) + EPS) * g.astype(jnp.float32)
    return y.astype(x.dtype)


def project(x, c, norm_g, w_ada, b_ada, w_in, q_norm_g, k_norm_g):
    b, s = x.shape[0], x.shape[1]
    mod = jax.nn.silu(c) @ w_ada + b_ada
    shift, scale, gate = jnp.split(mod, 3, axis=-1)
    h = rms_norm(x, norm_g) * (1.0 + scale[:, None]) + shift[:, None]
    z = h @ w_in
    idx = [int(i) for i in np.cumsum(IN_SPLITS)[:-1]]
    q, k, v, ga, u, gs, ma, ms = jnp.split(z, idx, axis=-1)
    q = rms_norm(q.reshape(b, s, N_HEADS, HEAD_DIM), q_norm_g)
    k = rms_norm(k.reshape(b, s, N_HEADS, HEAD_DIM), k_norm_g)
    v = v.reshape(b, s, N_HEADS, HEAD_DIM)
    return q, k, v, ga, u, gs, ma, ms, gate


def attend(q, k, v, q_pos, k_pos, rel_bias):
    qc = q_pos // CHUNK
    kc = k_pos // CHUNK
    allowed = ((k_pos[:, None, :] >= 0)
               & (kc[:, None, :] <= qc[:, :, None])
               & (kc[:, None, :] >= qc[:, :, None] - N_PAST_CHUNKS))
    rel = jnp.clip(q_pos[:, :, None] - k_pos[:, None, :], -REL_CLIP, REL_CLIP) + REL_CLIP
    bias = jnp.transpose(rel_bias.astype(jnp.float32)[:, rel], (1, 0, 2, 3))
    s = jnp.einsum('bnqhd,bnkhd->bnhqk', q, k).astype(jnp.float32) * (HEAD_DIM ** -0.5) + bias[None]
    s = jnp.where(allowed[None, :, None], s, NEG_INF)
    p = jax.nn.softmax(s, axis=-1).astype(v.dtype)
    return jnp.einsum('bnhqk,bnkhd->bnqhd', p, v)


def attn_prompt(q, k, v, rel_bias):
    b, s = q.shape[0], q.shape[1]
    nc = s // CHUNK

    def band(t):
        tc = t.reshape(b, nc, CHUNK, N_HEADS, HEAD_DIM)
        tp = jnp.pad(tc, ((0, 0), (N_PAST_CHUNKS, 0), (0, 0), (0, 0), (0, 0)))
        return jnp.concatenate([tp[:, i:i + nc] for i in range(N_PAST_CHUNKS + 1)], axis=2)

    qc = q.reshape(b, nc, CHUNK, N_HEADS, HEAD_DIM)
    q_pos = jnp.arange(s).reshape(nc, CHUNK)
    k_chunk = jnp.arange(nc)[:, None] - N_PAST_CHUNKS + jnp.arange(N_PAST_CHUNKS + 1)[None, :]
    k_pos = (k_chunk[:, :, None] * CHUNK + jnp.arange(CHUNK)[None, None, :]).reshape(nc, (N_PAST_CHUNKS + 1) * CHUNK)
    o = attend(qc, band(k), band(v), q_pos, k_pos, rel_bias)
    return o.reshape(b, s, ATTN_WIDTH)


def attn_sample(q, k, v, cache_k, cache_v, rel_bias):
    b, t = q.shape[0], q.shape[1]
    l = cache_k.shape[1]
    kk = jnp.concatenate([cache_k.astype(k.dtype), k], axis=1)[:, None]
    vv = jnp.concatenate([cache_v.astype(v.dtype), v], axis=1)[:, None]
    q_pos = (PAST_LEN + jnp.arange(t))[None]
    k_pos = jnp.concatenate([PAST_LEN - l + jnp.arange(l), PAST_LEN + jnp.arange(t)])[None]
    o = attend(q[:, None], kk, vv, q_pos, k_pos, rel_bias)[:, 0]
    return o.reshape(b, t, ATTN_WIDTH)


def ssm_branch(u, x0_re, x0_im, lambda_re, lambda_im, log_dt, b_re, b_im, c_re, c_im, d_skip, w_glu, b_glu):
    f32 = jnp.float32
    bsz, s = u.shape[0], u.shape[1]
    uf = u.astype(f32)
    ug = uf.reshape(bsz, s, N_GROUPS, GROUP_CH)
    lr, li = lambda_re.astype(f32), lambda_im.astype(f32)
    dt = jnp.exp(log_dt.astype(f32))[:, None]
    mag = jnp.exp(lr * dt)
    a_re, a_im = mag * jnp.cos(li * dt), mag * jnp.sin(li * dt)
    den = lr * lr + li * li
    nr, ni = a_re - 1.0, a_im
    co_re = (nr * lr + ni * li) / den
    co_im = (ni * lr - nr * li) / den
    br, bi = b_re.astype(f32), b_im.astype(f32)
    bb_re = co_re[..., None] * br - co_im[..., None] * bi
    bb_im = co_re[..., None] * bi + co_im[..., None] * br
    bu_re = jnp.einsum('bsgc,gpc->bsgp', ug, bb_re)
    bu_im = jnp.einsum('bsgc,gpc->bsgp', ug, bb_im)
    x0r, x0i = x0_re.astype(f32), x0_im.astype(f32)
    bu_re = bu_re.at[:, 0].add(a_re * x0r - a_im * x0i)
    bu_im = bu_im.at[:, 0].add(a_re * x0i + a_im * x0r)
    ar = jnp.broadcast_to(a_re, bu_re.shape)
    ai = jnp.broadcast_to(a_im, bu_re.shape)

    def combine(e1, e2):
        a1r, a1i, b1r, b1i = e1
        a2r, a2i, b2r, b2i = e2
        return (a1r * a2r - a1i * a2i,
                a1r * a2i + a1i * a2r,
                a2r * b1r - a2i * b1i + b2r,
                a2r * b1i + a2i * b1r + b2i)

    _, _, xr, xi = lax.associative_scan(combine, (ar, ai, bu_re, bu_im), axis=1)
    y = (jnp.einsum('bsgp,gcp->bsgc', xr, c_re.astype(f32))
         - jnp.einsum('bsgp,gcp->bsgc', xi, c_im.astype(f32))).reshape(bsz, s, SSM_WIDTH)
    y = (y + d_skip.astype(f32) * uf).astype(u.dtype)
    a_half, b_half = jnp.split(y @ w_glu + b_glu, 2, axis=-1)
    return a_half * jax.nn.sigmoid(b_half), xr[:, -1], xi[:, -1]


def finish(x, attn_o, ssm_o, ga, gs, ma, ms, gate, w_oa, w_os, w_out):
    branch_a = (attn_o * jax.nn.silu(ga)) @ w_oa
    branch_s = (ssm_o * jax.nn.silu(gs)) @ w_os
    merged = jax.nn.sigmoid(ma) * branch_a + jax.nn.sigmoid(ms) * branch_s
    return x + gate[:, None] * (merged @ w_out)


def setup_inputs(seed: int = 0) -> dict:
    key = jax.random.key(seed)
    ks = jax.random.split(key, 32)
    f32 = jnp.float32
    nrm = lambda k, shp, sc: jax.random.normal(k, shp, f32) * sc
    cache_len = min(KV_BAND, PAST_LEN)
    n_idx = jnp.arange(STATE_DIM, dtype=f32)
    return {
        "x_prompt": nrm(ks[0], (BATCH, SEQ, D_MODEL), 1.0),
        "x_sample": nrm(ks[1], (DEC_BATCH, DEC_SEQ, D_MODEL), 1.0),
        "c_prompt": nrm(ks[2], (BATCH, D_MODEL), 1.0),
        "c_sample": nrm(ks[3], (DEC_BATCH, D_MODEL), 1.0),
        "cache_k": nrm(ks[4], (DEPTH, DEC_BATCH, cache_len, N_HEADS, HEAD_DIM), 1.0),
        "cache_v": nrm(ks[5], (DEPTH, DEC_BATCH, cache_len, N_HEADS, HEAD_DIM), 1.0),
        "state_ssm_re": nrm(ks[6], (DEPTH, DEC_BATCH, N_GROUPS, STATE_DIM), 0.1),
        "state_ssm_im": nrm(ks[7], (DEPTH, DEC_BATCH, N_GROUPS, STATE_DIM), 0.1),
        "norm_g": 1.0 + nrm(ks[8], (DEPTH, D_MODEL), 0.1),
        "w_ada": nrm(ks[9], (DEPTH, D_MODEL, 3 * D_MODEL), 0.5 * D_MODEL ** -0.5),
        "b_ada": nrm(ks[10], (DEPTH, 3 * D_MODEL), 0.02),
        "w_in": nrm(ks[11], (DEPTH, D_MODEL, IN_WIDTH), D_MODEL ** -0.5),
        "q_norm_g": 1.0 + nrm(ks[12], (DEPTH, HEAD_DIM), 0.1),
        "k_norm_g": 1.0 + nrm(ks[13], (DEPTH, HEAD_DIM), 0.1),
        "rel_bias": nrm(ks[14], (DEPTH, N_HEADS, 2 * REL_CLIP + 1), 0.1),
        "lambda_re": -0.5 + nrm(ks[15], (DEPTH, N_GROUPS, STATE_DIM), 0.01),
        "lambda_im": math.pi * n_idx + nrm(ks[16], (DEPTH, N_GROUPS, STATE_DIM), 0.01),
        "log_dt": jax.random.uniform(ks[17], (DEPTH, N_GROUPS), f32, math.log(DT_MIN), math.log(DT_MAX)),
        "b_re": nrm(ks[18], (DEPTH, N_GROUPS, STATE_DIM, GROUP_CH), (2 * GROUP_CH) ** -0.5),
        "b_im": nrm(ks[19], (DEPTH, N_GROUPS, STATE_DIM, GROUP_CH), (2 * GROUP_CH) ** -0.5),
        "c_re": nrm(ks[20], (DEPTH, N_GROUPS, GROUP_CH, STATE_DIM), STATE_DIM ** -0.5),
        "c_im": nrm(ks[21], (DEPTH, N_GROUPS, GROUP_CH, STATE_DIM), STATE_DIM ** -0.5),
        "d_skip": nrm(ks[22], (DEPTH, SSM_WIDTH), 1.0),
        "w_glu": nrm(ks[23], (DEPTH, SSM_WIDTH, 2 * SSM_WIDTH), SSM_WIDTH ** -0.5),
        "b_glu": nrm(ks[24], (DEPTH, 2 * SSM_WIDTH), 0.02),
        "w_oa": nrm(ks[25], (DEPTH, ATTN_WIDTH, D_MODEL), ATTN_WIDTH ** -0.5),
        "w_os": nrm(ks[26], (DEPTH, SSM_WIDTH, D_MODEL), SSM_WIDTH ** -0.5),
        "w_out": nrm(ks[27], (DEPTH, D_MODEL, D_MODEL), D_MODEL ** -0.5),
    }


def reference(x_prompt, x_sample, c_prompt, c_sample, cache_k, cache_v, state_ssm_re, state_ssm_im,
              norm_g, w_ada, b_ada, w_in, q_norm_g, k_norm_g, rel_bias, lambda_re, lambda_im, log_dt,
              b_re, b_im, c_re, c_im, d_skip, w_glu, b_glu, w_oa, w_os, w_out):
    xp, xs = x_prompt, x_sample
    kp_l, vp_l, srp_l, sip_l = [], [], [], []
    ks_l, vs_l, srs_l, sis_l = [], [], [], []
    for l in range(DEPTH):
        ssm_p = (lambda_re[l], lambda_im[l], log_dt[l], b_re[l], b_im[l], c_re[l], c_im[l], d_skip[l], w_glu[l], b_glu[l])
        q, k, v, ga, u, gs, ma, ms, gate = project(xp, c_prompt, norm_g[l], w_ada[l], b_ada[l], w_in[l], q_norm_g[l], k_norm_g[l])
        a_o = attn_prompt(q, k, v, rel_bias[l])
        zeros = jnp.zeros((xp.shape[0], N_GROUPS, STATE_DIM), jnp.float32)
        s_o, sr, si = ssm_branch(u, zeros, zeros, *ssm_p)
        keep = min(KV_BAND, xp.shape[1])
        kp_l.append(k[:, -keep:]); vp_l.append(v[:, -keep:]); srp_l.append(sr); sip_l.append(si)
        xp = finish(xp, a_o, s_o, ga, gs, ma, ms, gate, w_oa[l], w_os[l], w_out[l])
        q, k, v, ga, u, gs, ma, ms, gate = project(xs, c_sample, norm_g[l], w_ada[l], b_ada[l], w_in[l], q_norm_g[l], k_norm_g[l])
        a_o = attn_sample(q, k, v, cache_k[l], cache_v[l], rel_bias[l])
        s_o, sr, si = ssm_branch(u, state_ssm_re[l], state_ssm_im[l], *ssm_p)
        ks_l.append(k); vs_l.append(v); srs_l.append(sr); sis_l.append(si)
        xs = finish(xs, a_o, s_o, ga, gs, ma, ms, gate, w_oa[l], w_os[l], w_out[l])
    return (xp, xs,
            jnp.stack(kp_l), jnp.stack(vp_l), jnp.stack(srp_l), jnp.stack(sip_l),
            jnp.stack(ks_l), jnp.stack(vs_l), jnp.stack(srs_l), jnp.stack(sis_l))
```

```python
import numpy as np
import os
STAGE = int(os.environ.get('KSTAGE', '99'))
SUB = int(os.environ.get('KSUB', '99'))
KRMS = int(os.environ.get('KRMS', '99'))
import ml_dtypes
from contextlib import ExitStack
import concourse.bass as bass
import concourse.mybir as mybir
from concourse.bass_utils import run_bass_kernel_spmd

F32 = mybir.dt.float32
BF16 = mybir.dt.bfloat16
ALU = mybir.AluOpType
AF = mybir.ActivationFunctionType
AX = mybir.AxisListType

D = 1024
NOWN = 2048
NHALO = 512
NSAMP = 32
EPS = 1e-6


PSUM_KEYS = ("pA", "pT", "pS", "pO", "pA0", "pA1", "pA2", "pS0", "pS1", "pO0", "pO1")


class Prog:
    ENG = ("pe", "act", "dve", "pool", "sp")

    def __init__(self):
        self.ops = []
        self.last_w = {}
        self.readers = {}

    def defer_start(self):
        self._pend = []
        self._defer = True

    def defer_stop(self):
        self._defer = False

    def flush(self, n=None):
        pend = getattr(self, "_pend", [])
        k = len(pend) if n is None else min(n, len(pend))
        was = getattr(self, "_defer", False)
        self._defer = False
        for a in pend[:k]:
            self.add(*a[0], **a[1])
        self._defer = was
        self._pend = pend[k:]

    def add(self, eng, name, reads=(), writes=(), dma=False, nophase=False, **kw):
        if getattr(self, "_defer", False):
            self._pend.append(((eng, name), dict(reads=list(reads), writes=list(writes), dma=dma, nophase=nophase, **kw)))
            return None
        op = dict(eng=eng, name=name, kw=kw, dma=dma, deps=set(), idx=len(self.ops), sig=dma)
        reads = list(reads)
        if not nophase:
            reads.append("PHASE")
        writes = list(writes) + [r for r in reads if r in PSUM_KEYS]
        reads = [r for r in reads if r not in PSUM_KEYS]
        for r in reads:
            lw = self.last_w.get(r)
            if lw is not None:
                op["deps"].add(lw)
        for w in writes:
            lw = self.last_w.get(w)
            if lw is not None:
                op["deps"].add(lw)
            for rd in self.readers.get(w, ()):
                op["deps"].add(rd)
        for r in reads:
            self.readers.setdefault(r, []).append(op["idx"])
        for w in writes:
            self.last_w[w] = op["idx"]
            self.readers[w] = []
        op["deps"].discard(op["idx"])
        self.ops.append(op)
        return op

    def barrier(self, arena_ap):
        self.add("dve", "memset", reads=[], writes=["PHASE"], nophase=True, ap=arena_ap, constant=0.0)

    def emit(self, nc, ndma_sems=16):
        ops = self.ops
        for op in ops:
            nd = set()
            for d in op["deps"]:
                p = ops[d]
                if (not p["dma"]) and p["eng"] == op["eng"] and p["eng"] == "pe" and not op["dma"]:
                    continue
                nd.add(d)
            op["deps"] = nd
            for d in nd:
                ops[d]["sig"] = True
        cnt = {e: 0 for e in self.ENG}
        dcnt = {e: 0 for e in self.ENG}
        for op in ops:
            e = op["eng"]
            if op["dma"]:
                i = dcnt[e]
                dcnt[e] += 1
                op["sem"] = ("d", e, i % ndma_sems)
                op["val"] = 16 * (i // ndma_sems + 1)
            elif op["sig"]:
                cnt[e] += 1
                op["sem"] = ("c", e)
                op["val"] = cnt[e]
        with ExitStack() as st:
            sems = {}
            for e in self.ENG:
                sems[("c", e)] = st.enter_context(nc.semaphore("c_" + e))
                if dcnt[e]:
                    for i in range(ndma_sems):
                        sems[("d", e, i)] = st.enter_context(nc.semaphore("d_%s_%d" % (e, i)))
            block = st.enter_context(nc.Block())
            byeng = {e: [o for o in ops if o["eng"] == e] for e in self.ENG}

            def run(engname, eng):
                known = {}
                for op in byeng[engname]:
                    waits = {}
                    for d in op["deps"]:
                        p = ops[d]
                        waits[p["sem"]] = max(waits.get(p["sem"], 0), p["val"])
                    if op["dma"] and op["val"] > 16:
                        waits[op["sem"]] = max(waits.get(op["sem"], 0), op["val"] - 16)
                    for s, v in waits.items():
                        if known.get(s, 0) >= v:
                            continue
                        eng.wait_ge(sems[s], v)
                        known[s] = v
                    ins = getattr(eng, op["name"])(**op["kw"])
                    if op["sig"]:
                        ins.then_inc(sems[op["sem"]], 16 if op["dma"] else 1)
                last = {}
                for op in byeng[engname]:
                    if op["dma"]:
                        last[op["sem"]] = op["val"]
                for s, v in last.items():
                    if known.get(s, 0) < v:
                        eng.wait_ge(sems[s], v)

            block.tensor(lambda eng: run("pe", eng))
            block.scalar(lambda eng: run("act", eng))
            block.vector(lambda eng: run("dve", eng))
            block.gpsimd(lambda eng: run("pool", eng))
            block.sync(lambda eng: run("sp", eng))


def build():
    nc = bass.Bass("TRN2", target_bir_lowering=False)

    def din(name, shape, dt=F32):
        return nc.dram_tensor(name, list(shape), dt, kind="ExternalInput").ap()

    def dout(name, shape, dt=F32):
        return nc.dram_tensor(name, list(shape), dt, kind="ExternalOutput").ap()

    NTOK = NHALO + NOWN + NSAMP
    NT = NOWN + NSAMP
    NCH = NT // 16
    x_all = din("x_all", [NTOK, D])
    x_prev = din("x_prev", [3 * NOWN, D])
    pmask_d = din("pmask", [128, 4])
    c3 = din("c3", [3, D])
    cache_k = din("cache_k", [2, 512, 512])
    cache_v = din("cache_v", [2, 512, 512])
    st_re = din("st_re", [64, 64])
    st_im = din("st_im", [64, 64])
    hbias = din("hbias", [128, 1])
    sel = din("sel", [3, 256])
    ident_d = din("ident", [128, 128])
    cmask_d = din("cmask", [128, 16])
    selm_d = din("selm", [128, 24])
    norm_g = din("norm_g", [D])
    w_ada = din("w_ada", [D, 3 * D])
    b_ada = din("b_ada", [3 * D])
    w_in = din("w_in", [D, 5120])
    q_norm_g = din("q_norm_g", [64])
    k_norm_g = din("k_norm_g", [64])
    rel_bias = din("rel_bias", [8, 257])
    lam_re = din("lambda_re", [32, 64])
    lam_im = din("lambda_im", [32, 64])
    log_dt = din("log_dt", [32])
    b_re = din("b_re", [32, 64, 16])
    b_im = din("b_im", [32, 64, 16])
    c_re = din("c_re", [512, 64])
    c_im = din("c_im", [512, 64])
    d_skip = din("d_skip", [512])
    w_glu = din("w_glu", [512, 1024])
    b_glu = din("b_glu", [1024])
    w_oa = din("w_oa", [512, D])
    w_os = din("w_os", [512, D])
    w_out = din("w_out", [D, D])

    y_out = dout("y_out", [NT, D])
    k_last = dout("k_last", [512, 512])
    v_last = dout("v_last", [512, 512])
    k_samp = dout("k_samp", [NSAMP, 512])
    v_samp = dout("v_samp", [NSAMP, 512])
    ssm_p = dout("ssm_p", [32, 128])
    ssm_s = dout("ssm_s", [64, 128])

    e_d = nc.dram_tensor("e_d", [8, 768], F32, kind="Internal").ap()
    m_d = nc.dram_tensor("m_d", [8, 130 * 768], F32, kind="Internal").ap()
    mg_d = nc.dram_tensor("mg_d", [17, 128, 1024], BF16, kind="Internal").ap()
    sloc_d = nc.dram_tensor("sloc_d", [128, 32], F32, kind="Internal").ap()
    wq_d = nc.dram_tensor("wq_d", [D, 1536], BF16, kind="Internal").ap()
    wgm_d = nc.dram_tensor("wgm_d", [D, 1536], BF16, kind="Internal").ap()
    woa_d = nc.dram_tensor("woa_d", [512, D], BF16, kind="Internal").ap()
    wglu_d = nc.dram_tensor("wglu_d", [512, D], BF16, kind="Internal").ap()
    wgsms_d = nc.dram_tensor("wgsms_d", [D, 1536], BF16, kind="Internal").ap()
    wos_d = nc.dram_tensor("wos_d", [512, D], BF16, kind="Internal").ap()
    wout_d = nc.dram_tensor("wout_d", [D, D], BF16, kind="Internal").ap()
    sall_d = nc.dram_tensor("sall_d", [1024, 32], F32, kind="Internal").ap()

    P = Prog()
    st = ExitStack()
    with st:
        def sb(name, shape, dt=F32):
            return st.enter_context(nc.sbuf_tensor("s_" + name, list(shape), dt))

        def ps(name, shape, dt=F32):
            return st.enter_context(nc.psum_tensor("p_" + name, list(shape), dt))

        dummy = sb("dummy", [128, 2])
        epsc = sb("epsc", [128, 1])
        ident = sb("ident", [128, 128], BF16)
        identf = sb("identf", [128, 128])
        selT = sb("selT", [3, 256])
        hb = sb("hb", [128, 1])
        cmask = sb("cmask", [128, 16])
        selm = sb("selm", [128, 24])
        pmask = sb("pmask", [128, 4])
        stg = sb("stg", [68, 128])
        smalls = sb("smalls", [128, 68])
        cT = smalls[:, 0:24].rearrange("p (b k) -> p k b", b=3)
        bada = smalls[:, 24:48]
        ng = smalls[:, 48:56]
        dq = smalls[:, 56:60]
        bglu = smalls[:, 60:68]
        scT = sb("scT", [128, 8, 3], BF16)
        modT = sb("modT", [128, 24, 3])
        Amod = sb("Amod", [128, 8, 3])
        gate_bc = sb("gate_bc", [128, 3, 1024], BF16)
        gqk = sb("gqk", [128, 1024], BF16)
        gqg = sb("gqg", [128, 512], BF16)
        xin = [sb("xin%d" % i, [128, 1024]) for i in range(2)]
        ss2 = [sb("ss%d" % i, [128, 4]) for i in range(2)]
        xn2 = [sb("xn%d" % i, [128, 1024], BF16) for i in range(2)]
        hT2 = [sb("hT%d" % i, [128, 8, 128], BF16) for i in range(2)]
        e_s = xin[1][0:8, 0:768]
        qk_sb = sb("qk_sb", [128, 1024])
        sq = sb("sq", [128, 1024])
        ss16 = sb("ss16", [128, 16])
        qkb = sb("qkb", [128, 1024], BF16)
        kout = sb("kout", [128, 512])
        vout = sb("vout", [128, 512])
        qT = sb("qT", [128, 4, 128], BF16)
        stmp2 = [sb("stmp%d" % i, [128, 5, 128]) for i in range(2)]
        PT2 = [sb("PT%d" % i, [128, 5, 128], BF16) for i in range(2)]
        stmp, PT = stmp2[0], PT2[0]
        rden = sb("rden", [128, 8])
        AO = sb("AO", [128, 8, 64], BF16)
        sga = sb("sga", [128, 4, 128], BF16)
        sma = sb("sma", [128, 8, 128], BF16)
        AOgT = sb("AOgT", [128, 4, 128], BF16)
        mgt = sb("mgt", [128, 8, 128], BF16)
        uT = sb("uT", [128, 4, NT], BF16)
        XS = sb("XS", [128, 129, 32])
        prm = sb("prm", [128, 36, 32])
        PW1 = sb("PW1", [128, 32, 17])
        PW2 = sb("PW2", [128, 32, 17])
        Bstk = sb("Bstk", [128, 32, 16])
        Bsw = sb("Bsw", [128, 32, 16])
        Bbb = sb("Bbb", [128, 32, 16], BF16)
        Cstk = sb("Cstk", [128, 512])
        Csw = sb("Csw", [128, 512])
        xs0 = sb("xs0", [128, 2, 32])
        WSs = sb("WSs", [128, 2, 32])
        Gall = sb("Gall", [128, 8, 32])
        ki32 = sb("ki32", [128, 32], mybir.dt.int32)

        ARN = 45184
        arena = sb("arena", [128, ARN], BF16)

        def av(off, n):
            return arena[:, off:off + n]

        wada = [av(28672, 4096).rearrange("p (k n) -> p k n", k=8), av(32768, 4096).rearrange("p (k n) -> p k n", k=8)]
        w_out_s = av(24064, 8192).rearrange("p (k n) -> p k n", k=8)
        scr = av(28672, 16512).bitcast(F32)
        w_qkv = av(0, 12288).rearrange("p (k n) -> p k n", k=8)
        w_gm = av(12288, 12288).rearrange("p (k n) -> p k n", k=8)
        w_oa_s = av(24576, 4096).rearrange("p (k n) -> p k n", k=4)
        KT = av(28672, 3072).rearrange("p (k n) -> p k n", k=4)
        V = av(31744, 3120).rearrange("p (s h e) -> p s h e", s=6, h=8)
        BM = av(34880, 10240).bitcast(F32).rearrange("p (h t q) -> p h t q", h=8, t=5)
        w_u = av(0, 4096).rearrange("p (k n) -> p k n", k=8)
        W1z = av(4096, 16384).rearrange("p (q r j m) -> p q r j m", q=4, r=2, j=16)
        Pst = av(20480, 8192).rearrange("p (q t g c) -> p q t g c", q=4, t=16, g=8)
        CAz = av(0, 17408).rearrange("p (g t r c) -> p g t r c", g=32, t=17, r=2)
        Kblk = av(17408, 8192).rearrange("p (q t m) -> p q t m", q=4, t=16)
        Xb = av(25600, 4160).rearrange("p (g k) -> p g k", g=32)
        Cin = av(29760, 1024).bitcast(F32).rearrange("p (q m) -> p q m", q=4)
        yT = av(36608, 8320).rearrange("p (q n) -> p q n", q=4)
        w_glu_s = av(0, 4096).rearrange("p (k n) -> p k n", k=4)
        w_gsms = av(4096, 12288).rearrange("p (k n) -> p k n", k=8)
        w_os_s = av(16384, 4096).rearrange("p (k n) -> p k n", k=4)
        sgb = av(20480, 512).rearrange("p (k n) -> p k n", k=4)
        sgs = av(20992, 512).rearrange("p (k n) -> p k n", k=4)
        sms = av(21504, 1024).rearrange("p (k n) -> p k n", k=8)
        sgt = av(22528, 512).rearrange("p (k n) -> p k n", k=4)
        bst = av(23040, 1024).rearrange("p (k n) -> p k n", k=8)

        pA = ps("pA", [128, 1536])
        pT = ps("pT", [128, 8, 128], BF16)
        pS = ps("pS", [128, 1024])
        pO = ps("pO", [128, 2, 512])

        def bfv(ap_):
            return ap_.bitcast(BF16).rearrange("p (k t) -> p k t", t=128)

        TA = [pT[:, 0:4, :], bfv(pO[:, 0, :])]
        TAk = ["pT", "pO0"]
        TB = [bfv(pA[:, 1024:1536]), bfv(pO[:, 1, :])]
        TBk = ["pA2", "pO1"]
        UB = [pA[:, 0:512], pA[:, 512:1024]]
        UBk = ["pA0", "pA1"]

        def dma(eng, out, in_, reads, writes, **kw):
            P.add(eng, "dma_start", reads=reads, writes=writes, dma=True, out=out, in_=in_, **kw)

        def wload(dst, src, key):
            dma("pool", dst, src.rearrange("(kt p) n -> p kt n", p=128), [], [key])

        def tt(eng, out, in0, in1, op, reads, writes, **kw):
            P.add(eng, "tensor_tensor", reads=reads, writes=writes, out=out, in0=in0, in1=in1, op=op, **kw)

        def ts(eng, out, in0, s1, s2, op0, op1, reads, writes, **kw):
            if s2 is None:
                P.add(eng, "tensor_scalar", reads=reads, writes=writes, out=out, in0=in0, scalar1=s1, scalar2=None,
                      op0=op0, **kw)
            else:
                P.add(eng, "tensor_scalar", reads=reads, writes=writes, out=out, in0=in0, scalar1=s1, scalar2=s2,
                      op0=op0, op1=op1, **kw)

        def cp(eng, out, in_, reads, writes, **kw):
            P.add(eng, "tensor_copy", reads=reads, writes=writes, out=out, in_=in_, **kw)

        P.add("pool", "memset", reads=[], writes=["epsc"], nophase=True, ap=epsc[:, :], constant=EPS)
        dma("pool", ident[:, :], ident_d[:, :], [], ["ident"])
        dma("sp", identf[:, :], ident_d[:, :], [], ["identf"])
        dma("sp", selT[:, :], sel[:, :], [], ["selT"])
        dma("sp", hb[:, :], hbias[:, :], [], ["hb"])
        dma("sp", cmask[:, :], cmask_d[:, :], [], ["cmask"])
        dma("sp", selm[:, :], selm_d[:, :], [], ["selm"])
        dma("sp", pmask[:, :], pmask_d[:, :], [], ["pmask"])
        dma("sp", stg[0:24, :], c3.rearrange("b (kt p) -> (b kt) p", p=128), [], ["stg"])
        dma("sp", stg[24:48, :], b_ada.rearrange("(ct p) -> ct p", p=128), [], ["stg1"])
        dma("sp", stg[48:56, :], norm_g.rearrange("(kt p) -> kt p", p=128), [], ["stg2"])
        dma("sp", stg[56:60, :], d_skip.rearrange("(kt p) -> kt p", p=128), [], ["stg3"])
        dma("sp", stg[60:68, :], b_glu.rearrange("(kt p) -> kt p", p=128), [], ["stg4"])
        P.add("pe", "transpose", reads=["stg", "stg1", "stg2", "stg3", "stg4", "identf"], writes=["pS"], out=pS[:, 0:68],
              in_=stg[0:68, :], identity=identf[0:68, 0:68])
        cp("dve", smalls[:, :], pS[:, 0:68], ["pS"], ["cT", "bada", "ng"])
        bgate = sq[0:3, :]
        gate_tok = qk_sb[0:3, :]
        dma("pool", gqk[:, 0:512], bass.AP(tensor=q_norm_g.tensor, offset=0, ap=[[0, 128], [0, 8], [1, 64]]),
            [], ["gqk_q"])
        dma("pool", gqk[:, 512:1024], bass.AP(tensor=k_norm_g.tensor, offset=0, ap=[[0, 128], [0, 8], [1, 64]]),
            [], ["gqk_k"])
        tt("dve", gqg[:, :], gqk[:, 0:512], gqk[:, 512:1024], ALU.mult, ["gqk_q", "gqk_k"], ["gqg"])
        dma("sp", e_s[:, 129:385], rel_bias[:, 1:257], [], ["xin1"])
        cp("dve", e_s[:, 0:129], e_s[:, 384:385].to_broadcast([8, 129]), ["xin1"], ["xin1"])
        cp("dve", e_s[:, 385:768], e_s[:, 384:385].to_broadcast([8, 383]), ["xin1"], ["xin1"])
        dma("sp", e_d[:, :], e_s[:, :], ["xin1"], ["e_d"])
        dma("sp", m_d.rearrange("h (r e) -> h r e", e=768),
            bass.AP(tensor=e_d.tensor, offset=0, ap=[[768, 8], [0, 130], [1, 768]]), ["e_d"], ["m_d"])

        TI = {"n": 0, "pend": None}

        def rms_front(row0, nt, src=None):
            src = x_all if src is None else src
            i = TI["n"] % 2
            TI["n"] += 1
            xt, xk = xin[i], "xin%d" % i
            xn, xnk = xn2[i], "xn%d" % i
            sst, ssk = ss2[i], "ss%d" % i
            dma("sp", xt[0:nt, :], src[row0:row0 + nt, :], [], [xk])
            P.add("act", "activation", reads=[xk], writes=[xnk, ssk], out=xn[0:nt, :], in_=xt[0:nt, :],
                  func=AF.Square, accum_out=sst[0:nt, 0:1])
            P.add("act", "activation", reads=[ssk, "epsc"], writes=[ssk], out=sst[0:nt, 2:3], in_=sst[0:nt, 0:1], func=AF.Ln,
                  scale=1.0 / D, bias=epsc[0:nt, 0:1])
            P.add("act", "activation", reads=[ssk], writes=[ssk], out=sst[0:nt, 3:4], in_=sst[0:nt, 2:3], func=AF.Exp, scale=-0.5)
            P.add("act", "activation", reads=[xk, ssk], writes=[xnk], out=xn[0:nt, :], in_=xt[0:nt, :],
                  func=AF.Copy, scale=sst[0:nt, 3:4])
            return i

        def rms_back(i, nt, mods):
            CUR["i"] = i
            xn, xnk = xn2[i], "xn%d" % i
            hT, hk = HT(), HK()
            for k in range(8):
                P.add("pe", "transpose", reads=[xnk, "ident"], writes=["pT"], out=pT[:, k, 0:nt],
                      in_=xn[0:nt, k * 128:(k + 1) * 128], identity=ident[0:nt, 0:nt])
            for k in range(8):
                for (c0, c1, b) in mods:
                    if k < 4:
                        ts("dve", hT[:, k, c0:c1], pT[:, k, c0:c1], Amod[:, k, b:b + 1], modT[:, k, b:b + 1],
                           ALU.mult, ALU.add, ["pT", "Amod", "modT"], [hk])
                    else:
                        P.add("act", "activation", reads=["pT", "Amod", "modT"], writes=[hk], out=hT[:, k, c0:c1],
                              in_=pT[:, k, c0:c1], func=AF.Identity, scale=Amod[:, k, b:b + 1], bias=modT[:, k, b:b + 1])

        def run_tiles(tiles, body, src=None):
            nxt = rms_front(tiles[0][0], tiles[0][1], src)
            for n, tl in enumerate(tiles):
                cur = nxt
                rms_back(cur, tl[1], tl[2])
                if n + 1 < len(tiles):
                    nxt = rms_front(tiles[n + 1][0], tiles[n + 1][1], src)
                CUR["i"] = cur
                body(n, tl, cur)

        CUR = {"i": 0}

        def HT():
            return hT2[CUR["i"]]

        def HK():
            return "hT%d" % CUR["i"]

        def qkv_mm(nt, with_q):
            for cb in range(0 if with_q else 1, 3):
                for k in range(8):
                    P.add("pe", "matmul", reads=[HK(), "w_qkv"], writes=["pA"], out=pA[0:nt, cb * 512:(cb + 1) * 512],
                          lhsT=HT()[:, k, 0:nt], rhs=w_qkv[:, k, cb * 512:(cb + 1) * 512], start=(k == 0), stop=(k == 7))

        def qkv_tile(nt, slot, with_q, kdst, vdst, out_rows, gm=False, fold=True, pre_mm=False):
            c_lo = 0 if with_q else 512
            if not pre_mm:
                qkv_mm(nt, with_q)
            if gm:
                for ct in range(12):
                    for k in range(8):
                        if ct < 4:
                            o_, ok_ = pO[:, 0, ct * 128:ct * 128 + nt], "pO"
                        else:
                            o_, ok_ = pS[:, (ct - 4) * 128:(ct - 4) * 128 + nt], "pS"
                        P.add("pe", "matmul", reads=[HK(), "w_gm", "w_gm2"], writes=[ok_], out=o_,
                              lhsT=w_gm[:, k, ct * 128:(ct + 1) * 128], rhs=HT()[:, k, 0:nt], start=(k == 0), stop=(k == 7))
            nh = 16 if with_q else 8
            h0 = 0 if with_q else 8
            P.add("act", "activation", reads=["pA"], writes=["sq"], out=sq[0:nt, c_lo:1024], in_=pA[0:nt, c_lo:1024],
                  func=AF.Square)
            P.add("dve", "tensor_reduce", reads=["sq"], writes=["ss16"], out=ss16[0:nt, h0:16],
                  in_=sq[0:nt, c_lo:1024].rearrange("p (h d) -> p h d", d=64), axis=AX.X, op=ALU.add)
            P.add("act", "activation", reads=["ss16", "epsc"], writes=["ss16"], out=ss16[0:nt, h0:16], in_=ss16[0:nt, h0:16],
                  func=AF.Ln, scale=1.0 / 64, bias=epsc[0:nt, 0:1])
            P.add("act", "activation", reads=["ss16"], writes=["ss16"], out=ss16[0:nt, h0:16], in_=ss16[0:nt, h0:16],
                  func=AF.Exp, scale=-0.5)
            rk = ss16[0:nt, 8:16].rearrange("p (h o) -> p h o", o=1).to_broadcast([nt, 8, 64])
            rq = ss16[0:nt, 0:8].rearrange("p (h o) -> p h o", o=1).to_broadcast([nt, 8, 64])
            pAk = pA[0:nt, 512:1024].rearrange("p (h d) -> p h d", d=64)
            pAq = pA[0:nt, 0:512].rearrange("p (h d) -> p h d", d=64)
            if fold:
                tt("dve", qkb[0:nt, 512:1024].rearrange("p (h d) -> p h d", d=64), pAk, rk, ALU.mult, ["pA", "ss16"], ["qkb"])
            else:
                tt("dve", sq[0:nt, 512:1024].rearrange("p (h d) -> p h d", d=64), pAk, rk, ALU.mult, ["pA", "ss16"], ["sq"])
                tt("dve", qkb[0:nt, 512:1024], sq[0:nt, 512:1024], gqk[0:nt, 512:1024], ALU.mult, ["sq", "gqk_k"], ["qkb"])
            if with_q:
                tt("dve", sq[0:nt, 0:512].rearrange("p (h d) -> p h d", d=64), pAq, rq, ALU.mult, ["pA", "ss16"], ["sq"])
                tt("dve", qkb[0:nt, 0:512], sq[0:nt, 0:512], gqg[0:nt, :] if fold else gqk[0:nt, 0:512], ALU.mult,
                   ["sq", "gqk_q", "gqg"], ["qkb"])
            P.add("act", "activation", reads=["pA", "Vones"], writes=[("V", slot)], out=V[0:nt, slot, :, 0:64],
                  in_=pA[0:nt, 1024:1536].rearrange("p (h d) -> p h d", d=64), func=AF.Copy)
            if gm:
                P.add("act", "activation", reads=["pS"], writes=["sma"], out=sma[:, :, 0:nt],
                      in_=pS[:, :].rearrange("p (c t) -> p c t", t=128)[:, :, 0:nt], func=AF.Sigmoid)
                P.add("act", "activation", reads=["pO"], writes=["sga"], out=sga[:, :, 0:nt],
                      in_=pO[:, 0, :].rearrange("p (c t) -> p c t", t=128)[:, :, 0:nt], func=AF.Silu)
            if kdst is not None:
                tt("dve", kout[0:nt, :].rearrange("p (h d) -> p h d", d=64), pAk, rk, ALU.mult, ["pA", "ss16"], ["kout"])
                tt("dve", kout[0:nt, :], kout[0:nt, :], gqk[0:nt, 512:1024], ALU.mult, ["kout", "gqk_k"], ["kout"])
                dma("sp", kdst[out_rows:out_rows + nt, :], kout[0:nt, :], ["kout"], [])
                cp("dve", vout[0:nt, :], pA[0:nt, 1024:1536], ["pA"], ["vout"])
                dma("sp", vdst[out_rows:out_rows + nt, :], vout[0:nt, :], ["vout"], [])
            for i in range(0 if with_q else 4, 8):
                P.add("pe", "transpose", reads=["qkb", "ident"], writes=["pT"], out=pT[:, i, 0:nt],
                      in_=qkb[0:nt, i * 128:(i + 1) * 128], identity=ident[0:nt, 0:nt])
            if with_q:
                cp("dve", qT[:, :, 0:nt], pT[:, 0:4, 0:nt], ["pT"], ["qT"])
            P.add("act", "activation", reads=["pT"], writes=[("KT", slot)], out=KT[:, :, slot * 128:slot * 128 + nt],
                  in_=pT[:, 4:8, 0:nt], func=AF.Copy)

        def attention(nq, ktiles, mtile):
            nkt = len(ktiles)
            nk_last = ktiles[4][1]
            for h in range(8):
                hp, h2 = h // 2, h % 2
                pr = slice(64 * h2, 64 * h2 + 64)
                for j, (slot, nk, tp_, halo) in enumerate(ktiles):
                    P.add("pe", "matmul", reads=[("KT", slot), "qT"], writes=["pS"], out=pS[0:nk, (4 - j) * 128:(4 - j) * 128 + nq],
                          lhsT=KT[pr, hp, slot * 128:slot * 128 + nk], rhs=qT[pr, hp, 0:nq], start=True, stop=True)
                pSv = pS[:, 0:640].rearrange("p (t q) -> p t q", q=128)
                P.add("dve", "scalar_tensor_tensor", reads=["pS", "BM"], writes=["stmp"], out=stmp[:, 1:5, 0:nq],
                      in0=pSv[:, 1:5, 0:nq], scalar=0.125, in1=BM[:, h, 1:5, 0:nq], op0=ALU.mult, op1=ALU.add)
                P.add("dve", "scalar_tensor_tensor", reads=["pS", "BM"], writes=["stmp"], out=stmp[0:nk_last, 0, 0:nq],
                      in0=pSv[0:nk_last, 0, 0:nq], scalar=0.125, in1=BM[0:nk_last, h, 0, 0:nq], op0=ALU.mult, op1=ALU.add)
                P.add("act", "activation", reads=["stmp"], writes=["PT"], out=PT[:, 1:5, 0:nq], in_=stmp[:, 1:5, 0:nq], func=AF.Exp)
                P.add("act", "activation", reads=["stmp"], writes=["PT"], out=PT[0:nk_last, 0, 0:nq], in_=stmp[0:nk_last, 0, 0:nq],
                      func=AF.Exp)
                for j, (slot, nk, tp_, halo) in enumerate(ktiles):
                    P.add("pe", "matmul", reads=["PT", ("V", slot)], writes=["pO"],
                          out=pO[0:nq, h // 4, (h % 4) * 65:(h % 4) * 65 + 65],
                          lhsT=PT[0:nk, 4 - j, 0:nq], rhs=V[0:nk, slot, h, :], start=(j == 0), stop=(j == nkt - 1))
            attention_tail(nq, mtile)

        def attention_tail(nq, mtile, gm_done=False):
            pOv = pO[0:nq, :, 0:260].rearrange("p a (h e) -> p a h e", e=65)
            P.add("dve", "reciprocal", reads=["pO"], writes=["rden"],
                  out=rden[0:nq, :].rearrange("p (a h o) -> p a h o", a=2, o=1), in_=pOv[:, :, :, 64:65])
            tt("dve", AO[0:nq, :, :].rearrange("p (a h) d -> p a h d", a=2), pOv[:, :, :, 0:64],
               rden[0:nq, :].rearrange("p (a h o) -> p a h o", a=2, o=1).to_broadcast([nq, 2, 4, 64]), ALU.mult,
               ["pO", "rden"], ["AO"])
            AOf = AO[:, :, :].rearrange("p h d -> p (h d)")
            for i in range(4):
                P.add("pe", "transpose", reads=["AO", "ident"], writes=["pT"], out=pT[:, i, 0:nq],
                      in_=AOf[0:nq, i * 128:(i + 1) * 128], identity=ident[0:nq, 0:nq])
            if not gm_done:
                for ct in range(12):
                    for k in range(8):
                        P.add("pe", "matmul", reads=[HK(), "w_gm", "w_gm2"], writes=["pA"], out=pA[:, ct * 128:ct * 128 + nq],
                              lhsT=w_gm[:, k, ct * 128:(ct + 1) * 128], rhs=HT()[:, k, 0:nq], start=(k == 0), stop=(k == 7))
                pAv = pA[:, :].rearrange("p (c t) -> p c t", t=128)
                P.add("act", "activation", reads=["pA"], writes=["sga"], out=sga[:, :, 0:nq], in_=pAv[:, 0:4, 0:nq], func=AF.Silu)
                P.add("act", "activation", reads=["pA"], writes=["sma"], out=sma[:, :, 0:nq], in_=pAv[:, 4:12, 0:nq], func=AF.Sigmoid)
            tt("dve", AOgT[:, :, 0:nq], pT[:, 0:4, 0:nq], sga[:, :, 0:nq], ALU.mult, ["pT", "sga"], ["AOgT"])
            for ct in range(8):
                for k in range(4):
                    P.add("pe", "matmul", reads=["AOgT", "w_oa"], writes=["pS"], out=pS[:, ct * 128:ct * 128 + nq],
                          lhsT=w_oa_s[:, k, ct * 128:(ct + 1) * 128], rhs=AOgT[:, k, 0:nq], start=(k == 0), stop=(k == 3))
            tt("dve", mgt[:, :, 0:nq], pS[:, :].rearrange("p (c t) -> p c t", t=128)[:, :, 0:nq], sma[:, :, 0:nq], ALU.mult,
               ["pS", "sma"], ["mgt"])
            dma("sp", mg_d[mtile[0], :, :].rearrange("p (k t) -> p k t", k=8)[:, :, mtile[1]:mtile[1] + nq],
                mgt[:, :, 0:nq], ["mgt"], [("mg_d", mtile[0])])

        def attention_fast(ktiles, mtile, nh, before_tail=None):
            nq = 128

            def qk(h):
                hp, h2 = h // 2, h % 2
                pr = slice(64 * h2, 64 * h2 + 64)
                Sb, skey = (pS, "pS") if h % 2 == 0 else (pA, "pA")
                for j, (slot, nk, tp_, halo) in enumerate(ktiles):
                    P.add("pe", "matmul", reads=[("KT", slot), "qT"], writes=[skey], out=Sb[:, (4 - j) * 128:(5 - j) * 128],
                          lhsT=KT[pr, hp, slot * 128:slot * 128 + 128], rhs=qT[pr, hp, 0:nq], start=True, stop=True)

            def softmax(h):
                Sb, skey = (pS, "pS") if h % 2 == 0 else (pA, "pA")
                st_, stk = stmp2[h % 2], "stmp%d" % (h % 2)
                PTb, ptk = PT2[h % 2], "PT%d" % (h % 2)
                P.add("dve", "scalar_tensor_tensor", reads=[skey, "BM"], writes=[stk], out=st_[:, :, :].rearrange("p t q -> p (t q)"),
                      in0=Sb[:, 0:640], scalar=0.125, in1=BM[:, h, :, :].rearrange("p t q -> p (t q)"),
                      op0=ALU.mult, op1=ALU.add)
                c_h = (5 - nh) * 128
                stf = st_[:, :, :].rearrange("p t q -> p (t q)")
                ptf = PTb[:, :, :].rearrange("p t q -> p (t q)")
                if nh < 5:
                    P.add("act", "activation", reads=[stk], writes=[ptk], out=ptf[:, 0:c_h], in_=stf[:, 0:c_h], func=AF.Exp)
                if nh > 0:
                    P.add("act", "activation", reads=[stk, "hb"], writes=[ptk], out=ptf[:, c_h:640], in_=stf[:, c_h:640],
                          func=AF.Exp, bias=hb[:, 0:1])

            def pv(h):
                PTb, ptk = PT2[h % 2], "PT%d" % (h % 2)
                for j, (slot, nk, tp_, halo) in enumerate(ktiles):
                    P.add("pe", "matmul", reads=[ptk, ("V", slot)], writes=["pO"],
                          out=pO[0:nq, h // 4, (h % 4) * 65:(h % 4) * 65 + 65],
                          lhsT=PTb[:, 4 - j, :], rhs=V[:, slot, h, :], start=(j == 0), stop=(j == 4))

            qk(0)
            for h in range(8):
                softmax(h)
                if h + 1 < 8:
                    qk(h + 1)
                pv(h)
            if before_tail is not None:
                before_tail()
            attention_tail(nq, mtile, True)


        own_tiles = [(NHALO + i * 128, 128, [(0, 128, 0)], i * 128) for i in range(16)]
        samp_tiles = [(NHALO + NOWN + s * 16, 16, [(0, 16, 1 + s)], NOWN + s * 16) for s in range(2)]

        dma("sp", bgate, bass.AP(tensor=b_ada.tensor, offset=2048, ap=[[0, 3], [1, 1024]]), [], ["sq"])
        P.add("act", "activation", reads=["cT"], writes=["scT"], out=scT[:, :, :], in_=cT, func=AF.Silu)
        for ch in range(6):
            wb = wada[ch % 2]
            wk = "wada%d" % (ch % 2)
            wload(wb, w_ada[:, ch * 512:(ch + 1) * 512], wk)
            for c4 in range(4):
                for k in range(8):
                    P.add("pe", "matmul", reads=[wk, "scT"], writes=["pA"], out=pA[:, c4 * 4:c4 * 4 + 3],
                          lhsT=wb[:, k, c4 * 128:(c4 + 1) * 128], rhs=scT[:, k, :], start=(k == 0), stop=(k == 7))
            tt("dve", modT[:, ch * 4:(ch + 1) * 4, :], pA[:, 0:16].rearrange("p (c b) -> p c b", b=4)[:, :, 0:3],
               bada[:, ch * 4:(ch + 1) * 4].rearrange("p (c o) -> p c o", o=1).to_broadcast([128, 4, 3]), ALU.add,
               ["pA", "bada"], ["modT"])
            if ch >= 4:
                for k in range(8):
                    P.add("pe", "matmul", reads=[wk, "scT"], writes=["pS"], out=pS[0:3, 0:512],
                          lhsT=scT[:, k, :], rhs=wb[:, k, :], start=(k == 0), stop=(k == 7))
                tt("dve", gate_tok[:, (ch - 4) * 512:(ch - 3) * 512], pS[0:3, 0:512],
                   bgate[:, (ch - 4) * 512:(ch - 3) * 512], ALU.add, ["pS", "sq"], ["qk_sb"])
        ts("dve", Amod[:, :, :], modT[:, 8:16, :], 1.0, None, ALU.add, None, ["modT"], ["Amod"])
        tt("dve", Amod[:, :, :], Amod[:, :, :], ng.rearrange("p (k o) -> p k o", o=1).to_broadcast([128, 8, 3]), ALU.mult,
           ["Amod", "ng"], ["Amod"])
        for which, (c0, nt) in enumerate(((0, 128), (128, 16), (144, 16))):
            for half in range(2):
                P.add("pe", "matmul", reads=["selT", "qk_sb"], writes=["pS"], out=pS[0:nt, half * 512:(half + 1) * 512],
                      lhsT=selT[:, c0:c0 + nt], rhs=gate_tok[:, half * 512:(half + 1) * 512], start=True, stop=True)
            cp("dve", gate_bc[0:nt, which, :], pS[0:nt, :], ["pS"], ["gate_bc"])

        P.barrier(dummy[:, 0:1])

        wload(w_u, w_in[:, 2048:2560], "w_u")
        def wstage(dst, src, key):
            dma("pool", dst, src, [], [key], nophase=True)

        wstage(wq_d[:, :], w_in[:, 0:1536], "wq_d")
        wstage(wgm_d[:, 0:512], w_in[:, 1536:2048], "wgm_d")
        wstage(wgm_d[:, 512:1536], w_in[:, 3072:4096], "wgm_d2")
        wstage(woa_d[:, :], w_oa[:, :], "woa_d")
        wstage(wglu_d[:, :], w_glu[:, :], "wglu_d")
        wstage(wgsms_d[:, 0:512], w_in[:, 2560:3072], "wgsms_d")
        wstage(wgsms_d[:, 512:1536], w_in[:, 4096:5120], "wgsms_d2")
        wstage(wos_d[:, :], w_os[:, :], "wos_d")
        wstage(wout_d[:, :], w_out[:, :], "wout_d")

        def wload2(dst, src, rkeys, key):
            dma("sp", dst, src.rearrange("(kt p) n -> p kt n", p=128), rkeys, [key])

        P.defer_start()
        PRM = {}

        def pt(name):
            if name not in PRM:
                PRM[name] = len(PRM)
                assert len(PRM) <= 34
            return prm[:, PRM[name], :]

        k_ = ["prm"]
        lamst = scr[0:32, 6496:6624]
        dma("sp", lamst[:, 0:64], lam_re[:, :], [], ["sp_lam"])
        dma("sp", lamst[:, 64:128], lam_im[:, :], [], ["sp_lamb"])
        P.add("pe", "transpose", reads=["sp_lam", "sp_lamb", "identf"], writes=["pS"], out=pS[:, 0:32], in_=lamst,
              identity=identf[0:32, 0:32])
        cp("dve", pt("lam"), pS[:, 0:32], ["pS"], k_)
        cp("dve", pt("lr")[0:64, :], pt("lam")[0:64, :], k_, k_)
        cp("dve", pt("lr")[64:128, :], pt("lam")[0:64, :], k_, k_)
        cp("dve", pt("li")[0:64, :], pt("lam")[64:128, :], k_, k_)
        cp("dve", pt("li")[64:128, :], pt("lam")[64:128, :], k_, k_)
        dma("sp", pt("dt"), bass.AP(tensor=log_dt.tensor, offset=0, ap=[[0, 128], [1, 32]]), k_, k_)
        P.add("act", "activation", reads=k_, writes=k_, out=pt("dt"), in_=pt("dt"), func=AF.Exp)
        tt("dve", pt("x"), pt("lr"), pt("dt"), ALU.mult, k_, k_)
        ts("dve", pt("m"), pt("x"), 0.25, 1.0, ALU.mult, ALU.add, k_, k_)
        for cc in (1.0 / 3, 0.5, 1.0):
            P.add("dve", "scalar_tensor_tensor", reads=k_, writes=k_, out=pt("m"), in0=pt("x"), scalar=cc, in1=pt("m"),
                  op0=ALU.mult, op1=ALU.mult)
            ts("dve", pt("m"), pt("m"), 1.0, None, ALU.add, None, k_, k_)
        tt("dve", pt("ang"), pt("li"), pt("dt"), ALU.mult, k_, k_)
        ts("dve", pt("t0"), pt("ang"), 1.0 / (2 * np.pi), None, ALU.mult, None, k_, k_)
        cp("dve", ki32[:, :], pt("t0"), k_, ["ki32"])
        cp("dve", pt("t0"), ki32[:, :], ["ki32"], k_)
        P.add("dve", "scalar_tensor_tensor", reads=k_, writes=k_, out=pt("ang"), in0=pt("t0"), scalar=-2 * np.pi,
              in1=pt("ang"), op0=ALU.mult, op1=ALU.add)
        ts("dve", pt("psi"), pt("ang"), 1.0 / 32, None, ALU.mult, None, k_, k_)
        tt("dve", pt("p2"), pt("psi"), pt("psi"), ALU.mult, k_, k_)
        ts("dve", pt("s"), pt("p2"), -1.0 / 42, 1.0, ALU.mult, ALU.add, k_, k_)
        for cc in (-1.0 / 20, -1.0 / 6):
            P.add("dve", "scalar_tensor_tensor", reads=k_, writes=k_, out=pt("s"), in0=pt("p2"), scalar=cc, in1=pt("s"),
                  op0=ALU.mult, op1=ALU.mult)
            ts("dve", pt("s"), pt("s"), 1.0, None, ALU.add, None, k_, k_)
        tt("dve", pt("s"), pt("s"), pt("psi"), ALU.mult, k_, k_)
        ts("dve", pt("c"), pt("p2"), -1.0 / 56, 1.0, ALU.mult, ALU.add, k_, k_)
        for cc in (-1.0 / 30, -1.0 / 12, -0.5):
            P.add("dve", "scalar_tensor_tensor", reads=k_, writes=k_, out=pt("c"), in0=pt("p2"), scalar=cc, in1=pt("c"),
                  op0=ALU.mult, op1=ALU.mult)
            ts("dve", pt("c"), pt("c"), 1.0, None, ALU.add, None, k_, k_)

        def csquare(r, i, t1, t2):
            tt("dve", t1, r, r, ALU.mult, k_, k_)
            tt("dve", t2, i, i, ALU.mult, k_, k_)
            P.add("dve", "scalar_tensor_tensor", reads=k_, writes=k_, out=i, in0=r, scalar=2.0, in1=i,
                  op0=ALU.mult, op1=ALU.mult)
            tt("dve", r, t1, t2, ALU.subtract, k_, k_)

        for _ in range(5):
            csquare(pt("c"), pt("s"), pt("t0"), pt("t1"))
        tt("dve", pt("ar"), pt("m"), pt("c"), ALU.mult, k_, k_)
        tt("dve", pt("ai"), pt("m"), pt("s"), ALU.mult, k_, k_)
        tt("dve", pt("den"), pt("lr"), pt("lr"), ALU.mult, k_, k_)
        tt("dve", pt("t0"), pt("li"), pt("li"), ALU.mult, k_, k_)
        tt("dve", pt("den"), pt("den"), pt("t0"), ALU.add, k_, k_)
        P.add("dve", "reciprocal", reads=k_, writes=k_, out=pt("den"), in_=pt("den"))
        ts("dve", pt("nr"), pt("ar"), -1.0, None, ALU.add, None, k_, k_)
        tt("dve", pt("t0"), pt("nr"), pt("lr"), ALU.mult, k_, k_)
        tt("dve", pt("t1"), pt("ai"), pt("li"), ALU.mult, k_, k_)
        tt("dve", pt("cor"), pt("t0"), pt("t1"), ALU.add, k_, k_)
        tt("dve", pt("cor"), pt("cor"), pt("den"), ALU.mult, k_, k_)
        tt("dve", pt("t0"), pt("ai"), pt("lr"), ALU.mult, k_, k_)
        tt("dve", pt("t1"), pt("nr"), pt("li"), ALU.mult, k_, k_)
        tt("dve", pt("coi"), pt("t0"), pt("t1"), ALU.subtract, k_, k_)
        tt("dve", pt("coi"), pt("coi"), pt("den"), ALU.mult, k_, k_)
        ts("dve", pt("cois"), pt("coi"), cmask[:, 0:1], None, ALU.mult, None, k_ + ["cmask"], k_)
        dma("sp", Bstk[0:64, :, :], b_re.rearrange("g n c -> n g c"), [], ["Bstk"])
        dma("sp", Bstk[64:128, :, :], b_im.rearrange("g n c -> n g c"), [], ["Bstk2"])
        cp("dve", Bsw[0:64, :, :], Bstk[64:128, :, :], ["Bstk", "Bstk2"], ["Bsw"])
        cp("dve", Bsw[64:128, :, :], Bstk[0:64, :, :], ["Bstk", "Bstk2"], ["Bsw"])

        def bc_c(name):
            return pt(name).rearrange("p (g o) -> p g o", o=1).to_broadcast([128, 32, 16])

        tt("dve", Bstk[:, :, :], Bstk[:, :, :], bc_c("cor"), ALU.mult, ["Bstk", "Bstk2", "Bsw"] + k_, ["Bstk"])
        tt("dve", Bsw[:, :, :], Bsw[:, :, :], bc_c("cois"), ALU.mult, ["Bsw"] + k_, ["Bsw"])
        tt("dve", Bstk[:, :, :], Bstk[:, :, :], Bsw[:, :, :], ALU.add, ["Bstk", "Bsw"], ["Bstk"])
        cp("dve", Bsw[0:64, :, :], Bstk[64:128, :, :], ["Bstk"], ["Bsw"])
        cp("dve", Bsw[64:128, :, :], Bstk[0:64, :, :], ["Bstk"], ["Bsw"])
        cp("dve", Bbb[:, :, :], Bstk[:, :, :], ["Bstk"], ["Bbb"])
        kp = ["PW"]
        P.add("dve", "memset", reads=[], writes=kp, ap=PW1[:, :, 0:1], constant=1.0)
        P.add("dve", "memset", reads=[], writes=kp, ap=PW2[:, :, 0:1], constant=0.0)
        cp("dve", PW1[:, :, 1:2], pt("ar").rearrange("p (g o) -> p g o", o=1), k_ + kp, kp)
        cp("dve", PW2[:, :, 1:2], pt("ai").rearrange("p (g o) -> p g o", o=1), k_ + kp, kp)
        tw = scr[:, 4192:4704].rearrange("p (a g t) -> p a g t", a=2, g=32)
        for kk in (1, 2, 4, 8):
            src_r, src_i = PW1[:, :, 1:kk + 1], PW2[:, :, 1:kk + 1]
            kr = PW1[:, :, kk:kk + 1].to_broadcast([128, 32, kk])
            ki = PW2[:, :, kk:kk + 1].to_broadcast([128, 32, kk])
            t_a, t_b = tw[:, 0, :, 0:kk], tw[:, 1, :, 0:kk]
            tt("dve", t_a, src_r, kr, ALU.mult, kp, ["sp_tw"])
            tt("dve", t_b, src_i, ki, ALU.mult, kp, ["sp_tw"])
            tt("dve", PW1[:, :, kk + 1:2 * kk + 1], t_a, t_b, ALU.subtract, ["sp_tw"], kp)
            tt("dve", t_a, src_r, ki, ALU.mult, kp, ["sp_tw"])
            tt("dve", t_b, src_i, kr, ALU.mult, kp, ["sp_tw"])
            tt("dve", PW2[:, :, kk + 1:2 * kk + 1], t_a, t_b, ALU.add, ["sp_tw"], kp)
        PW2s = scr[:, 4704:5248].rearrange("p (g t) -> p g t", g=32)
        ts("dve", PW2s, PW2[:, :, :], cmask[:, 0:1], None, ALU.mult, None, kp + ["cmask"], ["sp_pw2s"])
        cp("dve", pt("Y1"), PW1[:, :, 16], kp, k_)
        cp("dve", pt("Y2"), PW2s[:, :, 16], ["sp_pw2s"], k_)
        cp("dve", pt("e1r"), PW1[:, :, 16], kp, k_)
        cp("dve", pt("e1i"), PW2[:, :, 16], kp, k_)
        for _ in range(7):
            csquare(pt("e1r"), pt("e1i"), pt("t0"), pt("t1"))
        cp("dve", pt("e2r"), pt("e1r"), k_, k_)
        cp("dve", pt("e2i"), pt("e1i"), k_, k_)
        csquare(pt("e2r"), pt("e2i"), pt("t0"), pt("t1"))
        ts("dve", pt("e1i"), pt("e1i"), cmask[:, 0:1], None, ALU.mult, None, k_ + ["cmask"], k_)
        ts("dve", pt("e2i"), pt("e2i"), cmask[:, 0:1], None, ALU.mult, None, k_ + ["cmask"], k_)
        for gq in range(8):
            gs_ = slice(gq * 4, gq * 4 + 4)
            t_a = scr[:, 2144:3168].rearrange("p (g t c) -> p g t c", g=4, t=16)
            t_b = scr[:, 3168:4192].rearrange("p (g t c) -> p g t c", g=4, t=16)
            tt("dve", t_a, Bstk[:, gs_, :].rearrange("p g (o c) -> p g o c", o=1).to_broadcast([128, 4, 16, 16]),
               PW1[:, gs_, 0:16].rearrange("p g (t o) -> p g t o", o=1).to_broadcast([128, 4, 16, 16]), ALU.mult,
               ["Bstk"] + kp, ["sp_ta"])
            tt("pool", t_b, Bsw[:, gs_, :].rearrange("p g (o c) -> p g o c", o=1).to_broadcast([128, 4, 16, 16]),
               PW2s[:, gs_, 0:16].rearrange("p g (t o) -> p g t o", o=1).to_broadcast([128, 4, 16, 16]), ALU.mult,
               ["Bsw", "sp_pw2s"], ["sp_tb"])
            tt("dve", Pst[:, gq // 2, :, (gq % 2) * 4:(gq % 2) * 4 + 4, :].rearrange("p t g c -> p g t c"), t_a, t_b, ALU.add, ["sp_ta", "sp_tb"], ["Pst"])
        pSb = bfv(pS[:, 0:512])
        for q in range(4):
            for half in range(2):
                for tl in range(8):
                    tau = half * 8 + tl
                    P.add("pe", "transpose", reads=["Pst", "ident"], writes=["pS"], out=pSb[:, tl, :],
                          in_=Pst[:, q, tau, :, :].rearrange("p g c -> p (g c)"), identity=ident[:, :])
                for tl in range(8):
                    j = 15 - (half * 8 + tl)
                    ts("dve", W1z[:, q, 0, j, :], pSb[:, tl, :], cmask[:, 2:3], None, ALU.mult, None, ["pS", "cmask"], ["W1z"])
                    P.add("act", "activation", reads=["pS", "cmask"], writes=["W1z"], out=W1z[:, q, 1, j, :],
                          in_=pSb[:, tl, :], func=AF.Copy, scale=cmask[:, 3:4])
        def cmul_acc(dst, x, y1, y2, t_sw, t_a, keys_r, keys_w):
            cp("dve", t_sw[0:64], x[64:128], keys_r, ["scan_t"], nophase=True)
            cp("dve", t_sw[64:128], x[0:64], keys_r, ["scan_t"], nophase=True)
            tt("dve", t_a, x, y1, ALU.mult, keys_r + ["prm"], ["scan_t2"], nophase=True)
            tt("dve", t_sw, t_sw, y2, ALU.mult, ["scan_t", "prm"], ["scan_t"], nophase=True)
            tt("dve", dst, dst, t_a, ALU.add, ["scan_t2"] + keys_w, keys_w, nophase=True)
            tt("dve", dst, dst, t_sw, ALU.add, ["scan_t"] + keys_w, keys_w, nophase=True)

        sc_a = prm[:, 34, :]
        sc_b = prm[:, 35, :]
        Q1 = scr[:, 0:544].rearrange("p (m g) -> p m g", g=32)
        Q2 = scr[:, 544:1088].rearrange("p (m g) -> p m g", g=32)
        Q2s = scr[:, 1088:1632].rearrange("p (m g) -> p m g", g=32)
        tsw8 = scr[:, 1632:1888].rearrange("p (b g) -> p b g", g=32)
        ta8 = scr[:, 1888:2144].rearrange("p (b g) -> p b g", g=32)
        tb1 = scr[:, 2144:4064].rearrange("p (b i g) -> p b i g", b=4, i=15)
        tb2 = scr[:, 4064:5984].rearrange("p (b i g) -> p b i g", b=4, i=15)
        qa = scr[:, 5984:6240].rearrange("p (m g) -> p m g", g=32)
        qb = scr[:, 6240:6496].rearrange("p (m g) -> p m g", g=32)
        kq = ["Q"]
        P.add("dve", "memset", reads=[], writes=kq, ap=Q1[:, 0, :], constant=1.0)
        P.add("dve", "memset", reads=[], writes=kq, ap=Q2[:, 0, :], constant=0.0)
        cp("dve", Q1[:, 1, :], PW1[:, :, 16], kp + kq, kq)
        cp("dve", Q2[:, 1, :], PW2[:, :, 16], kp + kq, kq)
        for kk in (1, 2, 4, 8):
            src_r, src_i = Q1[:, 1:kk + 1, :], Q2[:, 1:kk + 1, :]
            kr = Q1[:, kk:kk + 1, :].to_broadcast([128, kk, 32])
            ki = Q2[:, kk:kk + 1, :].to_broadcast([128, kk, 32])
            t_a, t_b = qa[:, 0:kk, :], qb[:, 0:kk, :]
            tt("dve", t_a, src_r, kr, ALU.mult, kq, ["qa"])
            tt("dve", t_b, src_i, ki, ALU.mult, kq, ["qb"])
            tt("dve", Q1[:, kk + 1:2 * kk + 1, :], t_a, t_b, ALU.subtract, ["qa", "qb"] + kq, kq)
            tt("dve", t_a, src_r, ki, ALU.mult, kq, ["qa"])
            tt("dve", t_b, src_i, kr, ALU.mult, kq, ["qb"])
            tt("dve", Q2[:, kk + 1:2 * kk + 1, :], t_a, t_b, ALU.add, ["qa", "qb"] + kq, kq)
        ts("dve", Q2s, Q2, cmask[:, 0:1], None, ALU.mult, None, kq + ["cmask"], kq)
        y1b8 = pt("Y1").rearrange("p (o g) -> p o g", o=1).to_broadcast([128, 8, 32])
        y2b8 = pt("Y2").rearrange("p (o g) -> p o g", o=1).to_broadcast([128, 8, 32])
        P.defer_stop()
        P.add("pool", "memset", reads=[], writes=["XS", "XSfree"], nophase=True, ap=XS[:, 0, :], constant=0.0)
        for seg in range(4):
            if seg < 3:
                seg_tiles = [(seg * NOWN + i * 128, 128, [(0, 128, 0)], i * 128) for i in range(16)]
            else:
                seg_tiles = own_tiles + samp_tiles
            srcA = x_prev if seg < 3 else x_all
            NA = len(seg_tiles)

            def a_front(n):
                (row0, nt, mods, col0) = seg_tiles[n]
                i = n % 2
                xt, xk = xin[i], "xin%d" % i
                xn, xnk = xn2[i], "xn%d" % i
                sst, ssk = ss2[i], "ss%d" % i
                dma("sp", xt[0:nt, :], srcA[row0:row0 + nt, :], [], [xk])
                P.add("act", "activation", reads=[xk], writes=[xnk, ssk], out=xn[0:nt, :], in_=xt[0:nt, :],
                      func=AF.Square, accum_out=sst[0:nt, 0:1])
                P.add("act", "activation", reads=[ssk, "epsc"], writes=[ssk], out=sst[0:nt, 2:3], in_=sst[0:nt, 0:1], func=AF.Ln,
                      scale=1.0 / D, bias=epsc[0:nt, 0:1])
                P.add("act", "activation", reads=[ssk], writes=[ssk], out=sst[0:nt, 3:4], in_=sst[0:nt, 2:3], func=AF.Exp, scale=-0.5)
                ts("dve", xn[0:nt, :], xt[0:nt, :], sst[0:nt, 3:4], None, ALU.mult, None, [xk, ssk], [xnk])

            def a_T(n):
                (row0, nt, mods, col0) = seg_tiles[n]
                i = n % 2
                xn, xnk = xn2[i], "xn%d" % i
                for k in range(8):
                    tb, tk = (TA[i], TAk[i]) if k < 4 else (TB[i], TBk[i])
                    P.add("pe", "transpose", reads=[xnk, "ident"], writes=[tk], out=tb[:, k % 4, 0:nt],
                          in_=xn[0:nt, k * 128:(k + 1) * 128], identity=ident[0:nt, 0:nt])

            def a_evac(n):
                (row0, nt, mods, col0) = seg_tiles[n]
                i = n % 2
                hT, hk = hT2[i], "hT%d" % i
                for k in range(8):
                    tb, tk = (TA[i], TAk[i]) if k < 4 else (TB[i], TBk[i])
                    for (c0, c1, bsel) in mods:
                        if k < 4:
                            ts("dve", hT[:, k, c0:c1], tb[:, k % 4, c0:c1], Amod[:, k, bsel:bsel + 1], modT[:, k, bsel:bsel + 1],
                               ALU.mult, ALU.add, [tk, "Amod", "modT"], [hk])
                        else:
                            P.add("act", "activation", reads=[tk, "Amod", "modT"], writes=[hk], out=hT[:, k, c0:c1],
                                  in_=tb[:, k % 4, c0:c1], func=AF.Identity, scale=Amod[:, k, bsel:bsel + 1],
                                  bias=modT[:, k, bsel:bsel + 1])

            def a_mm(n):
                (row0, nt, mods, col0) = seg_tiles[n]
                i = n % 2
                hT, hk = hT2[i], "hT%d" % i
                for q in range(4):
                    for k in range(8):
                        P.add("pe", "matmul", reads=[hk, "w_u"], writes=[UBk[i]], out=UB[i][:, q * 128:q * 128 + nt],
                              lhsT=w_u[:, k, q * 128:(q + 1) * 128], rhs=hT[:, k, 0:nt], start=(k == 0), stop=(k == 7))

            def a_uevac(n, seg=seg):
                (row0, nt, mods, col0) = seg_tiles[n]
                i = n % 2
                ts("dve", uT[:, :, col0:col0 + nt], UB[i].rearrange("p (q t) -> p q t", q=4)[:, :, 0:nt],
                   pmask[:, seg:seg + 1], None, ALU.mult, None, [UBk[i], "pmask"], ["uT"])

            a_front(0)
            if NA > 1:
                a_front(1)
            a_T(0)
            a_evac(0)
            for n in range(NA):
                if n + 1 < NA:
                    a_T(n + 1)
                a_mm(n)
                if n + 1 < NA:
                    a_evac(n + 1)
                if n + 2 < NA:
                    a_front(n + 2)
                a_uevac(n)
                if seg == 0:
                    P.flush(34)
            if seg == 0:
                P.flush(None)
            nch = 128 if seg < 3 else NCH
            wbanks = [(pA, 0, "pA0"), (pA, 512, "pA1"), (pS, 0, "pS0"), (pS, 512, "pS1")]
            for q in range(4):
                for par in range(2):
                    for j in range(16):
                        for s_ in range(4):
                            bk, off, bkey = wbanks[s_]
                            P.add("pe", "matmul", reads=["uT", "W1z"], writes=[bkey], out=bk[:, off:off + nch],
                                  lhsT=W1z[32 * s_:32 * s_ + 32, q, par, j, :],
                                  rhs=uT[32 * s_:32 * s_ + 32, q, 0:nch * 16].rearrange("p (k i) -> p k i", i=16)[:, :, j],
                                  start=(j == 0), stop=(j == 15), tile_position=(32 * s_, 0))
                    for s_ in range(4):
                        bk, off, bkey = wbanks[s_]
                        g = q * 8 + s_ * 2 + par
                        if s_ % 2 == 0:
                            cp("dve", XS[:, 1:129, g], bk[:, off:off + 128], [bkey, "XSfree"], [("XSg", g)], nophase=True)
                        else:
                            P.add("act", "activation", reads=[bkey, "XSfree"], writes=[("XSg", g)], nophase=True,
                                  out=XS[:, 1:129, g], in_=bk[:, off:off + 128], func=AF.Copy)
                        if seg == 3:
                            cp("dve", WSs[:, :, g], bk[:, off + 128:off + 130], [bkey], ["WSs"])
            P.add("dve", "memset", reads=[("XSg", g_) for g_ in range(32)], writes=["XS"], nophase=True, ap=dummy[:, 1:2],
                  constant=0.0)
            XSb = XS[:, 1:129, :].rearrange("p (b i) g -> p b i g", i=16)
            for i in range(1, 16):
                cmul_acc(XSb[:, :, i, :], XSb[:, :, i - 1, :], y1b8, y2b8, tsw8, ta8, ["XS"], ["XS"])
            for bb in range(8):
                cmul_acc(XS[:, 16 * (bb + 1), :], XS[:, 16 * bb, :], Q1[:, 16, :], Q2s[:, 16, :], sc_a, sc_b, ["XS", "Q"], ["XS"])
            Cst = XS[:, 0:128, :].rearrange("p (b i) g -> p b i g", i=16)
            for b0 in ((0, 4) if seg == 3 else ()):
                Ch = Cst[:, b0:b0 + 4, 0, :]
                cp("dve", tsw8[0:64, 0:4, :], Ch[64:128], ["XS"], ["scan_t"], nophase=True)
                cp("dve", tsw8[64:128, 0:4, :], Ch[0:64], ["XS"], ["scan_t"], nophase=True)
                tt("dve", tb1, Ch.rearrange("p b (o g) -> p b o g", o=1).to_broadcast([128, 4, 15, 32]),
                   Q1[:, 1:16, :].rearrange("p (o m) g -> p o m g", o=1).to_broadcast([128, 4, 15, 32]), ALU.mult,
                   ["XS", "Q"], ["tb1"], nophase=True)
                tt("dve", tb2, tsw8[:, 0:4, :].rearrange("p b (o g) -> p b o g", o=1).to_broadcast([128, 4, 15, 32]),
                   Q2s[:, 1:16, :].rearrange("p (o m) g -> p o m g", o=1).to_broadcast([128, 4, 15, 32]), ALU.mult,
                   ["scan_t", "Q"], ["tb2"], nophase=True)
                tt("dve", XSb[:, b0:b0 + 4, 0:15, :], XSb[:, b0:b0 + 4, 0:15, :], tb1, ALU.add, ["XS", "tb1"], ["XS"], nophase=True)
                tt("dve", XSb[:, b0:b0 + 4, 0:15, :], XSb[:, b0:b0 + 4, 0:15, :], tb2, ALU.add, ["XS", "tb2"], ["XS"], nophase=True)
            if seg < 3:
                cp("dve", XS[:, 0, :], XS[:, 128, :], ["XS"], ["XS", "XSfree"], nophase=True)
        P.add("pe", "transpose", reads=["XS", "identf"], writes=["pS"], nophase=True, out=pS[0:32, 0:128], in_=XS[:, 128, :],
              identity=identf[:, :])
        cp("dve", kout[0:32, 0:128], pS[0:32, 0:128], ["pS"], ["kout"], nophase=True)
        dma("sp", ssm_p[:, :], kout[0:32, 0:128], ["kout"], [], nophase=True)
        dma("sp", xin[0][0:64, 0:64], st_re[:, :], [], ["xin0"])
        dma("sp", xin[0][0:64, 64:128], st_im[:, :], [], ["xin0b"])
        P.add("pe", "transpose", reads=["xin0", "xin0b", "identf"], writes=["pS"], out=pS[:, 0:64], in_=xin[0][0:64, 0:128],
              identity=identf[0:64, 0:64])
        cp("dve", xs0[:, :, :], pS[:, 0:64].rearrange("p (s g) -> p s g", s=2), ["pS"], ["xs0"])
        for s_ in range(2):
            cmul_acc(WSs[:, s_, :], xs0[:, s_, :], pt("Y1"), pt("Y2"), sc_a, sc_b, ["xs0", "WSs"], ["WSs"])
        P.add("pe", "transpose", reads=["WSs", "identf"], writes=["pS"], out=pS[0:64, 0:128],
              in_=WSs[:, :, :].rearrange("p s g -> p (s g)"), identity=identf[:, :])
        cp("dve", vout[0:64, 0:128], pS[0:64, 0:128], ["pS"], ["vout"])
        dma("sp", ssm_s[:, :], vout[0:64, 0:128], ["vout"], [])

        P.barrier(dummy[:, 0:1])
        for h in range(8):
            dma("sp", BM[:, h, :, :],
                bass.AP(tensor=m_d.tensor, offset=h * 130 * 768 + 256, ap=[[767, 128], [128, 5], [1, 128]]),
                ["m_d"], ["BM"])
        P.add("pool", "memset", reads=[], writes=["BM"], ap=BM[0:64, :, 4, 64:128], constant=-30000.0)
        P.add("pool", "memset", reads=[], writes=["BM"], ap=BM[64:128, :, 0, 0:64], constant=-30000.0)
        P.add("pool", "memset", reads=[], writes=["Vones"], ap=V[:, :, :, 64:65], constant=1.0)
        wload2(w_qkv[:, :, :], wq_d[:, :], ["wq_d"], "w_qkv")
        wload2(w_gm[:, :, :], wgm_d[:, :], ["wgm_d", "wgm_d2"], "w_gm")
        wload2(w_oa_s[:, :, :], woa_d[:, :], ["woa_d"], "w_oa")
        tilesB = [(t * 128, 128, [(0, 128, 0)], 0) for t in range(20)]
        stB = {"nxt": rms_front(tilesB[0][0], 128), "pre": False}

        def b_prepare(n):
            cur = stB["nxt"]
            rms_back(cur, 128, tilesB[n][2])
            if n + 1 < 20:
                stB["nxt"] = rms_front(tilesB[n + 1][0], 128)
            CUR["i"] = cur
            qkv_mm(128, n >= 4)
            stB["cur"] = cur

        b_prepare(0)
        for t in range(20):
            slot = t % 6
            CUR["i"] = stB["cur"]
            if t < 4:
                qkv_tile(128, slot, False, None, None, 0, pre_mm=True)
                if t + 1 < 20:
                    b_prepare(t + 1)
            else:
                own = t - 4
                last = own >= 12
                qkv_tile(128, slot, True, k_last if last else None, v_last if last else None, (own - 12) * 128, gm=True,
                         pre_mm=True)
                ktiles = [((t - 4 + j) % 6, 128, 4 - j, (t - 4 + j) < 4) for j in range(5)]
                mycur = stB["cur"]

                def hook(t=t):
                    if t + 1 < 20:
                        b_prepare(t + 1)

                attention_fast(ktiles, (own, 0), max(0, 4 - own), before_tail=hook)
        kc = qkb[:, :]
        for s_ in range(2):
            for tk in range(4):
                dma("pool", kc[:, 0:512], cache_k[s_, tk * 128:(tk + 1) * 128, :], [], ["qkb"])
                for i in range(4):
                    P.add("pe", "transpose", reads=["qkb", "ident"], writes=["pT"], out=pT[:, 4 + i, :],
                          in_=kc[:, i * 128:(i + 1) * 128], identity=ident[:, :])
                P.add("act", "activation", reads=["pT"], writes=[("KT", tk)], out=KT[:, :, tk * 128:(tk + 1) * 128],
                      in_=pT[:, 4:8, :], func=AF.Copy)
                dma("pool", V[:, tk, :, 0:64], cache_v[s_, tk * 128:(tk + 1) * 128, :].rearrange("p (h d) -> p h d", d=64),
                    ["Vones"], [("V", tk)])
            (row0, nt, mods, col0) = samp_tiles[s_]
            rms_back(rms_front(row0, nt), nt, mods)
            qkv_tile(16, 4, True, k_samp, v_samp, s_ * 16, fold=False)
            ktiles = [(0, 128, 4, False), (1, 128, 3, False), (2, 128, 2, False), (3, 128, 1, False), (4, 16, 0, False)]
            attention(16, ktiles, (16, s_ * 16))

        P.barrier(dummy[:, 0:1])
        P.add("pool", "memset", reads=[], writes=["CAz"], ap=arena[:, 0:8704], constant=0.0)
        P.add("dve", "memset", reads=[], writes=["CAz2"], ap=arena[:, 8704:17408], constant=0.0)
        for q in range(4):
            dma("sp", Cin[:, q, 0:64], c_re[q * 128:(q + 1) * 128, :], [], ["Cin"])
            dma("sp", Cin[:, q, 64:128], c_im[q * 128:(q + 1) * 128, :], [], ["Cin2"])
        for q in range(4):
            P.add("pe", "transpose", reads=["Cin", "Cin2", "identf"], writes=["pS"], out=pS[:, q * 128:(q + 1) * 128],
                  in_=Cin[:, q, :], identity=identf[:, :])
        cp("dve", Cstk[:, :], pS[:, 0:512], ["pS"], ["Cstk"])
        cp("dve", Csw[0:64, :], Cstk[64:128, :], ["Cstk"], ["Csw"])
        cp("dve", Csw[64:128, :], Cstk[0:64, :], ["Cstk"], ["Csw"])
        AA1 = sq[:, 0:544].rearrange("p (g t) -> p g t", g=32)
        AA2 = xin[1][:, 0:544].rearrange("p (g t) -> p g t", g=32)
        ts("dve", AA1, PW1[:, :, :], cmask[:, 1:2], None, ALU.mult, None, ["PW", "cmask"], ["sq"])
        ts("dve", AA2, PW2[:, :, :], -1.0, None, ALU.mult, None, ["PW"], ["xin1"])
        Cst4 = Cstk[:, :].rearrange("p (s e c) -> p s e c", s=16, e=2)
        Csw4 = Csw[:, :].rearrange("p (s e c) -> p s e c", s=16, e=2)
        AA1v = AA1.rearrange("p (s e) t -> p s e t", e=2)
        AA2v = AA2.rearrange("p (s e) t -> p s e t", e=2)
        CAzv = CAz.rearrange("p (s e) t r c -> p s e t r c", e=2)
        for s0 in range(0, 16, 3):
            ns = min(3, 16 - s0)
            for e in range(2):
                t_a = qk_sb[:, 0:ns * 272].rearrange("p (s t c) -> p s t c", s=ns, t=17)
                t_b = xin[0][:, 0:ns * 272].rearrange("p (s t c) -> p s t c", s=ns, t=17)
                tt("dve", t_a, Cst4[:, s0:s0 + ns, e, :].rearrange("p s (o c) -> p s o c", o=1).to_broadcast([128, ns, 17, 16]),
                   AA1v[:, s0:s0 + ns, e, :].rearrange("p s (t o) -> p s t o", o=1).to_broadcast([128, ns, 17, 16]), ALU.mult,
                   ["Cstk", "sq"], ["qk_sb"])
                tt("pool", t_b, Csw4[:, s0:s0 + ns, e, :].rearrange("p s (o c) -> p s o c", o=1).to_broadcast([128, ns, 17, 16]),
                   AA2v[:, s0:s0 + ns, e, :].rearrange("p s (t o) -> p s t o", o=1).to_broadcast([128, ns, 17, 16]), ALU.mult,
                   ["Csw", "xin1"], ["xin0"])
                tt("dve", CAzv[:, s0:s0 + ns, e, :, e, :], t_a, t_b, ALU.add, ["qk_sb", "xin0", "CAz", "CAz2"], ["CAz"])
        for g in range(32):
            q, g8 = g // 8, g % 8
            pk = "pS%d" % (g % 2)
            P.add("pe", "matmul", reads=["Bbb", "CAz"], writes=[pk], out=pS[:, (g % 2) * 512:(g % 2) * 512 + 272],
                  lhsT=Bbb[:, :, :].rearrange("p g c -> p (g c)")[:, q * 128:(q + 1) * 128], rhs=CAz[:, g, :, g % 2, :], start=True, stop=True)
            if g % 2 == 0:
                ts("dve", Kblk[:, q, :, g8 * 16:(g8 + 1) * 16],
                   pS[:, 0:256].rearrange("p (t c) -> p t c", t=16), cmask[:, 4 + g8:5 + g8], None,
                   ALU.mult, None, [pk, "cmask"], [("Kb", g)])
            else:
                P.add("act", "activation", reads=[pk, "cmask"], writes=[("Kb", g)], out=Kblk[:, q, :, g8 * 16:(g8 + 1) * 16],
                      in_=pS[:, 512:768].rearrange("p (t c) -> p t c", t=16), func=AF.Copy, scale=cmask[:, 4 + g8:5 + g8])
        P.add("dve", "memset", reads=[("Kb", g_) for g_ in range(32)], writes=["Kblk"], ap=dummy[:, 1:2], constant=0.0)
        for q in range(4):
            P.add("dve", "scalar_tensor_tensor", reads=["Kblk", "identf", "bada"], writes=["Kblk"], out=Kblk[:, q, 0, :],
                  in0=identf[:, :], scalar=dq[:, q:q + 1], in1=Kblk[:, q, 0, :], op0=ALU.mult, op1=ALU.add)
        cp("dve", Xb[:, :, 0:128], XS[:, 0:128, :].rearrange("p k g -> p g k"), ["XS"], ["Xb"])
        cp("dve", Xb[:, :, 128:130], xs0[:, :, :].rearrange("p s g -> p g s"), ["xs0"], ["Xb"])
        blocks = [(b0 * 512, 512) for b0 in range(4)] + [(2048, 32)]
        for bi, (t0, ntk) in enumerate(blocks):
            nk = ntk // 16
            k0 = t0 // 16
            for q in range(4):
                bank = pA if q % 2 == 0 else pS
                bkey = "pA" if q % 2 == 0 else "pS"
                for tau in range(16):
                    ov = bank[:, 0:ntk].rearrange("p (k i) -> p k i", i=16)[:, :, tau:16]
                    uv = uT[:, q, t0:t0 + ntk].rearrange("p (k i) -> p k i", i=16)[:, :, 0:16 - tau]
                    P.add("pe", "matmul", reads=["uT", "Kblk"], writes=[bkey], out=ov, lhsT=Kblk[:, q, tau, :], rhs=uv,
                          start=(tau == 0), stop=False)
                for s_ in range(4):
                    for i in range(16):
                        for e in range(2):
                            g = q * 8 + s_ * 2 + e
                            last = (s_ == 3 and i == 15 and e == 1)
                            P.add("pe", "matmul", reads=["Xb", "CAz"], writes=[bkey],
                                  out=bank[32 * s_:32 * s_ + 32, 0:ntk].rearrange("p (k i) -> p k i", i=16)[:, :, i],
                                  lhsT=CAz[:, g, i + 1, :, :].rearrange("p r c -> p (r c)"), rhs=Xb[:, g, k0:k0 + nk], start=False, stop=last,
                                  tile_position=(0, 32 * s_), skip_group_check=True)
                P.add("act", "activation", reads=[bkey], writes=["yT"], out=yT[:, q, t0:t0 + ntk], in_=bank[:, 0:ntk],
                      func=AF.Copy)

        P.barrier(dummy[:, 0:1])
        wload2(w_glu_s, wglu_d[:, :], ["wglu_d"], "w_glu")
        wload2(w_gsms[:, :, :], wgsms_d[:, :], ["wgsms_d", "wgsms_d2"], "w_gs")
        wload2(w_os_s, wos_d[:, :], ["wos_d"], "w_os")
        dma("sp", w_out_s[:, :, :], wout_d.rearrange("(kt p) n -> p kt n", p=128), ["wout_d"], ["wada0", "wada1"])
        def body_c2(ti, tl, cur):
            (row0, nt, mods, col0) = tl
            xt, xk = xin[cur], "xin%d" % cur
            mti = (ti, 0) if ti < 16 else (16, (ti - 16) * 16)
            dma("sp", mgt[:, :, 0:nt], mg_d[mti[0], :, :].rearrange("p (k t) -> p k t", k=8)[:, :, mti[1]:mti[1] + nt],
                [("mg_d", mti[0])], ["mgt"])
            for ct in range(8):
                for k in range(4):
                    P.add("pe", "matmul", reads=["yT", "w_glu"], writes=["pS"], out=pS[:, ct * 128:ct * 128 + nt],
                          lhsT=w_glu_s[:, k, ct * 128:(ct + 1) * 128], rhs=yT[:, k, col0:col0 + nt], start=(k == 0), stop=(k == 3))
            pSv = pS[:, :].rearrange("p (c t) -> p c t", t=128)
            for ct in range(4):
                P.add("act", "activation", reads=["pS", "bada"], writes=["sgb"], out=sgb[:, ct, 0:nt], in_=pSv[:, 4 + ct, 0:nt],
                      func=AF.Sigmoid, bias=bglu[:, 4 + ct:5 + ct])
            for ct in range(12):
                for k in range(8):
                    P.add("pe", "matmul", reads=[HK(), "w_gs", "w_ms"], writes=["pA"], out=pA[:, ct * 128:ct * 128 + nt],
                          lhsT=w_gsms[:, k, ct * 128:(ct + 1) * 128], rhs=HT()[:, k, 0:nt], start=(k == 0), stop=(k == 7))
            pAv = pA[:, :].rearrange("p (c t) -> p c t", t=128)
            P.add("act", "activation", reads=["pA"], writes=["sgs"], out=sgs[:, :, 0:nt], in_=pAv[:, 0:4, 0:nt], func=AF.Silu)
            P.add("act", "activation", reads=["pA"], writes=["sms"], out=sms[:, :, 0:nt], in_=pAv[:, 4:12, 0:nt], func=AF.Sigmoid)
            for ct in range(4):
                P.add("dve", "scalar_tensor_tensor", reads=["pS", "sgb", "bada"], writes=["sgt"], out=sgt[:, ct, 0:nt],
                      in0=pSv[:, ct, 0:nt], scalar=bglu[:, ct:ct + 1], in1=sgb[:, ct, 0:nt], op0=ALU.add, op1=ALU.mult)
            tt("dve", sgt[:, :, 0:nt], sgt[:, :, 0:nt], sgs[:, :, 0:nt], ALU.mult, ["sgt", "sgs"], ["sgt"])
            for ct in range(8):
                for k in range(4):
                    P.add("pe", "matmul", reads=["sgt", "w_os"], writes=["pO"], out=pO[:, ct // 4, (ct % 4) * 128:(ct % 4) * 128 + nt],
                          lhsT=w_os_s[:, k, ct * 128:(ct + 1) * 128], rhs=sgt[:, k, 0:nt], start=(k == 0), stop=(k == 3))
            tt("dve", bst[:, :, 0:nt], pO[:, :, :].rearrange("p a (c t) -> p (a c) t", t=128)[:, :, 0:nt], sms[:, :, 0:nt],
               ALU.mult, ["pO", "sms"], ["bst"])
            tt("dve", bst[:, :, 0:nt], bst[:, :, 0:nt], mgt[:, :, 0:nt], ALU.add, ["bst", "mgt"], ["bst"])
            for half in range(2):
                for k in range(8):
                    P.add("pe", "matmul", reads=["bst", "wada0", "wada1"], writes=["pA"],
                          out=pA[0:nt, half * 512:(half + 1) * 512], lhsT=bst[:, k, 0:nt],
                          rhs=w_out_s[:, k, half * 512:(half + 1) * 512], start=(k == 0), stop=(k == 7))
            tt("dve", qk_sb[0:nt, :], pA[0:nt, 0:1024], gate_bc[0:nt, 0 if ti < 16 else ti - 15, :], ALU.mult,
               ["pA", "gate_bc"], ["qk_sb"])
            tt("dve", sq[0:nt, :], qk_sb[0:nt, :], xt[0:nt, :], ALU.add, ["qk_sb", xk], ["sq"])
            dma("sp", y_out[col0:col0 + nt, :], sq[0:nt, :], ["sq"], [])

        run_tiles(own_tiles + samp_tiles, body_c2)

        P.emit(nc)
    return nc


_NC = None


def kernel(x_prompt, x_sample, c_prompt, c_sample, cache_k, cache_v, state_ssm_re, state_ssm_im,
           norm_g, w_ada, b_ada, w_in, q_norm_g, k_norm_g, rel_bias, lambda_re, lambda_im, log_dt,
           b_re, b_im, c_re, c_im, d_skip, w_glu, b_glu, w_oa, w_os, w_out):
    global _NC
    f = lambda a: np.ascontiguousarray(np.asarray(a, dtype=np.float32))
    x_prompt, x_sample = f(x_prompt), f(x_sample)
    if _NC is None:
        _NC = build()
    nc = _NC
    ident = np.eye(128, dtype=np.float32)
    sel = np.zeros((3, 256), np.float32)
    sel[0, 0:128] = 1.0
    sel[1, 128:144] = 1.0
    sel[2, 144:160] = 1.0
    pidx = np.arange(128)
    cmask = np.zeros((128, 16), np.float32)
    cmask[:, 0] = np.where(pidx < 64, -1.0, 1.0)
    cmask[:, 1] = -cmask[:, 0]
    for par in range(2):
        cmask[:, 2 + par] = ((pidx // 16) % 2 == par)
    for g8 in range(8):
        cmask[:, 4 + g8] = (pidx // 16 == g8)
    in_maps = []
    for c in range(8):
        b, j = c // 4, c % 4
        t0 = j * NOWN
        halo = np.zeros((NHALO, D), np.float32) if j == 0 else x_prompt[b, t0 - NHALO:t0]
        xs = x_sample[2 * c:2 * c + 2].reshape(NSAMP, D)
        x_all = np.concatenate([halo, x_prompt[b, t0:t0 + NOWN], xs], axis=0)
        c3 = np.stack([f(c_prompt)[b], f(c_sample)[2 * c], f(c_sample)[2 * c + 1]])
        hbias = np.full((128, 1), -30000.0 if j == 0 else 0.0, np.float32)
        x_prev = np.zeros((3 * NOWN, D), np.float32)
        pmask = np.ones((128, 4), np.float32)
        for i in range(3):
            js = j - 3 + i
            if js >= 0:
                x_prev[i * NOWN:(i + 1) * NOWN] = x_prompt[b, js * NOWN:(js + 1) * NOWN]
            else:
                pmask[:, i] = 0.0
        selm = np.zeros((128, 24), np.float32)
        for jr in range(j):
            selm[:, (b * 4 + jr) * 3 + (j - 1 - jr)] = 1.0
        in_maps.append({
            "x_all": np.ascontiguousarray(x_all), "c3": np.ascontiguousarray(c3),
            "cache_k": f(cache_k)[0, 2 * c:2 * c + 2].reshape(2, 512, 512),
            "cache_v": f(cache_v)[0, 2 * c:2 * c + 2].reshape(2, 512, 512),
            "hbias": hbias, "sel": sel, "ident": ident, "cmask": cmask, "selm": selm, "x_prev": x_prev, "pmask": pmask,
            "st_re": f(state_ssm_re)[0, 2 * c:2 * c + 2].reshape(64, 64),
            "st_im": f(state_ssm_im)[0, 2 * c:2 * c + 2].reshape(64, 64),
            "lambda_re": f(lambda_re)[0], "lambda_im": f(lambda_im)[0], "log_dt": f(log_dt)[0],
            "b_re": f(b_re)[0], "b_im": f(b_im)[0], "c_re": f(c_re)[0].reshape(512, 64), "c_im": f(c_im)[0].reshape(512, 64),
            "d_skip": f(d_skip)[0], "w_glu": f(w_glu)[0], "b_glu": f(b_glu)[0], "w_os": f(w_os)[0],
            "norm_g": f(norm_g)[0], "w_ada": f(w_ada)[0], "b_ada": f(b_ada)[0], "w_in": f(w_in)[0],
            "q_norm_g": f(q_norm_g)[0], "k_norm_g": f(k_norm_g)[0], "rel_bias": f(rel_bias)[0],
            "w_oa": f(w_oa)[0], "w_out": f(w_out)[0],
        })
    res = run_bass_kernel_spmd(nc, in_maps, core_ids=list(range(8)))
    R = res.results
    y_prompt = np.zeros((2, 8192, D), np.float32)
    y_sample = np.zeros((16, 16, D), np.float32)
    nk_p = np.zeros((1, 2, 512, 8, 64), np.float32)
    nv_p = np.zeros((1, 2, 512, 8, 64), np.float32)
    sr_p = np.zeros((1, 2, 32, 64), np.float32)
    si_p = np.zeros((1, 2, 32, 64), np.float32)
    nk_s = np.zeros((1, 16, 16, 8, 64), np.float32)
    nv_s = np.zeros((1, 16, 16, 8, 64), np.float32)
    sr_s = np.zeros((1, 16, 32, 64), np.float32)
    si_s = np.zeros((1, 16, 32, 64), np.float32)
    for c in range(8):
        b, j = c // 4, c % 4
        r = R[c]
        y_prompt[b, j * NOWN:(j + 1) * NOWN] = r["y_out"][0:NOWN]
        y_sample[2 * c:2 * c + 2] = r["y_out"][NOWN:].reshape(2, 16, D)
        if j == 3:
            nk_p[0, b] = r["k_last"].reshape(512, 8, 64)
            nv_p[0, b] = r["v_last"].reshape(512, 8, 64)
        nk_s[0, 2 * c:2 * c + 2] = r["k_samp"].reshape(2, 16, 8, 64)
        if j == 3:
            sr_p[0, b] = r["ssm_p"][:, 0:64]
            si_p[0, b] = r["ssm_p"][:, 64:128]
        sr_s[0, 2 * c:2 * c + 2] = r["ssm_s"][:, 0:64].reshape(2, 32, 64)
        si_s[0, 2 * c:2 * c + 2] = r["ssm_s"][:, 64:128].reshape(2, 32, 64)
        nv_s[0, 2 * c:2 * c + 2] = r["v_samp"].reshape(2, 16, 8, 64)
    return (y_prompt, y_sample, nk_p, nv_p, sr_p, si_p, nk_s, nv_s, sr_s, si_s)
```

```python
import numpy as np
import os
STAGE = int(os.environ.get('KSTAGE', '99'))
SUB = int(os.environ.get('KSUB', '99'))
KRMS = int(os.environ.get('KRMS', '99'))
import ml_dtypes
from contextlib import ExitStack
import concourse.bass as bass
import concourse.mybir as mybir
from concourse.bass_utils import run_bass_kernel_spmd

F32 = mybir.dt.float32
BF16 = mybir.dt.bfloat16
ALU = mybir.AluOpType
AF = mybir.ActivationFunctionType
AX = mybir.AxisListType

D = 1024
NOWN = 2048
NHALO = 512
NSAMP = 32
EPS = 1e-6


PSUM_KEYS = ("pA", "pT", "pS", "pO", "pA0", "pA1", "pA2", "pS0", "pS1", "pO0", "pO1")


class Prog:
    ENG = ("pe", "act", "dve", "pool", "sp")

    def __init__(self):
        self.ops = []
        self.last_w = {}
        self.readers = {}

    def add(self, eng, name, reads=(), writes=(), dma=False, nophase=False, **kw):
        op = dict(eng=eng, name=name, kw=kw, dma=dma, deps=set(), idx=len(self.ops), sig=dma)
        reads = list(reads)
        if not nophase:
            reads.append("PHASE")
        writes = list(writes) + [r for r in reads if r in PSUM_KEYS]
        reads = [r for r in reads if r not in PSUM_KEYS]
        for r in reads:
            lw = self.last_w.get(r)
            if lw is not None:
                op["deps"].add(lw)
        for w in writes:
            lw = self.last_w.get(w)
            if lw is not None:
                op["deps"].add(lw)
            for rd in self.readers.get(w, ()):
                op["deps"].add(rd)
        for r in reads:
            self.readers.setdefault(r, []).append(op["idx"])
        for w in writes:
            self.last_w[w] = op["idx"]
            self.readers[w] = []
        op["deps"].discard(op["idx"])
        self.ops.append(op)
        return op

    def barrier(self, arena_ap):
        self.add("dve", "memset", reads=[], writes=["PHASE"], nophase=True, ap=arena_ap, constant=0.0)

    def emit(self, nc, ndma_sems=16):
        ops = self.ops
        for op in ops:
            nd = set()
            for d in op["deps"]:
                p = ops[d]
                if (not p["dma"]) and p["eng"] == op["eng"] and p["eng"] == "pe" and not op["dma"]:
                    continue
                nd.add(d)
            op["deps"] = nd
            for d in nd:
                ops[d]["sig"] = True
        cnt = {e: 0 for e in self.ENG}
        dcnt = {e: 0 for e in self.ENG}
        for op in ops:
            e = op["eng"]
            if op["dma"]:
                i = dcnt[e]
                dcnt[e] += 1
                op["sem"] = ("d", e, i % ndma_sems)
                op["val"] = 16 * (i // ndma_sems + 1)
            elif op["sig"]:
                cnt[e] += 1
                op["sem"] = ("c", e)
                op["val"] = cnt[e]
        with ExitStack() as st:
            sems = {}
            for e in self.ENG:
                sems[("c", e)] = st.enter_context(nc.semaphore("c_" + e))
                if dcnt[e]:
                    for i in range(ndma_sems):
                        sems[("d", e, i)] = st.enter_context(nc.semaphore("d_%s_%d" % (e, i)))
            block = st.enter_context(nc.Block())
            byeng = {e: [o for o in ops if o["eng"] == e] for e in self.ENG}

            def run(engname, eng):
                known = {}
                for op in byeng[engname]:
                    waits = {}
                    for d in op["deps"]:
                        p = ops[d]
                        waits[p["sem"]] = max(waits.get(p["sem"], 0), p["val"])
                    if op["dma"] and op["val"] > 16:
                        waits[op["sem"]] = max(waits.get(op["sem"], 0), op["val"] - 16)
                    for s, v in waits.items():
                        if known.get(s, 0) >= v:
                            continue
                        eng.wait_ge(sems[s], v)
                        known[s] = v
                    ins = getattr(eng, op["name"])(**op["kw"])
                    if op["sig"]:
                        ins.then_inc(sems[op["sem"]], 16 if op["dma"] else 1)
                last = {}
                for op in byeng[engname]:
                    if op["dma"]:
                        last[op["sem"]] = op["val"]
                for s, v in last.items():
                    if known.get(s, 0) < v:
                        eng.wait_ge(sems[s], v)

            block.tensor(lambda eng: run("pe", eng))
            block.scalar(lambda eng: run("act", eng))
            block.vector(lambda eng: run("dve", eng))
            block.gpsimd(lambda eng: run("pool", eng))
            block.sync(lambda eng: run("sp", eng))


def build():
    nc = bass.Bass("TRN2", target_bir_lowering=False)

    def din(name, shape, dt=F32):
        return nc.dram_tensor(name, list(shape), dt, kind="ExternalInput").ap()

    def dout(name, shape, dt=F32):
        return nc.dram_tensor(name, list(shape), dt, kind="ExternalOutput").ap()

    NTOK = NHALO + NOWN + NSAMP
    NT = NOWN + NSAMP
    NCH = NT // 16
    x_all = din("x_all", [NTOK, D])
    x_prev = din("x_prev", [3 * NOWN, D])
    pmask_d = din("pmask", [128, 4])
    c3 = din("c3", [3, D])
    cache_k = din("cache_k", [2, 512, 512])
    cache_v = din("cache_v", [2, 512, 512])
    st_re = din("st_re", [64, 64])
    st_im = din("st_im", [64, 64])
    hbias = din("hbias", [128, 1])
    sel = din("sel", [3, 256])
    ident_d = din("ident", [128, 128])
    cmask_d = din("cmask", [128, 16])
    selm_d = din("selm", [128, 24])
    norm_g = din("norm_g", [D])
    w_ada = din("w_ada", [D, 3 * D])
    b_ada = din("b_ada", [3 * D])
    w_in = din("w_in", [D, 5120])
    q_norm_g = din("q_norm_g", [64])
    k_norm_g = din("k_norm_g", [64])
    rel_bias = din("rel_bias", [8, 257])
    lam_re = din("lambda_re", [32, 64])
    lam_im = din("lambda_im", [32, 64])
    log_dt = din("log_dt", [32])
    b_re = din("b_re", [32, 64, 16])
    b_im = din("b_im", [32, 64, 16])
    c_re = din("c_re", [512, 64])
    c_im = din("c_im", [512, 64])
    d_skip = din("d_skip", [512])
    w_glu = din("w_glu", [512, 1024])
    b_glu = din("b_glu", [1024])
    w_oa = din("w_oa", [512, D])
    w_os = din("w_os", [512, D])
    w_out = din("w_out", [D, D])

    y_out = dout("y_out", [NT, D])
    k_last = dout("k_last", [512, 512])
    v_last = dout("v_last", [512, 512])
    k_samp = dout("k_samp", [NSAMP, 512])
    v_samp = dout("v_samp", [NSAMP, 512])
    ssm_p = dout("ssm_p", [32, 128])
    ssm_s = dout("ssm_s", [64, 128])

    e_d = nc.dram_tensor("e_d", [8, 768], F32, kind="Internal").ap()
    m_d = nc.dram_tensor("m_d", [8, 130 * 768], F32, kind="Internal").ap()
    mg_d = nc.dram_tensor("mg_d", [17, 128, 1024], BF16, kind="Internal").ap()
    sloc_d = nc.dram_tensor("sloc_d", [128, 32], F32, kind="Internal").ap()
    wq_d = nc.dram_tensor("wq_d", [D, 1536], BF16, kind="Internal").ap()
    wgm_d = nc.dram_tensor("wgm_d", [D, 1536], BF16, kind="Internal").ap()
    woa_d = nc.dram_tensor("woa_d", [512, D], BF16, kind="Internal").ap()
    wglu_d = nc.dram_tensor("wglu_d", [512, D], BF16, kind="Internal").ap()
    wgsms_d = nc.dram_tensor("wgsms_d", [D, 1536], BF16, kind="Internal").ap()
    wos_d = nc.dram_tensor("wos_d", [512, D], BF16, kind="Internal").ap()
    wout_d = nc.dram_tensor("wout_d", [D, D], BF16, kind="Internal").ap()
    sall_d = nc.dram_tensor("sall_d", [1024, 32], F32, kind="Internal").ap()

    P = Prog()
    st = ExitStack()
    with st:
        def sb(name, shape, dt=F32):
            return st.enter_context(nc.sbuf_tensor("s_" + name, list(shape), dt))

        def ps(name, shape, dt=F32):
            return st.enter_context(nc.psum_tensor("p_" + name, list(shape), dt))

        dummy = sb("dummy", [128, 2])
        epsc = sb("epsc", [128, 1])
        ident = sb("ident", [128, 128], BF16)
        identf = sb("identf", [128, 128])
        selT = sb("selT", [3, 256])
        hb = sb("hb", [128, 1])
        cmask = sb("cmask", [128, 16])
        selm = sb("selm", [128, 24])
        pmask = sb("pmask", [128, 4])
        stg = sb("stg", [68, 128])
        smalls = sb("smalls", [128, 68])
        cT = smalls[:, 0:24].rearrange("p (b k) -> p k b", b=3)
        bada = smalls[:, 24:48]
        ng = smalls[:, 48:56]
        dq = smalls[:, 56:60]
        bglu = smalls[:, 60:68]
        scT = sb("scT", [128, 8, 3], BF16)
        modT = sb("modT", [128, 24, 3])
        Amod = sb("Amod", [128, 8, 3])
        gate_bc = sb("gate_bc", [128, 3, 1024], BF16)
        gqk = sb("gqk", [128, 1024], BF16)
        gqg = sb("gqg", [128, 512], BF16)
        xin = [sb("xin%d" % i, [128, 1024]) for i in range(2)]
        xin2c = sb("xin2c", [128, 1024])
        ss2 = [sb("ss%d" % i, [128, 4]) for i in range(2)]
        xn2 = [sb("xn%d" % i, [128, 1024], BF16) for i in range(2)]
        hT2 = [sb("hT%d" % i, [128, 8, 128], BF16) for i in range(2)]
        e_s = xin[1][0:8, 0:768]
        qk_sb = sb("qk_sb", [128, 1024])
        sq = sb("sq", [128, 1024])
        ss16 = sb("ss16", [128, 16])
        qkb = sb("qkb", [128, 1024], BF16)
        kout = sb("kout", [128, 512])
        vout = sb("vout", [128, 512])
        qT = sb("qT", [128, 4, 128], BF16)
        stmp2 = [sb("stmp%d" % i, [128, 5, 128]) for i in range(2)]
        PT2 = [sb("PT%d" % i, [128, 5, 128], BF16) for i in range(2)]
        stmp, PT = stmp2[0], PT2[0]
        rden = sb("rden", [128, 8])
        AO = sb("AO", [128, 8, 64], BF16)
        sga = sb("sga", [128, 4, 128], BF16)
        sma = sb("sma", [128, 8, 128], BF16)
        AOgT = sb("AOgT", [128, 4, 128], BF16)
        mgt = sb("mgt", [128, 8, 128], BF16)
        uT = sb("uT", [128, 4, NT], BF16)
        XS = sb("XS", [128, 129, 32])
        prm = sb("prm", [128, 36, 32])
        PW1 = sb("PW1", [128, 32, 17])
        PW2 = sb("PW2", [128, 32, 17])
        Bstk = sb("Bstk", [128, 32, 16])
        Bsw = sb("Bsw", [128, 32, 16])
        Bbb = sb("Bbb", [128, 32, 16], BF16)
        Cstk = sb("Cstk", [128, 512])
        Csw = sb("Csw", [128, 512])
        xs0 = sb("xs0", [128, 2, 32])
        WSs = sb("WSs", [128, 2, 32])
        Gall = sb("Gall", [128, 8, 32])
        ki32 = sb("ki32", [128, 32], mybir.dt.int32)

        ARN = 45184
        arena = sb("arena", [128, ARN], BF16)

        def av(off, n):
            return arena[:, off:off + n]

        wada = [av(28672, 4096).rearrange("p (k n) -> p k n", k=8), av(32768, 4096).rearrange("p (k n) -> p k n", k=8)]
        w_out_s = av(24064, 8192).rearrange("p (k n) -> p k n", k=8)
        scr = av(28672, 16512).bitcast(F32)
        w_qkv = av(0, 12288).rearrange("p (k n) -> p k n", k=8)
        w_gm = av(12288, 12288).rearrange("p (k n) -> p k n", k=8)
        w_oa_s = av(24576, 4096).rearrange("p (k n) -> p k n", k=4)
        KT = av(28672, 3072).rearrange("p (k n) -> p k n", k=4)
        V = av(31744, 3120).rearrange("p (s h e) -> p s h e", s=6, h=8)
        BM = av(34880, 10240).bitcast(F32).rearrange("p (h t q) -> p h t q", h=8, t=5)
        w_u = av(0, 4096).rearrange("p (k n) -> p k n", k=8)
        W1z = av(4096, 16384).rearrange("p (q r j m) -> p q r j m", q=4, r=2, j=16)
        Pst = av(20480, 8192).rearrange("p (q t g c) -> p q t g c", q=4, t=16, g=8)
        CAz = av(0, 17408).rearrange("p (g t r c) -> p g t r c", g=32, t=17, r=2)
        Kblk = av(17408, 8192).rearrange("p (q t m) -> p q t m", q=4, t=16)
        Xb = av(25600, 4160).rearrange("p (g k) -> p g k", g=32)
        Cin = av(29760, 1024).bitcast(F32).rearrange("p (q m) -> p q m", q=4)
        yT = av(36608, 8320).rearrange("p (q n) -> p q n", q=4)
        w_glu_s = av(0, 4096).rearrange("p (k n) -> p k n", k=4)
        w_gsms = av(4096, 12288).rearrange("p (k n) -> p k n", k=8)
        w_os_s = av(16384, 4096).rearrange("p (k n) -> p k n", k=4)
        sgb = av(20480, 512).rearrange("p (k n) -> p k n", k=4)
        sgs = av(20992, 512).rearrange("p (k n) -> p k n", k=4)
        sms = av(21504, 1024).rearrange("p (k n) -> p k n", k=8)
        sgt = av(22528, 512).rearrange("p (k n) -> p k n", k=4)
        bst = av(23040, 1024).rearrange("p (k n) -> p k n", k=8)

        pA = ps("pA", [128, 1536])
        pT = ps("pT", [128, 8, 128], BF16)
        pS = ps("pS", [128, 1024])
        pO = ps("pO", [128, 2, 512])

        def bfv(ap_):
            return ap_.bitcast(BF16).rearrange("p (k t) -> p k t", t=128)

        TA = [pT[:, 0:4, :], bfv(pO[:, 0, :])]
        TAk = ["pT", "pO0"]
        TB = [bfv(pA[:, 1024:1536]), bfv(pO[:, 1, :])]
        TBk = ["pA2", "pO1"]
        UB = [pA[:, 0:512], pA[:, 512:1024]]
        UBk = ["pA0", "pA1"]

        def dma(eng, out, in_, reads, writes, **kw):
            P.add(eng, "dma_start", reads=reads, writes=writes, dma=True, out=out, in_=in_, **kw)

        def wload(dst, src, key):
            dma("pool", dst, src.rearrange("(kt p) n -> p kt n", p=128), [], [key])

        def tt(eng, out, in0, in1, op, reads, writes, **kw):
            P.add(eng, "tensor_tensor", reads=reads, writes=writes, out=out, in0=in0, in1=in1, op=op, **kw)

        def ts(eng, out, in0, s1, s2, op0, op1, reads, writes, **kw):
            if s2 is None:
                P.add(eng, "tensor_scalar", reads=reads, writes=writes, out=out, in0=in0, scalar1=s1, scalar2=None,
                      op0=op0, **kw)
            else:
                P.add(eng, "tensor_scalar", reads=reads, writes=writes, out=out, in0=in0, scalar1=s1, scalar2=s2,
                      op0=op0, op1=op1, **kw)

        def cp(eng, out, in_, reads, writes, **kw):
            P.add(eng, "tensor_copy", reads=reads, writes=writes, out=out, in_=in_, **kw)

        P.add("pool", "memset", reads=[], writes=["epsc"], nophase=True, ap=epsc[:, :], constant=EPS)
        dma("pool", ident[:, :], ident_d[:, :], [], ["ident"])
        dma("sp", identf[:, :], ident_d[:, :], [], ["identf"])
        dma("sp", selT[:, :], sel[:, :], [], ["selT"])
        dma("sp", hb[:, :], hbias[:, :], [], ["hb"])
        dma("sp", cmask[:, :], cmask_d[:, :], [], ["cmask"])
        dma("sp", selm[:, :], selm_d[:, :], [], ["selm"])
        dma("sp", pmask[:, :], pmask_d[:, :], [], ["pmask"])
        dma("sp", stg[0:24, :], c3.rearrange("b (kt p) -> (b kt) p", p=128), [], ["stg"])
        dma("sp", stg[24:48, :], b_ada.rearrange("(ct p) -> ct p", p=128), [], ["stg1"])
        dma("sp", stg[48:56, :], norm_g.rearrange("(kt p) -> kt p", p=128), [], ["stg2"])
        dma("sp", stg[56:60, :], d_skip.rearrange("(kt p) -> kt p", p=128), [], ["stg3"])
        dma("sp", stg[60:68, :], b_glu.rearrange("(kt p) -> kt p", p=128), [], ["stg4"])
        P.add("pe", "transpose", reads=["stg", "stg1", "stg2", "stg3", "stg4", "identf"], writes=["pS"], out=pS[:, 0:68],
              in_=stg[0:68, :], identity=identf[0:68, 0:68])
        cp("dve", smalls[:, :], pS[:, 0:68], ["pS"], ["cT", "bada", "ng"])
        bgate = sq[0:3, :]
        gate_tok = qk_sb[0:3, :]
        dma("pool", gqk[:, 0:512], bass.AP(tensor=q_norm_g.tensor, offset=0, ap=[[0, 128], [0, 8], [1, 64]]),
            [], ["gqk_q"])
        dma("pool", gqk[:, 512:1024], bass.AP(tensor=k_norm_g.tensor, offset=0, ap=[[0, 128], [0, 8], [1, 64]]),
            [], ["gqk_k"])
        tt("dve", gqg[:, :], gqk[:, 0:512], gqk[:, 512:1024], ALU.mult, ["gqk_q", "gqk_k"], ["gqg"])
        dma("sp", e_s[:, 129:385], rel_bias[:, 1:257], [], ["xin1"])
        cp("dve", e_s[:, 0:129], e_s[:, 384:385].to_broadcast([8, 129]), ["xin1"], ["xin1"])
        cp("dve", e_s[:, 385:768], e_s[:, 384:385].to_broadcast([8, 383]), ["xin1"], ["xin1"])
        dma("sp", e_d[:, :], e_s[:, :], ["xin1"], ["e_d"])
        dma("sp", m_d.rearrange("h (r e) -> h r e", e=768),
            bass.AP(tensor=e_d.tensor, offset=0, ap=[[768, 8], [0, 130], [1, 768]]), ["e_d"], ["m_d"])

        TI = {"n": 0, "pend": None}

        def rms_front(row0, nt, src=None):
            src = x_all if src is None else src
            i = TI["n"] % 2
            TI["n"] += 1
            xt, xk = xin[i], "xin%d" % i
            xn, xnk = xn2[i], "xn%d" % i
            sst, ssk = ss2[i], "ss%d" % i
            dma("sp", xt[0:nt, :], src[row0:row0 + nt, :], [], [xk])
            P.add("act", "activation", reads=[xk], writes=[xnk, ssk], out=xn[0:nt, :], in_=xt[0:nt, :],
                  func=AF.Square, accum_out=sst[0:nt, 0:1])
            P.add("act", "activation", reads=[ssk, "epsc"], writes=[ssk], out=sst[0:nt, 2:3], in_=sst[0:nt, 0:1], func=AF.Ln,
                  scale=1.0 / D, bias=epsc[0:nt, 0:1])
            P.add("act", "activation", reads=[ssk], writes=[ssk], out=sst[0:nt, 3:4], in_=sst[0:nt, 2:3], func=AF.Exp, scale=-0.5)
            P.add("act", "activation", reads=[xk, ssk], writes=[xnk], out=xn[0:nt, :], in_=xt[0:nt, :],
                  func=AF.Copy, scale=sst[0:nt, 3:4])
            return i

        def rms_back(i, nt, mods):
            CUR["i"] = i
            xn, xnk = xn2[i], "xn%d" % i
            hT, hk = HT(), HK()
            for k in range(8):
                P.add("pe", "transpose", reads=[xnk, "ident"], writes=["pT"], out=pT[:, k, 0:nt],
                      in_=xn[0:nt, k * 128:(k + 1) * 128], identity=ident[0:nt, 0:nt])
            for k in range(8):
                for (c0, c1, b) in mods:
                    if k < 4:
                        ts("dve", hT[:, k, c0:c1], pT[:, k, c0:c1], Amod[:, k, b:b + 1], modT[:, k, b:b + 1],
                           ALU.mult, ALU.add, ["pT", "Amod", "modT"], [hk])
                    else:
                        P.add("act", "activation", reads=["pT", "Amod", "modT"], writes=[hk], out=hT[:, k, c0:c1],
                              in_=pT[:, k, c0:c1], func=AF.Identity, scale=Amod[:, k, b:b + 1], bias=modT[:, k, b:b + 1])

        def run_tiles(tiles, body, src=None):
            nxt = rms_front(tiles[0][0], tiles[0][1], src)
            for n, tl in enumerate(tiles):
                cur = nxt
                rms_back(cur, tl[1], tl[2])
                if n + 1 < len(tiles):
                    nxt = rms_front(tiles[n + 1][0], tiles[n + 1][1], src)
                CUR["i"] = cur
                body(n, tl, cur)

        CUR = {"i": 0}

        def HT():
            return hT2[CUR["i"]]

        def HK():
            return "hT%d" % CUR["i"]

        def qkv_mm(nt, with_q):
            for cb in range(0 if with_q else 1, 3):
                for k in range(8):
                    P.add("pe", "matmul", reads=[HK(), "w_qkv"], writes=["pA"], out=pA[0:nt, cb * 512:(cb + 1) * 512],
                          lhsT=HT()[:, k, 0:nt], rhs=w_qkv[:, k, cb * 512:(cb + 1) * 512], start=(k == 0), stop=(k == 7))

        def qkv_tile(nt, slot, with_q, kdst, vdst, out_rows, gm=False, fold=True, pre_mm=False):
            c_lo = 0 if with_q else 512
            if not pre_mm:
                qkv_mm(nt, with_q)
            if gm:
                for ct in range(12):
                    for k in range(8):
                        if ct < 4:
                            o_, ok_ = pO[:, 0, ct * 128:ct * 128 + nt], "pO"
                        else:
                            o_, ok_ = pS[:, (ct - 4) * 128:(ct - 4) * 128 + nt], "pS"
                        P.add("pe", "matmul", reads=[HK(), "w_gm", "w_gm2"], writes=[ok_], out=o_,
                              lhsT=w_gm[:, k, ct * 128:(ct + 1) * 128], rhs=HT()[:, k, 0:nt], start=(k == 0), stop=(k == 7))
            nh = 16 if with_q else 8
            h0 = 0 if with_q else 8
            P.add("act", "activation", reads=["pA"], writes=["sq"], out=sq[0:nt, c_lo:1024], in_=pA[0:nt, c_lo:1024],
                  func=AF.Square)
            P.add("dve", "tensor_reduce", reads=["sq"], writes=["ss16"], out=ss16[0:nt, h0:16],
                  in_=sq[0:nt, c_lo:1024].rearrange("p (h d) -> p h d", d=64), axis=AX.X, op=ALU.add)
            P.add("act", "activation", reads=["ss16", "epsc"], writes=["ss16"], out=ss16[0:nt, h0:16], in_=ss16[0:nt, h0:16],
                  func=AF.Ln, scale=1.0 / 64, bias=epsc[0:nt, 0:1])
            P.add("act", "activation", reads=["ss16"], writes=["ss16"], out=ss16[0:nt, h0:16], in_=ss16[0:nt, h0:16],
                  func=AF.Exp, scale=-0.5)
            rk = ss16[0:nt, 8:16].rearrange("p (h o) -> p h o", o=1).to_broadcast([nt, 8, 64])
            rq = ss16[0:nt, 0:8].rearrange("p (h o) -> p h o", o=1).to_broadcast([nt, 8, 64])
            pAk = pA[0:nt, 512:1024].rearrange("p (h d) -> p h d", d=64)
            pAq = pA[0:nt, 0:512].rearrange("p (h d) -> p h d", d=64)
            if fold:
                tt("dve", qkb[0:nt, 512:1024].rearrange("p (h d) -> p h d", d=64), pAk, rk, ALU.mult, ["pA", "ss16"], ["qkb"])
            else:
                tt("dve", sq[0:nt, 512:1024].rearrange("p (h d) -> p h d", d=64), pAk, rk, ALU.mult, ["pA", "ss16"], ["sq"])
                tt("dve", qkb[0:nt, 512:1024], sq[0:nt, 512:1024], gqk[0:nt, 512:1024], ALU.mult, ["sq", "gqk_k"], ["qkb"])
            if with_q:
                tt("dve", sq[0:nt, 0:512].rearrange("p (h d) -> p h d", d=64), pAq, rq, ALU.mult, ["pA", "ss16"], ["sq"])
                tt("dve", qkb[0:nt, 0:512], sq[0:nt, 0:512], gqg[0:nt, :] if fold else gqk[0:nt, 0:512], ALU.mult,
                   ["sq", "gqk_q", "gqg"], ["qkb"])
            P.add("act", "activation", reads=["pA", "Vones"], writes=[("V", slot)], out=V[0:nt, slot, :, 0:64],
                  in_=pA[0:nt, 1024:1536].rearrange("p (h d) -> p h d", d=64), func=AF.Copy)
            if gm:
                P.add("act", "activation", reads=["pS"], writes=["sma"], out=sma[:, :, 0:nt],
                      in_=pS[:, :].rearrange("p (c t) -> p c t", t=128)[:, :, 0:nt], func=AF.Sigmoid)
                P.add("act", "activation", reads=["pO"], writes=["sga"], out=sga[:, :, 0:nt],
                      in_=pO[:, 0, :].rearrange("p (c t) -> p c t", t=128)[:, :, 0:nt], func=AF.Silu)
            if kdst is not None:
                tt("dve", kout[0:nt, :].rearrange("p (h d) -> p h d", d=64), pAk, rk, ALU.mult, ["pA", "ss16"], ["kout"])
                tt("dve", kout[0:nt, :], kout[0:nt, :], gqk[0:nt, 512:1024], ALU.mult, ["kout", "gqk_k"], ["kout"])
                dma("sp", kdst[out_rows:out_rows + nt, :], kout[0:nt, :], ["kout"], [])
                cp("dve", vout[0:nt, :], pA[0:nt, 1024:1536], ["pA"], ["vout"])
                dma("sp", vdst[out_rows:out_rows + nt, :], vout[0:nt, :], ["vout"], [])
            for i in range(0 if with_q else 4, 8):
                P.add("pe", "transpose", reads=["qkb", "ident"], writes=["pT"], out=pT[:, i, 0:nt],
                      in_=qkb[0:nt, i * 128:(i + 1) * 128], identity=ident[0:nt, 0:nt])
            if with_q:
                cp("dve", qT[:, :, 0:nt], pT[:, 0:4, 0:nt], ["pT"], ["qT"])
            P.add("act", "activation", reads=["pT"], writes=[("KT", slot)], out=KT[:, :, slot * 128:slot * 128 + nt],
                  in_=pT[:, 4:8, 0:nt], func=AF.Copy)

        def attention(nq, ktiles, mtile):
            nkt = len(ktiles)
            nk_last = ktiles[4][1]
            for h in range(8):
                hp, h2 = h // 2, h % 2
                pr = slice(64 * h2, 64 * h2 + 64)
                for j, (slot, nk, tp_, halo) in enumerate(ktiles):
                    P.add("pe", "matmul", reads=[("KT", slot), "qT"], writes=["pS"], out=pS[0:nk, (4 - j) * 128:(4 - j) * 128 + nq],
                          lhsT=KT[pr, hp, slot * 128:slot * 128 + nk], rhs=qT[pr, hp, 0:nq], start=True, stop=True)
                pSv = pS[:, 0:640].rearrange("p (t q) -> p t q", q=128)
                P.add("dve", "scalar_tensor_tensor", reads=["pS", "BM"], writes=["stmp"], out=stmp[:, 1:5, 0:nq],
                      in0=pSv[:, 1:5, 0:nq], scalar=0.125, in1=BM[:, h, 1:5, 0:nq], op0=ALU.mult, op1=ALU.add)
                P.add("dve", "scalar_tensor_tensor", reads=["pS", "BM"], writes=["stmp"], out=stmp[0:nk_last, 0, 0:nq],
                      in0=pSv[0:nk_last, 0, 0:nq], scalar=0.125, in1=BM[0:nk_last, h, 0, 0:nq], op0=ALU.mult, op1=ALU.add)
                P.add("act", "activation", reads=["stmp"], writes=["PT"], out=PT[:, 1:5, 0:nq], in_=stmp[:, 1:5, 0:nq], func=AF.Exp)
                P.add("act", "activation", reads=["stmp"], writes=["PT"], out=PT[0:nk_last, 0, 0:nq], in_=stmp[0:nk_last, 0, 0:nq],
                      func=AF.Exp)
                for j, (slot, nk, tp_, halo) in enumerate(ktiles):
                    P.add("pe", "matmul", reads=["PT", ("V", slot)], writes=["pO"],
                          out=pO[0:nq, h // 4, (h % 4) * 65:(h % 4) * 65 + 65],
                          lhsT=PT[0:nk, 4 - j, 0:nq], rhs=V[0:nk, slot, h, :], start=(j == 0), stop=(j == nkt - 1))
            attention_tail(nq, mtile)

        def attention_tail(nq, mtile, gm_done=False):
            pOv = pO[0:nq, :, 0:260].rearrange("p a (h e) -> p a h e", e=65)
            P.add("dve", "reciprocal", reads=["pO"], writes=["rden"],
                  out=rden[0:nq, :].rearrange("p (a h o) -> p a h o", a=2, o=1), in_=pOv[:, :, :, 64:65])
            tt("dve", AO[0:nq, :, :].rearrange("p (a h) d -> p a h d", a=2), pOv[:, :, :, 0:64],
               rden[0:nq, :].rearrange("p (a h o) -> p a h o", a=2, o=1).to_broadcast([nq, 2, 4, 64]), ALU.mult,
               ["pO", "rden"], ["AO"])
            AOf = AO[:, :, :].rearrange("p h d -> p (h d)")
            for i in range(4):
                P.add("pe", "transpose", reads=["AO", "ident"], writes=["pT"], out=pT[:, i, 0:nq],
                      in_=AOf[0:nq, i * 128:(i + 1) * 128], identity=ident[0:nq, 0:nq])
            if not gm_done:
                for ct in range(12):
                    for k in range(8):
                        P.add("pe", "matmul", reads=[HK(), "w_gm", "w_gm2"], writes=["pA"], out=pA[:, ct * 128:ct * 128 + nq],
                              lhsT=w_gm[:, k, ct * 128:(ct + 1) * 128], rhs=HT()[:, k, 0:nq], start=(k == 0), stop=(k == 7))
                pAv = pA[:, :].rearrange("p (c t) -> p c t", t=128)
                P.add("act", "activation", reads=["pA"], writes=["sga"], out=sga[:, :, 0:nq], in_=pAv[:, 0:4, 0:nq], func=AF.Silu)
                P.add("act", "activation", reads=["pA"], writes=["sma"], out=sma[:, :, 0:nq], in_=pAv[:, 4:12, 0:nq], func=AF.Sigmoid)
            tt("dve", AOgT[:, :, 0:nq], pT[:, 0:4, 0:nq], sga[:, :, 0:nq], ALU.mult, ["pT", "sga"], ["AOgT"])
            for ct in range(8):
                for k in range(4):
                    P.add("pe", "matmul", reads=["AOgT", "w_oa"], writes=["pS"], out=pS[:, ct * 128:ct * 128 + nq],
                          lhsT=w_oa_s[:, k, ct * 128:(ct + 1) * 128], rhs=AOgT[:, k, 0:nq], start=(k == 0), stop=(k == 3))
            tt("dve", mgt[:, :, 0:nq], pS[:, :].rearrange("p (c t) -> p c t", t=128)[:, :, 0:nq], sma[:, :, 0:nq], ALU.mult,
               ["pS", "sma"], ["mgt"])
            dma("sp", mg_d[mtile[0], :, :].rearrange("p (k t) -> p k t", k=8)[:, :, mtile[1]:mtile[1] + nq],
                mgt[:, :, 0:nq], ["mgt"], [("mg_d", mtile[0])])

        def attention_fast(ktiles, mtile, nh, before_tail=None):
            nq = 128

            def qk(h):
                hp, h2 = h // 2, h % 2
                pr = slice(64 * h2, 64 * h2 + 64)
                Sb, skey = (pS, "pS") if h % 2 == 0 else (pA, "pA")
                for j, (slot, nk, tp_, halo) in enumerate(ktiles):
                    P.add("pe", "matmul", reads=[("KT", slot), "qT"], writes=[skey], out=Sb[:, (4 - j) * 128:(5 - j) * 128],
                          lhsT=KT[pr, hp, slot * 128:slot * 128 + 128], rhs=qT[pr, hp, 0:nq], start=True, stop=True)

            def softmax(h):
                Sb, skey = (pS, "pS") if h % 2 == 0 else (pA, "pA")
                st_, stk = stmp2[h % 2], "stmp%d" % (h % 2)
                PTb, ptk = PT2[h % 2], "PT%d" % (h % 2)
                P.add("dve", "scalar_tensor_tensor", reads=[skey, "BM"], writes=[stk], out=st_[:, :, :].rearrange("p t q -> p (t q)"),
                      in0=Sb[:, 0:640], scalar=0.125, in1=BM[:, h, :, :].rearrange("p t q -> p (t q)"),
                      op0=ALU.mult, op1=ALU.add)
                c_h = (5 - nh) * 128
                stf = st_[:, :, :].rearrange("p t q -> p (t q)")
                ptf = PTb[:, :, :].rearrange("p t q -> p (t q)")
                if nh < 5:
                    P.add("act", "activation", reads=[stk], writes=[ptk], out=ptf[:, 0:c_h], in_=stf[:, 0:c_h], func=AF.Exp)
                if nh > 0:
                    P.add("act", "activation", reads=[stk, "hb"], writes=[ptk], out=ptf[:, c_h:640], in_=stf[:, c_h:640],
                          func=AF.Exp, bias=hb[:, 0:1])

            def pv(h):
                PTb, ptk = PT2[h % 2], "PT%d" % (h % 2)
                for j, (slot, nk, tp_, halo) in enumerate(ktiles):
                    P.add("pe", "matmul", reads=[ptk, ("V", slot)], writes=["pO"],
                          out=pO[0:nq, h // 4, (h % 4) * 65:(h % 4) * 65 + 65],
                          lhsT=PTb[:, 4 - j, :], rhs=V[:, slot, h, :], start=(j == 0), stop=(j == 4))

            qk(0)
            for h in range(8):
                softmax(h)
                if h + 1 < 8:
                    qk(h + 1)
                pv(h)
            if before_tail is not None:
                before_tail()
            attention_tail(nq, mtile, True)


        own_tiles = [(NHALO + i * 128, 128, [(0, 128, 0)], i * 128) for i in range(16)]
        samp_tiles = [(NHALO + NOWN + s * 16, 16, [(0, 16, 1 + s)], NOWN + s * 16) for s in range(2)]

        wload(w_u, w_in[:, 2048:2560], "w_u")
        PRM = {}

        def pt(name):
            if name not in PRM:
                PRM[name] = len(PRM)
                assert len(PRM) <= 34
            return prm[:, PRM[name], :]

        k_ = ["prm"]
        dma("sp", xin[0][0:32, 0:64], lam_re[:, :], [], ["xin0"])
        dma("sp", xin[0][0:32, 64:128], lam_im[:, :], [], ["xin0b"])
        P.add("pe", "transpose", reads=["xin0", "xin0b", "identf"], writes=["pS"], out=pS[:, 0:32], in_=xin[0][0:32, 0:128],
              identity=identf[0:32, 0:32])
        cp("dve", pt("lam"), pS[:, 0:32], ["pS"], k_)
        cp("dve", pt("lr")[0:64, :], pt("lam")[0:64, :], k_, k_)
        cp("dve", pt("lr")[64:128, :], pt("lam")[0:64, :], k_, k_)
        cp("dve", pt("li")[0:64, :], pt("lam")[64:128, :], k_, k_)
        cp("dve", pt("li")[64:128, :], pt("lam")[64:128, :], k_, k_)
        dma("sp", pt("dt"), bass.AP(tensor=log_dt.tensor, offset=0, ap=[[0, 128], [1, 32]]), k_, k_)
        P.add("act", "activation", reads=k_, writes=k_, out=pt("dt"), in_=pt("dt"), func=AF.Exp)
        tt("dve", pt("x"), pt("lr"), pt("dt"), ALU.mult, k_, k_)
        ts("dve", pt("m"), pt("x"), 0.25, 1.0, ALU.mult, ALU.add, k_, k_)
        for cc in (1.0 / 3, 0.5, 1.0):
            P.add("dve", "scalar_tensor_tensor", reads=k_, writes=k_, out=pt("m"), in0=pt("x"), scalar=cc, in1=pt("m"),
                  op0=ALU.mult, op1=ALU.mult)
            ts("dve", pt("m"), pt("m"), 1.0, None, ALU.add, None, k_, k_)
        tt("dve", pt("ang"), pt("li"), pt("dt"), ALU.mult, k_, k_)
        ts("dve", pt("t0"), pt("ang"), 1.0 / (2 * np.pi), None, ALU.mult, None, k_, k_)
        cp("dve", ki32[:, :], pt("t0"), k_, ["ki32"])
        cp("dve", pt("t0"), ki32[:, :], ["ki32"], k_)
        P.add("dve", "scalar_tensor_tensor", reads=k_, writes=k_, out=pt("ang"), in0=pt("t0"), scalar=-2 * np.pi,
              in1=pt("ang"), op0=ALU.mult, op1=ALU.add)
        ts("dve", pt("psi"), pt("ang"), 1.0 / 32, None, ALU.mult, None, k_, k_)
        tt("dve", pt("p2"), pt("psi"), pt("psi"), ALU.mult, k_, k_)
        ts("dve", pt("s"), pt("p2"), -1.0 / 42, 1.0, ALU.mult, ALU.add, k_, k_)
        for cc in (-1.0 / 20, -1.0 / 6):
            P.add("dve", "scalar_tensor_tensor", reads=k_, writes=k_, out=pt("s"), in0=pt("p2"), scalar=cc, in1=pt("s"),
                  op0=ALU.mult, op1=ALU.mult)
            ts("dve", pt("s"), pt("s"), 1.0, None, ALU.add, None, k_, k_)
        tt("dve", pt("s"), pt("s"), pt("psi"), ALU.mult, k_, k_)
        ts("dve", pt("c"), pt("p2"), -1.0 / 56, 1.0, ALU.mult, ALU.add, k_, k_)
        for cc in (-1.0 / 30, -1.0 / 12, -0.5):
            P.add("dve", "scalar_tensor_tensor", reads=k_, writes=k_, out=pt("c"), in0=pt("p2"), scalar=cc, in1=pt("c"),
                  op0=ALU.mult, op1=ALU.mult)
            ts("dve", pt("c"), pt("c"), 1.0, None, ALU.add, None, k_, k_)

        def csquare(r, i, t1, t2):
            tt("dve", t1, r, r, ALU.mult, k_, k_)
            tt("dve", t2, i, i, ALU.mult, k_, k_)
            P.add("dve", "scalar_tensor_tensor", reads=k_, writes=k_, out=i, in0=r, scalar=2.0, in1=i,
                  op0=ALU.mult, op1=ALU.mult)
            tt("dve", r, t1, t2, ALU.subtract, k_, k_)

        for _ in range(5):
            csquare(pt("c"), pt("s"), pt("t0"), pt("t1"))
        tt("dve", pt("ar"), pt("m"), pt("c"), ALU.mult, k_, k_)
        tt("dve", pt("ai"), pt("m"), pt("s"), ALU.mult, k_, k_)
        tt("dve", pt("den"), pt("lr"), pt("lr"), ALU.mult, k_, k_)
        tt("dve", pt("t0"), pt("li"), pt("li"), ALU.mult, k_, k_)
        tt("dve", pt("den"), pt("den"), pt("t0"), ALU.add, k_, k_)
        P.add("dve", "reciprocal", reads=k_, writes=k_, out=pt("den"), in_=pt("den"))
        ts("dve", pt("nr"), pt("ar"), -1.0, None, ALU.add, None, k_, k_)
        tt("dve", pt("t0"), pt("nr"), pt("lr"), ALU.mult, k_, k_)
        tt("dve", pt("t1"), pt("ai"), pt("li"), ALU.mult, k_, k_)
        tt("dve", pt("cor"), pt("t0"), pt("t1"), ALU.add, k_, k_)
        tt("dve", pt("cor"), pt("cor"), pt("den"), ALU.mult, k_, k_)
        tt("dve", pt("t0"), pt("ai"), pt("lr"), ALU.mult, k_, k_)
        tt("dve", pt("t1"), pt("nr"), pt("li"), ALU.mult, k_, k_)
        tt("dve", pt("coi"), pt("t0"), pt("t1"), ALU.subtract, k_, k_)
        tt("dve", pt("coi"), pt("coi"), pt("den"), ALU.mult, k_, k_)
        ts("dve", pt("cois"), pt("coi"), cmask[:, 0:1], None, ALU.mult, None, k_ + ["cmask"], k_)
        dma("sp", Bstk[0:64, :, :], b_re.rearrange("g n c -> n g c"), [], ["Bstk"])
        dma("sp", Bstk[64:128, :, :], b_im.rearrange("g n c -> n g c"), [], ["Bstk2"])
        cp("dve", Bsw[0:64, :, :], Bstk[64:128, :, :], ["Bstk", "Bstk2"], ["Bsw"])
        cp("dve", Bsw[64:128, :, :], Bstk[0:64, :, :], ["Bstk", "Bstk2"], ["Bsw"])

        def bc_c(name):
            return pt(name).rearrange("p (g o) -> p g o", o=1).to_broadcast([128, 32, 16])

        tt("dve", Bstk[:, :, :], Bstk[:, :, :], bc_c("cor"), ALU.mult, ["Bstk", "Bstk2", "Bsw"] + k_, ["Bstk"])
        tt("dve", Bsw[:, :, :], Bsw[:, :, :], bc_c("cois"), ALU.mult, ["Bsw"] + k_, ["Bsw"])
        tt("dve", Bstk[:, :, :], Bstk[:, :, :], Bsw[:, :, :], ALU.add, ["Bstk", "Bsw"], ["Bstk"])
        cp("dve", Bsw[0:64, :, :], Bstk[64:128, :, :], ["Bstk"], ["Bsw"])
        cp("dve", Bsw[64:128, :, :], Bstk[0:64, :, :], ["Bstk"], ["Bsw"])
        cp("dve", Bbb[:, :, :], Bstk[:, :, :], ["Bstk"], ["Bbb"])
        kp = ["PW"]
        P.add("dve", "memset", reads=[], writes=kp, ap=PW1[:, :, 0:1], constant=1.0)
        P.add("dve", "memset", reads=[], writes=kp, ap=PW2[:, :, 0:1], constant=0.0)
        cp("dve", PW1[:, :, 1:2], pt("ar").rearrange("p (g o) -> p g o", o=1), k_ + kp, kp)
        cp("dve", PW2[:, :, 1:2], pt("ai").rearrange("p (g o) -> p g o", o=1), k_ + kp, kp)
        tw = qk_sb[:, 0:512].rearrange("p (a g t) -> p a g t", a=2, g=32)
        for kk in (1, 2, 4, 8):
            src_r, src_i = PW1[:, :, 1:kk + 1], PW2[:, :, 1:kk + 1]
            kr = PW1[:, :, kk:kk + 1].to_broadcast([128, 32, kk])
            ki = PW2[:, :, kk:kk + 1].to_broadcast([128, 32, kk])
            t_a, t_b = tw[:, 0, :, 0:kk], tw[:, 1, :, 0:kk]
            tt("dve", t_a, src_r, kr, ALU.mult, kp, ["qk_sb"])
            tt("dve", t_b, src_i, ki, ALU.mult, kp, ["qk_sb"])
            tt("dve", PW1[:, :, kk + 1:2 * kk + 1], t_a, t_b, ALU.subtract, ["qk_sb"], kp)
            tt("dve", t_a, src_r, ki, ALU.mult, kp, ["qk_sb"])
            tt("dve", t_b, src_i, kr, ALU.mult, kp, ["qk_sb"])
            tt("dve", PW2[:, :, kk + 1:2 * kk + 1], t_a, t_b, ALU.add, ["qk_sb"], kp)
        PW2s = sq[:, 0:544].rearrange("p (g t) -> p g t", g=32)
        ts("dve", PW2s, PW2[:, :, :], cmask[:, 0:1], None, ALU.mult, None, kp + ["cmask"], ["sq"])
        cp("dve", pt("Y1"), PW1[:, :, 16], kp, k_)
        cp("dve", pt("Y2"), PW2s[:, :, 16], ["sq"], k_)
        cp("dve", pt("e1r"), PW1[:, :, 16], kp, k_)
        cp("dve", pt("e1i"), PW2[:, :, 16], kp, k_)
        for _ in range(7):
            csquare(pt("e1r"), pt("e1i"), pt("t0"), pt("t1"))
        cp("dve", pt("e2r"), pt("e1r"), k_, k_)
        cp("dve", pt("e2i"), pt("e1i"), k_, k_)
        csquare(pt("e2r"), pt("e2i"), pt("t0"), pt("t1"))
        ts("dve", pt("e1i"), pt("e1i"), cmask[:, 0:1], None, ALU.mult, None, k_ + ["cmask"], k_)
        ts("dve", pt("e2i"), pt("e2i"), cmask[:, 0:1], None, ALU.mult, None, k_ + ["cmask"], k_)
        for gq in range(8):
            gs_ = slice(gq * 4, gq * 4 + 4)
            t_a = qk_sb[:, 0:1024].rearrange("p (g t c) -> p g t c", g=4, t=16)
            t_b = xin[1][:, 0:1024].rearrange("p (g t c) -> p g t c", g=4, t=16)
            tt("dve", t_a, Bstk[:, gs_, :].rearrange("p g (o c) -> p g o c", o=1).to_broadcast([128, 4, 16, 16]),
               PW1[:, gs_, 0:16].rearrange("p g (t o) -> p g t o", o=1).to_broadcast([128, 4, 16, 16]), ALU.mult,
               ["Bstk"] + kp, ["qk_sb"])
            tt("pool", t_b, Bsw[:, gs_, :].rearrange("p g (o c) -> p g o c", o=1).to_broadcast([128, 4, 16, 16]),
               PW2s[:, gs_, 0:16].rearrange("p g (t o) -> p g t o", o=1).to_broadcast([128, 4, 16, 16]), ALU.mult,
               ["Bsw", "sq"], ["xin1"])
            tt("dve", Pst[:, gq // 2, :, (gq % 2) * 4:(gq % 2) * 4 + 4, :].rearrange("p t g c -> p g t c"), t_a, t_b, ALU.add, ["qk_sb", "xin1"], ["Pst"])
        for q in range(4):
            for half in range(2):
                for tl in range(8):
                    tau = half * 8 + tl
                    P.add("pe", "transpose", reads=["Pst", "ident"], writes=["pT"], out=pT[:, tl, :],
                          in_=Pst[:, q, tau, :, :].rearrange("p g c -> p (g c)"), identity=ident[:, :])
                for tl in range(8):
                    j = 15 - (half * 8 + tl)
                    ts("dve", W1z[:, q, 0, j, :], pT[:, tl, :], cmask[:, 2:3], None, ALU.mult, None, ["pT", "cmask"], ["W1z"])
                    P.add("act", "activation", reads=["pT", "cmask"], writes=["W1z"], out=W1z[:, q, 1, j, :],
                          in_=pT[:, tl, :], func=AF.Copy, scale=cmask[:, 3:4])
        def cmul_acc(dst, x, y1, y2, t_sw, t_a, keys_r, keys_w):
            cp("dve", t_sw[0:64], x[64:128], keys_r, ["scan_t"], nophase=True)
            cp("dve", t_sw[64:128], x[0:64], keys_r, ["scan_t"], nophase=True)
            tt("dve", t_a, x, y1, ALU.mult, keys_r + ["prm"], ["scan_t2"], nophase=True)
            tt("dve", t_sw, t_sw, y2, ALU.mult, ["scan_t", "prm"], ["scan_t"], nophase=True)
            tt("dve", dst, dst, t_a, ALU.add, ["scan_t2"] + keys_w, keys_w, nophase=True)
            tt("dve", dst, dst, t_sw, ALU.add, ["scan_t"] + keys_w, keys_w, nophase=True)

        sc_a = prm[:, 34, :]
        sc_b = prm[:, 35, :]
        dma("sp", bgate, bass.AP(tensor=b_ada.tensor, offset=2048, ap=[[0, 3], [1, 1024]]), [], ["sq"])
        P.add("act", "activation", reads=["cT"], writes=["scT"], out=scT[:, :, :], in_=cT, func=AF.Silu)
        for ch in range(6):
            wb = wada[ch % 2]
            wk = "wada%d" % (ch % 2)
            wload(wb, w_ada[:, ch * 512:(ch + 1) * 512], wk)
            for c4 in range(4):
                for k in range(8):
                    P.add("pe", "matmul", reads=[wk, "scT"], writes=["pA"], out=pA[:, c4 * 4:c4 * 4 + 3],
                          lhsT=wb[:, k, c4 * 128:(c4 + 1) * 128], rhs=scT[:, k, :], start=(k == 0), stop=(k == 7))
            tt("dve", modT[:, ch * 4:(ch + 1) * 4, :], pA[:, 0:16].rearrange("p (c b) -> p c b", b=4)[:, :, 0:3],
               bada[:, ch * 4:(ch + 1) * 4].rearrange("p (c o) -> p c o", o=1).to_broadcast([128, 4, 3]), ALU.add,
               ["pA", "bada"], ["modT"])
            if ch >= 4:
                for k in range(8):
                    P.add("pe", "matmul", reads=[wk, "scT"], writes=["pS"], out=pS[0:3, 0:512],
                          lhsT=scT[:, k, :], rhs=wb[:, k, :], start=(k == 0), stop=(k == 7))
                tt("dve", gate_tok[:, (ch - 4) * 512:(ch - 3) * 512], pS[0:3, 0:512],
                   bgate[:, (ch - 4) * 512:(ch - 3) * 512], ALU.add, ["pS", "sq"], ["qk_sb"])
        ts("dve", Amod[:, :, :], modT[:, 8:16, :], 1.0, None, ALU.add, None, ["modT"], ["Amod"])
        tt("dve", Amod[:, :, :], Amod[:, :, :], ng.rearrange("p (k o) -> p k o", o=1).to_broadcast([128, 8, 3]), ALU.mult,
           ["Amod", "ng"], ["Amod"])
        for which, (c0, nt) in enumerate(((0, 128), (128, 16), (144, 16))):
            for half in range(2):
                P.add("pe", "matmul", reads=["selT", "qk_sb"], writes=["pS"], out=pS[0:nt, half * 512:(half + 1) * 512],
                      lhsT=selT[:, c0:c0 + nt], rhs=gate_tok[:, half * 512:(half + 1) * 512], start=True, stop=True)
            cp("dve", gate_bc[0:nt, which, :], pS[0:nt, :], ["pS"], ["gate_bc"])

        P.barrier(dummy[:, 0:1])

        def wstage(dst, src, key):
            dma("pool", dst, src, [], [key], nophase=True)

        wstage(wq_d[:, :], w_in[:, 0:1536], "wq_d")
        wstage(wgm_d[:, 0:512], w_in[:, 1536:2048], "wgm_d")
        wstage(wgm_d[:, 512:1536], w_in[:, 3072:4096], "wgm_d2")
        wstage(woa_d[:, :], w_oa[:, :], "woa_d")
        wstage(wglu_d[:, :], w_glu[:, :], "wglu_d")
        wstage(wgsms_d[:, 0:512], w_in[:, 2560:3072], "wgsms_d")
        wstage(wgsms_d[:, 512:1536], w_in[:, 4096:5120], "wgsms_d2")
        wstage(wos_d[:, :], w_os[:, :], "wos_d")
        wstage(wout_d[:, :], w_out[:, :], "wout_d")

        def wload2(dst, src, rkeys, key):
            dma("sp", dst, src.rearrange("(kt p) n -> p kt n", p=128), rkeys, [key])

        Q1 = scr[:, 0:544].rearrange("p (m g) -> p m g", g=32)
        Q2 = scr[:, 544:1088].rearrange("p (m g) -> p m g", g=32)
        Q2s = scr[:, 1088:1632].rearrange("p (m g) -> p m g", g=32)
        tsw8 = scr[:, 1632:1888].rearrange("p (b g) -> p b g", g=32)
        ta8 = scr[:, 1888:2144].rearrange("p (b g) -> p b g", g=32)
        tb1 = scr[:, 2144:4064].rearrange("p (b i g) -> p b i g", b=4, i=15)
        tb2 = scr[:, 4064:5984].rearrange("p (b i g) -> p b i g", b=4, i=15)
        qa = scr[:, 5984:6240].rearrange("p (m g) -> p m g", g=32)
        qb = scr[:, 6240:6496].rearrange("p (m g) -> p m g", g=32)
        kq = ["Q"]
        P.add("dve", "memset", reads=[], writes=kq, ap=Q1[:, 0, :], constant=1.0)
        P.add("dve", "memset", reads=[], writes=kq, ap=Q2[:, 0, :], constant=0.0)
        cp("dve", Q1[:, 1, :], PW1[:, :, 16], kp + kq, kq)
        cp("dve", Q2[:, 1, :], PW2[:, :, 16], kp + kq, kq)
        for kk in (1, 2, 4, 8):
            src_r, src_i = Q1[:, 1:kk + 1, :], Q2[:, 1:kk + 1, :]
            kr = Q1[:, kk:kk + 1, :].to_broadcast([128, kk, 32])
            ki = Q2[:, kk:kk + 1, :].to_broadcast([128, kk, 32])
            t_a, t_b = qa[:, 0:kk, :], qb[:, 0:kk, :]
            tt("dve", t_a, src_r, kr, ALU.mult, kq, ["qa"])
            tt("dve", t_b, src_i, ki, ALU.mult, kq, ["qb"])
            tt("dve", Q1[:, kk + 1:2 * kk + 1, :], t_a, t_b, ALU.subtract, ["qa", "qb"] + kq, kq)
            tt("dve", t_a, src_r, ki, ALU.mult, kq, ["qa"])
            tt("dve", t_b, src_i, kr, ALU.mult, kq, ["qb"])
            tt("dve", Q2[:, kk + 1:2 * kk + 1, :], t_a, t_b, ALU.add, ["qa", "qb"] + kq, kq)
        ts("dve", Q2s, Q2, cmask[:, 0:1], None, ALU.mult, None, kq + ["cmask"], kq)
        y1b8 = pt("Y1").rearrange("p (o g) -> p o g", o=1).to_broadcast([128, 8, 32])
        y2b8 = pt("Y2").rearrange("p (o g) -> p o g", o=1).to_broadcast([128, 8, 32])
        P.add("pool", "memset", reads=[], writes=["XS", "XSfree"], nophase=True, ap=XS[:, 0, :], constant=0.0)
        for seg in range(4):
            if seg < 3:
                seg_tiles = [(seg * NOWN + i * 128, 128, [(0, 128, 0)], i * 128) for i in range(16)]
            else:
                seg_tiles = own_tiles + samp_tiles
            srcA = x_prev if seg < 3 else x_all
            NA = len(seg_tiles)

            def a_front(n):
                (row0, nt, mods, col0) = seg_tiles[n]
                i = n % 2
                xt, xk = xin[i], "xin%d" % i
                xn, xnk = xn2[i], "xn%d" % i
                sst, ssk = ss2[i], "ss%d" % i
                dma("sp", xt[0:nt, :], srcA[row0:row0 + nt, :], [], [xk])
                P.add("act", "activation", reads=[xk], writes=[xnk, ssk], out=xn[0:nt, :], in_=xt[0:nt, :],
                      func=AF.Square, accum_out=sst[0:nt, 0:1])
                P.add("act", "activation", reads=[ssk, "epsc"], writes=[ssk], out=sst[0:nt, 2:3], in_=sst[0:nt, 0:1], func=AF.Ln,
                      scale=1.0 / D, bias=epsc[0:nt, 0:1])
                P.add("act", "activation", reads=[ssk], writes=[ssk], out=sst[0:nt, 3:4], in_=sst[0:nt, 2:3], func=AF.Exp, scale=-0.5)
                ts("dve", xn[0:nt, :], xt[0:nt, :], sst[0:nt, 3:4], None, ALU.mult, None, [xk, ssk], [xnk])

            def a_T(n):
                (row0, nt, mods, col0) = seg_tiles[n]
                i = n % 2
                xn, xnk = xn2[i], "xn%d" % i
                for k in range(8):
                    tb, tk = (TA[i], TAk[i]) if k < 4 else (TB[i], TBk[i])
                    P.add("pe", "transpose", reads=[xnk, "ident"], writes=[tk], out=tb[:, k % 4, 0:nt],
                          in_=xn[0:nt, k * 128:(k + 1) * 128], identity=ident[0:nt, 0:nt])

            def a_evac(n):
                (row0, nt, mods, col0) = seg_tiles[n]
                i = n % 2
                hT, hk = hT2[i], "hT%d" % i
                for k in range(8):
                    tb, tk = (TA[i], TAk[i]) if k < 4 else (TB[i], TBk[i])
                    for (c0, c1, bsel) in mods:
                        if k < 4:
                            ts("dve", hT[:, k, c0:c1], tb[:, k % 4, c0:c1], Amod[:, k, bsel:bsel + 1], modT[:, k, bsel:bsel + 1],
                               ALU.mult, ALU.add, [tk, "Amod", "modT"], [hk])
                        else:
                            P.add("act", "activation", reads=[tk, "Amod", "modT"], writes=[hk], out=hT[:, k, c0:c1],
                                  in_=tb[:, k % 4, c0:c1], func=AF.Identity, scale=Amod[:, k, bsel:bsel + 1],
                                  bias=modT[:, k, bsel:bsel + 1])

            def a_mm(n):
                (row0, nt, mods, col0) = seg_tiles[n]
                i = n % 2
                hT, hk = hT2[i], "hT%d" % i
                for q in range(4):
                    for k in range(8):
                        P.add("pe", "matmul", reads=[hk, "w_u"], writes=[UBk[i]], out=UB[i][:, q * 128:q * 128 + nt],
                              lhsT=w_u[:, k, q * 128:(q + 1) * 128], rhs=hT[:, k, 0:nt], start=(k == 0), stop=(k == 7))

            def a_uevac(n, seg=seg):
                (row0, nt, mods, col0) = seg_tiles[n]
                i = n % 2
                ts("dve", uT[:, :, col0:col0 + nt], UB[i].rearrange("p (q t) -> p q t", q=4)[:, :, 0:nt],
                   pmask[:, seg:seg + 1], None, ALU.mult, None, [UBk[i], "pmask"], ["uT"])

            a_front(0)
            if NA > 1:
                a_front(1)
            a_T(0)
            a_evac(0)
            for n in range(NA):
                if n + 1 < NA:
                    a_T(n + 1)
                a_mm(n)
                if n + 1 < NA:
                    a_evac(n + 1)
                if n + 2 < NA:
                    a_front(n + 2)
                a_uevac(n)
            nch = 128 if seg < 3 else NCH
            wbanks = [(pA, 0, "pA0"), (pA, 512, "pA1"), (pS, 0, "pS0"), (pS, 512, "pS1")]
            for q in range(4):
                for par in range(2):
                    for j in range(16):
                        for s_ in range(4):
                            bk, off, bkey = wbanks[s_]
                            P.add("pe", "matmul", reads=["uT", "W1z"], writes=[bkey], out=bk[:, off:off + nch],
                                  lhsT=W1z[32 * s_:32 * s_ + 32, q, par, j, :],
                                  rhs=uT[32 * s_:32 * s_ + 32, q, 0:nch * 16].rearrange("p (k i) -> p k i", i=16)[:, :, j],
                                  start=(j == 0), stop=(j == 15), tile_position=(32 * s_, 0))
                    for s_ in range(4):
                        bk, off, bkey = wbanks[s_]
                        g = q * 8 + s_ * 2 + par
                        if s_ % 2 == 0:
                            cp("dve", XS[:, 1:129, g], bk[:, off:off + 128], [bkey, "XSfree"], [("XSg", g)], nophase=True)
                        else:
                            P.add("act", "activation", reads=[bkey, "XSfree"], writes=[("XSg", g)], nophase=True,
                                  out=XS[:, 1:129, g], in_=bk[:, off:off + 128], func=AF.Copy)
                        if seg == 3:
                            cp("dve", WSs[:, :, g], bk[:, off + 128:off + 130], [bkey], ["WSs"])
            P.add("dve", "memset", reads=[("XSg", g_) for g_ in range(32)], writes=["XS"], nophase=True, ap=dummy[:, 1:2],
                  constant=0.0)
            XSb = XS[:, 1:129, :].rearrange("p (b i) g -> p b i g", i=16)
            for i in range(1, 16):
                cmul_acc(XSb[:, :, i, :], XSb[:, :, i - 1, :], y1b8, y2b8, tsw8, ta8, ["XS"], ["XS"])
            for bb in range(8):
                cmul_acc(XS[:, 16 * (bb + 1), :], XS[:, 16 * bb, :], Q1[:, 16, :], Q2s[:, 16, :], sc_a, sc_b, ["XS", "Q"], ["XS"])
            Cst = XS[:, 0:128, :].rearrange("p (b i) g -> p b i g", i=16)
            for b0 in ((0, 4) if seg == 3 else ()):
                Ch = Cst[:, b0:b0 + 4, 0, :]
                cp("dve", tsw8[0:64, 0:4, :], Ch[64:128], ["XS"], ["scan_t"], nophase=True)
                cp("dve", tsw8[64:128, 0:4, :], Ch[0:64], ["XS"], ["scan_t"], nophase=True)
                tt("dve", tb1, Ch.rearrange("p b (o g) -> p b o g", o=1).to_broadcast([128, 4, 15, 32]),
                   Q1[:, 1:16, :].rearrange("p (o m) g -> p o m g", o=1).to_broadcast([128, 4, 15, 32]), ALU.mult,
                   ["XS", "Q"], ["tb1"], nophase=True)
                tt("dve", tb2, tsw8[:, 0:4, :].rearrange("p b (o g) -> p b o g", o=1).to_broadcast([128, 4, 15, 32]),
                   Q2s[:, 1:16, :].rearrange("p (o m) g -> p o m g", o=1).to_broadcast([128, 4, 15, 32]), ALU.mult,
                   ["scan_t", "Q"], ["tb2"], nophase=True)
                tt("dve", XSb[:, b0:b0 + 4, 0:15, :], XSb[:, b0:b0 + 4, 0:15, :], tb1, ALU.add, ["XS", "tb1"], ["XS"], nophase=True)
                tt("dve", XSb[:, b0:b0 + 4, 0:15, :], XSb[:, b0:b0 + 4, 0:15, :], tb2, ALU.add, ["XS", "tb2"], ["XS"], nophase=True)
            if seg < 3:
                cp("dve", XS[:, 0, :], XS[:, 128, :], ["XS"], ["XS", "XSfree"], nophase=True)
        P.add("pe", "transpose", reads=["XS", "identf"], writes=["pS"], nophase=True, out=pS[0:32, 0:128], in_=XS[:, 128, :],
              identity=identf[:, :])
        cp("dve", kout[0:32, 0:128], pS[0:32, 0:128], ["pS"], ["kout"], nophase=True)
        dma("sp", ssm_p[:, :], kout[0:32, 0:128], ["kout"], [], nophase=True)
        dma("sp", xin[0][0:64, 0:64], st_re[:, :], [], ["xin0"])
        dma("sp", xin[0][0:64, 64:128], st_im[:, :], [], ["xin0b"])
        P.add("pe", "transpose", reads=["xin0", "xin0b", "identf"], writes=["pS"], out=pS[:, 0:64], in_=xin[0][0:64, 0:128],
              identity=identf[0:64, 0:64])
        cp("dve", xs0[:, :, :], pS[:, 0:64].rearrange("p (s g) -> p s g", s=2), ["pS"], ["xs0"])
        for s_ in range(2):
            cmul_acc(WSs[:, s_, :], xs0[:, s_, :], pt("Y1"), pt("Y2"), sc_a, sc_b, ["xs0", "WSs"], ["WSs"])
        P.add("pe", "transpose", reads=["WSs", "identf"], writes=["pS"], out=pS[0:64, 0:128],
              in_=WSs[:, :, :].rearrange("p s g -> p (s g)"), identity=identf[:, :])
        cp("dve", vout[0:64, 0:128], pS[0:64, 0:128], ["pS"], ["vout"])
        dma("sp", ssm_s[:, :], vout[0:64, 0:128], ["vout"], [])

        P.barrier(dummy[:, 0:1])
        for h in range(8):
            dma("sp", BM[:, h, :, :],
                bass.AP(tensor=m_d.tensor, offset=h * 130 * 768 + 256, ap=[[767, 128], [128, 5], [1, 128]]),
                ["m_d"], ["BM"])
        P.add("pool", "memset", reads=[], writes=["BM"], ap=BM[0:64, :, 4, 64:128], constant=-30000.0)
        P.add("pool", "memset", reads=[], writes=["BM"], ap=BM[64:128, :, 0, 0:64], constant=-30000.0)
        P.add("pool", "memset", reads=[], writes=["Vones"], ap=V[:, :, :, 64:65], constant=1.0)
        wload2(w_qkv[:, :, :], wq_d[:, :], ["wq_d"], "w_qkv")
        wload2(w_gm[:, :, :], wgm_d[:, :], ["wgm_d", "wgm_d2"], "w_gm")
        wload2(w_oa_s[:, :, :], woa_d[:, :], ["woa_d"], "w_oa")
        tilesB = [(t * 128, 128, [(0, 128, 0)], 0) for t in range(20)]
        stB = {"nxt": rms_front(tilesB[0][0], 128), "pre": False}

        def b_prepare(n):
            cur = stB["nxt"]
            rms_back(cur, 128, tilesB[n][2])
            if n + 1 < 20:
                stB["nxt"] = rms_front(tilesB[n + 1][0], 128)
            CUR["i"] = cur
            qkv_mm(128, n >= 4)
            stB["cur"] = cur

        b_prepare(0)
        for t in range(20):
            slot = t % 6
            CUR["i"] = stB["cur"]
            if t < 4:
                qkv_tile(128, slot, False, None, None, 0, pre_mm=True)
                if t + 1 < 20:
                    b_prepare(t + 1)
            else:
                own = t - 4
                last = own >= 12
                qkv_tile(128, slot, True, k_last if last else None, v_last if last else None, (own - 12) * 128, gm=True,
                         pre_mm=True)
                ktiles = [((t - 4 + j) % 6, 128, 4 - j, (t - 4 + j) < 4) for j in range(5)]
                mycur = stB["cur"]

                def hook(t=t):
                    if t + 1 < 20:
                        b_prepare(t + 1)

                attention_fast(ktiles, (own, 0), max(0, 4 - own), before_tail=hook)
        kc = qkb[:, :]
        for s_ in range(2):
            for tk in range(4):
                dma("pool", kc[:, 0:512], cache_k[s_, tk * 128:(tk + 1) * 128, :], [], ["qkb"])
                for i in range(4):
                    P.add("pe", "transpose", reads=["qkb", "ident"], writes=["pT"], out=pT[:, 4 + i, :],
                          in_=kc[:, i * 128:(i + 1) * 128], identity=ident[:, :])
                P.add("act", "activation", reads=["pT"], writes=[("KT", tk)], out=KT[:, :, tk * 128:(tk + 1) * 128],
                      in_=pT[:, 4:8, :], func=AF.Copy)
                dma("pool", V[:, tk, :, 0:64], cache_v[s_, tk * 128:(tk + 1) * 128, :].rearrange("p (h d) -> p h d", d=64),
                    ["Vones"], [("V", tk)])
            (row0, nt, mods, col0) = samp_tiles[s_]
            rms_back(rms_front(row0, nt), nt, mods)
            qkv_tile(16, 4, True, k_samp, v_samp, s_ * 16, fold=False)
            ktiles = [(0, 128, 4, False), (1, 128, 3, False), (2, 128, 2, False), (3, 128, 1, False), (4, 16, 0, False)]
            attention(16, ktiles, (16, s_ * 16))

        P.barrier(dummy[:, 0:1])
        P.add("pool", "memset", reads=[], writes=["CAz"], ap=arena[:, 0:8704], constant=0.0)
        P.add("dve", "memset", reads=[], writes=["CAz2"], ap=arena[:, 8704:17408], constant=0.0)
        for q in range(4):
            dma("sp", Cin[:, q, 0:64], c_re[q * 128:(q + 1) * 128, :], [], ["Cin"])
            dma("sp", Cin[:, q, 64:128], c_im[q * 128:(q + 1) * 128, :], [], ["Cin2"])
        for q in range(4):
            P.add("pe", "transpose", reads=["Cin", "Cin2", "identf"], writes=["pS"], out=pS[:, q * 128:(q + 1) * 128],
                  in_=Cin[:, q, :], identity=identf[:, :])
        cp("dve", Cstk[:, :], pS[:, 0:512], ["pS"], ["Cstk"])
        cp("dve", Csw[0:64, :], Cstk[64:128, :], ["Cstk"], ["Csw"])
        cp("dve", Csw[64:128, :], Cstk[0:64, :], ["Cstk"], ["Csw"])
        AA1 = sq[:, 0:544].rearrange("p (g t) -> p g t", g=32)
        AA2 = xin[1][:, 0:544].rearrange("p (g t) -> p g t", g=32)
        ts("dve", AA1, PW1[:, :, :], cmask[:, 1:2], None, ALU.mult, None, ["PW", "cmask"], ["sq"])
        ts("dve", AA2, PW2[:, :, :], -1.0, None, ALU.mult, None, ["PW"], ["xin1"])
        Cst4 = Cstk[:, :].rearrange("p (s e c) -> p s e c", s=16, e=2)
        Csw4 = Csw[:, :].rearrange("p (s e c) -> p s e c", s=16, e=2)
        AA1v = AA1.rearrange("p (s e) t -> p s e t", e=2)
        AA2v = AA2.rearrange("p (s e) t -> p s e t", e=2)
        CAzv = CAz.rearrange("p (s e) t r c -> p s e t r c", e=2)
        for s0 in range(0, 16, 3):
            ns = min(3, 16 - s0)
            for e in range(2):
                t_a = qk_sb[:, 0:ns * 272].rearrange("p (s t c) -> p s t c", s=ns, t=17)
                t_b = xin[0][:, 0:ns * 272].rearrange("p (s t c) -> p s t c", s=ns, t=17)
                tt("dve", t_a, Cst4[:, s0:s0 + ns, e, :].rearrange("p s (o c) -> p s o c", o=1).to_broadcast([128, ns, 17, 16]),
                   AA1v[:, s0:s0 + ns, e, :].rearrange("p s (t o) -> p s t o", o=1).to_broadcast([128, ns, 17, 16]), ALU.mult,
                   ["Cstk", "sq"], ["qk_sb"])
                tt("pool", t_b, Csw4[:, s0:s0 + ns, e, :].rearrange("p s (o c) -> p s o c", o=1).to_broadcast([128, ns, 17, 16]),
                   AA2v[:, s0:s0 + ns, e, :].rearrange("p s (t o) -> p s t o", o=1).to_broadcast([128, ns, 17, 16]), ALU.mult,
                   ["Csw", "xin1"], ["xin0"])
                tt("dve", CAzv[:, s0:s0 + ns, e, :, e, :], t_a, t_b, ALU.add, ["qk_sb", "xin0", "CAz", "CAz2"], ["CAz"])
        for g in range(32):
            q, g8 = g // 8, g % 8
            pk = "pS%d" % (g % 2)
            P.add("pe", "matmul", reads=["Bbb", "CAz"], writes=[pk], out=pS[:, (g % 2) * 512:(g % 2) * 512 + 272],
                  lhsT=Bbb[:, :, :].rearrange("p g c -> p (g c)")[:, q * 128:(q + 1) * 128], rhs=CAz[:, g, :, g % 2, :], start=True, stop=True)
            if g % 2 == 0:
                ts("dve", Kblk[:, q, :, g8 * 16:(g8 + 1) * 16],
                   pS[:, 0:256].rearrange("p (t c) -> p t c", t=16), cmask[:, 4 + g8:5 + g8], None,
                   ALU.mult, None, [pk, "cmask"], [("Kb", g)])
            else:
                P.add("act", "activation", reads=[pk, "cmask"], writes=[("Kb", g)], out=Kblk[:, q, :, g8 * 16:(g8 + 1) * 16],
                      in_=pS[:, 512:768].rearrange("p (t c) -> p t c", t=16), func=AF.Copy, scale=cmask[:, 4 + g8:5 + g8])
        P.add("dve", "memset", reads=[("Kb", g_) for g_ in range(32)], writes=["Kblk"], ap=dummy[:, 1:2], constant=0.0)
        for q in range(4):
            P.add("dve", "scalar_tensor_tensor", reads=["Kblk", "identf", "bada"], writes=["Kblk"], out=Kblk[:, q, 0, :],
                  in0=identf[:, :], scalar=dq[:, q:q + 1], in1=Kblk[:, q, 0, :], op0=ALU.mult, op1=ALU.add)
        cp("dve", Xb[:, :, 0:128], XS[:, 0:128, :].rearrange("p k g -> p g k"), ["XS"], ["Xb"])
        cp("dve", Xb[:, :, 128:130], xs0[:, :, :].rearrange("p s g -> p g s"), ["xs0"], ["Xb"])
        blocks = [(b0 * 512, 512) for b0 in range(4)] + [(2048, 32)]
        for bi, (t0, ntk) in enumerate(blocks):
            nk = ntk // 16
            k0 = t0 // 16
            for q in range(4):
                bank = pA if q % 2 == 0 else pS
                bkey = "pA" if q % 2 == 0 else "pS"
                for tau in range(16):
                    ov = bank[:, 0:ntk].rearrange("p (k i) -> p k i", i=16)[:, :, tau:16]
                    uv = uT[:, q, t0:t0 + ntk].rearrange("p (k i) -> p k i", i=16)[:, :, 0:16 - tau]
                    P.add("pe", "matmul", reads=["uT", "Kblk"], writes=[bkey], out=ov, lhsT=Kblk[:, q, tau, :], rhs=uv,
                          start=(tau == 0), stop=False)
                for s_ in range(4):
                    for i in range(16):
                        for e in range(2):
                            g = q * 8 + s_ * 2 + e
                            last = (s_ == 3 and i == 15 and e == 1)
                            P.add("pe", "matmul", reads=["Xb", "CAz"], writes=[bkey],
                                  out=bank[32 * s_:32 * s_ + 32, 0:ntk].rearrange("p (k i) -> p k i", i=16)[:, :, i],
                                  lhsT=CAz[:, g, i + 1, :, :].rearrange("p r c -> p (r c)"), rhs=Xb[:, g, k0:k0 + nk], start=False, stop=last,
                                  tile_position=(0, 32 * s_), skip_group_check=True)
                P.add("act", "activation", reads=[bkey], writes=["yT"], out=yT[:, q, t0:t0 + ntk], in_=bank[:, 0:ntk],
                      func=AF.Copy)

        P.barrier(dummy[:, 0:1])
        wload2(w_glu_s, wglu_d[:, :], ["wglu_d"], "w_glu")
        wload2(w_gsms[:, :, :], wgsms_d[:, :], ["wgsms_d", "wgsms_d2"], "w_gs")
        wload2(w_os_s, wos_d[:, :], ["wos_d"], "w_os")
        dma("sp", w_out_s[:, :, :], wout_d.rearrange("(kt p) n -> p kt n", p=128), ["wout_d"], ["wada0", "wada1"])
        tilesC = own_tiles + samp_tiles
        NC2 = len(tilesC)
        xin3 = [xin[0], xin[1], xin2c]
        sgb2 = [sgb, av(32256, 512).rearrange("p (k n) -> p k n", k=4)]
        sgs2 = [sgs, av(32768, 512).rearrange("p (k n) -> p k n", k=4)]
        sms2 = [sms, av(33280, 1024).rearrange("p (k n) -> p k n", k=8)]
        sgt2 = [sgt, av(34304, 512).rearrange("p (k n) -> p k n", k=4)]
        mgt2 = [mgt, av(34816, 1024).rearrange("p (k n) -> p k n", k=8)]

        def c2_rms_front(n):
            (row0, nt, mods, col0) = tilesC[n]
            i3, i2 = n % 3, n % 2
            xt, xk = xin3[i3], "xinC%d" % i3
            xn, xnk = xn2[i2], "xn%d" % i2
            sst, ssk = ss2[i2], "ss%d" % i2
            dma("sp", xt[0:nt, :], x_all[row0:row0 + nt, :], [], [xk])
            P.add("act", "activation", reads=[xk], writes=[xnk, ssk], out=xn[0:nt, :], in_=xt[0:nt, :],
                  func=AF.Square, accum_out=sst[0:nt, 0:1])
            P.add("act", "activation", reads=[ssk, "epsc"], writes=[ssk], out=sst[0:nt, 2:3], in_=sst[0:nt, 0:1], func=AF.Ln,
                  scale=1.0 / D, bias=epsc[0:nt, 0:1])
            P.add("act", "activation", reads=[ssk], writes=[ssk], out=sst[0:nt, 3:4], in_=sst[0:nt, 2:3], func=AF.Exp, scale=-0.5)
            P.add("act", "activation", reads=[xk, ssk], writes=[xnk], out=xn[0:nt, :], in_=xt[0:nt, :],
                  func=AF.Copy, scale=sst[0:nt, 3:4])

        def c2_front_a(ti):
            (row0, nt, mods, col0) = tilesC[ti]
            c2_rms_front(ti)
            rms_back(ti % 2, nt, mods)

        def c2_front(ti):
            (row0, nt, mods, col0) = tilesC[ti]
            p = ti % 2
            CUR["i"] = p
            mti = (ti, 0) if ti < 16 else (16, (ti - 16) * 16)
            dma("sp", mgt2[p][:, :, 0:nt], mg_d[mti[0], :, :].rearrange("p (k t) -> p k t", k=8)[:, :, mti[1]:mti[1] + nt],
                [("mg_d", mti[0])], ["mgt%d" % p])
            for ct in range(8):
                for k in range(4):
                    P.add("pe", "matmul", reads=["yT", "w_glu"], writes=["pS"], out=pS[:, ct * 128:ct * 128 + nt],
                          lhsT=w_glu_s[:, k, ct * 128:(ct + 1) * 128], rhs=yT[:, k, col0:col0 + nt], start=(k == 0), stop=(k == 3))
            pSv = pS[:, :].rearrange("p (c t) -> p c t", t=128)
            for ct in range(4):
                P.add("act", "activation", reads=["pS", "bada"], writes=["sgb%d" % p], out=sgb2[p][:, ct, 0:nt], in_=pSv[:, 4 + ct, 0:nt],
                      func=AF.Sigmoid, bias=bglu[:, 4 + ct:5 + ct])
            for ct in range(12):
                for k in range(8):
                    P.add("pe", "matmul", reads=[HK(), "w_gs", "w_ms"], writes=["pA"], out=pA[:, ct * 128:ct * 128 + nt],
                          lhsT=w_gsms[:, k, ct * 128:(ct + 1) * 128], rhs=HT()[:, k, 0:nt], start=(k == 0), stop=(k == 7))
            pAv = pA[:, :].rearrange("p (c t) -> p c t", t=128)
            P.add("act", "activation", reads=["pA"], writes=["sgs%d" % p], out=sgs2[p][:, :, 0:nt], in_=pAv[:, 0:4, 0:nt], func=AF.Silu)
            P.add("act", "activation", reads=["pA"], writes=["sms%d" % p], out=sms2[p][:, :, 0:nt], in_=pAv[:, 4:12, 0:nt], func=AF.Sigmoid)
            for ct in range(4):
                P.add("dve", "scalar_tensor_tensor", reads=["pS", "sgb%d" % p, "bada"], writes=["sgt%d" % p], out=sgt2[p][:, ct, 0:nt],
                      in0=pSv[:, ct, 0:nt], scalar=bglu[:, ct:ct + 1], in1=sgb2[p][:, ct, 0:nt], op0=ALU.add, op1=ALU.mult)
            tt("dve", sgt2[p][:, :, 0:nt], sgt2[p][:, :, 0:nt], sgs2[p][:, :, 0:nt], ALU.mult, ["sgt%d" % p, "sgs%d" % p], ["sgt%d" % p])

        def c2_back(ti):
            (row0, nt, mods, col0) = tilesC[ti]
            p = ti % 2
            xt, xk = xin3[ti % 3], "xinC%d" % (ti % 3)
            for ct in range(8):
                for k in range(4):
                    P.add("pe", "matmul", reads=["sgt%d" % p, "w_os"], writes=["pO"], out=pO[:, ct // 4, (ct % 4) * 128:(ct % 4) * 128 + nt],
                          lhsT=w_os_s[:, k, ct * 128:(ct + 1) * 128], rhs=sgt2[p][:, k, 0:nt], start=(k == 0), stop=(k == 3))
            tt("dve", bst[:, :, 0:nt], pO[:, :, :].rearrange("p a (c t) -> p (a c) t", t=128)[:, :, 0:nt], sms2[p][:, :, 0:nt],
               ALU.mult, ["pO", "sms%d" % p], ["bst"])
            tt("dve", bst[:, :, 0:nt], bst[:, :, 0:nt], mgt2[p][:, :, 0:nt], ALU.add, ["bst", "mgt%d" % p], ["bst"])

        def c2_back_b(ti):
            (row0, nt, mods, col0) = tilesC[ti]
            p = ti % 2
            xt, xk = xin3[ti % 3], "xinC%d" % (ti % 3)
            for half in range(2):
                for k in range(8):
                    P.add("pe", "matmul", reads=["bst", "wada0", "wada1"], writes=["pO"],
                          out=pO[0:nt, half, :], lhsT=bst[:, k, 0:nt],
                          rhs=w_out_s[:, k, half * 512:(half + 1) * 512], start=(k == 0), stop=(k == 7))
            tt("dve", qk_sb[0:nt, :], pO[0:nt, :, :].rearrange("p a c -> p (a c)"), gate_bc[0:nt, 0 if ti < 16 else ti - 15, :], ALU.mult,
               ["pO", "gate_bc"], ["qk_sb"])
            tt("dve", sq[0:nt, :], qk_sb[0:nt, :], xt[0:nt, :], ALU.add, ["qk_sb", xk], ["sq"])
            dma("sp", y_out[col0:col0 + nt, :], sq[0:nt, :], ["sq"], [])

        c2_front_a(0)
        c2_front(0)
        for ti in range(NC2):
            if ti + 1 < NC2:
                c2_front_a(ti + 1)
            c2_back(ti)
            if ti + 1 < NC2:
                c2_front(ti + 1)
            c2_back_b(ti)

        P.emit(nc)
    return nc


_NC = None


def kernel(x_prompt, x_sample, c_prompt, c_sample, cache_k, cache_v, state_ssm_re, state_ssm_im,
           norm_g, w_ada, b_ada, w_in, q_norm_g, k_norm_g, rel_bias, lambda_re, lambda_im, log_dt,
           b_re, b_im, c_re, c_im, d_skip, w_glu, b_glu, w_oa, w_os, w_out):
    global _NC
    f = lambda a: np.ascontiguousarray(np.asarray(a, dtype=np.float32))
    x_prompt, x_sample = f(x_prompt), f(x_sample)
    if _NC is None:
        _NC = build()
    nc = _NC
    ident = np.eye(128, dtype=np.float32)
    sel = np.zeros((3, 256), np.float32)
    sel[0, 0:128] = 1.0
    sel[1, 128:144] = 1.0
    sel[2, 144:160] = 1.0
    pidx = np.arange(128)
    cmask = np.zeros((128, 16), np.float32)
    cmask[:, 0] = np.where(pidx < 64, -1.0, 1.0)
    cmask[:, 1] = -cmask[:, 0]
    for par in range(2):
        cmask[:, 2 + par] = ((pidx // 16) % 2 == par)
    for g8 in range(8):
        cmask[:, 4 + g8] = (pidx // 16 == g8)
    in_maps = []
    for c in range(8):
        b, j = c // 4, c % 4
        t0 = j * NOWN
        halo = np.zeros((NHALO, D), np.float32) if j == 0 else x_prompt[b, t0 - NHALO:t0]
        xs = x_sample[2 * c:2 * c + 2].reshape(NSAMP, D)
        x_all = np.concatenate([halo, x_prompt[b, t0:t0 + NOWN], xs], axis=0)
        c3 = np.stack([f(c_prompt)[b], f(c_sample)[2 * c], f(c_sample)[2 * c + 1]])
        hbias = np.full((128, 1), -30000.0 if j == 0 else 0.0, np.float32)
        x_prev = np.zeros((3 * NOWN, D), np.float32)
        pmask = np.ones((128, 4), np.float32)
        for i in range(3):
            js = j - 3 + i
            if js >= 0:
                x_prev[i * NOWN:(i + 1) * NOWN] = x_prompt[b, js * NOWN:(js + 1) * NOWN]
            else:
                pmask[:, i] = 0.0
        selm = np.zeros((128, 24), np.float32)
        for jr in range(j):
            selm[:, (b * 4 + jr) * 3 + (j - 1 - jr)] = 1.0
        in_maps.append({
            "x_all": np.ascontiguousarray(x_all), "c3": np.ascontiguousarray(c3),
            "cache_k": f(cache_k)[0, 2 * c:2 * c + 2].reshape(2, 512, 512),
            "cache_v": f(cache_v)[0, 2 * c:2 * c + 2].reshape(2, 512, 512),
            "hbias": hbias, "sel": sel, "ident": ident, "cmask": cmask, "selm": selm, "x_prev": x_prev, "pmask": pmask,
            "st_re": f(state_ssm_re)[0, 2 * c:2 * c + 2].reshape(64, 64),
            "st_im": f(state_ssm_im)[0, 2 * c:2 * c + 2].reshape(64, 64),
            "lambda_re": f(lambda_re)[0], "lambda_im": f(lambda_im)[0], "log_dt": f(log_dt)[0],
            "b_re": f(b_re)[0], "b_im": f(b_im)[0], "c_re": f(c_re)[0].reshape(512, 64), "c_im": f(c_im)[0].reshape(512, 64),
            "d_skip": f(d_skip)[0], "w_glu": f(w_glu)[0], "b_glu": f(b_glu)[0], "w_os": f(w_os)[0],
            "norm_g": f(norm_g)[0], "w_ada": f(w_ada)[0], "b_ada": f(b_ada)[0], "w_in": f(w_in)[0],
            "q_norm_g": f(q_norm_g)[0], "k_norm_g": f(k_norm_g)[0], "rel_bias": f(rel_bias)[0],
            "w_oa": f(w_oa)[0], "w_out": f(w_out)[0],
        })
    res = run_bass_kernel_spmd(nc, in_maps, core_ids=list(range(8)))
    R = res.results
    y_prompt = np.zeros((2, 8192, D), np.float32)
    y_sample = np.zeros((16, 16, D), np.float32)
    nk_p = np.zeros((1, 2, 512, 8, 64), np.float32)
    nv_p = np.zeros((1, 2, 512, 8, 64), np.float32)
    sr_p = np.zeros((1, 2, 32, 64), np.float32)
    si_p = np.zeros((1, 2, 32, 64), np.float32)
    nk_s = np.zeros((1, 16, 16, 8, 64), np.float32)
    nv_s = np.zeros((1, 16, 16, 8, 64), np.float32)
    sr_s = np.zeros((1, 16, 32, 64), np.float32)
    si_s = np.zeros((1, 16, 32, 64), np.float32)
    for c in range(8):
        b, j = c // 4, c % 4
        r = R[c]
        y_prompt[b, j * NOWN:(j + 1) * NOWN] = r["y_out"][0:NOWN]
        y_sample[2 * c:2 * c + 2] = r["y_out"][NOWN:].reshape(2, 16, D)
        if j == 3:
            nk_p[0, b] = r["k_last"].reshape(512, 8, 64)
            nv_p[0, b] = r["v_last"].reshape(512, 8, 64)
        nk_s[0, 2 * c:2 * c + 2] = r["k_samp"].reshape(2, 16, 8, 64)
        if j == 3:
            sr_p[0, b] = r["ssm_p"][:, 0:64]
            si_p[0, b] = r["ssm_p"][:, 64:128]
        sr_s[0, 2 * c:2 * c + 2] = r["ssm_s"][:, 0:64].reshape(2, 32, 64)
        si_s[0, 2 * c:2 * c + 2] = r["ssm_s"][:, 64:128].reshape(2, 32, 64)
        nv_s[0, 2 * c:2 * c + 2] = r["v_samp"].reshape(2, 16, 8, 64)
    return (y_prompt, y_sample, nk_p, nv_p, sr_p, si_p, nk_s, nv_s, sr_s, si_s)
```

```python
import numpy as np
import os
STAGE = int(os.environ.get('KSTAGE', '99'))
SUB = int(os.environ.get('KSUB', '99'))
KRMS = int(os.environ.get('KRMS', '99'))
import ml_dtypes
from contextlib import ExitStack
import concourse.bass as bass
import concourse.mybir as mybir
from concourse.bass_utils import run_bass_kernel_spmd

F32 = mybir.dt.float32
BF16 = mybir.dt.bfloat16
ALU = mybir.AluOpType
AF = mybir.ActivationFunctionType
AX = mybir.AxisListType

D = 1024
NOWN = 2048
NHALO = 512
NSAMP = 32
EPS = 1e-6


PSUM_KEYS = ("pA", "pT", "pS", "pO", "pA0", "pA1", "pA2", "pS0", "pS1", "pO0", "pO1")


class Prog:
    ENG = ("pe", "act", "dve", "pool", "sp")

    def __init__(self):
        self.ops = []
        self.last_w = {}
        self.readers = {}

    def defer_start(self):
        self._pend = []
        self._defer = True

    def defer_stop(self):
        self._defer = False

    def flush(self, n=None):
        pend = getattr(self, "_pend", [])
        k = len(pend) if n is None else min(n, len(pend))
        was = getattr(self, "_defer", False)
        self._defer = False
        for a in pend[:k]:
            self.add(*a[0], **a[1])
        self._defer = was
        self._pend = pend[k:]

    def add(self, eng, name, reads=(), writes=(), dma=False, nophase=False, **kw):
        if getattr(self, "_defer", False):
            self._pend.append(((eng, name), dict(reads=list(reads), writes=list(writes), dma=dma, nophase=nophase, **kw)))
            return None
        op = dict(eng=eng, name=name, kw=kw, dma=dma, deps=set(), idx=len(self.ops), sig=dma)
        reads = list(reads)
        if not nophase:
            reads.append("PHASE")
        writes = list(writes) + [r for r in reads if r in PSUM_KEYS]
        reads = [r for r in reads if r not in PSUM_KEYS]
        for r in reads:
            lw = self.last_w.get(r)
            if lw is not None:
                op["deps"].add(lw)
        for w in writes:
            lw = self.last_w.get(w)
            if lw is not None:
                op["deps"].add(lw)
            for rd in self.readers.get(w, ()):
                op["deps"].add(rd)
        for r in reads:
            self.readers.setdefault(r, []).append(op["idx"])
        for w in writes:
            self.last_w[w] = op["idx"]
            self.readers[w] = []
        op["deps"].discard(op["idx"])
        self.ops.append(op)
        return op

    def barrier(self, arena_ap):
        self.add("dve", "memset", reads=[], writes=["PHASE"], nophase=True, ap=arena_ap, constant=0.0)

    def emit(self, nc, ndma_sems=16):
        ops = self.ops
        for op in ops:
            nd = set()
            for d in op["deps"]:
                p = ops[d]
                if (not p["dma"]) and p["eng"] == op["eng"] and p["eng"] == "pe" and not op["dma"]:
                    continue
                nd.add(d)
            op["deps"] = nd
            for d in nd:
                ops[d]["sig"] = True
        cnt = {e: 0 for e in self.ENG}
        dcnt = {e: 0 for e in self.ENG}
        for op in ops:
            e = op["eng"]
            if op["dma"]:
                i = dcnt[e]
                dcnt[e] += 1
                op["sem"] = ("d", e, i % ndma_sems)
                op["val"] = 16 * (i // ndma_sems + 1)
            elif op["sig"]:
                cnt[e] += 1
                op["sem"] = ("c", e)
                op["val"] = cnt[e]
        with ExitStack() as st:
            sems = {}
            for e in self.ENG:
                sems[("c", e)] = st.enter_context(nc.semaphore("c_" + e))
                if dcnt[e]:
                    for i in range(ndma_sems):
                        sems[("d", e, i)] = st.enter_context(nc.semaphore("d_%s_%d" % (e, i)))
            block = st.enter_context(nc.Block())
            byeng = {e: [o for o in ops if o["eng"] == e] for e in self.ENG}

            def run(engname, eng):
                known = {}
                for op in byeng[engname]:
                    waits = {}
                    for d in op["deps"]:
                        p = ops[d]
                        waits[p["sem"]] = max(waits.get(p["sem"], 0), p["val"])
                    if op["dma"] and op["val"] > 16:
                        waits[op["sem"]] = max(waits.get(op["sem"], 0), op["val"] - 16)
                    for s, v in waits.items():
                        if known.get(s, 0) >= v:
                            continue
                        eng.wait_ge(sems[s], v)
                        known[s] = v
                    ins = getattr(eng, op["name"])(**op["kw"])
                    if op["sig"]:
                        ins.then_inc(sems[op["sem"]], 16 if op["dma"] else 1)
                last = {}
                for op in byeng[engname]:
                    if op["dma"]:
                        last[op["sem"]] = op["val"]
                for s, v in last.items():
                    if known.get(s, 0) < v:
                        eng.wait_ge(sems[s], v)

            block.tensor(lambda eng: run("pe", eng))
            block.scalar(lambda eng: run("act", eng))
            block.vector(lambda eng: run("dve", eng))
            block.gpsimd(lambda eng: run("pool", eng))
            block.sync(lambda eng: run("sp", eng))


def build():
    nc = bass.Bass("TRN2", target_bir_lowering=False)

    def din(name, shape, dt=F32):
        return nc.dram_tensor(name, list(shape), dt, kind="ExternalInput").ap()

    def dout(name, shape, dt=F32):
        return nc.dram_tensor(name, list(shape), dt, kind="ExternalOutput").ap()

    NTOK = NHALO + NOWN + NSAMP
    NT = NOWN + NSAMP
    NCH = NT // 16
    x_all = din("x_all", [NTOK, D])
    x_prev = din("x_prev", [3 * NOWN, D])
    pmask_d = din("pmask", [128, 4])
    c3 = din("c3", [3, D])
    cache_k = din("cache_k", [2, 512, 512])
    cache_v = din("cache_v", [2, 512, 512])
    st_re = din("st_re", [64, 64])
    st_im = din("st_im", [64, 64])
    hbias = din("hbias", [128, 1])
    sel = din("sel", [3, 256])
    ident_d = din("ident", [128, 128])
    cmask_d = din("cmask", [128, 16])
    selm_d = din("selm", [128, 24])
    norm_g = din("norm_g", [D])
    w_ada = din("w_ada", [D, 3 * D])
    b_ada = din("b_ada", [3 * D])
    w_in = din("w_in", [D, 5120])
    q_norm_g = din("q_norm_g", [64])
    k_norm_g = din("k_norm_g", [64])
    rel_bias = din("rel_bias", [8, 257])
    lam_re = din("lambda_re", [32, 64])
    lam_im = din("lambda_im", [32, 64])
    log_dt = din("log_dt", [32])
    b_re = din("b_re", [32, 64, 16])
    b_im = din("b_im", [32, 64, 16])
    c_re = din("c_re", [512, 64])
    c_im = din("c_im", [512, 64])
    d_skip = din("d_skip", [512])
    w_glu = din("w_glu", [512, 1024])
    b_glu = din("b_glu", [1024])
    w_oa = din("w_oa", [512, D])
    w_os = din("w_os", [512, D])
    w_out = din("w_out", [D, D])

    y_out = dout("y_out", [NT, D])
    k_last = dout("k_last", [512, 512])
    v_last = dout("v_last", [512, 512])
    k_samp = dout("k_samp", [NSAMP, 512])
    v_samp = dout("v_samp", [NSAMP, 512])
    ssm_p = dout("ssm_p", [32, 128])
    ssm_s = dout("ssm_s", [64, 128])

    e_d = nc.dram_tensor("e_d", [8, 768], F32, kind="Internal").ap()
    m_d = nc.dram_tensor("m_d", [8, 130 * 768], F32, kind="Internal").ap()
    mg_d = nc.dram_tensor("mg_d", [17, 128, 1024], BF16, kind="Internal").ap()
    sloc_d = nc.dram_tensor("sloc_d", [128, 32], F32, kind="Internal").ap()
    wq_d = nc.dram_tensor("wq_d", [D, 1536], BF16, kind="Internal").ap()
    wgm_d = nc.dram_tensor("wgm_d", [D, 1536], BF16, kind="Internal").ap()
    woa_d = nc.dram_tensor("woa_d", [512, D], BF16, kind="Internal").ap()
    wglu_d = nc.dram_tensor("wglu_d", [512, D], BF16, kind="Internal").ap()
    wgsms_d = nc.dram_tensor("wgsms_d", [D, 1536], BF16, kind="Internal").ap()
    wos_d = nc.dram_tensor("wos_d", [512, D], BF16, kind="Internal").ap()
    wout_d = nc.dram_tensor("wout_d", [D, D], BF16, kind="Internal").ap()
    sall_d = nc.dram_tensor("sall_d", [1024, 32], F32, kind="Internal").ap()

    P = Prog()
    st = ExitStack()
    with st:
        def sb(name, shape, dt=F32):
            return st.enter_context(nc.sbuf_tensor("s_" + name, list(shape), dt))

        def ps(name, shape, dt=F32):
            return st.enter_context(nc.psum_tensor("p_" + name, list(shape), dt))

        dummy = sb("dummy", [128, 2])
        epsc = sb("epsc", [128, 1])
        ident = sb("ident", [128, 128], BF16)
        identf = sb("identf", [128, 128])
        selT = sb("selT", [3, 256])
        hb = sb("hb", [128, 1])
        cmask = sb("cmask", [128, 16])
        selm = sb("selm", [128, 24])
        pmask = sb("pmask", [128, 4])
        stg = sb("stg", [68, 128])
        smalls = sb("smalls", [128, 68])
        cT = smalls[:, 0:24].rearrange("p (b k) -> p k b", b=3)
        bada = smalls[:, 24:48]
        ng = smalls[:, 48:56]
        dq = smalls[:, 56:60]
        bglu = smalls[:, 60:68]
        scT = sb("scT", [128, 8, 3], BF16)
        modT = sb("modT", [128, 24, 3])
        Amod = sb("Amod", [128, 8, 3])
        gate_bc = sb("gate_bc", [128, 3, 1024], BF16)
        gqk = sb("gqk", [128, 1024], BF16)
        gqg = sb("gqg", [128, 512], BF16)
        xin = [sb("xin%d" % i, [128, 1024]) for i in range(2)]
        xin2c = sb("xin2c", [128, 1024])
        ss2 = [sb("ss%d" % i, [128, 4]) for i in range(2)]
        xn2 = [sb("xn%d" % i, [128, 1024], BF16) for i in range(2)]
        hT2 = [sb("hT%d" % i, [128, 8, 128], BF16) for i in range(2)]
        e_s = xin[1][0:8, 0:768]
        qk_sb = sb("qk_sb", [128, 1024])
        sq = sb("sq", [128, 1024])
        ss16 = sb("ss16", [128, 16])
        qkb = sb("qkb", [128, 1024], BF16)
        kout = sb("kout", [128, 512])
        vout = sb("vout", [128, 512])
        qT = sb("qT", [128, 4, 128], BF16)
        stmp2 = [sb("stmp%d" % i, [128, 5, 128]) for i in range(2)]
        PT2 = [sb("PT%d" % i, [128, 5, 128], BF16) for i in range(2)]
        stmp, PT = stmp2[0], PT2[0]
        rden = sb("rden", [128, 8])
        AO = sb("AO", [128, 8, 64], BF16)
        sga = sb("sga", [128, 4, 128], BF16)
        sma = sb("sma", [128, 8, 128], BF16)
        AOgT = sb("AOgT", [128, 4, 128], BF16)
        mgt = sb("mgt", [128, 8, 128], BF16)
        uT = sb("uT", [128, 4, NT], BF16)
        XS = sb("XS", [128, 129, 32])
        prm = sb("prm", [128, 36, 32])
        PW1 = sb("PW1", [128, 32, 17])
        PW2 = sb("PW2", [128, 32, 17])
        Bstk = sb("Bstk", [128, 32, 16])
        Bsw = sb("Bsw", [128, 32, 16])
        Bbb = sb("Bbb", [128, 32, 16], BF16)
        Cstk = sb("Cstk", [128, 512])
        Csw = sb("Csw", [128, 512])
        xs0 = sb("xs0", [128, 2, 32])
        WSs = sb("WSs", [128, 2, 32])
        Gall = sb("Gall", [128, 8, 32])
        ki32 = sb("ki32", [128, 32], mybir.dt.int32)

        ARN = 45184
        arena = sb("arena", [128, ARN], BF16)

        def av(off, n):
            return arena[:, off:off + n]

        wada = [av(28672, 4096).rearrange("p (k n) -> p k n", k=8), av(32768, 4096).rearrange("p (k n) -> p k n", k=8)]
        w_out_s = av(24064, 8192).rearrange("p (k n) -> p k n", k=8)
        scr = av(28672, 16512).bitcast(F32)
        w_qkv = av(0, 12288).rearrange("p (k n) -> p k n", k=8)
        w_gm = av(12288, 12288).rearrange("p (k n) -> p k n", k=8)
        w_oa_s = av(24576, 4096).rearrange("p (k n) -> p k n", k=4)
        KT = av(28672, 3072).rearrange("p (k n) -> p k n", k=4)
        V = av(31744, 3120).rearrange("p (s h e) -> p s h e", s=6, h=8)
        BM = av(34880, 10240).bitcast(F32).rearrange("p (h t q) -> p h t q", h=8, t=5)
        w_u = av(0, 4096).rearrange("p (k n) -> p k n", k=8)
        W1z = av(4096, 16384).rearrange("p (q r j m) -> p q r j m", q=4, r=2, j=16)
        Pst = av(20480, 8192).rearrange("p (q t g c) -> p q t g c", q=4, t=16, g=8)
        CAz = av(0, 17408).rearrange("p (g t r c) -> p g t r c", g=32, t=17, r=2)
        Kblk = av(17408, 8192).rearrange("p (q t m) -> p q t m", q=4, t=16)
        Xb = av(25600, 4160).rearrange("p (g k) -> p g k", g=32)
        Cin = av(29760, 1024).bitcast(F32).rearrange("p (q m) -> p q m", q=4)
        yT = av(36608, 8320).rearrange("p (q n) -> p q n", q=4)
        w_glu_s = av(0, 4096).rearrange("p (k n) -> p k n", k=4)
        w_gsms = av(4096, 12288).rearrange("p (k n) -> p k n", k=8)
        w_os_s = av(16384, 4096).rearrange("p (k n) -> p k n", k=4)
        sgb = av(20480, 512).rearrange("p (k n) -> p k n", k=4)
        sgs = av(20992, 512).rearrange("p (k n) -> p k n", k=4)
        sms = av(21504, 1024).rearrange("p (k n) -> p k n", k=8)
        sgt = av(22528, 512).rearrange("p (k n) -> p k n", k=4)
        bst = av(23040, 1024).rearrange("p (k n) -> p k n", k=8)

        pA = ps("pA", [128, 1536])
        pT = ps("pT", [128, 8, 128], BF16)
        pS = ps("pS", [128, 1024])
        pO = ps("pO", [128, 2, 512])

        def bfv(ap_):
            return ap_.bitcast(BF16).rearrange("p (k t) -> p k t", t=128)

        TA = [pT[:, 0:4, :], bfv(pO[:, 0, :])]
        TAk = ["pT", "pO0"]
        TB = [bfv(pA[:, 1024:1536]), bfv(pO[:, 1, :])]
        TBk = ["pA2", "pO1"]
        UB = [pA[:, 0:512], pA[:, 512:1024]]
        UBk = ["pA0", "pA1"]

        def dma(eng, out, in_, reads, writes, **kw):
            P.add(eng, "dma_start", reads=reads, writes=writes, dma=True, out=out, in_=in_, **kw)

        def wload(dst, src, key):
            dma("pool", dst, src.rearrange("(kt p) n -> p kt n", p=128), [], [key])

        def tt(eng, out, in0, in1, op, reads, writes, **kw):
            P.add(eng, "tensor_tensor", reads=reads, writes=writes, out=out, in0=in0, in1=in1, op=op, **kw)

        def ts(eng, out, in0, s1, s2, op0, op1, reads, writes, **kw):
            if s2 is None:
                P.add(eng, "tensor_scalar", reads=reads, writes=writes, out=out, in0=in0, scalar1=s1, scalar2=None,
                      op0=op0, **kw)
            else:
                P.add(eng, "tensor_scalar", reads=reads, writes=writes, out=out, in0=in0, scalar1=s1, scalar2=s2,
                      op0=op0, op1=op1, **kw)

        def cp(eng, out, in_, reads, writes, **kw):
            P.add(eng, "tensor_copy", reads=reads, writes=writes, out=out, in_=in_, **kw)

        P.add("pool", "memset", reads=[], writes=["epsc"], nophase=True, ap=epsc[:, :], constant=EPS)
        dma("pool", ident[:, :], ident_d[:, :], [], ["ident"])
        dma("sp", identf[:, :], ident_d[:, :], [], ["identf"])
        dma("sp", selT[:, :], sel[:, :], [], ["selT"])
        dma("sp", hb[:, :], hbias[:, :], [], ["hb"])
        dma("sp", cmask[:, :], cmask_d[:, :], [], ["cmask"])
        dma("sp", selm[:, :], selm_d[:, :], [], ["selm"])
        dma("sp", pmask[:, :], pmask_d[:, :], [], ["pmask"])
        dma("sp", stg[0:24, :], c3.rearrange("b (kt p) -> (b kt) p", p=128), [], ["stg"])
        dma("sp", stg[24:48, :], b_ada.rearrange("(ct p) -> ct p", p=128), [], ["stg1"])
        dma("sp", stg[48:56, :], norm_g.rearrange("(kt p) -> kt p", p=128), [], ["stg2"])
        dma("sp", stg[56:60, :], d_skip.rearrange("(kt p) -> kt p", p=128), [], ["stg3"])
        dma("sp", stg[60:68, :], b_glu.rearrange("(kt p) -> kt p", p=128), [], ["stg4"])
        P.add("pe", "transpose", reads=["stg", "stg1", "stg2", "stg3", "stg4", "identf"], writes=["pS"], out=pS[:, 0:68],
              in_=stg[0:68, :], identity=identf[0:68, 0:68])
        cp("dve", smalls[:, :], pS[:, 0:68], ["pS"], ["cT", "bada", "ng"])
        bgate = sq[0:3, :]
        gate_tok = qk_sb[0:3, :]
        dma("pool", gqk[:, 0:512], bass.AP(tensor=q_norm_g.tensor, offset=0, ap=[[0, 128], [0, 8], [1, 64]]),
            [], ["gqk_q"])
        dma("pool", gqk[:, 512:1024], bass.AP(tensor=k_norm_g.tensor, offset=0, ap=[[0, 128], [0, 8], [1, 64]]),
            [], ["gqk_k"])
        tt("dve", gqg[:, :], gqk[:, 0:512], gqk[:, 512:1024], ALU.mult, ["gqk_q", "gqk_k"], ["gqg"])
        dma("sp", e_s[:, 129:385], rel_bias[:, 1:257], [], ["xin1"])
        cp("dve", e_s[:, 0:129], e_s[:, 384:385].to_broadcast([8, 129]), ["xin1"], ["xin1"])
        cp("dve", e_s[:, 385:768], e_s[:, 384:385].to_broadcast([8, 383]), ["xin1"], ["xin1"])
        dma("sp", e_d[:, :], e_s[:, :], ["xin1"], ["e_d"])
        dma("sp", m_d.rearrange("h (r e) -> h r e", e=768),
            bass.AP(tensor=e_d.tensor, offset=0, ap=[[768, 8], [0, 130], [1, 768]]), ["e_d"], ["m_d"])

        TI = {"n": 0, "pend": None}

        def rms_front(row0, nt, src=None):
            src = x_all if src is None else src
            i = TI["n"] % 2
            TI["n"] += 1
            xt, xk = xin[i], "xin%d" % i
            xn, xnk = xn2[i], "xn%d" % i
            sst, ssk = ss2[i], "ss%d" % i
            dma("sp", xt[0:nt, :], src[row0:row0 + nt, :], [], [xk])
            P.add("act", "activation", reads=[xk], writes=[xnk, ssk], out=xn[0:nt, :], in_=xt[0:nt, :],
                  func=AF.Square, accum_out=sst[0:nt, 0:1])
            P.add("act", "activation", reads=[ssk, "epsc"], writes=[ssk], out=sst[0:nt, 2:3], in_=sst[0:nt, 0:1], func=AF.Ln,
                  scale=1.0 / D, bias=epsc[0:nt, 0:1])
            P.add("act", "activation", reads=[ssk], writes=[ssk], out=sst[0:nt, 3:4], in_=sst[0:nt, 2:3], func=AF.Exp, scale=-0.5)
            P.add("act", "activation", reads=[xk, ssk], writes=[xnk], out=xn[0:nt, :], in_=xt[0:nt, :],
                  func=AF.Copy, scale=sst[0:nt, 3:4])
            return i

        def rms_back(i, nt, mods):
            CUR["i"] = i
            xn, xnk = xn2[i], "xn%d" % i
            hT, hk = HT(), HK()
            for k in range(8):
                P.add("pe", "transpose", reads=[xnk, "ident"], writes=["pT"], out=pT[:, k, 0:nt],
                      in_=xn[0:nt, k * 128:(k + 1) * 128], identity=ident[0:nt, 0:nt])
            for k in range(8):
                for (c0, c1, b) in mods:
                    if k < 4:
                        ts("dve", hT[:, k, c0:c1], pT[:, k, c0:c1], Amod[:, k, b:b + 1], modT[:, k, b:b + 1],
                           ALU.mult, ALU.add, ["pT", "Amod", "modT"], [hk])
                    else:
                        P.add("act", "activation", reads=["pT", "Amod", "modT"], writes=[hk], out=hT[:, k, c0:c1],
                              in_=pT[:, k, c0:c1], func=AF.Identity, scale=Amod[:, k, b:b + 1], bias=modT[:, k, b:b + 1])

        def run_tiles(tiles, body, src=None):
            nxt = rms_front(tiles[0][0], tiles[0][1], src)
            for n, tl in enumerate(tiles):
                cur = nxt
                rms_back(cur, tl[1], tl[2])
                if n + 1 < len(tiles):
                    nxt = rms_front(tiles[n + 1][0], tiles[n + 1][1], src)
                CUR["i"] = cur
                body(n, tl, cur)

        CUR = {"i": 0}

        def HT():
            return hT2[CUR["i"]]

        def HK():
            return "hT%d" % CUR["i"]

        def qkv_mm(nt, with_q):
            for cb in range(0 if with_q else 1, 3):
                for k in range(8):
                    P.add("pe", "matmul", reads=[HK(), "w_qkv"], writes=["pA"], out=pA[0:nt, cb * 512:(cb + 1) * 512],
                          lhsT=HT()[:, k, 0:nt], rhs=w_qkv[:, k, cb * 512:(cb + 1) * 512], start=(k == 0), stop=(k == 7))

        def qkv_tile(nt, slot, with_q, kdst, vdst, out_rows, gm=False, fold=True, pre_mm=False):
            c_lo = 0 if with_q else 512
            if not pre_mm:
                qkv_mm(nt, with_q)
            if gm:
                for ct in range(12):
                    for k in range(8):
                        if ct < 4:
                            o_, ok_ = pO[:, 0, ct * 128:ct * 128 + nt], "pO"
                        else:
                            o_, ok_ = pS[:, (ct - 4) * 128:(ct - 4) * 128 + nt], "pS"
                        P.add("pe", "matmul", reads=[HK(), "w_gm", "w_gm2"], writes=[ok_], out=o_,
                              lhsT=w_gm[:, k, ct * 128:(ct + 1) * 128], rhs=HT()[:, k, 0:nt], start=(k == 0), stop=(k == 7))
            nh = 16 if with_q else 8
            h0 = 0 if with_q else 8
            P.add("act", "activation", reads=["pA"], writes=["sq"], out=sq[0:nt, c_lo:1024], in_=pA[0:nt, c_lo:1024],
                  func=AF.Square)
            P.add("dve", "tensor_reduce", reads=["sq"], writes=["ss16"], out=ss16[0:nt, h0:16],
                  in_=sq[0:nt, c_lo:1024].rearrange("p (h d) -> p h d", d=64), axis=AX.X, op=ALU.add)
            P.add("act", "activation", reads=["ss16", "epsc"], writes=["ss16"], out=ss16[0:nt, h0:16], in_=ss16[0:nt, h0:16],
                  func=AF.Ln, scale=1.0 / 64, bias=epsc[0:nt, 0:1])
            P.add("act", "activation", reads=["ss16"], writes=["ss16"], out=ss16[0:nt, h0:16], in_=ss16[0:nt, h0:16],
                  func=AF.Exp, scale=-0.5)
            rk = ss16[0:nt, 8:16].rearrange("p (h o) -> p h o", o=1).to_broadcast([nt, 8, 64])
            rq = ss16[0:nt, 0:8].rearrange("p (h o) -> p h o", o=1).to_broadcast([nt, 8, 64])
            pAk = pA[0:nt, 512:1024].rearrange("p (h d) -> p h d", d=64)
            pAq = pA[0:nt, 0:512].rearrange("p (h d) -> p h d", d=64)
            if fold:
                tt("dve", qkb[0:nt, 512:1024].rearrange("p (h d) -> p h d", d=64), pAk, rk, ALU.mult, ["pA", "ss16"], ["qkb"])
            else:
                tt("dve", sq[0:nt, 512:1024].rearrange("p (h d) -> p h d", d=64), pAk, rk, ALU.mult, ["pA", "ss16"], ["sq"])
                tt("dve", qkb[0:nt, 512:1024], sq[0:nt, 512:1024], gqk[0:nt, 512:1024], ALU.mult, ["sq", "gqk_k"], ["qkb"])
            if with_q:
                tt("dve", sq[0:nt, 0:512].rearrange("p (h d) -> p h d", d=64), pAq, rq, ALU.mult, ["pA", "ss16"], ["sq"])
                tt("dve", qkb[0:nt, 0:512], sq[0:nt, 0:512], gqg[0:nt, :] if fold else gqk[0:nt, 0:512], ALU.mult,
                   ["sq", "gqk_q", "gqg"], ["qkb"])
            P.add("act", "activation", reads=["pA", "Vones"], writes=[("V", slot)], out=V[0:nt, slot, :, 0:64],
                  in_=pA[0:nt, 1024:1536].rearrange("p (h d) -> p h d", d=64), func=AF.Copy)
            if gm:
                P.add("act", "activation", reads=["pS"], writes=["sma"], out=sma[:, :, 0:nt],
                      in_=pS[:, :].rearrange("p (c t) -> p c t", t=128)[:, :, 0:nt], func=AF.Sigmoid)
                P.add("act", "activation", reads=["pO"], writes=["sga"], out=sga[:, :, 0:nt],
                      in_=pO[:, 0, :].rearrange("p (c t) -> p c t", t=128)[:, :, 0:nt], func=AF.Silu)
            if kdst is not None:
                tt("dve", kout[0:nt, :].rearrange("p (h d) -> p h d", d=64), pAk, rk, ALU.mult, ["pA", "ss16"], ["kout"])
                tt("dve", kout[0:nt, :], kout[0:nt, :], gqk[0:nt, 512:1024], ALU.mult, ["kout", "gqk_k"], ["kout"])
                dma("sp", kdst[out_rows:out_rows + nt, :], kout[0:nt, :], ["kout"], [])
                cp("dve", vout[0:nt, :], pA[0:nt, 1024:1536], ["pA"], ["vout"])
                dma("sp", vdst[out_rows:out_rows + nt, :], vout[0:nt, :], ["vout"], [])
            for i in range(0 if with_q else 4, 8):
                P.add("pe", "transpose", reads=["qkb", "ident"], writes=["pT"], out=pT[:, i, 0:nt],
                      in_=qkb[0:nt, i * 128:(i + 1) * 128], identity=ident[0:nt, 0:nt])
            if with_q:
                cp("dve", qT[:, :, 0:nt], pT[:, 0:4, 0:nt], ["pT"], ["qT"])
            P.add("act", "activation", reads=["pT"], writes=[("KT", slot)], out=KT[:, :, slot * 128:slot * 128 + nt],
                  in_=pT[:, 4:8, 0:nt], func=AF.Copy)

        def attention(nq, ktiles, mtile):
            nkt = len(ktiles)
            nk_last = ktiles[4][1]
            for h in range(8):
                hp, h2 = h // 2, h % 2
                pr = slice(64 * h2, 64 * h2 + 64)
                for j, (slot, nk, tp_, halo) in enumerate(ktiles):
                    P.add("pe", "matmul", reads=[("KT", slot), "qT"], writes=["pS"], out=pS[0:nk, (4 - j) * 128:(4 - j) * 128 + nq],
                          lhsT=KT[pr, hp, slot * 128:slot * 128 + nk], rhs=qT[pr, hp, 0:nq], start=True, stop=True)
                pSv = pS[:, 0:640].rearrange("p (t q) -> p t q", q=128)
                P.add("dve", "scalar_tensor_tensor", reads=["pS", "BM"], writes=["stmp"], out=stmp[:, 1:5, 0:nq],
                      in0=pSv[:, 1:5, 0:nq], scalar=0.125, in1=BM[:, h, 1:5, 0:nq], op0=ALU.mult, op1=ALU.add)
                P.add("dve", "scalar_tensor_tensor", reads=["pS", "BM"], writes=["stmp"], out=stmp[0:nk_last, 0, 0:nq],
                      in0=pSv[0:nk_last, 0, 0:nq], scalar=0.125, in1=BM[0:nk_last, h, 0, 0:nq], op0=ALU.mult, op1=ALU.add)
                P.add("act", "activation", reads=["stmp"], writes=["PT"], out=PT[:, 1:5, 0:nq], in_=stmp[:, 1:5, 0:nq], func=AF.Exp)
                P.add("act", "activation", reads=["stmp"], writes=["PT"], out=PT[0:nk_last, 0, 0:nq], in_=stmp[0:nk_last, 0, 0:nq],
                      func=AF.Exp)
                for j, (slot, nk, tp_, halo) in enumerate(ktiles):
                    P.add("pe", "matmul", reads=["PT", ("V", slot)], writes=["pO"],
                          out=pO[0:nq, h // 4, (h % 4) * 65:(h % 4) * 65 + 65],
                          lhsT=PT[0:nk, 4 - j, 0:nq], rhs=V[0:nk, slot, h, :], start=(j == 0), stop=(j == nkt - 1))
            attention_tail(nq, mtile)

        def attention_tail(nq, mtile, gm_done=False):
            pOv = pO[0:nq, :, 0:260].rearrange("p a (h e) -> p a h e", e=65)
            P.add("dve", "reciprocal", reads=["pO"], writes=["rden"],
                  out=rden[0:nq, :].rearrange("p (a h o) -> p a h o", a=2, o=1), in_=pOv[:, :, :, 64:65])
            tt("dve", AO[0:nq, :, :].rearrange("p (a h) d -> p a h d", a=2), pOv[:, :, :, 0:64],
               rden[0:nq, :].rearrange("p (a h o) -> p a h o", a=2, o=1).to_broadcast([nq, 2, 4, 64]), ALU.mult,
               ["pO", "rden"], ["AO"])
            AOf = AO[:, :, :].rearrange("p h d -> p (h d)")
            for i in range(4):
                P.add("pe", "transpose", reads=["AO", "ident"], writes=["pT"], out=pT[:, i, 0:nq],
                      in_=AOf[0:nq, i * 128:(i + 1) * 128], identity=ident[0:nq, 0:nq])
            if not gm_done:
                for ct in range(12):
                    for k in range(8):
                        P.add("pe", "matmul", reads=[HK(), "w_gm", "w_gm2"], writes=["pA"], out=pA[:, ct * 128:ct * 128 + nq],
                              lhsT=w_gm[:, k, ct * 128:(ct + 1) * 128], rhs=HT()[:, k, 0:nq], start=(k == 0), stop=(k == 7))
                pAv = pA[:, :].rearrange("p (c t) -> p c t", t=128)
                P.add("act", "activation", reads=["pA"], writes=["sga"], out=sga[:, :, 0:nq], in_=pAv[:, 0:4, 0:nq], func=AF.Silu)
                P.add("act", "activation", reads=["pA"], writes=["sma"], out=sma[:, :, 0:nq], in_=pAv[:, 4:12, 0:nq], func=AF.Sigmoid)
            tt("dve", AOgT[:, :, 0:nq], pT[:, 0:4, 0:nq], sga[:, :, 0:nq], ALU.mult, ["pT", "sga"], ["AOgT"])
            for ct in range(8):
                for k in range(4):
                    P.add("pe", "matmul", reads=["AOgT", "w_oa"], writes=["pS"], out=pS[:, ct * 128:ct * 128 + nq],
                          lhsT=w_oa_s[:, k, ct * 128:(ct + 1) * 128], rhs=AOgT[:, k, 0:nq], start=(k == 0), stop=(k == 3))
            tt("dve", mgt[:, :, 0:nq], pS[:, :].rearrange("p (c t) -> p c t", t=128)[:, :, 0:nq], sma[:, :, 0:nq], ALU.mult,
               ["pS", "sma"], ["mgt"])
            dma("sp", mg_d[mtile[0], :, :].rearrange("p (k t) -> p k t", k=8)[:, :, mtile[1]:mtile[1] + nq],
                mgt[:, :, 0:nq], ["mgt"], [("mg_d", mtile[0])])

        def attention_fast(ktiles, mtile, nh, before_tail=None):
            nq = 128

            def qk(h):
                hp, h2 = h // 2, h % 2
                pr = slice(64 * h2, 64 * h2 + 64)
                Sb, skey = (pS, "pS") if h % 2 == 0 else (pA, "pA")
                for j, (slot, nk, tp_, halo) in enumerate(ktiles):
                    P.add("pe", "matmul", reads=[("KT", slot), "qT"], writes=[skey], out=Sb[:, (4 - j) * 128:(5 - j) * 128],
                          lhsT=KT[pr, hp, slot * 128:slot * 128 + 128], rhs=qT[pr, hp, 0:nq], start=True, stop=True)

            def softmax(h):
                Sb, skey = (pS, "pS") if h % 2 == 0 else (pA, "pA")
                st_, stk = stmp2[h % 2], "stmp%d" % (h % 2)
                PTb, ptk = PT2[h % 2], "PT%d" % (h % 2)
                P.add("dve", "scalar_tensor_tensor", reads=[skey, "BM"], writes=[stk], out=st_[:, :, :].rearrange("p t q -> p (t q)"),
                      in0=Sb[:, 0:640], scalar=0.125, in1=BM[:, h, :, :].rearrange("p t q -> p (t q)"),
                      op0=ALU.mult, op1=ALU.add)
                c_h = (5 - nh) * 128
                stf = st_[:, :, :].rearrange("p t q -> p (t q)")
                ptf = PTb[:, :, :].rearrange("p t q -> p (t q)")
                if nh < 5:
                    P.add("act", "activation", reads=[stk], writes=[ptk], out=ptf[:, 0:c_h], in_=stf[:, 0:c_h], func=AF.Exp)
                if nh > 0:
                    P.add("act", "activation", reads=[stk, "hb"], writes=[ptk], out=ptf[:, c_h:640], in_=stf[:, c_h:640],
                          func=AF.Exp, bias=hb[:, 0:1])

            def pv(h):
                PTb, ptk = PT2[h % 2], "PT%d" % (h % 2)
                for j, (slot, nk, tp_, halo) in enumerate(ktiles):
                    P.add("pe", "matmul", reads=[ptk, ("V", slot)], writes=["pO"],
                          out=pO[0:nq, h // 4, (h % 4) * 65:(h % 4) * 65 + 65],
                          lhsT=PTb[:, 4 - j, :], rhs=V[:, slot, h, :], start=(j == 0), stop=(j == 4))

            qk(0)
            for h in range(8):
                softmax(h)
                if h + 1 < 8:
                    qk(h + 1)
                pv(h)
            if before_tail is not None:
                before_tail()
            attention_tail(nq, mtile, True)


        own_tiles = [(NHALO + i * 128, 128, [(0, 128, 0)], i * 128) for i in range(16)]
        samp_tiles = [(NHALO + NOWN + s * 16, 16, [(0, 16, 1 + s)], NOWN + s * 16) for s in range(2)]

        wload(w_u, w_in[:, 2048:2560], "w_u")
        PRM = {}

        def pt(name):
            if name not in PRM:
                PRM[name] = len(PRM)
                assert len(PRM) <= 34
            return prm[:, PRM[name], :]

        k_ = ["prm"]
        dma("sp", xin[0][0:32, 0:64], lam_re[:, :], [], ["xin0"])
        dma("sp", xin[0][0:32, 64:128], lam_im[:, :], [], ["xin0b"])
        P.add("pe", "transpose", reads=["xin0", "xin0b", "identf"], writes=["pS"], out=pS[:, 0:32], in_=xin[0][0:32, 0:128],
              identity=identf[0:32, 0:32])
        cp("dve", pt("lam"), pS[:, 0:32], ["pS"], k_)
        cp("dve", pt("lr")[0:64, :], pt("lam")[0:64, :], k_, k_)
        cp("dve", pt("lr")[64:128, :], pt("lam")[0:64, :], k_, k_)
        cp("dve", pt("li")[0:64, :], pt("lam")[64:128, :], k_, k_)
        cp("dve", pt("li")[64:128, :], pt("lam")[64:128, :], k_, k_)
        dma("sp", pt("dt"), bass.AP(tensor=log_dt.tensor, offset=0, ap=[[0, 128], [1, 32]]), k_, k_)
        P.add("act", "activation", reads=k_, writes=k_, out=pt("dt"), in_=pt("dt"), func=AF.Exp)
        tt("dve", pt("x"), pt("lr"), pt("dt"), ALU.mult, k_, k_)
        ts("dve", pt("m"), pt("x"), 0.25, 1.0, ALU.mult, ALU.add, k_, k_)
        for cc in (1.0 / 3, 0.5, 1.0):
            P.add("dve", "scalar_tensor_tensor", reads=k_, writes=k_, out=pt("m"), in0=pt("x"), scalar=cc, in1=pt("m"),
                  op0=ALU.mult, op1=ALU.mult)
            ts("dve", pt("m"), pt("m"), 1.0, None, ALU.add, None, k_, k_)
        tt("dve", pt("ang"), pt("li"), pt("dt"), ALU.mult, k_, k_)
        ts("dve", pt("t0"), pt("ang"), 1.0 / (2 * np.pi), None, ALU.mult, None, k_, k_)
        cp("dve", ki32[:, :], pt("t0"), k_, ["ki32"])
        cp("dve", pt("t0"), ki32[:, :], ["ki32"], k_)
        P.add("dve", "scalar_tensor_tensor", reads=k_, writes=k_, out=pt("ang"), in0=pt("t0"), scalar=-2 * np.pi,
              in1=pt("ang"), op0=ALU.mult, op1=ALU.add)
        ts("dve", pt("psi"), pt("ang"), 1.0 / 32, None, ALU.mult, None, k_, k_)
        tt("dve", pt("p2"), pt("psi"), pt("psi"), ALU.mult, k_, k_)
        ts("dve", pt("s"), pt("p2"), -1.0 / 42, 1.0, ALU.mult, ALU.add, k_, k_)
        for cc in (-1.0 / 20, -1.0 / 6):
            P.add("dve", "scalar_tensor_tensor", reads=k_, writes=k_, out=pt("s"), in0=pt("p2"), scalar=cc, in1=pt("s"),
                  op0=ALU.mult, op1=ALU.mult)
            ts("dve", pt("s"), pt("s"), 1.0, None, ALU.add, None, k_, k_)
        tt("dve", pt("s"), pt("s"), pt("psi"), ALU.mult, k_, k_)
        ts("dve", pt("c"), pt("p2"), -1.0 / 56, 1.0, ALU.mult, ALU.add, k_, k_)
        for cc in (-1.0 / 30, -1.0 / 12, -0.5):
            P.add("dve", "scalar_tensor_tensor", reads=k_, writes=k_, out=pt("c"), in0=pt("p2"), scalar=cc, in1=pt("c"),
                  op0=ALU.mult, op1=ALU.mult)
            ts("dve", pt("c"), pt("c"), 1.0, None, ALU.add, None, k_, k_)

        def csquare(r, i, t1, t2):
            tt("dve", t1, r, r, ALU.mult, k_, k_)
            tt("dve", t2, i, i, ALU.mult, k_, k_)
            P.add("dve", "scalar_tensor_tensor", reads=k_, writes=k_, out=i, in0=r, scalar=2.0, in1=i,
                  op0=ALU.mult, op1=ALU.mult)
            tt("dve", r, t1, t2, ALU.subtract, k_, k_)

        for _ in range(5):
            csquare(pt("c"), pt("s"), pt("t0"), pt("t1"))
        tt("dve", pt("ar"), pt("m"), pt("c"), ALU.mult, k_, k_)
        tt("dve", pt("ai"), pt("m"), pt("s"), ALU.mult, k_, k_)
        tt("dve", pt("den"), pt("lr"), pt("lr"), ALU.mult, k_, k_)
        tt("dve", pt("t0"), pt("li"), pt("li"), ALU.mult, k_, k_)
        tt("dve", pt("den"), pt("den"), pt("t0"), ALU.add, k_, k_)
        P.add("dve", "reciprocal", reads=k_, writes=k_, out=pt("den"), in_=pt("den"))
        ts("dve", pt("nr"), pt("ar"), -1.0, None, ALU.add, None, k_, k_)
        tt("dve", pt("t0"), pt("nr"), pt("lr"), ALU.mult, k_, k_)
        tt("dve", pt("t1"), pt("ai"), pt("li"), ALU.mult, k_, k_)
        tt("dve", pt("cor"), pt("t0"), pt("t1"), ALU.add, k_, k_)
        tt("dve", pt("cor"), pt("cor"), pt("den"), ALU.mult, k_, k_)
        tt("dve", pt("t0"), pt("ai"), pt("lr"), ALU.mult, k_, k_)
        tt("dve", pt("t1"), pt("nr"), pt("li"), ALU.mult, k_, k_)
        tt("dve", pt("coi"), pt("t0"), pt("t1"), ALU.subtract, k_, k_)
        tt("dve", pt("coi"), pt("coi"), pt("den"), ALU.mult, k_, k_)
        ts("dve", pt("cois"), pt("coi"), cmask[:, 0:1], None, ALU.mult, None, k_ + ["cmask"], k_)
        dma("sp", Bstk[0:64, :, :], b_re.rearrange("g n c -> n g c"), [], ["Bstk"])
        dma("sp", Bstk[64:128, :, :], b_im.rearrange("g n c -> n g c"), [], ["Bstk2"])
        cp("dve", Bsw[0:64, :, :], Bstk[64:128, :, :], ["Bstk", "Bstk2"], ["Bsw"])
        cp("dve", Bsw[64:128, :, :], Bstk[0:64, :, :], ["Bstk", "Bstk2"], ["Bsw"])

        def bc_c(name):
            return pt(name).rearrange("p (g o) -> p g o", o=1).to_broadcast([128, 32, 16])

        tt("dve", Bstk[:, :, :], Bstk[:, :, :], bc_c("cor"), ALU.mult, ["Bstk", "Bstk2", "Bsw"] + k_, ["Bstk"])
        tt("dve", Bsw[:, :, :], Bsw[:, :, :], bc_c("cois"), ALU.mult, ["Bsw"] + k_, ["Bsw"])
        tt("dve", Bstk[:, :, :], Bstk[:, :, :], Bsw[:, :, :], ALU.add, ["Bstk", "Bsw"], ["Bstk"])
        cp("dve", Bsw[0:64, :, :], Bstk[64:128, :, :], ["Bstk"], ["Bsw"])
        cp("dve", Bsw[64:128, :, :], Bstk[0:64, :, :], ["Bstk"], ["Bsw"])
        cp("dve", Bbb[:, :, :], Bstk[:, :, :], ["Bstk"], ["Bbb"])
        kp = ["PW"]
        P.add("dve", "memset", reads=[], writes=kp, ap=PW1[:, :, 0:1], constant=1.0)
        P.add("dve", "memset", reads=[], writes=kp, ap=PW2[:, :, 0:1], constant=0.0)
        cp("dve", PW1[:, :, 1:2], pt("ar").rearrange("p (g o) -> p g o", o=1), k_ + kp, kp)
        cp("dve", PW2[:, :, 1:2], pt("ai").rearrange("p (g o) -> p g o", o=1), k_ + kp, kp)
        tw = qk_sb[:, 0:512].rearrange("p (a g t) -> p a g t", a=2, g=32)
        for kk in (1, 2, 4, 8):
            src_r, src_i = PW1[:, :, 1:kk + 1], PW2[:, :, 1:kk + 1]
            kr = PW1[:, :, kk:kk + 1].to_broadcast([128, 32, kk])
            ki = PW2[:, :, kk:kk + 1].to_broadcast([128, 32, kk])
            t_a, t_b = tw[:, 0, :, 0:kk], tw[:, 1, :, 0:kk]
            tt("dve", t_a, src_r, kr, ALU.mult, kp, ["qk_sb"])
            tt("dve", t_b, src_i, ki, ALU.mult, kp, ["qk_sb"])
            tt("dve", PW1[:, :, kk + 1:2 * kk + 1], t_a, t_b, ALU.subtract, ["qk_sb"], kp)
            tt("dve", t_a, src_r, ki, ALU.mult, kp, ["qk_sb"])
            tt("dve", t_b, src_i, kr, ALU.mult, kp, ["qk_sb"])
            tt("dve", PW2[:, :, kk + 1:2 * kk + 1], t_a, t_b, ALU.add, ["qk_sb"], kp)
        PW2s = sq[:, 0:544].rearrange("p (g t) -> p g t", g=32)
        ts("dve", PW2s, PW2[:, :, :], cmask[:, 0:1], None, ALU.mult, None, kp + ["cmask"], ["sq"])
        cp("dve", pt("Y1"), PW1[:, :, 16], kp, k_)
        cp("dve", pt("Y2"), PW2s[:, :, 16], ["sq"], k_)
        cp("dve", pt("e1r"), PW1[:, :, 16], kp, k_)
        cp("dve", pt("e1i"), PW2[:, :, 16], kp, k_)
        for _ in range(7):
            csquare(pt("e1r"), pt("e1i"), pt("t0"), pt("t1"))
        cp("dve", pt("e2r"), pt("e1r"), k_, k_)
        cp("dve", pt("e2i"), pt("e1i"), k_, k_)
        csquare(pt("e2r"), pt("e2i"), pt("t0"), pt("t1"))
        ts("dve", pt("e1i"), pt("e1i"), cmask[:, 0:1], None, ALU.mult, None, k_ + ["cmask"], k_)
        ts("dve", pt("e2i"), pt("e2i"), cmask[:, 0:1], None, ALU.mult, None, k_ + ["cmask"], k_)
        for gq in range(8):
            gs_ = slice(gq * 4, gq * 4 + 4)
            t_a = qk_sb[:, 0:1024].rearrange("p (g t c) -> p g t c", g=4, t=16)
            t_b = xin[1][:, 0:1024].rearrange("p (g t c) -> p g t c", g=4, t=16)
            tt("dve", t_a, Bstk[:, gs_, :].rearrange("p g (o c) -> p g o c", o=1).to_broadcast([128, 4, 16, 16]),
               PW1[:, gs_, 0:16].rearrange("p g (t o) -> p g t o", o=1).to_broadcast([128, 4, 16, 16]), ALU.mult,
               ["Bstk"] + kp, ["qk_sb"])
            tt("pool", t_b, Bsw[:, gs_, :].rearrange("p g (o c) -> p g o c", o=1).to_broadcast([128, 4, 16, 16]),
               PW2s[:, gs_, 0:16].rearrange("p g (t o) -> p g t o", o=1).to_broadcast([128, 4, 16, 16]), ALU.mult,
               ["Bsw", "sq"], ["xin1"])
            tt("dve", Pst[:, gq // 2, :, (gq % 2) * 4:(gq % 2) * 4 + 4, :].rearrange("p t g c -> p g t c"), t_a, t_b, ALU.add, ["qk_sb", "xin1"], ["Pst"])
        for q in range(4):
            for half in range(2):
                for tl in range(8):
                    tau = half * 8 + tl
                    P.add("pe", "transpose", reads=["Pst", "ident"], writes=["pT"], out=pT[:, tl, :],
                          in_=Pst[:, q, tau, :, :].rearrange("p g c -> p (g c)"), identity=ident[:, :])
                for tl in range(8):
                    j = 15 - (half * 8 + tl)
                    ts("dve", W1z[:, q, 0, j, :], pT[:, tl, :], cmask[:, 2:3], None, ALU.mult, None, ["pT", "cmask"], ["W1z"])
                    P.add("act", "activation", reads=["pT", "cmask"], writes=["W1z"], out=W1z[:, q, 1, j, :],
                          in_=pT[:, tl, :], func=AF.Copy, scale=cmask[:, 3:4])
        def cmul_acc(dst, x, y1, y2, t_sw, t_a, keys_r, keys_w):
            cp("dve", t_sw[0:64], x[64:128], keys_r, ["scan_t"], nophase=True)
            cp("dve", t_sw[64:128], x[0:64], keys_r, ["scan_t"], nophase=True)
            tt("dve", t_a, x, y1, ALU.mult, keys_r + ["prm"], ["scan_t2"], nophase=True)
            tt("dve", t_sw, t_sw, y2, ALU.mult, ["scan_t", "prm"], ["scan_t"], nophase=True)
            tt("dve", dst, dst, t_a, ALU.add, ["scan_t2"] + keys_w, keys_w, nophase=True)
            tt("dve", dst, dst, t_sw, ALU.add, ["scan_t"] + keys_w, keys_w, nophase=True)

        sc_a = prm[:, 34, :]
        sc_b = prm[:, 35, :]
        dma("sp", bgate, bass.AP(tensor=b_ada.tensor, offset=2048, ap=[[0, 3], [1, 1024]]), [], ["sq"])
        P.add("act", "activation", reads=["cT"], writes=["scT"], out=scT[:, :, :], in_=cT, func=AF.Silu)
        for ch in range(6):
            wb = wada[ch % 2]
            wk = "wada%d" % (ch % 2)
            wload(wb, w_ada[:, ch * 512:(ch + 1) * 512], wk)
            for c4 in range(4):
                for k in range(8):
                    P.add("pe", "matmul", reads=[wk, "scT"], writes=["pA"], out=pA[:, c4 * 4:c4 * 4 + 3],
                          lhsT=wb[:, k, c4 * 128:(c4 + 1) * 128], rhs=scT[:, k, :], start=(k == 0), stop=(k == 7))
            tt("dve", modT[:, ch * 4:(ch + 1) * 4, :], pA[:, 0:16].rearrange("p (c b) -> p c b", b=4)[:, :, 0:3],
               bada[:, ch * 4:(ch + 1) * 4].rearrange("p (c o) -> p c o", o=1).to_broadcast([128, 4, 3]), ALU.add,
               ["pA", "bada"], ["modT"])
            if ch >= 4:
                for k in range(8):
                    P.add("pe", "matmul", reads=[wk, "scT"], writes=["pS"], out=pS[0:3, 0:512],
                          lhsT=scT[:, k, :], rhs=wb[:, k, :], start=(k == 0), stop=(k == 7))
                tt("dve", gate_tok[:, (ch - 4) * 512:(ch - 3) * 512], pS[0:3, 0:512],
                   bgate[:, (ch - 4) * 512:(ch - 3) * 512], ALU.add, ["pS", "sq"], ["qk_sb"])
        ts("dve", Amod[:, :, :], modT[:, 8:16, :], 1.0, None, ALU.add, None, ["modT"], ["Amod"])
        tt("dve", Amod[:, :, :], Amod[:, :, :], ng.rearrange("p (k o) -> p k o", o=1).to_broadcast([128, 8, 3]), ALU.mult,
           ["Amod", "ng"], ["Amod"])
        for which, (c0, nt) in enumerate(((0, 128), (128, 16), (144, 16))):
            for half in range(2):
                P.add("pe", "matmul", reads=["selT", "qk_sb"], writes=["pS"], out=pS[0:nt, half * 512:(half + 1) * 512],
                      lhsT=selT[:, c0:c0 + nt], rhs=gate_tok[:, half * 512:(half + 1) * 512], start=True, stop=True)
            cp("dve", gate_bc[0:nt, which, :], pS[0:nt, :], ["pS"], ["gate_bc"])

        P.barrier(dummy[:, 0:1])

        def wstage(dst, src, key):
            dma("pool", dst, src, [], [key], nophase=True)

        wstage(wq_d[:, :], w_in[:, 0:1536], "wq_d")
        wstage(wgm_d[:, 0:512], w_in[:, 1536:2048], "wgm_d")
        wstage(wgm_d[:, 512:1536], w_in[:, 3072:4096], "wgm_d2")
        wstage(woa_d[:, :], w_oa[:, :], "woa_d")
        wstage(wglu_d[:, :], w_glu[:, :], "wglu_d")
        wstage(wgsms_d[:, 0:512], w_in[:, 2560:3072], "wgsms_d")
        wstage(wgsms_d[:, 512:1536], w_in[:, 4096:5120], "wgsms_d2")
        wstage(wos_d[:, :], w_os[:, :], "wos_d")
        wstage(wout_d[:, :], w_out[:, :], "wout_d")

        def wload2(dst, src, rkeys, key):
            dma("sp", dst, src.rearrange("(kt p) n -> p kt n", p=128), rkeys, [key])

        Q1 = scr[:, 0:544].rearrange("p (m g) -> p m g", g=32)
        Q2 = scr[:, 544:1088].rearrange("p (m g) -> p m g", g=32)
        Q2s = scr[:, 1088:1632].rearrange("p (m g) -> p m g", g=32)
        tsw8 = scr[:, 1632:1888].rearrange("p (b g) -> p b g", g=32)
        ta8 = scr[:, 1888:2144].rearrange("p (b g) -> p b g", g=32)
        tb1 = scr[:, 2144:4064].rearrange("p (b i g) -> p b i g", b=4, i=15)
        tb2 = scr[:, 4064:5984].rearrange("p (b i g) -> p b i g", b=4, i=15)
        qa = scr[:, 5984:6240].rearrange("p (m g) -> p m g", g=32)
        qb = scr[:, 6240:6496].rearrange("p (m g) -> p m g", g=32)
        kq = ["Q"]
        P.add("dve", "memset", reads=[], writes=kq, ap=Q1[:, 0, :], constant=1.0)
        P.add("dve", "memset", reads=[], writes=kq, ap=Q2[:, 0, :], constant=0.0)
        cp("dve", Q1[:, 1, :], PW1[:, :, 16], kp + kq, kq)
        cp("dve", Q2[:, 1, :], PW2[:, :, 16], kp + kq, kq)
        for kk in (1, 2, 4, 8):
            src_r, src_i = Q1[:, 1:kk + 1, :], Q2[:, 1:kk + 1, :]
            kr = Q1[:, kk:kk + 1, :].to_broadcast([128, kk, 32])
            ki = Q2[:, kk:kk + 1, :].to_broadcast([128, kk, 32])
            t_a, t_b = qa[:, 0:kk, :], qb[:, 0:kk, :]
            tt("dve", t_a, src_r, kr, ALU.mult, kq, ["qa"])
            tt("dve", t_b, src_i, ki, ALU.mult, kq, ["qb"])
            tt("dve", Q1[:, kk + 1:2 * kk + 1, :], t_a, t_b, ALU.subtract, ["qa", "qb"] + kq, kq)
            tt("dve", t_a, src_r, ki, ALU.mult, kq, ["qa"])
            tt("dve", t_b, src_i, kr, ALU.mult, kq, ["qb"])
            tt("dve", Q2[:, kk + 1:2 * kk + 1, :], t_a, t_b, ALU.add, ["qa", "qb"] + kq, kq)
        ts("dve", Q2s, Q2, cmask[:, 0:1], None, ALU.mult, None, kq + ["cmask"], kq)
        y1b8 = pt("Y1").rearrange("p (o g) -> p o g", o=1).to_broadcast([128, 8, 32])
        y2b8 = pt("Y2").rearrange("p (o g) -> p o g", o=1).to_broadcast([128, 8, 32])
        P.add("pool", "memset", reads=[], writes=["XS", "XSfree"], nophase=True, ap=XS[:, 0, :], constant=0.0)
        for seg in range(4):
            if seg < 3:
                seg_tiles = [(seg * NOWN + i * 128, 128, [(0, 128, 0)], i * 128) for i in range(16)]
            else:
                seg_tiles = own_tiles + samp_tiles
            srcA = x_prev if seg < 3 else x_all
            NA = len(seg_tiles)

            def a_front(n):
                (row0, nt, mods, col0) = seg_tiles[n]
                i = n % 2
                xt, xk = xin[i], "xin%d" % i
                xn, xnk = xn2[i], "xn%d" % i
                sst, ssk = ss2[i], "ss%d" % i
                dma("sp", xt[0:nt, :], srcA[row0:row0 + nt, :], [], [xk])
                P.add("act", "activation", reads=[xk], writes=[xnk, ssk], out=xn[0:nt, :], in_=xt[0:nt, :],
                      func=AF.Square, accum_out=sst[0:nt, 0:1])
                P.add("act", "activation", reads=[ssk, "epsc"], writes=[ssk], out=sst[0:nt, 2:3], in_=sst[0:nt, 0:1], func=AF.Ln,
                      scale=1.0 / D, bias=epsc[0:nt, 0:1])
                P.add("act", "activation", reads=[ssk], writes=[ssk], out=sst[0:nt, 3:4], in_=sst[0:nt, 2:3], func=AF.Exp, scale=-0.5)
                ts("dve", xn[0:nt, :], xt[0:nt, :], sst[0:nt, 3:4], None, ALU.mult, None, [xk, ssk], [xnk])

            def a_T(n):
                (row0, nt, mods, col0) = seg_tiles[n]
                i = n % 2
                xn, xnk = xn2[i], "xn%d" % i
                for k in range(8):
                    tb, tk = (TA[i], TAk[i]) if k < 4 else (TB[i], TBk[i])
                    P.add("pe", "transpose", reads=[xnk, "ident"], writes=[tk], out=tb[:, k % 4, 0:nt],
                          in_=xn[0:nt, k * 128:(k + 1) * 128], identity=ident[0:nt, 0:nt])

            def a_evac(n):
                (row0, nt, mods, col0) = seg_tiles[n]
                i = n % 2
                hT, hk = hT2[i], "hT%d" % i
                for k in range(8):
                    tb, tk = (TA[i], TAk[i]) if k < 4 else (TB[i], TBk[i])
                    for (c0, c1, bsel) in mods:
                        if k < 4:
                            ts("dve", hT[:, k, c0:c1], tb[:, k % 4, c0:c1], Amod[:, k, bsel:bsel + 1], modT[:, k, bsel:bsel + 1],
                               ALU.mult, ALU.add, [tk, "Amod", "modT"], [hk])
                        else:
                            P.add("act", "activation", reads=[tk, "Amod", "modT"], writes=[hk], out=hT[:, k, c0:c1],
                                  in_=tb[:, k % 4, c0:c1], func=AF.Identity, scale=Amod[:, k, bsel:bsel + 1],
                                  bias=modT[:, k, bsel:bsel + 1])

            def a_mm(n):
                (row0, nt, mods, col0) = seg_tiles[n]
                i = n % 2
                hT, hk = hT2[i], "hT%d" % i
                for q in range(4):
                    for k in range(8):
                        P.add("pe", "matmul", reads=[hk, "w_u"], writes=[UBk[i]], out=UB[i][:, q * 128:q * 128 + nt],
                              lhsT=w_u[:, k, q * 128:(q + 1) * 128], rhs=hT[:, k, 0:nt], start=(k == 0), stop=(k == 7))

            def a_uevac(n, seg=seg):
                (row0, nt, mods, col0) = seg_tiles[n]
                i = n % 2
                ts("dve", uT[:, :, col0:col0 + nt], UB[i].rearrange("p (q t) -> p q t", q=4)[:, :, 0:nt],
                   pmask[:, seg:seg + 1], None, ALU.mult, None, [UBk[i], "pmask"], ["uT"])

            a_front(0)
            if NA > 1:
                a_front(1)
            a_T(0)
            a_evac(0)
            for n in range(NA):
                if n + 1 < NA:
                    a_T(n + 1)
                a_mm(n)
                if n + 1 < NA:
                    a_evac(n + 1)
                if n + 2 < NA:
                    a_front(n + 2)
                a_uevac(n)
                if seg > 0:
                    P.flush(9)
            if seg > 0:
                P.flush(None)
            nch = 128 if seg < 3 else NCH
            wbanks = [(pA, 0, "pA0"), (pA, 512, "pA1"), (pS, 0, "pS0"), (pS, 512, "pS1")]
            for q in range(4):
                for par in range(2):
                    for j in range(16):
                        for s_ in range(4):
                            bk, off, bkey = wbanks[s_]
                            P.add("pe", "matmul", reads=["uT", "W1z"], writes=[bkey], out=bk[:, off:off + nch],
                                  lhsT=W1z[32 * s_:32 * s_ + 32, q, par, j, :],
                                  rhs=uT[32 * s_:32 * s_ + 32, q, 0:nch * 16].rearrange("p (k i) -> p k i", i=16)[:, :, j],
                                  start=(j == 0), stop=(j == 15), tile_position=(32 * s_, 0))
                    for s_ in range(4):
                        bk, off, bkey = wbanks[s_]
                        g = q * 8 + s_ * 2 + par
                        if s_ % 2 == 0:
                            cp("dve", XS[:, 1:129, g], bk[:, off:off + 128], [bkey, "XSfree"], [("XSg", g)], nophase=True)
                        else:
                            P.add("act", "activation", reads=[bkey, "XSfree"], writes=[("XSg", g)], nophase=True,
                                  out=XS[:, 1:129, g], in_=bk[:, off:off + 128], func=AF.Copy)
                        if seg == 3:
                            cp("dve", WSs[:, :, g], bk[:, off + 128:off + 130], [bkey], ["WSs"])
            P.add("dve", "memset", reads=[("XSg", g_) for g_ in range(32)], writes=["XS"], nophase=True, ap=dummy[:, 1:2],
                  constant=0.0)
            if seg < 3:
                P.defer_start()
            XSb = XS[:, 1:129, :].rearrange("p (b i) g -> p b i g", i=16)
            for i in range(1, 16):
                cmul_acc(XSb[:, :, i, :], XSb[:, :, i - 1, :], y1b8, y2b8, tsw8, ta8, ["XS"], ["XS"])
            for bb in range(8):
                cmul_acc(XS[:, 16 * (bb + 1), :], XS[:, 16 * bb, :], Q1[:, 16, :], Q2s[:, 16, :], sc_a, sc_b, ["XS", "Q"], ["XS"])
            Cst = XS[:, 0:128, :].rearrange("p (b i) g -> p b i g", i=16)
            for b0 in ((0, 4) if seg == 3 else ()):
                Ch = Cst[:, b0:b0 + 4, 0, :]
                cp("dve", tsw8[0:64, 0:4, :], Ch[64:128], ["XS"], ["scan_t"], nophase=True)
                cp("dve", tsw8[64:128, 0:4, :], Ch[0:64], ["XS"], ["scan_t"], nophase=True)
                tt("dve", tb1, Ch.rearrange("p b (o g) -> p b o g", o=1).to_broadcast([128, 4, 15, 32]),
                   Q1[:, 1:16, :].rearrange("p (o m) g -> p o m g", o=1).to_broadcast([128, 4, 15, 32]), ALU.mult,
                   ["XS", "Q"], ["tb1"], nophase=True)
                tt("dve", tb2, tsw8[:, 0:4, :].rearrange("p b (o g) -> p b o g", o=1).to_broadcast([128, 4, 15, 32]),
                   Q2s[:, 1:16, :].rearrange("p (o m) g -> p o m g", o=1).to_broadcast([128, 4, 15, 32]), ALU.mult,
                   ["scan_t", "Q"], ["tb2"], nophase=True)
                tt("dve", XSb[:, b0:b0 + 4, 0:15, :], XSb[:, b0:b0 + 4, 0:15, :], tb1, ALU.add, ["XS", "tb1"], ["XS"], nophase=True)
                tt("dve", XSb[:, b0:b0 + 4, 0:15, :], XSb[:, b0:b0 + 4, 0:15, :], tb2, ALU.add, ["XS", "tb2"], ["XS"], nophase=True)
            if seg < 3:
                cp("dve", XS[:, 0, :], XS[:, 128, :], ["XS"], ["XS", "XSfree"], nophase=True)
                P.defer_stop()
        P.add("pe", "transpose", reads=["XS", "identf"], writes=["pS"], nophase=True, out=pS[0:32, 0:128], in_=XS[:, 128, :],
              identity=identf[:, :])
        cp("dve", kout[0:32, 0:128], pS[0:32, 0:128], ["pS"], ["kout"], nophase=True)
        dma("sp", ssm_p[:, :], kout[0:32, 0:128], ["kout"], [], nophase=True)
        dma("sp", xin[0][0:64, 0:64], st_re[:, :], [], ["xin0"])
        dma("sp", xin[0][0:64, 64:128], st_im[:, :], [], ["xin0b"])
        P.add("pe", "transpose", reads=["xin0", "xin0b", "identf"], writes=["pS"], out=pS[:, 0:64], in_=xin[0][0:64, 0:128],
              identity=identf[0:64, 0:64])
        cp("dve", xs0[:, :, :], pS[:, 0:64].rearrange("p (s g) -> p s g", s=2), ["pS"], ["xs0"])
        for s_ in range(2):
            cmul_acc(WSs[:, s_, :], xs0[:, s_, :], pt("Y1"), pt("Y2"), sc_a, sc_b, ["xs0", "WSs"], ["WSs"])
        P.add("pe", "transpose", reads=["WSs", "identf"], writes=["pS"], out=pS[0:64, 0:128],
              in_=WSs[:, :, :].rearrange("p s g -> p (s g)"), identity=identf[:, :])
        cp("dve", vout[0:64, 0:128], pS[0:64, 0:128], ["pS"], ["vout"])
        dma("sp", ssm_s[:, :], vout[0:64, 0:128], ["vout"], [])

        P.barrier(dummy[:, 0:1])
        for h in range(8):
            dma("sp", BM[:, h, :, :],
                bass.AP(tensor=m_d.tensor, offset=h * 130 * 768 + 256, ap=[[767, 128], [128, 5], [1, 128]]),
                ["m_d"], ["BM"])
        P.add("pool", "memset", reads=[], writes=["BM"], ap=BM[0:64, :, 4, 64:128], constant=-30000.0)
        P.add("pool", "memset", reads=[], writes=["BM"], ap=BM[64:128, :, 0, 0:64], constant=-30000.0)
        P.add("pool", "memset", reads=[], writes=["Vones"], ap=V[:, :, :, 64:65], constant=1.0)
        wload2(w_qkv[:, :, :], wq_d[:, :], ["wq_d"], "w_qkv")
        wload2(w_gm[:, :, :], wgm_d[:, :], ["wgm_d", "wgm_d2"], "w_gm")
        wload2(w_oa_s[:, :, :], woa_d[:, :], ["woa_d"], "w_oa")
        tilesB = [(t * 128, 128, [(0, 128, 0)], 0) for t in range(20)]
        stB = {"nxt": rms_front(tilesB[0][0], 128), "pre": False}

        def b_prepare(n):
            cur = stB["nxt"]
            rms_back(cur, 128, tilesB[n][2])
            if n + 1 < 20:
                stB["nxt"] = rms_front(tilesB[n + 1][0], 128)
            CUR["i"] = cur
            qkv_mm(128, n >= 4)
            stB["cur"] = cur

        b_prepare(0)
        for t in range(20):
            slot = t % 6
            CUR["i"] = stB["cur"]
            if t < 4:
                qkv_tile(128, slot, False, None, None, 0, pre_mm=True)
                if t + 1 < 20:
                    b_prepare(t + 1)
            else:
                own = t - 4
                last = own >= 12
                qkv_tile(128, slot, True, k_last if last else None, v_last if last else None, (own - 12) * 128, gm=True,
                         pre_mm=True)
                ktiles = [((t - 4 + j) % 6, 128, 4 - j, (t - 4 + j) < 4) for j in range(5)]
                mycur = stB["cur"]

                def hook(t=t):
                    if t + 1 < 20:
                        b_prepare(t + 1)

                attention_fast(ktiles, (own, 0), max(0, 4 - own), before_tail=hook)
        kc = qkb[:, :]
        for s_ in range(2):
            for tk in range(4):
                dma("pool", kc[:, 0:512], cache_k[s_, tk * 128:(tk + 1) * 128, :], [], ["qkb"])
                for i in range(4):
                    P.add("pe", "transpose", reads=["qkb", "ident"], writes=["pT"], out=pT[:, 4 + i, :],
                          in_=kc[:, i * 128:(i + 1) * 128], identity=ident[:, :])
                P.add("act", "activation", reads=["pT"], writes=[("KT", tk)], out=KT[:, :, tk * 128:(tk + 1) * 128],
                      in_=pT[:, 4:8, :], func=AF.Copy)
                dma("pool", V[:, tk, :, 0:64], cache_v[s_, tk * 128:(tk + 1) * 128, :].rearrange("p (h d) -> p h d", d=64),
                    ["Vones"], [("V", tk)])
            (row0, nt, mods, col0) = samp_tiles[s_]
            rms_back(rms_front(row0, nt), nt, mods)
            qkv_tile(16, 4, True, k_samp, v_samp, s_ * 16, fold=False)
            ktiles = [(0, 128, 4, False), (1, 128, 3, False), (2, 128, 2, False), (3, 128, 1, False), (4, 16, 0, False)]
            attention(16, ktiles, (16, s_ * 16))

        P.barrier(dummy[:, 0:1])
        P.add("pool", "memset", reads=[], writes=["CAz"], ap=arena[:, 0:8704], constant=0.0)
        P.add("dve", "memset", reads=[], writes=["CAz2"], ap=arena[:, 8704:17408], constant=0.0)
        for q in range(4):
            dma("sp", Cin[:, q, 0:64], c_re[q * 128:(q + 1) * 128, :], [], ["Cin"])
            dma("sp", Cin[:, q, 64:128], c_im[q * 128:(q + 1) * 128, :], [], ["Cin2"])
        for q in range(4):
            P.add("pe", "transpose", reads=["Cin", "Cin2", "identf"], writes=["pS"], out=pS[:, q * 128:(q + 1) * 128],
                  in_=Cin[:, q, :], identity=identf[:, :])
        cp("dve", Cstk[:, :], pS[:, 0:512], ["pS"], ["Cstk"])
        cp("dve", Csw[0:64, :], Cstk[64:128, :], ["Cstk"], ["Csw"])
        cp("dve", Csw[64:128, :], Cstk[0:64, :], ["Cstk"], ["Csw"])
        AA1 = sq[:, 0:544].rearrange("p (g t) -> p g t", g=32)
        AA2 = xin[1][:, 0:544].rearrange("p (g t) -> p g t", g=32)
        ts("dve", AA1, PW1[:, :, :], cmask[:, 1:2], None, ALU.mult, None, ["PW", "cmask"], ["sq"])
        ts("dve", AA2, PW2[:, :, :], -1.0, None, ALU.mult, None, ["PW"], ["xin1"])
        Cst4 = Cstk[:, :].rearrange("p (s e c) -> p s e c", s=16, e=2)
        Csw4 = Csw[:, :].rearrange("p (s e c) -> p s e c", s=16, e=2)
        AA1v = AA1.rearrange("p (s e) t -> p s e t", e=2)
        AA2v = AA2.rearrange("p (s e) t -> p s e t", e=2)
        CAzv = CAz.rearrange("p (s e) t r c -> p s e t r c", e=2)
        for s0 in range(0, 16, 3):
            ns = min(3, 16 - s0)
            for e in range(2):
                t_a = qk_sb[:, 0:ns * 272].rearrange("p (s t c) -> p s t c", s=ns, t=17)
                t_b = xin[0][:, 0:ns * 272].rearrange("p (s t c) -> p s t c", s=ns, t=17)
                tt("dve", t_a, Cst4[:, s0:s0 + ns, e, :].rearrange("p s (o c) -> p s o c", o=1).to_broadcast([128, ns, 17, 16]),
                   AA1v[:, s0:s0 + ns, e, :].rearrange("p s (t o) -> p s t o", o=1).to_broadcast([128, ns, 17, 16]), ALU.mult,
                   ["Cstk", "sq"], ["qk_sb"])
                tt("pool", t_b, Csw4[:, s0:s0 + ns, e, :].rearrange("p s (o c) -> p s o c", o=1).to_broadcast([128, ns, 17, 16]),
                   AA2v[:, s0:s0 + ns, e, :].rearrange("p s (t o) -> p s t o", o=1).to_broadcast([128, ns, 17, 16]), ALU.mult,
                   ["Csw", "xin1"], ["xin0"])
                tt("dve", CAzv[:, s0:s0 + ns, e, :, e, :], t_a, t_b, ALU.add, ["qk_sb", "xin0", "CAz", "CAz2"], ["CAz"])
        for g in range(32):
            q, g8 = g // 8, g % 8
            pk = "pS%d" % (g % 2)
            P.add("pe", "matmul", reads=["Bbb", "CAz"], writes=[pk], out=pS[:, (g % 2) * 512:(g % 2) * 512 + 272],
                  lhsT=Bbb[:, :, :].rearrange("p g c -> p (g c)")[:, q * 128:(q + 1) * 128], rhs=CAz[:, g, :, g % 2, :], start=True, stop=True)
            if g % 2 == 0:
                ts("dve", Kblk[:, q, :, g8 * 16:(g8 + 1) * 16],
                   pS[:, 0:256].rearrange("p (t c) -> p t c", t=16), cmask[:, 4 + g8:5 + g8], None,
                   ALU.mult, None, [pk, "cmask"], [("Kb", g)])
            else:
                P.add("act", "activation", reads=[pk, "cmask"], writes=[("Kb", g)], out=Kblk[:, q, :, g8 * 16:(g8 + 1) * 16],
                      in_=pS[:, 512:768].rearrange("p (t c) -> p t c", t=16), func=AF.Copy, scale=cmask[:, 4 + g8:5 + g8])
        P.add("dve", "memset", reads=[("Kb", g_) for g_ in range(32)], writes=["Kblk"], ap=dummy[:, 1:2], constant=0.0)
        for q in range(4):
            P.add("dve", "scalar_tensor_tensor", reads=["Kblk", "identf", "bada"], writes=["Kblk"], out=Kblk[:, q, 0, :],
                  in0=identf[:, :], scalar=dq[:, q:q + 1], in1=Kblk[:, q, 0, :], op0=ALU.mult, op1=ALU.add)
        cp("dve", Xb[:, :, 0:128], XS[:, 0:128, :].rearrange("p k g -> p g k"), ["XS"], ["Xb"])
        cp("dve", Xb[:, :, 128:130], xs0[:, :, :].rearrange("p s g -> p g s"), ["xs0"], ["Xb"])
        blocks = [(b0 * 512, 512) for b0 in range(4)] + [(2048, 32)]
        for bi, (t0, ntk) in enumerate(blocks):
            nk = ntk // 16
            k0 = t0 // 16
            for q in range(4):
                bank = pA if q % 2 == 0 else pS
                bkey = "pA" if q % 2 == 0 else "pS"
                for tau in range(16):
                    ov = bank[:, 0:ntk].rearrange("p (k i) -> p k i", i=16)[:, :, tau:16]
                    uv = uT[:, q, t0:t0 + ntk].rearrange("p (k i) -> p k i", i=16)[:, :, 0:16 - tau]
                    P.add("pe", "matmul", reads=["uT", "Kblk"], writes=[bkey], out=ov, lhsT=Kblk[:, q, tau, :], rhs=uv,
                          start=(tau == 0), stop=False)
                for s_ in range(4):
                    for i in range(16):
                        for e in range(2):
                            g = q * 8 + s_ * 2 + e
                            last = (s_ == 3 and i == 15 and e == 1)
                            P.add("pe", "matmul", reads=["Xb", "CAz"], writes=[bkey],
                                  out=bank[32 * s_:32 * s_ + 32, 0:ntk].rearrange("p (k i) -> p k i", i=16)[:, :, i],
                                  lhsT=CAz[:, g, i + 1, :, :].rearrange("p r c -> p (r c)"), rhs=Xb[:, g, k0:k0 + nk], start=False, stop=last,
                                  tile_position=(0, 32 * s_), skip_group_check=True)
                P.add("act", "activation", reads=[bkey], writes=["yT"], out=yT[:, q, t0:t0 + ntk], in_=bank[:, 0:ntk],
                      func=AF.Copy)

        P.barrier(dummy[:, 0:1])
        wload2(w_glu_s, wglu_d[:, :], ["wglu_d"], "w_glu")
        wload2(w_gsms[:, :, :], wgsms_d[:, :], ["wgsms_d", "wgsms_d2"], "w_gs")
        wload2(w_os_s, wos_d[:, :], ["wos_d"], "w_os")
        dma("sp", w_out_s[:, :, :], wout_d.rearrange("(kt p) n -> p kt n", p=128), ["wout_d"], ["wada0", "wada1"])
        tilesC = own_tiles + samp_tiles
        NC2 = len(tilesC)
        xin3 = [xin[0], xin[1], xin2c]
        sgb2 = [sgb, av(32256, 512).rearrange("p (k n) -> p k n", k=4)]
        sgs2 = [sgs, av(32768, 512).rearrange("p (k n) -> p k n", k=4)]
        sms2 = [sms, av(33280, 1024).rearrange("p (k n) -> p k n", k=8)]
        sgt2 = [sgt, av(34304, 512).rearrange("p (k n) -> p k n", k=4)]
        mgt2 = [mgt, av(34816, 1024).rearrange("p (k n) -> p k n", k=8)]

        def c2_rms_front(n):
            (row0, nt, mods, col0) = tilesC[n]
            i3, i2 = n % 3, n % 2
            xt, xk = xin3[i3], "xinC%d" % i3
            xn, xnk = xn2[i2], "xn%d" % i2
            sst, ssk = ss2[i2], "ss%d" % i2
            dma("sp", xt[0:nt, :], x_all[row0:row0 + nt, :], [], [xk])
            P.add("act", "activation", reads=[xk], writes=[xnk, ssk], out=xn[0:nt, :], in_=xt[0:nt, :],
                  func=AF.Square, accum_out=sst[0:nt, 0:1])
            P.add("act", "activation", reads=[ssk, "epsc"], writes=[ssk], out=sst[0:nt, 2:3], in_=sst[0:nt, 0:1], func=AF.Ln,
                  scale=1.0 / D, bias=epsc[0:nt, 0:1])
            P.add("act", "activation", reads=[ssk], writes=[ssk], out=sst[0:nt, 3:4], in_=sst[0:nt, 2:3], func=AF.Exp, scale=-0.5)
            P.add("act", "activation", reads=[xk, ssk], writes=[xnk], out=xn[0:nt, :], in_=xt[0:nt, :],
                  func=AF.Copy, scale=sst[0:nt, 3:4])

        def c2_front_a(ti):
            (row0, nt, mods, col0) = tilesC[ti]
            c2_rms_front(ti)
            rms_back(ti % 2, nt, mods)

        def c2_front(ti):
            (row0, nt, mods, col0) = tilesC[ti]
            p = ti % 2
            CUR["i"] = p
            mti = (ti, 0) if ti < 16 else (16, (ti - 16) * 16)
            dma("sp", mgt2[p][:, :, 0:nt], mg_d[mti[0], :, :].rearrange("p (k t) -> p k t", k=8)[:, :, mti[1]:mti[1] + nt],
                [("mg_d", mti[0])], ["mgt%d" % p])
            for ct in range(8):
                for k in range(4):
                    P.add("pe", "matmul", reads=["yT", "w_glu"], writes=["pS"], out=pS[:, ct * 128:ct * 128 + nt],
                          lhsT=w_glu_s[:, k, ct * 128:(ct + 1) * 128], rhs=yT[:, k, col0:col0 + nt], start=(k == 0), stop=(k == 3))
            pSv = pS[:, :].rearrange("p (c t) -> p c t", t=128)
            for ct in range(4):
                P.add("act", "activation", reads=["pS", "bada"], writes=["sgb%d" % p], out=sgb2[p][:, ct, 0:nt], in_=pSv[:, 4 + ct, 0:nt],
                      func=AF.Sigmoid, bias=bglu[:, 4 + ct:5 + ct])
            for ct in range(12):
                for k in range(8):
                    P.add("pe", "matmul", reads=[HK(), "w_gs", "w_ms"], writes=["pA"], out=pA[:, ct * 128:ct * 128 + nt],
                          lhsT=w_gsms[:, k, ct * 128:(ct + 1) * 128], rhs=HT()[:, k, 0:nt], start=(k == 0), stop=(k == 7))
            pAv = pA[:, :].rearrange("p (c t) -> p c t", t=128)
            P.add("act", "activation", reads=["pA"], writes=["sgs%d" % p], out=sgs2[p][:, :, 0:nt], in_=pAv[:, 0:4, 0:nt], func=AF.Silu)
            P.add("act", "activation", reads=["pA"], writes=["sms%d" % p], out=sms2[p][:, :, 0:nt], in_=pAv[:, 4:12, 0:nt], func=AF.Sigmoid)
            for ct in range(4):
                P.add("dve", "scalar_tensor_tensor", reads=["pS", "sgb%d" % p, "bada"], writes=["sgt%d" % p], out=sgt2[p][:, ct, 0:nt],
                      in0=pSv[:, ct, 0:nt], scalar=bglu[:, ct:ct + 1], in1=sgb2[p][:, ct, 0:nt], op0=ALU.add, op1=ALU.mult)
            tt("dve", sgt2[p][:, :, 0:nt], sgt2[p][:, :, 0:nt], sgs2[p][:, :, 0:nt], ALU.mult, ["sgt%d" % p, "sgs%d" % p], ["sgt%d" % p])

        def c2_back(ti):
            (row0, nt, mods, col0) = tilesC[ti]
            p = ti % 2
            xt, xk = xin3[ti % 3], "xinC%d" % (ti % 3)
            for ct in range(8):
                for k in range(4):
                    P.add("pe", "matmul", reads=["sgt%d" % p, "w_os"], writes=["pO"], out=pO[:, ct // 4, (ct % 4) * 128:(ct % 4) * 128 + nt],
                          lhsT=w_os_s[:, k, ct * 128:(ct + 1) * 128], rhs=sgt2[p][:, k, 0:nt], start=(k == 0), stop=(k == 3))
            tt("dve", bst[:, :, 0:nt], pO[:, :, :].rearrange("p a (c t) -> p (a c) t", t=128)[:, :, 0:nt], sms2[p][:, :, 0:nt],
               ALU.mult, ["pO", "sms%d" % p], ["bst"])
            tt("dve", bst[:, :, 0:nt], bst[:, :, 0:nt], mgt2[p][:, :, 0:nt], ALU.add, ["bst", "mgt%d" % p], ["bst"])

        def c2_back_b(ti):
            (row0, nt, mods, col0) = tilesC[ti]
            p = ti % 2
            xt, xk = xin3[ti % 3], "xinC%d" % (ti % 3)
            for half in range(2):
                for k in range(8):
                    P.add("pe", "matmul", reads=["bst", "wada0", "wada1"], writes=["pO"],
                          out=pO[0:nt, half, :], lhsT=bst[:, k, 0:nt],
                          rhs=w_out_s[:, k, half * 512:(half + 1) * 512], start=(k == 0), stop=(k == 7))
            tt("dve", qk_sb[0:nt, :], pO[0:nt, :, :].rearrange("p a c -> p (a c)"), gate_bc[0:nt, 0 if ti < 16 else ti - 15, :], ALU.mult,
               ["pO", "gate_bc"], ["qk_sb"])
            tt("dve", sq[0:nt, :], qk_sb[0:nt, :], xt[0:nt, :], ALU.add, ["qk_sb", xk], ["sq"])
            dma("sp", y_out[col0:col0 + nt, :], sq[0:nt, :], ["sq"], [])

        c2_front_a(0)
        c2_front(0)
        for ti in range(NC2):
            if ti + 1 < NC2:
                c2_front_a(ti + 1)
            c2_back(ti)
            if ti + 1 < NC2:
                c2_front(ti + 1)
            c2_back_b(ti)

        P.emit(nc)
    return nc


_NC = None


def kernel(x_prompt, x_sample, c_prompt, c_sample, cache_k, cache_v, state_ssm_re, state_ssm_im,
           norm_g, w_ada, b_ada, w_in, q_norm_g, k_norm_g, rel_bias, lambda_re, lambda_im, log_dt,
           b_re, b_im, c_re, c_im, d_skip, w_glu, b_glu, w_oa, w_os, w_out):
    global _NC
    f = lambda a: np.ascontiguousarray(np.asarray(a, dtype=np.float32))
    x_prompt, x_sample = f(x_prompt), f(x_sample)
    if _NC is None:
        _NC = build()
    nc = _NC
    ident = np.eye(128, dtype=np.float32)
    sel = np.zeros((3, 256), np.float32)
    sel[0, 0:128] = 1.0
    sel[1, 128:144] = 1.0
    sel[2, 144:160] = 1.0
    pidx = np.arange(128)
    cmask = np.zeros((128, 16), np.float32)
    cmask[:, 0] = np.where(pidx < 64, -1.0, 1.0)
    cmask[:, 1] = -cmask[:, 0]
    for par in range(2):
        cmask[:, 2 + par] = ((pidx // 16) % 2 == par)
    for g8 in range(8):
        cmask[:, 4 + g8] = (pidx // 16 == g8)
    in_maps = []
    for c in range(8):
        b, j = c // 4, c % 4
        t0 = j * NOWN
        halo = np.zeros((NHALO, D), np.float32) if j == 0 else x_prompt[b, t0 - NHALO:t0]
        xs = x_sample[2 * c:2 * c + 2].reshape(NSAMP, D)
        x_all = np.concatenate([halo, x_prompt[b, t0:t0 + NOWN], xs], axis=0)
        c3 = np.stack([f(c_prompt)[b], f(c_sample)[2 * c], f(c_sample)[2 * c + 1]])
        hbias = np.full((128, 1), -30000.0 if j == 0 else 0.0, np.float32)
        x_prev = np.zeros((3 * NOWN, D), np.float32)
        pmask = np.ones((128, 4), np.float32)
        for i in range(3):
            js = j - 3 + i
            if js >= 0:
                x_prev[i * NOWN:(i + 1) * NOWN] = x_prompt[b, js * NOWN:(js + 1) * NOWN]
            else:
                pmask[:, i] = 0.0
        selm = np.zeros((128, 24), np.float32)
        for jr in range(j):
            selm[:, (b * 4 + jr) * 3 + (j - 1 - jr)] = 1.0
        in_maps.append({
            "x_all": np.ascontiguousarray(x_all), "c3": np.ascontiguousarray(c3),
            "cache_k": f(cache_k)[0, 2 * c:2 * c + 2].reshape(2, 512, 512),
            "cache_v": f(cache_v)[0, 2 * c:2 * c + 2].reshape(2, 512, 512),
            "hbias": hbias, "sel": sel, "ident": ident, "cmask": cmask, "selm": selm, "x_prev": x_prev, "pmask": pmask,
            "st_re": f(state_ssm_re)[0, 2 * c:2 * c + 2].reshape(64, 64),
            "st_im": f(state_ssm_im)[0, 2 * c:2 * c + 2].reshape(64, 64),
            "lambda_re": f(lambda_re)[0], "lambda_im": f(lambda_im)[0], "log_dt": f(log_dt)[0],
            "b_re": f(b_re)[0], "b_im": f(b_im)[0], "c_re": f(c_re)[0].reshape(512, 64), "c_im": f(c_im)[0].reshape(512, 64),
            "d_skip": f(d_skip)[0], "w_glu": f(w_glu)[0], "b_glu": f(b_glu)[0], "w_os": f(w_os)[0],
            "norm_g": f(norm_g)[0], "w_ada": f(w_ada)[0], "b_ada": f(b_ada)[0], "w_in": f(w_in)[0],
            "q_norm_g": f(q_norm_g)[0], "k_norm_g": f(k_norm_g)[0], "rel_bias": f(rel_bias)[0],
            "w_oa": f(w_oa)[0], "w_out": f(w_out)[0],
        })
    res = run_bass_kernel_spmd(nc, in_maps, core_ids=list(range(8)))
    R = res.results
    y_prompt = np.zeros((2, 8192, D), np.float32)
    y_sample = np.zeros((16, 16, D), np.float32)
    nk_p = np.zeros((1, 2, 512, 8, 64), np.float32)
    nv_p = np.zeros((1, 2, 512, 8, 64), np.float32)
    sr_p = np.zeros((1, 2, 32, 64), np.float32)
    si_p = np.zeros((1, 2, 32, 64), np.float32)
    nk_s = np.zeros((1, 16, 16, 8, 64), np.float32)
    nv_s = np.zeros((1, 16, 16, 8, 64), np.float32)
    sr_s = np.zeros((1, 16, 32, 64), np.float32)
    si_s = np.zeros((1, 16, 32, 64), np.float32)
    for c in range(8):
        b, j = c // 4, c % 4
        r = R[c]
        y_prompt[b, j * NOWN:(j + 1) * NOWN] = r["y_out"][0:NOWN]
        y_sample[2 * c:2 * c + 2] = r["y_out"][NOWN:].reshape(2, 16, D)
        if j == 3:
            nk_p[0, b] = r["k_last"].reshape(512, 8, 64)
            nv_p[0, b] = r["v_last"].reshape(512, 8, 64)
        nk_s[0, 2 * c:2 * c + 2] = r["k_samp"].reshape(2, 16, 8, 64)
        if j == 3:
            sr_p[0, b] = r["ssm_p"][:, 0:64]
            si_p[0, b] = r["ssm_p"][:, 64:128]
        sr_s[0, 2 * c:2 * c + 2] = r["ssm_s"][:, 0:64].reshape(2, 32, 64)
        si_s[0, 2 * c:2 * c + 2] = r["ssm_s"][:, 64:128].reshape(2, 32, 64)
        nv_s[0, 2 * c:2 * c + 2] = r["v_samp"].reshape(2, 16, 8, 64)
    return (y_prompt, y_sample, nk_p, nv_p, sr_p, si_p, nk_s, nv_s, sr_s, si_s)
```

```python
import numpy as np
import os
STAGE = int(os.environ.get('KSTAGE', '99'))
SUB = int(os.environ.get('KSUB', '99'))
KRMS = int(os.environ.get('KRMS', '99'))
import ml_dtypes
from contextlib import ExitStack
import concourse.bass as bass
import concourse.mybir as mybir
from concourse.bass_utils import run_bass_kernel_spmd

F32 = mybir.dt.float32
BF16 = mybir.dt.bfloat16
ALU = mybir.AluOpType
AF = mybir.ActivationFunctionType
AX = mybir.AxisListType

D = 1024
NOWN = 2048
NHALO = 512
NSAMP = 32
EPS = 1e-6


PSUM_KEYS = ("pA", "pT", "pS", "pO", "pA0", "pA1", "pA2", "pS0", "pS1", "pO0", "pO1")


class Prog:
    ENG = ("pe", "act", "dve", "pool", "sp")

    def __init__(self):
        self.ops = []
        self.last_w = {}
        self.readers = {}

    def add(self, eng, name, reads=(), writes=(), dma=False, nophase=False, **kw):
        op = dict(eng=eng, name=name, kw=kw, dma=dma, deps=set(), idx=len(self.ops), sig=dma)
        reads = list(reads)
        if not nophase:
            reads.append("PHASE")
        writes = list(writes) + [r for r in reads if r in PSUM_KEYS]
        reads = [r for r in reads if r not in PSUM_KEYS]
        for r in reads:
            lw = self.last_w.get(r)
            if lw is not None:
                op["deps"].add(lw)
        for w in writes:
            lw = self.last_w.get(w)
            if lw is not None:
                op["deps"].add(lw)
            for rd in self.readers.get(w, ()):
                op["deps"].add(rd)
        for r in reads:
            self.readers.setdefault(r, []).append(op["idx"])
        for w in writes:
            self.last_w[w] = op["idx"]
            self.readers[w] = []
        op["deps"].discard(op["idx"])
        self.ops.append(op)
        return op

    def barrier(self, arena_ap):
        self.add("dve", "memset", reads=[], writes=["PHASE"], nophase=True, ap=arena_ap, constant=0.0)

    def emit(self, nc, ndma_sems=16):
        ops = self.ops
        for op in ops:
            nd = set()
            for d in op["deps"]:
                p = ops[d]
                if (not p["dma"]) and p["eng"] == op["eng"] and p["eng"] == "pe" and not op["dma"]:
                    continue
                nd.add(d)
            op["deps"] = nd
            for d in nd:
                ops[d]["sig"] = True
        cnt = {e: 0 for e in self.ENG}
        dcnt = {e: 0 for e in self.ENG}
        for op in ops:
            e = op["eng"]
            if op["dma"]:
                i = dcnt[e]
                dcnt[e] += 1
                op["sem"] = ("d", e, i % ndma_sems)
                op["val"] = 16 * (i // ndma_sems + 1)
            elif op["sig"]:
                cnt[e] += 1
                op["sem"] = ("c", e)
                op["val"] = cnt[e]
        with ExitStack() as st:
            sems = {}
            for e in self.ENG:
                sems[("c", e)] = st.enter_context(nc.semaphore("c_" + e))
                if dcnt[e]:
                    for i in range(ndma_sems):
                        sems[("d", e, i)] = st.enter_context(nc.semaphore("d_%s_%d" % (e, i)))
            block = st.enter_context(nc.Block())
            byeng = {e: [o for o in ops if o["eng"] == e] for e in self.ENG}

            def run(engname, eng):
                known = {}
                for op in byeng[engname]:
                    waits = {}
                    for d in op["deps"]:
                        p = ops[d]
                        waits[p["sem"]] = max(waits.get(p["sem"], 0), p["val"])
                    if op["dma"] and op["val"] > 16:
                        waits[op["sem"]] = max(waits.get(op["sem"], 0), op["val"] - 16)
                    for s, v in waits.items():
                        if known.get(s, 0) >= v:
                            continue
                        eng.wait_ge(sems[s], v)
                        known[s] = v
                    ins = getattr(eng, op["name"])(**op["kw"])
                    if op["sig"]:
                        ins.then_inc(sems[op["sem"]], 16 if op["dma"] else 1)
                last = {}
                for op in byeng[engname]:
                    if op["dma"]:
                        last[op["sem"]] = op["val"]
                for s, v in last.items():
                    if known.get(s, 0) < v:
                        eng.wait_ge(sems[s], v)

            block.tensor(lambda eng: run("pe", eng))
            block.scalar(lambda eng: run("act", eng))
            block.vector(lambda eng: run("dve", eng))
            block.gpsimd(lambda eng: run("pool", eng))
            block.sync(lambda eng: run("sp", eng))


def build():
    nc = bass.Bass("TRN2", target_bir_lowering=False)

    def din(name, shape, dt=F32):
        return nc.dram_tensor(name, list(shape), dt, kind="ExternalInput").ap()

    def dout(name, shape, dt=F32):
        return nc.dram_tensor(name, list(shape), dt, kind="ExternalOutput").ap()

    NTOK = NHALO + NOWN + NSAMP
    NT = NOWN + NSAMP
    NCH = NT // 16
    x_all = din("x_all", [NTOK, D])
    x_prev = din("x_prev", [3 * NOWN, D])
    pmask_d = din("pmask", [128, 4])
    c3 = din("c3", [3, D])
    cache_k = din("cache_k", [2, 512, 512])
    cache_v = din("cache_v", [2, 512, 512])
    st_re = din("st_re", [64, 64])
    st_im = din("st_im", [64, 64])
    hbias = din("hbias", [128, 1])
    sel = din("sel", [3, 256])
    ident_d = din("ident", [128, 128])
    cmask_d = din("cmask", [128, 16])
    selm_d = din("selm", [128, 24])
    norm_g = din("norm_g", [D])
    w_ada = din("w_ada", [D, 3 * D])
    b_ada = din("b_ada", [3 * D])
    w_in = din("w_in", [D, 5120])
    q_norm_g = din("q_norm_g", [64])
    k_norm_g = din("k_norm_g", [64])
    rel_bias = din("rel_bias", [8, 257])
    lam_re = din("lambda_re", [32, 64])
    lam_im = din("lambda_im", [32, 64])
    log_dt = din("log_dt", [32])
    b_re = din("b_re", [32, 64, 16])
    b_im = din("b_im", [32, 64, 16])
    c_re = din("c_re", [512, 64])
    c_im = din("c_im", [512, 64])
    d_skip = din("d_skip", [512])
    w_glu = din("w_glu", [512, 1024])
    b_glu = din("b_glu", [1024])
    w_oa = din("w_oa", [512, D])
    w_os = din("w_os", [512, D])
    w_out = din("w_out", [D, D])

    y_out = dout("y_out", [NT, D])
    k_last = dout("k_last", [512, 512])
    v_last = dout("v_last", [512, 512])
    k_samp = dout("k_samp", [NSAMP, 512])
    v_samp = dout("v_samp", [NSAMP, 512])
    ssm_p = dout("ssm_p", [32, 128])
    ssm_s = dout("ssm_s", [64, 128])

    e_d = nc.dram_tensor("e_d", [8, 768], F32, kind="Internal").ap()
    m_d = nc.dram_tensor("m_d", [8, 130 * 768], F32, kind="Internal").ap()
    mg_d = nc.dram_tensor("mg_d", [17, 128, 1024], BF16, kind="Internal").ap()
    sloc_d = nc.dram_tensor("sloc_d", [128, 32], F32, kind="Internal").ap()
    wq_d = nc.dram_tensor("wq_d", [D, 1536], BF16, kind="Internal").ap()
    wgm_d = nc.dram_tensor("wgm_d", [D, 1536], BF16, kind="Internal").ap()
    woa_d = nc.dram_tensor("woa_d", [512, D], BF16, kind="Internal").ap()
    wglu_d = nc.dram_tensor("wglu_d", [512, D], BF16, kind="Internal").ap()
    wgsms_d = nc.dram_tensor("wgsms_d", [D, 1536], BF16, kind="Internal").ap()
    wos_d = nc.dram_tensor("wos_d", [512, D], BF16, kind="Internal").ap()
    wout_d = nc.dram_tensor("wout_d", [D, D], BF16, kind="Internal").ap()
    sall_d = nc.dram_tensor("sall_d", [1024, 32], F32, kind="Internal").ap()

    P = Prog()
    st = ExitStack()
    with st:
        def sb(name, shape, dt=F32):
            return st.enter_context(nc.sbuf_tensor("s_" + name, list(shape), dt))

        def ps(name, shape, dt=F32):
            return st.enter_context(nc.psum_tensor("p_" + name, list(shape), dt))

        dummy = sb("dummy", [128, 2])
        epsc = sb("epsc", [128, 1])
        ident = sb("ident", [128, 128], BF16)
        identf = sb("identf", [128, 128])
        selT = sb("selT", [3, 256])
        hb = sb("hb", [128, 1])
        cmask = sb("cmask", [128, 16])
        selm = sb("selm", [128, 24])
        pmask = sb("pmask", [128, 4])
        stg = sb("stg", [68, 128])
        smalls = sb("smalls", [128, 68])
        cT = smalls[:, 0:24].rearrange("p (b k) -> p k b", b=3)
        bada = smalls[:, 24:48]
        ng = smalls[:, 48:56]
        dq = smalls[:, 56:60]
        bglu = smalls[:, 60:68]
        scT = sb("scT", [128, 8, 3], BF16)
        modT = sb("modT", [128, 24, 3])
        Amod = sb("Amod", [128, 8, 3])
        gate_bc = sb("gate_bc", [128, 3, 1024], BF16)
        gqk = sb("gqk", [128, 1024], BF16)
        gqg = sb("gqg", [128, 512], BF16)
        xin = [sb("xin%d" % i, [128, 1024]) for i in range(2)]
        xin2c = sb("xin2c", [128, 1024])
        ss2 = [sb("ss%d" % i, [128, 4]) for i in range(2)]
        xn2 = [sb("xn%d" % i, [128, 1024], BF16) for i in range(2)]
        hT2 = [sb("hT%d" % i, [128, 8, 128], BF16) for i in range(2)]
        e_s = xin[1][0:8, 0:768]
        qk_sb = sb("qk_sb", [128, 1024])
        sq = sb("sq", [128, 1024])
        ss16 = sb("ss16", [128, 16])
        qkb = sb("qkb", [128, 1024], BF16)
        kout = sb("kout", [128, 512])
        vout = sb("vout", [128, 512])
        qT = sb("qT", [128, 4, 128], BF16)
        stmp2 = [sb("stmp%d" % i, [128, 5, 128]) for i in range(2)]
        PT2 = [sb("PT%d" % i, [128, 5, 128], BF16) for i in range(2)]
        stmp, PT = stmp2[0], PT2[0]
        rden = sb("rden", [128, 8])
        AO = sb("AO", [128, 8, 64], BF16)
        sga = sb("sga", [128, 4, 128], BF16)
        sma = sb("sma", [128, 8, 128], BF16)
        AOgT = sb("AOgT", [128, 4, 128], BF16)
        mgt = sb("mgt", [128, 8, 128], BF16)
        uT = sb("uT", [128, 4, NT], BF16)
        XS = sb("XS", [128, 129, 32])
        prm = sb("prm", [128, 36, 32])
        PW1 = sb("PW1", [128, 32, 17])
        PW2 = sb("PW2", [128, 32, 17])
        Bstk = sb("Bstk", [128, 32, 16])
        Bsw = sb("Bsw", [128, 32, 16])
        Bbb = sb("Bbb", [128, 32, 16], BF16)
        Cstk = sb("Cstk", [128, 512])
        Csw = sb("Csw", [128, 512])
        xs0 = sb("xs0", [128, 2, 32])
        WSs = sb("WSs", [128, 2, 32])
        Gall = sb("Gall", [128, 8, 32])
        ki32 = sb("ki32", [128, 32], mybir.dt.int32)

        ARN = 45184
        arena = sb("arena", [128, ARN], BF16)

        def av(off, n):
            return arena[:, off:off + n]

        wada = [av(28672, 4096).rearrange("p (k n) -> p k n", k=8), av(32768, 4096).rearrange("p (k n) -> p k n", k=8)]
        w_out_s = av(24064, 8192).rearrange("p (k n) -> p k n", k=8)
        scr = av(28672, 16512).bitcast(F32)
        w_qkv = av(0, 12288).rearrange("p (k n) -> p k n", k=8)
        w_gm = av(12288, 12288).rearrange("p (k n) -> p k n", k=8)
        w_oa_s = av(24576, 4096).rearrange("p (k n) -> p k n", k=4)
        KT = av(28672, 3072).rearrange("p (k n) -> p k n", k=4)
        V = av(31744, 3120).rearrange("p (s h e) -> p s h e", s=6, h=8)
        BM = av(34880, 10240).bitcast(F32).rearrange("p (h t q) -> p h t q", h=8, t=5)
        w_u = av(0, 4096).rearrange("p (k n) -> p k n", k=8)
        W1z = av(4096, 16384).rearrange("p (q r j m) -> p q r j m", q=4, r=2, j=16)
        Pst = av(20480, 8192).rearrange("p (q t g c) -> p q t g c", q=4, t=16, g=8)
        CAz = av(0, 17408).rearrange("p (g t r c) -> p g t r c", g=32, t=17, r=2)
        Kblk = av(17408, 8192).rearrange("p (q t m) -> p q t m", q=4, t=16)
        Xb = av(25600, 4160).rearrange("p (g k) -> p g k", g=32)
        Cin = av(29760, 1024).bitcast(F32).rearrange("p (q m) -> p q m", q=4)
        yT = av(36608, 8320).rearrange("p (q n) -> p q n", q=4)
        w_glu_s = av(0, 4096).rearrange("p (k n) -> p k n", k=4)
        w_gsms = av(4096, 12288).rearrange("p (k n) -> p k n", k=8)
        w_os_s = av(16384, 4096).rearrange("p (k n) -> p k n", k=4)
        sgb = av(20480, 512).rearrange("p (k n) -> p k n", k=4)
        sgs = av(20992, 512).rearrange("p (k n) -> p k n", k=4)
        sms = av(21504, 1024).rearrange("p (k n) -> p k n", k=8)
        sgt = av(22528, 512).rearrange("p (k n) -> p k n", k=4)
        bst = av(23040, 1024).rearrange("p (k n) -> p k n", k=8)

        pA = ps("pA", [128, 1536])
        pT = ps("pT", [128, 8, 128], BF16)
        pS = ps("pS", [128, 1024])
        pO = ps("pO", [128, 2, 512])

        def bfv(ap_):
            return ap_.bitcast(BF16).rearrange("p (k t) -> p k t", t=128)

        TA = [pT[:, 0:4, :], bfv(pO[:, 0, :])]
        TAk = ["pT", "pO0"]
        TB = [bfv(pA[:, 1024:1536]), bfv(pO[:, 1, :])]
        TBk = ["pA2", "pO1"]
        UB = [pA[:, 0:512], pA[:, 512:1024]]
        UBk = ["pA0", "pA1"]

        def dma(eng, out, in_, reads, writes, **kw):
            P.add(eng, "dma_start", reads=reads, writes=writes, dma=True, out=out, in_=in_, **kw)

        def wload(dst, src, key):
            dma("pool", dst, src.rearrange("(kt p) n -> p kt n", p=128), [], [key])

        def tt(eng, out, in0, in1, op, reads, writes, **kw):
            P.add(eng, "tensor_tensor", reads=reads, writes=writes, out=out, in0=in0, in1=in1, op=op, **kw)

        def ts(eng, out, in0, s1, s2, op0, op1, reads, writes, **kw):
            if s2 is None:
                P.add(eng, "tensor_scalar", reads=reads, writes=writes, out=out, in0=in0, scalar1=s1, scalar2=None,
                      op0=op0, **kw)
            else:
                P.add(eng, "tensor_scalar", reads=reads, writes=writes, out=out, in0=in0, scalar1=s1, scalar2=s2,
                      op0=op0, op1=op1, **kw)

        def cp(eng, out, in_, reads, writes, **kw):
            P.add(eng, "tensor_copy", reads=reads, writes=writes, out=out, in_=in_, **kw)

        P.add("pool", "memset", reads=[], writes=["epsc"], nophase=True, ap=epsc[:, :], constant=EPS)
        dma("pool", ident[:, :], ident_d[:, :], [], ["ident"])
        dma("sp", identf[:, :], ident_d[:, :], [], ["identf"])
        dma("sp", selT[:, :], sel[:, :], [], ["selT"])
        dma("sp", hb[:, :], hbias[:, :], [], ["hb"])
        dma("sp", cmask[:, :], cmask_d[:, :], [], ["cmask"])
        dma("sp", selm[:, :], selm_d[:, :], [], ["selm"])
        dma("sp", pmask[:, :], pmask_d[:, :], [], ["pmask"])
        dma("sp", stg[0:24, :], c3.rearrange("b (kt p) -> (b kt) p", p=128), [], ["stg"])
        dma("sp", stg[24:48, :], b_ada.rearrange("(ct p) -> ct p", p=128), [], ["stg1"])
        dma("sp", stg[48:56, :], norm_g.rearrange("(kt p) -> kt p", p=128), [], ["stg2"])
        dma("sp", stg[56:60, :], d_skip.rearrange("(kt p) -> kt p", p=128), [], ["stg3"])
        dma("sp", stg[60:68, :], b_glu.rearrange("(kt p) -> kt p", p=128), [], ["stg4"])
        P.add("pe", "transpose", reads=["stg", "stg1", "stg2", "stg3", "stg4", "identf"], writes=["pS"], out=pS[:, 0:68],
              in_=stg[0:68, :], identity=identf[0:68, 0:68])
        cp("dve", smalls[:, :], pS[:, 0:68], ["pS"], ["cT", "bada", "ng"])
        bgate = sq[0:3, :]
        gate_tok = qk_sb[0:3, :]
        dma("pool", gqk[:, 0:512], bass.AP(tensor=q_norm_g.tensor, offset=0, ap=[[0, 128], [0, 8], [1, 64]]),
            [], ["gqk_q"])
        dma("pool", gqk[:, 512:1024], bass.AP(tensor=k_norm_g.tensor, offset=0, ap=[[0, 128], [0, 8], [1, 64]]),
            [], ["gqk_k"])
        tt("dve", gqg[:, :], gqk[:, 0:512], gqk[:, 512:1024], ALU.mult, ["gqk_q", "gqk_k"], ["gqg"])
        dma("sp", e_s[:, 129:385], rel_bias[:, 1:257], [], ["xin1"])
        cp("dve", e_s[:, 0:129], e_s[:, 384:385].to_broadcast([8, 129]), ["xin1"], ["xin1"])
        cp("dve", e_s[:, 385:768], e_s[:, 384:385].to_broadcast([8, 383]), ["xin1"], ["xin1"])
        dma("sp", e_d[:, :], e_s[:, :], ["xin1"], ["e_d"])
        dma("sp", m_d.rearrange("h (r e) -> h r e", e=768),
            bass.AP(tensor=e_d.tensor, offset=0, ap=[[768, 8], [0, 130], [1, 768]]), ["e_d"], ["m_d"])

        TI = {"n": 0, "pend": None}

        def rms_front(row0, nt, src=None):
            src = x_all if src is None else src
            i = TI["n"] % 2
            TI["n"] += 1
            xt, xk = xin[i], "xin%d" % i
            xn, xnk = xn2[i], "xn%d" % i
            sst, ssk = ss2[i], "ss%d" % i
            dma("sp", xt[0:nt, :], src[row0:row0 + nt, :], [], [xk])
            P.add("act", "activation", reads=[xk], writes=[xnk, ssk], out=xn[0:nt, :], in_=xt[0:nt, :],
                  func=AF.Square, accum_out=sst[0:nt, 0:1])
            P.add("act", "activation", reads=[ssk, "epsc"], writes=[ssk], out=sst[0:nt, 2:3], in_=sst[0:nt, 0:1], func=AF.Ln,
                  scale=1.0 / D, bias=epsc[0:nt, 0:1])
            P.add("act", "activation", reads=[ssk], writes=[ssk], out=sst[0:nt, 3:4], in_=sst[0:nt, 2:3], func=AF.Exp, scale=-0.5)
            P.add("act", "activation", reads=[xk, ssk], writes=[xnk], out=xn[0:nt, :], in_=xt[0:nt, :],
                  func=AF.Copy, scale=sst[0:nt, 3:4])
            return i

        def rms_back(i, nt, mods):
            CUR["i"] = i
            xn, xnk = xn2[i], "xn%d" % i
            hT, hk = HT(), HK()
            for k in range(8):
                P.add("pe", "transpose", reads=[xnk, "ident"], writes=["pT"], out=pT[:, k, 0:nt],
                      in_=xn[0:nt, k * 128:(k + 1) * 128], identity=ident[0:nt, 0:nt])
            for k in range(8):
                for (c0, c1, b) in mods:
                    if k < 4:
                        ts("dve", hT[:, k, c0:c1], pT[:, k, c0:c1], Amod[:, k, b:b + 1], modT[:, k, b:b + 1],
                           ALU.mult, ALU.add, ["pT", "Amod", "modT"], [hk])
                    else:
                        P.add("act", "activation", reads=["pT", "Amod", "modT"], writes=[hk], out=hT[:, k, c0:c1],
                              in_=pT[:, k, c0:c1], func=AF.Identity, scale=Amod[:, k, b:b + 1], bias=modT[:, k, b:b + 1])

        def run_tiles(tiles, body, src=None):
            nxt = rms_front(tiles[0][0], tiles[0][1], src)
            for n, tl in enumerate(tiles):
                cur = nxt
                rms_back(cur, tl[1], tl[2])
                if n + 1 < len(tiles):
                    nxt = rms_front(tiles[n + 1][0], tiles[n + 1][1], src)
                CUR["i"] = cur
                body(n, tl, cur)

        CUR = {"i": 0}

        def HT():
            return hT2[CUR["i"]]

        def HK():
            return "hT%d" % CUR["i"]

        def qkv_mm(nt, with_q):
            for cb in range(0 if with_q else 1, 3):
                for k in range(8):
                    P.add("pe", "matmul", reads=[HK(), "w_qkv"], writes=["pA"], out=pA[0:nt, cb * 512:(cb + 1) * 512],
                          lhsT=HT()[:, k, 0:nt], rhs=w_qkv[:, k, cb * 512:(cb + 1) * 512], start=(k == 0), stop=(k == 7))

        def qkv_tile(nt, slot, with_q, kdst, vdst, out_rows, gm=False, fold=True, pre_mm=False):
            c_lo = 0 if with_q else 512
            if not pre_mm:
                qkv_mm(nt, with_q)
            if gm:
                for ct in range(12):
                    for k in range(8):
                        if ct < 4:
                            o_, ok_ = pO[:, 0, ct * 128:ct * 128 + nt], "pO"
                        else:
                            o_, ok_ = pS[:, (ct - 4) * 128:(ct - 4) * 128 + nt], "pS"
                        P.add("pe", "matmul", reads=[HK(), "w_gm", "w_gm2"], writes=[ok_], out=o_,
                              lhsT=w_gm[:, k, ct * 128:(ct + 1) * 128], rhs=HT()[:, k, 0:nt], start=(k == 0), stop=(k == 7))
            nh = 16 if with_q else 8
            h0 = 0 if with_q else 8
            P.add("act", "activation", reads=["pA"], writes=["sq"], out=sq[0:nt, c_lo:1024], in_=pA[0:nt, c_lo:1024],
                  func=AF.Square)
            P.add("dve", "tensor_reduce", reads=["sq"], writes=["ss16"], out=ss16[0:nt, h0:16],
                  in_=sq[0:nt, c_lo:1024].rearrange("p (h d) -> p h d", d=64), axis=AX.X, op=ALU.add)
            P.add("act", "activation", reads=["ss16", "epsc"], writes=["ss16"], out=ss16[0:nt, h0:16], in_=ss16[0:nt, h0:16],
                  func=AF.Ln, scale=1.0 / 64, bias=epsc[0:nt, 0:1])
            P.add("act", "activation", reads=["ss16"], writes=["ss16"], out=ss16[0:nt, h0:16], in_=ss16[0:nt, h0:16],
                  func=AF.Exp, scale=-0.5)
            rk = ss16[0:nt, 8:16].rearrange("p (h o) -> p h o", o=1).to_broadcast([nt, 8, 64])
            rq = ss16[0:nt, 0:8].rearrange("p (h o) -> p h o", o=1).to_broadcast([nt, 8, 64])
            pAk = pA[0:nt, 512:1024].rearrange("p (h d) -> p h d", d=64)
            pAq = pA[0:nt, 0:512].rearrange("p (h d) -> p h d", d=64)
            if fold:
                tt("dve", qkb[0:nt, 512:1024].rearrange("p (h d) -> p h d", d=64), pAk, rk, ALU.mult, ["pA", "ss16"], ["qkb"])
            else:
                tt("dve", sq[0:nt, 512:1024].rearrange("p (h d) -> p h d", d=64), pAk, rk, ALU.mult, ["pA", "ss16"], ["sq"])
                tt("dve", qkb[0:nt, 512:1024], sq[0:nt, 512:1024], gqk[0:nt, 512:1024], ALU.mult, ["sq", "gqk_k"], ["qkb"])
            if with_q:
                tt("dve", sq[0:nt, 0:512].rearrange("p (h d) -> p h d", d=64), pAq, rq, ALU.mult, ["pA", "ss16"], ["sq"])
                tt("dve", qkb[0:nt, 0:512], sq[0:nt, 0:512], gqg[0:nt, :] if fold else gqk[0:nt, 0:512], ALU.mult,
                   ["sq", "gqk_q", "gqg"], ["qkb"])
            P.add("act", "activation", reads=["pA", "Vones"], writes=[("V", slot)], out=V[0:nt, slot, :, 0:64],
                  in_=pA[0:nt, 1024:1536].rearrange("p (h d) -> p h d", d=64), func=AF.Copy)
            if gm:
                P.add("act", "activation", reads=["pS"], writes=["sma"], out=sma[:, :, 0:nt],
                      in_=pS[:, :].rearrange("p (c t) -> p c t", t=128)[:, :, 0:nt], func=AF.Sigmoid)
                P.add("act", "activation", reads=["pO"], writes=["sga"], out=sga[:, :, 0:nt],
                      in_=pO[:, 0, :].rearrange("p (c t) -> p c t", t=128)[:, :, 0:nt], func=AF.Silu)
            if kdst is not None:
                tt("dve", kout[0:nt, :].rearrange("p (h d) -> p h d", d=64), pAk, rk, ALU.mult, ["pA", "ss16"], ["kout"])
                tt("dve", kout[0:nt, :], kout[0:nt, :], gqk[0:nt, 512:1024], ALU.mult, ["kout", "gqk_k"], ["kout"])
                dma("sp", kdst[out_rows:out_rows + nt, :], kout[0:nt, :], ["kout"], [])
                cp("dve", vout[0:nt, :], pA[0:nt, 1024:1536], ["pA"], ["vout"])
                dma("sp", vdst[out_rows:out_rows + nt, :], vout[0:nt, :], ["vout"], [])
            for i in range(0 if with_q else 4, 8):
                P.add("pe", "transpose", reads=["qkb", "ident"], writes=["pT"], out=pT[:, i, 0:nt],
                      in_=qkb[0:nt, i * 128:(i + 1) * 128], identity=ident[0:nt, 0:nt])
            if with_q:
                cp("dve", qT[:, :, 0:nt], pT[:, 0:4, 0:nt], ["pT"], ["qT"])
            P.add("act", "activation", reads=["pT"], writes=[("KT", slot)], out=KT[:, :, slot * 128:slot * 128 + nt],
                  in_=pT[:, 4:8, 0:nt], func=AF.Copy)

        def attention(nq, ktiles, mtile):
            nkt = len(ktiles)
            nk_last = ktiles[4][1]
            for h in range(8):
                hp, h2 = h // 2, h % 2
                pr = slice(64 * h2, 64 * h2 + 64)
                for j, (slot, nk, tp_, halo) in enumerate(ktiles):
                    P.add("pe", "matmul", reads=[("KT", slot), "qT"], writes=["pS"], out=pS[0:nk, (4 - j) * 128:(4 - j) * 128 + nq],
                          lhsT=KT[pr, hp, slot * 128:slot * 128 + nk], rhs=qT[pr, hp, 0:nq], start=True, stop=True)
                pSv = pS[:, 0:640].rearrange("p (t q) -> p t q", q=128)
                P.add("dve", "scalar_tensor_tensor", reads=["pS", "BM"], writes=["stmp"], out=stmp[:, 1:5, 0:nq],
                      in0=pSv[:, 1:5, 0:nq], scalar=0.125, in1=BM[:, h, 1:5, 0:nq], op0=ALU.mult, op1=ALU.add)
                P.add("dve", "scalar_tensor_tensor", reads=["pS", "BM"], writes=["stmp"], out=stmp[0:nk_last, 0, 0:nq],
                      in0=pSv[0:nk_last, 0, 0:nq], scalar=0.125, in1=BM[0:nk_last, h, 0, 0:nq], op0=ALU.mult, op1=ALU.add)
                P.add("act", "activation", reads=["stmp"], writes=["PT"], out=PT[:, 1:5, 0:nq], in_=stmp[:, 1:5, 0:nq], func=AF.Exp)
                P.add("act", "activation", reads=["stmp"], writes=["PT"], out=PT[0:nk_last, 0, 0:nq], in_=stmp[0:nk_last, 0, 0:nq],
                      func=AF.Exp)
                for j, (slot, nk, tp_, halo) in enumerate(ktiles):
                    P.add("pe", "matmul", reads=["PT", ("V", slot)], writes=["pO"],
                          out=pO[0:nq, h // 4, (h % 4) * 65:(h % 4) * 65 + 65],
                          lhsT=PT[0:nk, 4 - j, 0:nq], rhs=V[0:nk, slot, h, :], start=(j == 0), stop=(j == nkt - 1))
            attention_tail(nq, mtile)

        def attention_tail(nq, mtile, gm_done=False):
            pOv = pO[0:nq, :, 0:260].rearrange("p a (h e) -> p a h e", e=65)
            P.add("dve", "reciprocal", reads=["pO"], writes=["rden"],
                  out=rden[0:nq, :].rearrange("p (a h o) -> p a h o", a=2, o=1), in_=pOv[:, :, :, 64:65])
            tt("dve", AO[0:nq, :, :].rearrange("p (a h) d -> p a h d", a=2), pOv[:, :, :, 0:64],
               rden[0:nq, :].rearrange("p (a h o) -> p a h o", a=2, o=1).to_broadcast([nq, 2, 4, 64]), ALU.mult,
               ["pO", "rden"], ["AO"])
            AOf = AO[:, :, :].rearrange("p h d -> p (h d)")
            for i in range(4):
                P.add("pe", "transpose", reads=["AO", "ident"], writes=["pT"], out=pT[:, i, 0:nq],
                      in_=AOf[0:nq, i * 128:(i + 1) * 128], identity=ident[0:nq, 0:nq])
            if not gm_done:
                for ct in range(12):
                    for k in range(8):
                        P.add("pe", "matmul", reads=[HK(), "w_gm", "w_gm2"], writes=["pA"], out=pA[:, ct * 128:ct * 128 + nq],
                              lhsT=w_gm[:, k, ct * 128:(ct + 1) * 128], rhs=HT()[:, k, 0:nq], start=(k == 0), stop=(k == 7))
                pAv = pA[:, :].rearrange("p (c t) -> p c t", t=128)
                P.add("act", "activation", reads=["pA"], writes=["sga"], out=sga[:, :, 0:nq], in_=pAv[:, 0:4, 0:nq], func=AF.Silu)
                P.add("act", "activation", reads=["pA"], writes=["sma"], out=sma[:, :, 0:nq], in_=pAv[:, 4:12, 0:nq], func=AF.Sigmoid)
            tt("dve", AOgT[:, :, 0:nq], pT[:, 0:4, 0:nq], sga[:, :, 0:nq], ALU.mult, ["pT", "sga"], ["AOgT"])
            for ct in range(8):
                for k in range(4):
                    P.add("pe", "matmul", reads=["AOgT", "w_oa"], writes=["pS"], out=pS[:, ct * 128:ct * 128 + nq],
                          lhsT=w_oa_s[:, k, ct * 128:(ct + 1) * 128], rhs=AOgT[:, k, 0:nq], start=(k == 0), stop=(k == 3))
            tt("dve", mgt[:, :, 0:nq], pS[:, :].rearrange("p (c t) -> p c t", t=128)[:, :, 0:nq], sma[:, :, 0:nq], ALU.mult,
               ["pS", "sma"], ["mgt"])
            dma("sp", mg_d[mtile[0], :, :].rearrange("p (k t) -> p k t", k=8)[:, :, mtile[1]:mtile[1] + nq],
                mgt[:, :, 0:nq], ["mgt"], [("mg_d", mtile[0])])

        def attention_fast(ktiles, mtile, nh, before_tail=None):
            nq = 128

            def qk(h):
                hp, h2 = h // 2, h % 2
                pr = slice(64 * h2, 64 * h2 + 64)
                Sb, skey = (pS, "pS") if h % 2 == 0 else (pA, "pA")
                for j, (slot, nk, tp_, halo) in enumerate(ktiles):
                    P.add("pe", "matmul", reads=[("KT", slot), "qT"], writes=[skey], out=Sb[:, (4 - j) * 128:(5 - j) * 128],
                          lhsT=KT[pr, hp, slot * 128:slot * 128 + 128], rhs=qT[pr, hp, 0:nq], start=True, stop=True)

            def softmax(h):
                Sb, skey = (pS, "pS") if h % 2 == 0 else (pA, "pA")
                st_, stk = stmp2[h % 2], "stmp%d" % (h % 2)
                PTb, ptk = PT2[h % 2], "PT%d" % (h % 2)
                P.add("dve", "scalar_tensor_tensor", reads=[skey, "BM"], writes=[stk], out=st_[:, :, :].rearrange("p t q -> p (t q)"),
                      in0=Sb[:, 0:640], scalar=0.125, in1=BM[:, h, :, :].rearrange("p t q -> p (t q)"),
                      op0=ALU.mult, op1=ALU.add)
                c_h = (5 - nh) * 128
                stf = st_[:, :, :].rearrange("p t q -> p (t q)")
                ptf = PTb[:, :, :].rearrange("p t q -> p (t q)")
                if nh < 5:
                    P.add("act", "activation", reads=[stk], writes=[ptk], out=ptf[:, 0:c_h], in_=stf[:, 0:c_h], func=AF.Exp)
                if nh > 0:
                    P.add("act", "activation", reads=[stk, "hb"], writes=[ptk], out=ptf[:, c_h:640], in_=stf[:, c_h:640],
                          func=AF.Exp, bias=hb[:, 0:1])

            def pv(h):
                PTb, ptk = PT2[h % 2], "PT%d" % (h % 2)
                for j, (slot, nk, tp_, halo) in enumerate(ktiles):
                    P.add("pe", "matmul", reads=[ptk, ("V", slot)], writes=["pO"],
                          out=pO[0:nq, h // 4, (h % 4) * 65:(h % 4) * 65 + 65],
                          lhsT=PTb[:, 4 - j, :], rhs=V[:, slot, h, :], start=(j == 0), stop=(j == 4))

            qk(0)
            for h in range(8):
                softmax(h)
                if h + 1 < 8:
                    qk(h + 1)
                pv(h)
            if before_tail is not None:
                before_tail()
            attention_tail(nq, mtile, True)


        own_tiles = [(NHALO + i * 128, 128, [(0, 128, 0)], i * 128) for i in range(16)]
        samp_tiles = [(NHALO + NOWN + s * 16, 16, [(0, 16, 1 + s)], NOWN + s * 16) for s in range(2)]

        wload(w_u, w_in[:, 2048:2560], "w_u")
        PRM = {}

        def pt(name):
            if name not in PRM:
                PRM[name] = len(PRM)
                assert len(PRM) <= 34
            return prm[:, PRM[name], :]

        k_ = ["prm"]
        dma("sp", xin[0][0:32, 0:64], lam_re[:, :], [], ["xin0"])
        dma("sp", xin[0][0:32, 64:128], lam_im[:, :], [], ["xin0b"])
        P.add("pe", "transpose", reads=["xin0", "xin0b", "identf"], writes=["pS"], out=pS[:, 0:32], in_=xin[0][0:32, 0:128],
              identity=identf[0:32, 0:32])
        cp("dve", pt("lam"), pS[:, 0:32], ["pS"], k_)
        cp("dve", pt("lr")[0:64, :], pt("lam")[0:64, :], k_, k_)
        cp("dve", pt("lr")[64:128, :], pt("lam")[0:64, :], k_, k_)
        cp("dve", pt("li")[0:64, :], pt("lam")[64:128, :], k_, k_)
        cp("dve", pt("li")[64:128, :], pt("lam")[64:128, :], k_, k_)
        dma("sp", pt("dt"), bass.AP(tensor=log_dt.tensor, offset=0, ap=[[0, 128], [1, 32]]), k_, k_)
        P.add("act", "activation", reads=k_, writes=k_, out=pt("dt"), in_=pt("dt"), func=AF.Exp)
        tt("dve", pt("x"), pt("lr"), pt("dt"), ALU.mult, k_, k_)
        ts("dve", pt("m"), pt("x"), 0.25, 1.0, ALU.mult, ALU.add, k_, k_)
        for cc in (1.0 / 3, 0.5, 1.0):
            P.add("dve", "scalar_tensor_tensor", reads=k_, writes=k_, out=pt("m"), in0=pt("x"), scalar=cc, in1=pt("m"),
                  op0=ALU.mult, op1=ALU.mult)
            ts("dve", pt("m"), pt("m"), 1.0, None, ALU.add, None, k_, k_)
        tt("dve", pt("ang"), pt("li"), pt("dt"), ALU.mult, k_, k_)
        ts("dve", pt("t0"), pt("ang"), 1.0 / (2 * np.pi), None, ALU.mult, None, k_, k_)
        cp("dve", ki32[:, :], pt("t0"), k_, ["ki32"])
        cp("dve", pt("t0"), ki32[:, :], ["ki32"], k_)
        P.add("dve", "scalar_tensor_tensor", reads=k_, writes=k_, out=pt("ang"), in0=pt("t0"), scalar=-2 * np.pi,
              in1=pt("ang"), op0=ALU.mult, op1=ALU.add)
        ts("dve", pt("psi"), pt("ang"), 1.0 / 32, None, ALU.mult, None, k_, k_)
        tt("dve", pt("p2"), pt("psi"), pt("psi"), ALU.mult, k_, k_)
        ts("dve", pt("s"), pt("p2"), -1.0 / 42, 1.0, ALU.mult, ALU.add, k_, k_)
        for cc in (-1.0 / 20, -1.0 / 6):
            P.add("dve", "scalar_tensor_tensor", reads=k_, writes=k_, out=pt("s"), in0=pt("p2"), scalar=cc, in1=pt("s"),
                  op0=ALU.mult, op1=ALU.mult)
            ts("dve", pt("s"), pt("s"), 1.0, None, ALU.add, None, k_, k_)
        tt("dve", pt("s"), pt("s"), pt("psi"), ALU.mult, k_, k_)
        ts("dve", pt("c"), pt("p2"), -1.0 / 56, 1.0, ALU.mult, ALU.add, k_, k_)
        for cc in (-1.0 / 30, -1.0 / 12, -0.5):
            P.add("dve", "scalar_tensor_tensor", reads=k_, writes=k_, out=pt("c"), in0=pt("p2"), scalar=cc, in1=pt("c"),
                  op0=ALU.mult, op1=ALU.mult)
            ts("dve", pt("c"), pt("c"), 1.0, None, ALU.add, None, k_, k_)

        def csquare(r, i, t1, t2):
            tt("dve", t1, r, r, ALU.mult, k_, k_)
            tt("dve", t2, i, i, ALU.mult, k_, k_)
            P.add("dve", "scalar_tensor_tensor", reads=k_, writes=k_, out=i, in0=r, scalar=2.0, in1=i,
                  op0=ALU.mult, op1=ALU.mult)
            tt("dve", r, t1, t2, ALU.subtract, k_, k_)

        for _ in range(5):
            csquare(pt("c"), pt("s"), pt("t0"), pt("t1"))
        tt("dve", pt("ar"), pt("m"), pt("c"), ALU.mult, k_, k_)
        tt("dve", pt("ai"), pt("m"), pt("s"), ALU.mult, k_, k_)
        tt("dve", pt("den"), pt("lr"), pt("lr"), ALU.mult, k_, k_)
        tt("dve", pt("t0"), pt("li"), pt("li"), ALU.mult, k_, k_)
        tt("dve", pt("den"), pt("den"), pt("t0"), ALU.add, k_, k_)
        P.add("dve", "reciprocal", reads=k_, writes=k_, out=pt("den"), in_=pt("den"))
        ts("dve", pt("nr"), pt("ar"), -1.0, None, ALU.add, None, k_, k_)
        tt("dve", pt("t0"), pt("nr"), pt("lr"), ALU.mult, k_, k_)
        tt("dve", pt("t1"), pt("ai"), pt("li"), ALU.mult, k_, k_)
        tt("dve", pt("cor"), pt("t0"), pt("t1"), ALU.add, k_, k_)
        tt("dve", pt("cor"), pt("cor"), pt("den"), ALU.mult, k_, k_)
        tt("dve", pt("t0"), pt("ai"), pt("lr"), ALU.mult, k_, k_)
        tt("dve", pt("t1"), pt("nr"), pt("li"), ALU.mult, k_, k_)
        tt("dve", pt("coi"), pt("t0"), pt("t1"), ALU.subtract, k_, k_)
        tt("dve", pt("coi"), pt("coi"), pt("den"), ALU.mult, k_, k_)
        ts("dve", pt("cois"), pt("coi"), cmask[:, 0:1], None, ALU.mult, None, k_ + ["cmask"], k_)
        dma("sp", Bstk[0:64, :, :], b_re.rearrange("g n c -> n g c"), [], ["Bstk"])
        dma("sp", Bstk[64:128, :, :], b_im.rearrange("g n c -> n g c"), [], ["Bstk2"])
        cp("dve", Bsw[0:64, :, :], Bstk[64:128, :, :], ["Bstk", "Bstk2"], ["Bsw"])
        cp("dve", Bsw[64:128, :, :], Bstk[0:64, :, :], ["Bstk", "Bstk2"], ["Bsw"])

        def bc_c(name):
            return pt(name).rearrange("p (g o) -> p g o", o=1).to_broadcast([128, 32, 16])

        tt("dve", Bstk[:, :, :], Bstk[:, :, :], bc_c("cor"), ALU.mult, ["Bstk", "Bstk2", "Bsw"] + k_, ["Bstk"])
        tt("dve", Bsw[:, :, :], Bsw[:, :, :], bc_c("cois"), ALU.mult, ["Bsw"] + k_, ["Bsw"])
        tt("dve", Bstk[:, :, :], Bstk[:, :, :], Bsw[:, :, :], ALU.add, ["Bstk", "Bsw"], ["Bstk"])
        cp("dve", Bsw[0:64, :, :], Bstk[64:128, :, :], ["Bstk"], ["Bsw"])
        cp("dve", Bsw[64:128, :, :], Bstk[0:64, :, :], ["Bstk"], ["Bsw"])
        cp("dve", Bbb[:, :, :], Bstk[:, :, :], ["Bstk"], ["Bbb"])
        kp = ["PW"]
        P.add("dve", "memset", reads=[], writes=kp, ap=PW1[:, :, 0:1], constant=1.0)
        P.add("dve", "memset", reads=[], writes=kp, ap=PW2[:, :, 0:1], constant=0.0)
        cp("dve", PW1[:, :, 1:2], pt("ar").rearrange("p (g o) -> p g o", o=1), k_ + kp, kp)
        cp("dve", PW2[:, :, 1:2], pt("ai").rearrange("p (g o) -> p g o", o=1), k_ + kp, kp)
        tw = qk_sb[:, 0:512].rearrange("p (a g t) -> p a g t", a=2, g=32)
        for kk in (1, 2, 4, 8):
            src_r, src_i = PW1[:, :, 1:kk + 1], PW2[:, :, 1:kk + 1]
            kr = PW1[:, :, kk:kk + 1].to_broadcast([128, 32, kk])
            ki = PW2[:, :, kk:kk + 1].to_broadcast([128, 32, kk])
            t_a, t_b = tw[:, 0, :, 0:kk], tw[:, 1, :, 0:kk]
            tt("dve", t_a, src_r, kr, ALU.mult, kp, ["qk_sb"])
            tt("dve", t_b, src_i, ki, ALU.mult, kp, ["qk_sb"])
            tt("dve", PW1[:, :, kk + 1:2 * kk + 1], t_a, t_b, ALU.subtract, ["qk_sb"], kp)
            tt("dve", t_a, src_r, ki, ALU.mult, kp, ["qk_sb"])
            tt("dve", t_b, src_i, kr, ALU.mult, kp, ["qk_sb"])
            tt("dve", PW2[:, :, kk + 1:2 * kk + 1], t_a, t_b, ALU.add, ["qk_sb"], kp)
        PW2s = sq[:, 0:544].rearrange("p (g t) -> p g t", g=32)
        ts("dve", PW2s, PW2[:, :, :], cmask[:, 0:1], None, ALU.mult, None, kp + ["cmask"], ["sq"])
        cp("dve", pt("Y1"), PW1[:, :, 16], kp, k_)
        cp("dve", pt("Y2"), PW2s[:, :, 16], ["sq"], k_)
        for gq in range(8):
            gs_ = slice(gq * 4, gq * 4 + 4)
            t_a = qk_sb[:, 0:1024].rearrange("p (g t c) -> p g t c", g=4, t=16)
            t_b = xin[1][:, 0:1024].rearrange("p (g t c) -> p g t c", g=4, t=16)
            tt("dve", t_a, Bstk[:, gs_, :].rearrange("p g (o c) -> p g o c", o=1).to_broadcast([128, 4, 16, 16]),
               PW1[:, gs_, 0:16].rearrange("p g (t o) -> p g t o", o=1).to_broadcast([128, 4, 16, 16]), ALU.mult,
               ["Bstk"] + kp, ["qk_sb"])
            tt("pool", t_b, Bsw[:, gs_, :].rearrange("p g (o c) -> p g o c", o=1).to_broadcast([128, 4, 16, 16]),
               PW2s[:, gs_, 0:16].rearrange("p g (t o) -> p g t o", o=1).to_broadcast([128, 4, 16, 16]), ALU.mult,
               ["Bsw", "sq"], ["xin1"])
            tt("dve", Pst[:, gq // 2, :, (gq % 2) * 4:(gq % 2) * 4 + 4, :].rearrange("p t g c -> p g t c"), t_a, t_b, ALU.add, ["qk_sb", "xin1"], ["Pst"])
        for q in range(4):
            for half in range(2):
                for tl in range(8):
                    tau = half * 8 + tl
                    P.add("pe", "transpose", reads=["Pst", "ident"], writes=["pT"], out=pT[:, 7 - tl, :],
                          in_=Pst[:, q, tau, :, :].rearrange("p g c -> p (g c)"), identity=ident[:, :])
                j0 = 8 - 8 * half
                ts("dve", W1z[:, q, 0, j0:j0 + 8, :], pT[:, :, :], cmask[:, 2:3], None, ALU.mult, None, ["pT", "cmask"], ["W1z"])
                P.add("act", "activation", reads=["pT", "cmask"], writes=["W1z"], out=W1z[:, q, 1, j0:j0 + 8, :],
                      in_=pT[:, :, :], func=AF.Copy, scale=cmask[:, 3:4])
        def cmul_acc(dst, x, y1, y2, t_sw, t_a, keys_r, keys_w):
            cp("dve", t_sw[0:64], x[64:128], keys_r, ["scan_t"], nophase=True)
            cp("dve", t_sw[64:128], x[0:64], keys_r, ["scan_t"], nophase=True)
            tt("dve", t_a, x, y1, ALU.mult, keys_r + ["prm"], ["scan_t2"], nophase=True)
            tt("dve", t_sw, t_sw, y2, ALU.mult, ["scan_t", "prm"], ["scan_t"], nophase=True)
            tt("dve", dst, dst, t_a, ALU.add, ["scan_t2"] + keys_w, keys_w, nophase=True)
            tt("dve", dst, dst, t_sw, ALU.add, ["scan_t"] + keys_w, keys_w, nophase=True)

        sc_a = prm[:, 34, :]
        sc_b = prm[:, 35, :]
        dma("sp", bgate, bass.AP(tensor=b_ada.tensor, offset=2048, ap=[[0, 3], [1, 1024]]), [], ["sq"])
        P.add("act", "activation", reads=["cT"], writes=["scT"], out=scT[:, :, :], in_=cT, func=AF.Silu)
        for ch in range(6):
            wb = wada[ch % 2]
            wk = "wada%d" % (ch % 2)
            wload(wb, w_ada[:, ch * 512:(ch + 1) * 512], wk)
            for c4 in range(4):
                for k in range(8):
                    P.add("pe", "matmul", reads=[wk, "scT"], writes=["pA"], out=pA[:, c4 * 4:c4 * 4 + 3],
                          lhsT=wb[:, k, c4 * 128:(c4 + 1) * 128], rhs=scT[:, k, :], start=(k == 0), stop=(k == 7))
            tt("dve", modT[:, ch * 4:(ch + 1) * 4, :], pA[:, 0:16].rearrange("p (c b) -> p c b", b=4)[:, :, 0:3],
               bada[:, ch * 4:(ch + 1) * 4].rearrange("p (c o) -> p c o", o=1).to_broadcast([128, 4, 3]), ALU.add,
               ["pA", "bada"], ["modT"])
            if ch >= 4:
                for k in range(8):
                    P.add("pe", "matmul", reads=[wk, "scT"], writes=["pS"], out=pS[0:3, 0:512],
                          lhsT=scT[:, k, :], rhs=wb[:, k, :], start=(k == 0), stop=(k == 7))
                tt("dve", gate_tok[:, (ch - 4) * 512:(ch - 3) * 512], pS[0:3, 0:512],
                   bgate[:, (ch - 4) * 512:(ch - 3) * 512], ALU.add, ["pS", "sq"], ["qk_sb"])
        ts("dve", Amod[:, :, :], modT[:, 8:16, :], 1.0, None, ALU.add, None, ["modT"], ["Amod"])
        tt("dve", Amod[:, :, :], Amod[:, :, :], ng.rearrange("p (k o) -> p k o", o=1).to_broadcast([128, 8, 3]), ALU.mult,
           ["Amod", "ng"], ["Amod"])
        for which, (c0, nt) in enumerate(((0, 128), (128, 16), (144, 16))):
            for half in range(2):
                P.add("pe", "matmul", reads=["selT", "qk_sb"], writes=["pS"], out=pS[0:nt, half * 512:(half + 1) * 512],
                      lhsT=selT[:, c0:c0 + nt], rhs=gate_tok[:, half * 512:(half + 1) * 512], start=True, stop=True)
            cp("dve", gate_bc[0:nt, which, :], pS[0:nt, :], ["pS"], ["gate_bc"])

        P.barrier(dummy[:, 0:1])

        def wstage(dst, src, key):
            dma("pool", dst, src, [], [key], nophase=True)

        wstage(wq_d[:, :], w_in[:, 0:1536], "wq_d")
        wstage(wgm_d[:, 0:512], w_in[:, 1536:2048], "wgm_d")
        wstage(wgm_d[:, 512:1536], w_in[:, 3072:4096], "wgm_d2")
        wstage(woa_d[:, :], w_oa[:, :], "woa_d")
        wstage(wglu_d[:, :], w_glu[:, :], "wglu_d")
        wstage(wgsms_d[:, 0:512], w_in[:, 2560:3072], "wgsms_d")
        wstage(wgsms_d[:, 512:1536], w_in[:, 4096:5120], "wgsms_d2")
        wstage(wos_d[:, :], w_os[:, :], "wos_d")
        wstage(wout_d[:, :], w_out[:, :], "wout_d")

        def wload2(dst, src, rkeys, key):
            dma("sp", dst, src.rearrange("(kt p) n -> p kt n", p=128), rkeys, [key])

        Q1 = scr[:, 0:544].rearrange("p (m g) -> p m g", g=32)
        Q2 = scr[:, 544:1088].rearrange("p (m g) -> p m g", g=32)
        Q2s = scr[:, 1088:1632].rearrange("p (m g) -> p m g", g=32)
        tsw8 = scr[:, 1632:1888].rearrange("p (b g) -> p b g", g=32)
        ta8 = scr[:, 1888:2144].rearrange("p (b g) -> p b g", g=32)
        tb1 = scr[:, 2144:4064].rearrange("p (b i g) -> p b i g", b=4, i=15)
        tb2 = scr[:, 4064:5984].rearrange("p (b i g) -> p b i g", b=4, i=15)
        qa = scr[:, 5984:6240].rearrange("p (m g) -> p m g", g=32)
        qb = scr[:, 6240:6496].rearrange("p (m g) -> p m g", g=32)
        kq = ["Q"]
        P.add("dve", "memset", reads=[], writes=kq, ap=Q1[:, 0, :], constant=1.0)
        P.add("dve", "memset", reads=[], writes=kq, ap=Q2[:, 0, :], constant=0.0)
        cp("dve", Q1[:, 1, :], PW1[:, :, 16], kp + kq, kq)
        cp("dve", Q2[:, 1, :], PW2[:, :, 16], kp + kq, kq)
        for kk in (1, 2, 4, 8):
            src_r, src_i = Q1[:, 1:kk + 1, :], Q2[:, 1:kk + 1, :]
            kr = Q1[:, kk:kk + 1, :].to_broadcast([128, kk, 32])
            ki = Q2[:, kk:kk + 1, :].to_broadcast([128, kk, 32])
            t_a, t_b = qa[:, 0:kk, :], qb[:, 0:kk, :]
            tt("dve", t_a, src_r, kr, ALU.mult, kq, ["qa"])
            tt("dve", t_b, src_i, ki, ALU.mult, kq, ["qb"])
            tt("dve", Q1[:, kk + 1:2 * kk + 1, :], t_a, t_b, ALU.subtract, ["qa", "qb"] + kq, kq)
            tt("dve", t_a, src_r, ki, ALU.mult, kq, ["qa"])
            tt("dve", t_b, src_i, kr, ALU.mult, kq, ["qb"])
            tt("dve", Q2[:, kk + 1:2 * kk + 1, :], t_a, t_b, ALU.add, ["qa", "qb"] + kq, kq)
        ts("dve", Q2s, Q2, cmask[:, 0:1], None, ALU.mult, None, kq + ["cmask"], kq)
        y1b8 = pt("Y1").rearrange("p (o g) -> p o g", o=1).to_broadcast([128, 8, 32])
        y2b8 = pt("Y2").rearrange("p (o g) -> p o g", o=1).to_broadcast([128, 8, 32])
        P.add("pool", "memset", reads=[], writes=["XS", "XSfree"], nophase=True, ap=XS[:, 0, :], constant=0.0)
        for seg in range(4):
            if seg < 3:
                seg_tiles = [(seg * NOWN + i * 128, 128, [(0, 128, 0)], i * 128) for i in range(16)]
            else:
                seg_tiles = own_tiles + samp_tiles
            srcA = x_prev if seg < 3 else x_all
            NA = len(seg_tiles)

            def a_front(n):
                (row0, nt, mods, col0) = seg_tiles[n]
                i = n % 2
                xt, xk = xin[i], "xin%d" % i
                xn, xnk = xn2[i], "xn%d" % i
                sst, ssk = ss2[i], "ss%d" % i
                dma("sp", xt[0:nt, :], srcA[row0:row0 + nt, :], [], [xk])
                P.add("act", "activation", reads=[xk], writes=[xnk, ssk], out=xn[0:nt, :], in_=xt[0:nt, :],
                      func=AF.Square, accum_out=sst[0:nt, 0:1])
                P.add("act", "activation", reads=[ssk, "epsc"], writes=[ssk], out=sst[0:nt, 2:3], in_=sst[0:nt, 0:1], func=AF.Ln,
                      scale=1.0 / D, bias=epsc[0:nt, 0:1])
                P.add("act", "activation", reads=[ssk], writes=[ssk], out=sst[0:nt, 3:4], in_=sst[0:nt, 2:3], func=AF.Exp, scale=-0.5)
                ts("dve", xn[0:nt, :], xt[0:nt, :], sst[0:nt, 3:4], None, ALU.mult, None, [xk, ssk], [xnk])

            def a_T(n):
                (row0, nt, mods, col0) = seg_tiles[n]
                i = n % 2
                xn, xnk = xn2[i], "xn%d" % i
                for k in range(8):
                    tb, tk = (TA[i], TAk[i]) if k < 4 else (TB[i], TBk[i])
                    P.add("pe", "transpose", reads=[xnk, "ident"], writes=[tk], out=tb[:, k % 4, 0:nt],
                          in_=xn[0:nt, k * 128:(k + 1) * 128], identity=ident[0:nt, 0:nt])

            def a_evac(n):
                (row0, nt, mods, col0) = seg_tiles[n]
                i = n % 2
                hT, hk = hT2[i], "hT%d" % i
                for k in range(8):
                    tb, tk = (TA[i], TAk[i]) if k < 4 else (TB[i], TBk[i])
                    for (c0, c1, bsel) in mods:
                        if k < 4:
                            ts("dve", hT[:, k, c0:c1], tb[:, k % 4, c0:c1], Amod[:, k, bsel:bsel + 1], modT[:, k, bsel:bsel + 1],
                               ALU.mult, ALU.add, [tk, "Amod", "modT"], [hk])
                        else:
                            P.add("act", "activation", reads=[tk, "Amod", "modT"], writes=[hk], out=hT[:, k, c0:c1],
                                  in_=tb[:, k % 4, c0:c1], func=AF.Identity, scale=Amod[:, k, bsel:bsel + 1],
                                  bias=modT[:, k, bsel:bsel + 1])

            def a_mm(n):
                (row0, nt, mods, col0) = seg_tiles[n]
                i = n % 2
                hT, hk = hT2[i], "hT%d" % i
                for q in range(4):
                    for k in range(8):
                        P.add("pe", "matmul", reads=[hk, "w_u"], writes=[UBk[i]], out=UB[i][:, q * 128:q * 128 + nt],
                              lhsT=w_u[:, k, q * 128:(q + 1) * 128], rhs=hT[:, k, 0:nt], start=(k == 0), stop=(k == 7))

            def a_uevac(n, seg=seg):
                (row0, nt, mods, col0) = seg_tiles[n]
                i = n % 2
                ts("dve", uT[:, :, col0:col0 + nt], UB[i].rearrange("p (q t) -> p q t", q=4)[:, :, 0:nt],
                   pmask[:, seg:seg + 1], None, ALU.mult, None, [UBk[i], "pmask"], ["uT"])

            a_front(0)
            if NA > 1:
                a_front(1)
            a_T(0)
            a_evac(0)
            for n in range(NA):
                if n + 1 < NA:
                    a_T(n + 1)
                a_mm(n)
                if n + 1 < NA:
                    a_evac(n + 1)
                if n + 2 < NA:
                    a_front(n + 2)
                a_uevac(n)
            nch = 128 if seg < 3 else NCH
            wbanks = [(pA, 0, "pA0"), (pA, 512, "pA1"), (pS, 0, "pS0"), (pS, 512, "pS1")]
            for q in range(4):
                for par in range(2):
                    for j in range(16):
                        for s_ in range(4):
                            bk, off, bkey = wbanks[s_]
                            P.add("pe", "matmul", reads=["uT", "W1z"], writes=[bkey], out=bk[:, off:off + nch],
                                  lhsT=W1z[32 * s_:32 * s_ + 32, q, par, j, :],
                                  rhs=uT[32 * s_:32 * s_ + 32, q, 0:nch * 16].rearrange("p (k i) -> p k i", i=16)[:, :, j],
                                  start=(j == 0), stop=(j == 15), tile_position=(32 * s_, 0))
                    for s_ in range(4):
                        bk, off, bkey = wbanks[s_]
                        g = q * 8 + s_ * 2 + par
                        if s_ % 2 == 0:
                            cp("dve", XS[:, 1:129, g], bk[:, off:off + 128], [bkey, "XSfree"], [("XSg", g)], nophase=True)
                        else:
                            P.add("act", "activation", reads=[bkey, "XSfree"], writes=[("XSg", g)], nophase=True,
                                  out=XS[:, 1:129, g], in_=bk[:, off:off + 128], func=AF.Copy)
                        if seg == 3:
                            cp("dve", WSs[:, :, g], bk[:, off + 128:off + 130], [bkey], ["WSs"])
            P.add("dve", "memset", reads=[("XSg", g_) for g_ in range(32)], writes=["XS"], nophase=True, ap=dummy[:, 1:2],
                  constant=0.0)
            XSb = XS[:, 1:129, :].rearrange("p (b i) g -> p b i g", i=16)
            for i in range(1, 16):
                cmul_acc(XSb[:, :, i, :], XSb[:, :, i - 1, :], y1b8, y2b8, tsw8, ta8, ["XS"], ["XS"])
            for bb in range(8):
                cmul_acc(XS[:, 16 * (bb + 1), :], XS[:, 16 * bb, :], Q1[:, 16, :], Q2s[:, 16, :], sc_a, sc_b, ["XS", "Q"], ["XS"])
            Cst = XS[:, 0:128, :].rearrange("p (b i) g -> p b i g", i=16)
            for b0 in ((0, 4) if seg == 3 else ()):
                Ch = Cst[:, b0:b0 + 4, 0, :]
                cp("dve", tsw8[0:64, 0:4, :], Ch[64:128], ["XS"], ["scan_t"], nophase=True)
                cp("dve", tsw8[64:128, 0:4, :], Ch[0:64], ["XS"], ["scan_t"], nophase=True)
                tt("dve", tb1, Ch.rearrange("p b (o g) -> p b o g", o=1).to_broadcast([128, 4, 15, 32]),
                   Q1[:, 1:16, :].rearrange("p (o m) g -> p o m g", o=1).to_broadcast([128, 4, 15, 32]), ALU.mult,
                   ["XS", "Q"], ["tb1"], nophase=True)
                tt("dve", tb2, tsw8[:, 0:4, :].rearrange("p b (o g) -> p b o g", o=1).to_broadcast([128, 4, 15, 32]),
                   Q2s[:, 1:16, :].rearrange("p (o m) g -> p o m g", o=1).to_broadcast([128, 4, 15, 32]), ALU.mult,
                   ["scan_t", "Q"], ["tb2"], nophase=True)
                tt("dve", XSb[:, b0:b0 + 4, 0:15, :], XSb[:, b0:b0 + 4, 0:15, :], tb1, ALU.add, ["XS", "tb1"], ["XS"], nophase=True)
                tt("dve", XSb[:, b0:b0 + 4, 0:15, :], XSb[:, b0:b0 + 4, 0:15, :], tb2, ALU.add, ["XS", "tb2"], ["XS"], nophase=True)
            if seg < 3:
                cp("dve", XS[:, 0, :], XS[:, 128, :], ["XS"], ["XS", "XSfree"], nophase=True)
        P.add("pe", "transpose", reads=["XS", "identf"], writes=["pS"], nophase=True, out=pS[0:32, 0:128], in_=XS[:, 128, :],
              identity=identf[:, :])
        cp("dve", kout[0:32, 0:128], pS[0:32, 0:128], ["pS"], ["kout"], nophase=True)
        dma("sp", ssm_p[:, :], kout[0:32, 0:128], ["kout"], [], nophase=True)
        dma("sp", xin[0][0:64, 0:64], st_re[:, :], [], ["xin0"])
        dma("sp", xin[0][0:64, 64:128], st_im[:, :], [], ["xin0b"])
        P.add("pe", "transpose", reads=["xin0", "xin0b", "identf"], writes=["pS"], out=pS[:, 0:64], in_=xin[0][0:64, 0:128],
              identity=identf[0:64, 0:64])
        cp("dve", xs0[:, :, :], pS[:, 0:64].rearrange("p (s g) -> p s g", s=2), ["pS"], ["xs0"])
        for s_ in range(2):
            cmul_acc(WSs[:, s_, :], xs0[:, s_, :], pt("Y1"), pt("Y2"), sc_a, sc_b, ["xs0", "WSs"], ["WSs"])
        P.add("pe", "transpose", reads=["WSs", "identf"], writes=["pS"], out=pS[0:64, 0:128],
              in_=WSs[:, :, :].rearrange("p s g -> p (s g)"), identity=identf[:, :])
        cp("dve", vout[0:64, 0:128], pS[0:64, 0:128], ["pS"], ["vout"])
        dma("sp", ssm_s[:, :], vout[0:64, 0:128], ["vout"], [])

        P.barrier(dummy[:, 0:1])
        for h in range(8):
            dma("sp", BM[:, h, :, :],
                bass.AP(tensor=m_d.tensor, offset=h * 130 * 768 + 256, ap=[[767, 128], [128, 5], [1, 128]]),
                ["m_d"], ["BM"])
        P.add("pool", "memset", reads=[], writes=["BM"], ap=BM[0:64, :, 4, 64:128], constant=-30000.0)
        P.add("pool", "memset", reads=[], writes=["BM"], ap=BM[64:128, :, 0, 0:64], constant=-30000.0)
        P.add("pool", "memset", reads=[], writes=["Vones"], ap=V[:, :, :, 64:65], constant=1.0)
        wload2(w_qkv[:, :, :], wq_d[:, :], ["wq_d"], "w_qkv")
        wload2(w_gm[:, :, :], wgm_d[:, :], ["wgm_d", "wgm_d2"], "w_gm")
        wload2(w_oa_s[:, :, :], woa_d[:, :], ["woa_d"], "w_oa")
        tilesB = [(t * 128, 128, [(0, 128, 0)], 0) for t in range(20)]
        stB = {"nxt": rms_front(tilesB[0][0], 128), "pre": False}

        def b_prepare(n):
            cur = stB["nxt"]
            rms_back(cur, 128, tilesB[n][2])
            if n + 1 < 20:
                stB["nxt"] = rms_front(tilesB[n + 1][0], 128)
            CUR["i"] = cur
            qkv_mm(128, n >= 4)
            stB["cur"] = cur

        b_prepare(0)
        for t in range(20):
            slot = t % 6
            CUR["i"] = stB["cur"]
            if t < 4:
                qkv_tile(128, slot, False, None, None, 0, pre_mm=True)
                if t + 1 < 20:
                    b_prepare(t + 1)
            else:
                own = t - 4
                last = own >= 12
                qkv_tile(128, slot, True, k_last if last else None, v_last if last else None, (own - 12) * 128, gm=True,
                         pre_mm=True)
                ktiles = [((t - 4 + j) % 6, 128, 4 - j, (t - 4 + j) < 4) for j in range(5)]
                mycur = stB["cur"]

                def hook(t=t):
                    if t + 1 < 20:
                        b_prepare(t + 1)

                attention_fast(ktiles, (own, 0), max(0, 4 - own), before_tail=hook)
        kc = qkb[:, :]
        for s_ in range(2):
            for tk in range(4):
                dma("pool", kc[:, 0:512], cache_k[s_, tk * 128:(tk + 1) * 128, :], [], ["qkb"])
                for i in range(4):
                    P.add("pe", "transpose", reads=["qkb", "ident"], writes=["pT"], out=pT[:, 4 + i, :],
                          in_=kc[:, i * 128:(i + 1) * 128], identity=ident[:, :])
                P.add("act", "activation", reads=["pT"], writes=[("KT", tk)], out=KT[:, :, tk * 128:(tk + 1) * 128],
                      in_=pT[:, 4:8, :], func=AF.Copy)
                dma("pool", V[:, tk, :, 0:64], cache_v[s_, tk * 128:(tk + 1) * 128, :].rearrange("p (h d) -> p h d", d=64),
                    ["Vones"], [("V", tk)])
            (row0, nt, mods, col0) = samp_tiles[s_]
            rms_back(rms_front(row0, nt), nt, mods)
            qkv_tile(16, 4, True, k_samp, v_samp, s_ * 16, fold=False)
            ktiles = [(0, 128, 4, False), (1, 128, 3, False), (2, 128, 2, False), (3, 128, 1, False), (4, 16, 0, False)]
            attention(16, ktiles, (16, s_ * 16))

        P.barrier(dummy[:, 0:1])
        P.add("pool", "memset", reads=[], writes=["CAz"], ap=arena[:, 0:8704], constant=0.0)
        P.add("dve", "memset", reads=[], writes=["CAz2"], ap=arena[:, 8704:17408], constant=0.0)
        for q in range(4):
            dma("sp", Cin[:, q, 0:64], c_re[q * 128:(q + 1) * 128, :], [], ["Cin"])
            dma("sp", Cin[:, q, 64:128], c_im[q * 128:(q + 1) * 128, :], [], ["Cin2"])
        for q in range(4):
            P.add("pe", "transpose", reads=["Cin", "Cin2", "identf"], writes=["pS"], out=pS[:, q * 128:(q + 1) * 128],
                  in_=Cin[:, q, :], identity=identf[:, :])
        cp("dve", Cstk[:, :], pS[:, 0:512], ["pS"], ["Cstk"])
        cp("dve", Csw[0:64, :], Cstk[64:128, :], ["Cstk"], ["Csw"])
        cp("dve", Csw[64:128, :], Cstk[0:64, :], ["Cstk"], ["Csw"])
        AA1 = sq[:, 0:544].rearrange("p (g t) -> p g t", g=32)
        AA2 = xin[1][:, 0:544].rearrange("p (g t) -> p g t", g=32)
        ts("dve", AA1, PW1[:, :, :], cmask[:, 1:2], None, ALU.mult, None, ["PW", "cmask"], ["sq"])
        ts("dve", AA2, PW2[:, :, :], -1.0, None, ALU.mult, None, ["PW"], ["xin1"])
        Cst4 = Cstk[:, :].rearrange("p (s e c) -> p s e c", s=16, e=2)
        Csw4 = Csw[:, :].rearrange("p (s e c) -> p s e c", s=16, e=2)
        AA1v = AA1.rearrange("p (s e) t -> p s e t", e=2)
        AA2v = AA2.rearrange("p (s e) t -> p s e t", e=2)
        CAzv = CAz.rearrange("p (s e) t r c -> p s e t r c", e=2)
        for s0 in range(0, 16, 3):
            ns = min(3, 16 - s0)
            for e in range(2):
                t_a = qk_sb[:, 0:ns * 272].rearrange("p (s t c) -> p s t c", s=ns, t=17)
                t_b = xin[0][:, 0:ns * 272].rearrange("p (s t c) -> p s t c", s=ns, t=17)
                tt("dve", t_a, Cst4[:, s0:s0 + ns, e, :].rearrange("p s (o c) -> p s o c", o=1).to_broadcast([128, ns, 17, 16]),
                   AA1v[:, s0:s0 + ns, e, :].rearrange("p s (t o) -> p s t o", o=1).to_broadcast([128, ns, 17, 16]), ALU.mult,
                   ["Cstk", "sq"], ["qk_sb"])
                tt("pool", t_b, Csw4[:, s0:s0 + ns, e, :].rearrange("p s (o c) -> p s o c", o=1).to_broadcast([128, ns, 17, 16]),
                   AA2v[:, s0:s0 + ns, e, :].rearrange("p s (t o) -> p s t o", o=1).to_broadcast([128, ns, 17, 16]), ALU.mult,
                   ["Csw", "xin1"], ["xin0"])
                tt("dve", CAzv[:, s0:s0 + ns, e, :, e, :], t_a, t_b, ALU.add, ["qk_sb", "xin0", "CAz", "CAz2"], ["CAz"])
        for g in range(32):
            q, g8 = g // 8, g % 8
            pk = "pS%d" % (g % 2)
            P.add("pe", "matmul", reads=["Bbb", "CAz"], writes=[pk], out=pS[:, (g % 2) * 512:(g % 2) * 512 + 272],
                  lhsT=Bbb[:, :, :].rearrange("p g c -> p (g c)")[:, q * 128:(q + 1) * 128], rhs=CAz[:, g, :, g % 2, :], start=True, stop=True)
            if g % 2 == 0:
                ts("dve", Kblk[:, q, :, g8 * 16:(g8 + 1) * 16],
                   pS[:, 0:256].rearrange("p (t c) -> p t c", t=16), cmask[:, 4 + g8:5 + g8], None,
                   ALU.mult, None, [pk, "cmask"], [("Kb", g)])
            else:
                P.add("act", "activation", reads=[pk, "cmask"], writes=[("Kb", g)], out=Kblk[:, q, :, g8 * 16:(g8 + 1) * 16],
                      in_=pS[:, 512:768].rearrange("p (t c) -> p t c", t=16), func=AF.Copy, scale=cmask[:, 4 + g8:5 + g8])
        P.add("dve", "memset", reads=[("Kb", g_) for g_ in range(32)], writes=["Kblk"], ap=dummy[:, 1:2], constant=0.0)
        for q in range(4):
            P.add("dve", "scalar_tensor_tensor", reads=["Kblk", "identf", "bada"], writes=["Kblk"], out=Kblk[:, q, 0, :],
                  in0=identf[:, :], scalar=dq[:, q:q + 1], in1=Kblk[:, q, 0, :], op0=ALU.mult, op1=ALU.add)
        cp("dve", Xb[:, :, 0:128], XS[:, 0:128, :].rearrange("p k g -> p g k"), ["XS"], ["Xb"])
        cp("dve", Xb[:, :, 128:130], xs0[:, :, :].rearrange("p s g -> p g s"), ["xs0"], ["Xb"])
        blocks = [(b0 * 512, 512) for b0 in range(4)] + [(2048, 32)]
        for bi, (t0, ntk) in enumerate(blocks):
            nk = ntk // 16
            k0 = t0 // 16
            for q in range(4):
                bank = pA if q % 2 == 0 else pS
                bkey = "pA" if q % 2 == 0 else "pS"
                for tau in range(16):
                    ov = bank[:, 0:ntk].rearrange("p (k i) -> p k i", i=16)[:, :, tau:16]
                    uv = uT[:, q, t0:t0 + ntk].rearrange("p (k i) -> p k i", i=16)[:, :, 0:16 - tau]
                    P.add("pe", "matmul", reads=["uT", "Kblk"], writes=[bkey], out=ov, lhsT=Kblk[:, q, tau, :], rhs=uv,
                          start=(tau == 0), stop=False)
                for s_ in range(4):
                    for i in range(16):
                        for e in range(2):
                            g = q * 8 + s_ * 2 + e
                            last = (s_ == 3 and i == 15 and e == 1)
                            P.add("pe", "matmul", reads=["Xb", "CAz"], writes=[bkey],
                                  out=bank[32 * s_:32 * s_ + 32, 0:ntk].rearrange("p (k i) -> p k i", i=16)[:, :, i],
                                  lhsT=CAz[:, g, i + 1, :, :].rearrange("p r c -> p (r c)"), rhs=Xb[:, g, k0:k0 + nk], start=False, stop=last,
                                  tile_position=(0, 32 * s_), skip_group_check=True)
                P.add("act", "activation", reads=[bkey], writes=["yT"], out=yT[:, q, t0:t0 + ntk], in_=bank[:, 0:ntk],
                      func=AF.Copy)

        P.barrier(dummy[:, 0:1])
        wload2(w_glu_s, wglu_d[:, :], ["wglu_d"], "w_glu")
        wload2(w_gsms[:, :, :], wgsms_d[:, :], ["wgsms_d", "wgsms_d2"], "w_gs")
        wload2(w_os_s, wos_d[:, :], ["wos_d"], "w_os")
        dma("sp", w_out_s[:, :, :], wout_d.rearrange("(kt p) n -> p kt n", p=128), ["wout_d"], ["wada0", "wada1"])
        tilesC = own_tiles + samp_tiles
        NC2 = len(tilesC)
        xin3 = [xin[0], xin[1], xin2c]
        sgb2 = [sgb, av(32256, 512).rearrange("p (k n) -> p k n", k=4)]
        sgs2 = [sgs, av(32768, 512).rearrange("p (k n) -> p k n", k=4)]
        sms2 = [sms, av(33280, 1024).rearrange("p (k n) -> p k n", k=8)]
        sgt2 = [sgt, av(34304, 512).rearrange("p (k n) -> p k n", k=4)]
        mgt2 = [mgt, av(34816, 1024).rearrange("p (k n) -> p k n", k=8)]

        def c2_rms_front(n):
            (row0, nt, mods, col0) = tilesC[n]
            i3, i2 = n % 3, n % 2
            xt, xk = xin3[i3], "xinC%d" % i3
            xn, xnk = xn2[i2], "xn%d" % i2
            sst, ssk = ss2[i2], "ss%d" % i2
            dma("sp", xt[0:nt, :], x_all[row0:row0 + nt, :], [], [xk])
            P.add("act", "activation", reads=[xk], writes=[xnk, ssk], out=xn[0:nt, :], in_=xt[0:nt, :],
                  func=AF.Square, accum_out=sst[0:nt, 0:1])
            P.add("act", "activation", reads=[ssk, "epsc"], writes=[ssk], out=sst[0:nt, 2:3], in_=sst[0:nt, 0:1], func=AF.Ln,
                  scale=1.0 / D, bias=epsc[0:nt, 0:1])
            P.add("act", "activation", reads=[ssk], writes=[ssk], out=sst[0:nt, 3:4], in_=sst[0:nt, 2:3], func=AF.Exp, scale=-0.5)
            P.add("act", "activation", reads=[xk, ssk], writes=[xnk], out=xn[0:nt, :], in_=xt[0:nt, :],
                  func=AF.Copy, scale=sst[0:nt, 3:4])

        def c2_front_a(ti):
            (row0, nt, mods, col0) = tilesC[ti]
            c2_rms_front(ti)
            rms_back(ti % 2, nt, mods)

        def c2_front(ti):
            (row0, nt, mods, col0) = tilesC[ti]
            p = ti % 2
            CUR["i"] = p
            mti = (ti, 0) if ti < 16 else (16, (ti - 16) * 16)
            dma("sp", mgt2[p][:, :, 0:nt], mg_d[mti[0], :, :].rearrange("p (k t) -> p k t", k=8)[:, :, mti[1]:mti[1] + nt],
                [("mg_d", mti[0])], ["mgt%d" % p])
            for ct in range(8):
                for k in range(4):
                    P.add("pe", "matmul", reads=["yT", "w_glu"], writes=["pS"], out=pS[:, ct * 128:ct * 128 + nt],
                          lhsT=w_glu_s[:, k, ct * 128:(ct + 1) * 128], rhs=yT[:, k, col0:col0 + nt], start=(k == 0), stop=(k == 3))
            pSv = pS[:, :].rearrange("p (c t) -> p c t", t=128)
            for ct in range(4):
                P.add("act", "activation", reads=["pS", "bada"], writes=["sgb%d" % p], out=sgb2[p][:, ct, 0:nt], in_=pSv[:, 4 + ct, 0:nt],
                      func=AF.Sigmoid, bias=bglu[:, 4 + ct:5 + ct])
            for ct in range(12):
                for k in range(8):
                    P.add("pe", "matmul", reads=[HK(), "w_gs", "w_ms"], writes=["pA"], out=pA[:, ct * 128:ct * 128 + nt],
                          lhsT=w_gsms[:, k, ct * 128:(ct + 1) * 128], rhs=HT()[:, k, 0:nt], start=(k == 0), stop=(k == 7))
            pAv = pA[:, :].rearrange("p (c t) -> p c t", t=128)
            P.add("act", "activation", reads=["pA"], writes=["sgs%d" % p], out=sgs2[p][:, :, 0:nt], in_=pAv[:, 0:4, 0:nt], func=AF.Silu)
            P.add("act", "activation", reads=["pA"], writes=["sms%d" % p], out=sms2[p][:, :, 0:nt], in_=pAv[:, 4:12, 0:nt], func=AF.Sigmoid)
            for ct in range(4):
                P.add("dve", "scalar_tensor_tensor", reads=["pS", "sgb%d" % p, "bada"], writes=["sgt%d" % p], out=sgt2[p][:, ct, 0:nt],
                      in0=pSv[:, ct, 0:nt], scalar=bglu[:, ct:ct + 1], in1=sgb2[p][:, ct, 0:nt], op0=ALU.add, op1=ALU.mult)
            tt("dve", sgt2[p][:, :, 0:nt], sgt2[p][:, :, 0:nt], sgs2[p][:, :, 0:nt], ALU.mult, ["sgt%d" % p, "sgs%d" % p], ["sgt%d" % p])

        def c2_back(ti):
            (row0, nt, mods, col0) = tilesC[ti]
            p = ti % 2
            xt, xk = xin3[ti % 3], "xinC%d" % (ti % 3)
            for ct in range(8):
                for k in range(4):
                    P.add("pe", "matmul", reads=["sgt%d" % p, "w_os"], writes=["pO"], out=pO[:, ct // 4, (ct % 4) * 128:(ct % 4) * 128 + nt],
                          lhsT=w_os_s[:, k, ct * 128:(ct + 1) * 128], rhs=sgt2[p][:, k, 0:nt], start=(k == 0), stop=(k == 3))
            tt("dve", bst[:, :, 0:nt], pO[:, :, :].rearrange("p a (c t) -> p (a c) t", t=128)[:, :, 0:nt], sms2[p][:, :, 0:nt],
               ALU.mult, ["pO", "sms%d" % p], ["bst"])
            tt("dve", bst[:, :, 0:nt], bst[:, :, 0:nt], mgt2[p][:, :, 0:nt], ALU.add, ["bst", "mgt%d" % p], ["bst"])

        def c2_back_b(ti):
            (row0, nt, mods, col0) = tilesC[ti]
            p = ti % 2
            xt, xk = xin3[ti % 3], "xinC%d" % (ti % 3)
            for half in range(2):
                for k in range(8):
                    P.add("pe", "matmul", reads=["bst", "wada0", "wada1"], writes=["pO"],
                          out=pO[0:nt, half, :], lhsT=bst[:, k, 0:nt],
                          rhs=w_out_s[:, k, half * 512:(half + 1) * 512], start=(k == 0), stop=(k == 7))
            tt("dve", qk_sb[0:nt, :], pO[0:nt, :, :].rearrange("p a c -> p (a c)"), gate_bc[0:nt, 0 if ti < 16 else ti - 15, :], ALU.mult,
               ["pO", "gate_bc"], ["qk_sb"])
            tt("dve", sq[0:nt, :], qk_sb[0:nt, :], xt[0:nt, :], ALU.add, ["qk_sb", xk], ["sq"])
            dma("sp", y_out[col0:col0 + nt, :], sq[0:nt, :], ["sq"], [])

        c2_front_a(0)
        c2_front(0)
        for ti in range(NC2):
            if ti + 1 < NC2:
                c2_front_a(ti + 1)
            c2_back(ti)
            if ti + 1 < NC2:
                c2_front(ti + 1)
            c2_back_b(ti)

        P.emit(nc)
    return nc


_NC = None


def kernel(x_prompt, x_sample, c_prompt, c_sample, cache_k, cache_v, state_ssm_re, state_ssm_im,
           norm_g, w_ada, b_ada, w_in, q_norm_g, k_norm_g, rel_bias, lambda_re, lambda_im, log_dt,
           b_re, b_im, c_re, c_im, d_skip, w_glu, b_glu, w_oa, w_os, w_out):
    global _NC
    f = lambda a: np.ascontiguousarray(np.asarray(a, dtype=np.float32))
    x_prompt, x_sample = f(x_prompt), f(x_sample)
    if _NC is None:
        _NC = build()
    nc = _NC
    ident = np.eye(128, dtype=np.float32)
    sel = np.zeros((3, 256), np.float32)
    sel[0, 0:128] = 1.0
    sel[1, 128:144] = 1.0
    sel[2, 144:160] = 1.0
    pidx = np.arange(128)
    cmask = np.zeros((128, 16), np.float32)
    cmask[:, 0] = np.where(pidx < 64, -1.0, 1.0)
    cmask[:, 1] = -cmask[:, 0]
    for par in range(2):
        cmask[:, 2 + par] = ((pidx // 16) % 2 == par)
    for g8 in range(8):
        cmask[:, 4 + g8] = (pidx // 16 == g8)
    in_maps = []
    for c in range(8):
        b, j = c // 4, c % 4
        t0 = j * NOWN
        halo = np.zeros((NHALO, D), np.float32) if j == 0 else x_prompt[b, t0 - NHALO:t0]
        xs = x_sample[2 * c:2 * c + 2].reshape(NSAMP, D)
        x_all = np.concatenate([halo, x_prompt[b, t0:t0 + NOWN], xs], axis=0)
        c3 = np.stack([f(c_prompt)[b], f(c_sample)[2 * c], f(c_sample)[2 * c + 1]])
        hbias = np.full((128, 1), -30000.0 if j == 0 else 0.0, np.float32)
        x_prev = np.zeros((3 * NOWN, D), np.float32)
        pmask = np.ones((128, 4), np.float32)
        for i in range(3):
            js = j - 3 + i
            if js >= 0:
                x_prev[i * NOWN:(i + 1) * NOWN] = x_prompt[b, js * NOWN:(js + 1) * NOWN]
            else:
                pmask[:, i] = 0.0
        selm = np.zeros((128, 24), np.float32)
        for jr in range(j):
            selm[:, (b * 4 + jr) * 3 + (j - 1 - jr)] = 1.0
        in_maps.append({
            "x_all": np.ascontiguousarray(x_all), "c3": np.ascontiguousarray(c3),
            "cache_k": f(cache_k)[0, 2 * c:2 * c + 2].reshape(2, 512, 512),
            "cache_v": f(cache_v)[0, 2 * c:2 * c + 2].reshape(2, 512, 512),
            "hbias": hbias, "sel": sel, "ident": ident, "cmask": cmask, "selm": selm, "x_prev": x_prev, "pmask": pmask,
            "st_re": f(state_ssm_re)[0, 2 * c:2 * c + 2].reshape(64, 64),
            "st_im": f(state_ssm_im)[0, 2 * c:2 * c + 2].reshape(64, 64),
            "lambda_re": f(lambda_re)[0], "lambda_im": f(lambda_im)[0], "log_dt": f(log_dt)[0],
            "b_re": f(b_re)[0], "b_im": f(b_im)[0], "c_re": f(c_re)[0].reshape(512, 64), "c_im": f(c_im)[0].reshape(512, 64),
            "d_skip": f(d_skip)[0], "w_glu": f(w_glu)[0], "b_glu": f(b_glu)[0], "w_os": f(w_os)[0],
            "norm_g": f(norm_g)[0], "w_ada": f(w_ada)[0], "b_ada": f(b_ada)[0], "w_in": f(w_in)[0],
            "q_norm_g": f(q_norm_g)[0], "k_norm_g": f(k_norm_g)[0], "rel_bias": f(rel_bias)[0],
            "w_oa": f(w_oa)[0], "w_out": f(w_out)[0],
        })
    res = run_bass_kernel_spmd(nc, in_maps, core_ids=list(range(8)))
    R = res.results
    y_prompt = np.zeros((2, 8192, D), np.float32)
    y_sample = np.zeros((16, 16, D), np.float32)
    nk_p = np.zeros((1, 2, 512, 8, 64), np.float32)
    nv_p = np.zeros((1, 2, 512, 8, 64), np.float32)
    sr_p = np.zeros((1, 2, 32, 64), np.float32)
    si_p = np.zeros((1, 2, 32, 64), np.float32)
    nk_s = np.zeros((1, 16, 16, 8, 64), np.float32)
    nv_s = np.zeros((1, 16, 16, 8, 64), np.float32)
    sr_s = np.zeros((1, 16, 32, 64), np.float32)
    si_s = np.zeros((1, 16, 32, 64), np.float32)
    for c in range(8):
        b, j = c // 4, c % 4
        r = R[c]
        y_prompt[b, j * NOWN:(j + 1) * NOWN] = r["y_out"][0:NOWN]
        y_sample[2 * c:2 * c + 2] = r["y_out"][NOWN:].reshape(2, 16, D)
        if j == 3:
            nk_p[0, b] = r["k_last"].reshape(512, 8, 64)
            nv_p[0, b] = r["v_last"].reshape(512, 8, 64)
        nk_s[0, 2 * c:2 * c + 2] = r["k_samp"].reshape(2, 16, 8, 64)
        if j == 3:
            sr_p[0, b] = r["ssm_p"][:, 0:64]
            si_p[0, b] = r["ssm_p"][:, 64:128]
        sr_s[0, 2 * c:2 * c + 2] = r["ssm_s"][:, 0:64].reshape(2, 32, 64)
        si_s[0, 2 * c:2 * c + 2] = r["ssm_s"][:, 64:128].reshape(2, 32, 64)
        nv_s[0, 2 * c:2 * c + 2] = r["v_samp"].reshape(2, 16, 8, 64)
    return (y_prompt, y_sample, nk_p, nv_p, sr_p, si_p, nk_s, nv_s, sr_s, si_s)
```

```python
import numpy as np
import os
STAGE = int(os.environ.get('KSTAGE', '99'))
SUB = int(os.environ.get('KSUB', '99'))
KRMS = int(os.environ.get('KRMS', '99'))
import ml_dtypes
from contextlib import ExitStack
import concourse.bass as bass
import concourse.mybir as mybir
from concourse.bass_utils import run_bass_kernel_spmd

F32 = mybir.dt.float32
BF16 = mybir.dt.bfloat16
ALU = mybir.AluOpType
AF = mybir.ActivationFunctionType
AX = mybir.AxisListType

D = 1024
NOWN = 2048
NHALO = 512
NSAMP = 32
EPS = 1e-6


PSUM_KEYS = ("pA", "pT", "pS", "pO", "pA0", "pA1", "pA2", "pS0", "pS1", "pO0", "pO1")


class Prog:
    ENG = ("pe", "act", "dve", "pool", "sp")

    def __init__(self):
        self.ops = []
        self.last_w = {}
        self.readers = {}

    def add(self, eng, name, reads=(), writes=(), dma=False, nophase=False, **kw):
        op = dict(eng=eng, name=name, kw=kw, dma=dma, deps=set(), idx=len(self.ops), sig=dma)
        reads = list(reads)
        if not nophase:
            reads.append("PHASE")
        writes = list(writes) + [r for r in reads if r in PSUM_KEYS]
        reads = [r for r in reads if r not in PSUM_KEYS]
        for r in reads:
            lw = self.last_w.get(r)
            if lw is not None:
                op["deps"].add(lw)
        for w in writes:
            lw = self.last_w.get(w)
            if lw is not None:
                op["deps"].add(lw)
            for rd in self.readers.get(w, ()):
                op["deps"].add(rd)
        for r in reads:
            self.readers.setdefault(r, []).append(op["idx"])
        for w in writes:
            self.last_w[w] = op["idx"]
            self.readers[w] = []
        op["deps"].discard(op["idx"])
        self.ops.append(op)
        return op

    def barrier(self, arena_ap):
        self.add("dve", "memset", reads=[], writes=["PHASE"], nophase=True, ap=arena_ap, constant=0.0)

    def emit(self, nc, ndma_sems=16):
        ops = self.ops
        for op in ops:
            nd = set()
            for d in op["deps"]:
                p = ops[d]
                if (not p["dma"]) and p["eng"] == op["eng"] and p["eng"] == "pe" and not op["dma"]:
                    continue
                nd.add(d)
            op["deps"] = nd
            for d in nd:
                ops[d]["sig"] = True
        cnt = {e: 0 for e in self.ENG}
        dcnt = {e: 0 for e in self.ENG}
        for op in ops:
            e = op["eng"]
            if op["dma"]:
                i = dcnt[e]
                dcnt[e] += 1
                op["sem"] = ("d", e, i % ndma_sems)
                op["val"] = 16 * (i // ndma_sems + 1)
            elif op["sig"]:
                cnt[e] += 1
                op["sem"] = ("c", e)
                op["val"] = cnt[e]
        with ExitStack() as st:
            sems = {}
            for e in self.ENG:
                sems[("c", e)] = st.enter_context(nc.semaphore("c_" + e))
                if dcnt[e]:
                    for i in range(ndma_sems):
                        sems[("d", e, i)] = st.enter_context(nc.semaphore("d_%s_%d" % (e, i)))
            block = st.enter_context(nc.Block())
            byeng = {e: [o for o in ops if o["eng"] == e] for e in self.ENG}

            def run(engname, eng):
                known = {}
                for op in byeng[engname]:
                    waits = {}
                    for d in op["deps"]:
                        p = ops[d]
                        waits[p["sem"]] = max(waits.get(p["sem"], 0), p["val"])
                    if op["dma"] and op["val"] > 16:
                        waits[op["sem"]] = max(waits.get(op["sem"], 0), op["val"] - 16)
                    for s, v in waits.items():
                        if known.get(s, 0) >= v:
                            continue
                        eng.wait_ge(sems[s], v)
                        known[s] = v
                    ins = getattr(eng, op["name"])(**op["kw"])
                    if op["sig"]:
                        ins.then_inc(sems[op["sem"]], 16 if op["dma"] else 1)
                last = {}
                for op in byeng[engname]:
                    if op["dma"]:
                        last[op["sem"]] = op["val"]
                for s, v in last.items():
                    if known.get(s, 0) < v:
                        eng.wait_ge(sems[s], v)

            block.tensor(lambda eng: run("pe", eng))
            block.scalar(lambda eng: run("act", eng))
            block.vector(lambda eng: run("dve", eng))
            block.gpsimd(lambda eng: run("pool", eng))
            block.sync(lambda eng: run("sp", eng))


def build():
    nc = bass.Bass("TRN2", target_bir_lowering=False)

    def din(name, shape, dt=F32):
        return nc.dram_tensor(name, list(shape), dt, kind="ExternalInput").ap()

    def dout(name, shape, dt=F32):
        return nc.dram_tensor(name, list(shape), dt, kind="ExternalOutput").ap()

    NTOK = NHALO + NOWN + NSAMP
    NT = NOWN + NSAMP
    NCH = NT // 16
    x_all = din("x_all", [NTOK, D])
    x_prev = din("x_prev", [3 * NOWN, D])
    pmask_d = din("pmask", [128, 4])
    c3 = din("c3", [3, D])
    cache_k = din("cache_k", [2, 512, 512])
    cache_v = din("cache_v", [2, 512, 512])
    st_re = din("st_re", [64, 64])
    st_im = din("st_im", [64, 64])
    hbias = din("hbias", [128, 1])
    sel = din("sel", [3, 256])
    ident_d = din("ident", [128, 128])
    cmask_d = din("cmask", [128, 16])
    selm_d = din("selm", [128, 24])
    norm_g = din("norm_g", [D])
    w_ada = din("w_ada", [D, 3 * D])
    b_ada = din("b_ada", [3 * D])
    w_in = din("w_in", [D, 5120])
    q_norm_g = din("q_norm_g", [64])
    k_norm_g = din("k_norm_g", [64])
    rel_bias = din("rel_bias", [8, 257])
    lam_re = din("lambda_re", [32, 64])
    lam_im = din("lambda_im", [32, 64])
    log_dt = din("log_dt", [32])
    b_re = din("b_re", [32, 64, 16])
    b_im = din("b_im", [32, 64, 16])
    c_re = din("c_re", [512, 64])
    c_im = din("c_im", [512, 64])
    d_skip = din("d_skip", [512])
    w_glu = din("w_glu", [512, 1024])
    b_glu = din("b_glu", [1024])
    w_oa = din("w_oa", [512, D])
    w_os = din("w_os", [512, D])
    w_out = din("w_out", [D, D])

    y_out = dout("y_out", [NT, D])
    k_last = dout("k_last", [512, 512])
    v_last = dout("v_last", [512, 512])
    k_samp = dout("k_samp", [NSAMP, 512])
    v_samp = dout("v_samp", [NSAMP, 512])
    ssm_p = dout("ssm_p", [32, 128])
    ssm_s = dout("ssm_s", [64, 128])

    e_d = nc.dram_tensor("e_d", [8, 768], F32, kind="Internal").ap()
    m_d = nc.dram_tensor("m_d", [8, 130 * 768], F32, kind="Internal").ap()
    mg_d = nc.dram_tensor("mg_d", [17, 128, 1024], BF16, kind="Internal").ap()
    sloc_d = nc.dram_tensor("sloc_d", [128, 32], F32, kind="Internal").ap()
    wq_d = nc.dram_tensor("wq_d", [D, 1536], BF16, kind="Internal").ap()
    wgm_d = nc.dram_tensor("wgm_d", [D, 1536], BF16, kind="Internal").ap()
    woa_d = nc.dram_tensor("woa_d", [512, D], BF16, kind="Internal").ap()
    wglu_d = nc.dram_tensor("wglu_d", [512, D], BF16, kind="Internal").ap()
    wgsms_d = nc.dram_tensor("wgsms_d", [D, 1536], BF16, kind="Internal").ap()
    wos_d = nc.dram_tensor("wos_d", [512, D], BF16, kind="Internal").ap()
    wout_d = nc.dram_tensor("wout_d", [D, D], BF16, kind="Internal").ap()
    sall_d = nc.dram_tensor("sall_d", [1024, 32], F32, kind="Internal").ap()

    P = Prog()
    st = ExitStack()
    with st:
        def sb(name, shape, dt=F32):
            return st.enter_context(nc.sbuf_tensor("s_" + name, list(shape), dt))

        def ps(name, shape, dt=F32):
            return st.enter_context(nc.psum_tensor("p_" + name, list(shape), dt))

        dummy = sb("dummy", [128, 2])
        epsc = sb("epsc", [128, 1])
        ident = sb("ident", [128, 128], BF16)
        identf = sb("identf", [128, 128])
        selT = sb("selT", [3, 256])
        hb = sb("hb", [128, 1])
        cmask = sb("cmask", [128, 16])
        selm = sb("selm", [128, 24])
        pmask = sb("pmask", [128, 4])
        stg = sb("stg", [68, 128])
        smalls = sb("smalls", [128, 68])
        cT = smalls[:, 0:24].rearrange("p (b k) -> p k b", b=3)
        bada = smalls[:, 24:48]
        ng = smalls[:, 48:56]
        dq = smalls[:, 56:60]
        bglu = smalls[:, 60:68]
        scT = sb("scT", [128, 8, 3], BF16)
        modT = sb("modT", [128, 24, 3])
        Amod = sb("Amod", [128, 8, 3])
        gate_bc = sb("gate_bc", [128, 3, 1024], BF16)
        gqk = sb("gqk", [128, 1024], BF16)
        gqg = sb("gqg", [128, 512], BF16)
        xin = [sb("xin%d" % i, [128, 1024]) for i in range(2)]
        xin2c = sb("xin2c", [128, 1024])
        ss2 = [sb("ss%d" % i, [128, 4]) for i in range(2)]
        xn2 = [sb("xn%d" % i, [128, 1024], BF16) for i in range(2)]
        hT2 = [sb("hT%d" % i, [128, 8, 128], BF16) for i in range(2)]
        e_s = xin[1][0:8, 0:768]
        qk_sb = sb("qk_sb", [128, 1024])
        sq = sb("sq", [128, 1024])
        ss16 = sb("ss16", [128, 16])
        qkb = sb("qkb", [128, 1024], BF16)
        kout = sb("kout", [128, 512])
        vout = sb("vout", [128, 512])
        qT = sb("qT", [128, 4, 128], BF16)
        stmp2 = [sb("stmp%d" % i, [128, 5, 128]) for i in range(2)]
        PT2 = [sb("PT%d" % i, [128, 5, 128], BF16) for i in range(2)]
        stmp, PT = stmp2[0], PT2[0]
        rden = sb("rden", [128, 8])
        AO = sb("AO", [128, 8, 64], BF16)
        sga = sb("sga", [128, 4, 128], BF16)
        sma = sb("sma", [128, 8, 128], BF16)
        AOgT = sb("AOgT", [128, 4, 128], BF16)
        mgt = sb("mgt", [128, 8, 128], BF16)
        uT = sb("uT", [128, 4, NT], BF16)
        XS = sb("XS", [128, 129, 32])
        prm = sb("prm", [128, 36, 32])
        PW1 = sb("PW1", [128, 32, 17])
        PW2 = sb("PW2", [128, 32, 17])
        Bstk = sb("Bstk", [128, 32, 16])
        Bsw = sb("Bsw", [128, 32, 16])
        Bbb = sb("Bbb", [128, 32, 16], BF16)
        Cstk = sb("Cstk", [128, 512])
        Csw = sb("Csw", [128, 512])
        xs0 = sb("xs0", [128, 2, 32])
        WSs = sb("WSs", [128, 2, 32])
        Gall = sb("Gall", [128, 8, 32])
        ki32 = sb("ki32", [128, 32], mybir.dt.int32)

        ARN = 45184
        arena = sb("arena", [128, ARN], BF16)

        def av(off, n):
            return arena[:, off:off + n]

        wada = [av(28672, 4096).rearrange("p (k n) -> p k n", k=8), av(32768, 4096).rearrange("p (k n) -> p k n", k=8)]
        w_out_s = av(24064, 8192).rearrange("p (k n) -> p k n", k=8)
        scr = av(28672, 16512).bitcast(F32)
        w_qkv = av(0, 12288).rearrange("p (k n) -> p k n", k=8)
        w_gm = av(12288, 12288).rearrange("p (k n) -> p k n", k=8)
        w_oa_s = av(24576, 4096).rearrange("p (k n) -> p k n", k=4)
        KT = av(28672, 3072).rearrange("p (k n) -> p k n", k=4)
        V = av(31744, 3120).rearrange("p (s h e) -> p s h e", s=6, h=8)
        BM = av(34880, 10240).bitcast(F32).rearrange("p (h t q) -> p h t q", h=8, t=5)
        w_u = av(0, 4096).rearrange("p (k n) -> p k n", k=8)
        W1z = av(4096, 16384).rearrange("p (q r j m) -> p q r j m", q=4, r=2, j=16)
        Pst = av(20480, 8192).rearrange("p (q t g c) -> p q t g c", q=4, t=16, g=8)
        CAz = av(0, 17408).rearrange("p (g t r c) -> p g t r c", g=32, t=17, r=2)
        Kblk = av(17408, 8192).rearrange("p (q t m) -> p q t m", q=4, t=16)
        Xb = av(25600, 4160).rearrange("p (g k) -> p g k", g=32)
        Cin = av(29760, 1024).bitcast(F32).rearrange("p (q m) -> p q m", q=4)
        yT = av(36608, 8320).rearrange("p (q n) -> p q n", q=4)
        w_glu_s = av(0, 4096).rearrange("p (k n) -> p k n", k=4)
        w_gsms = av(4096, 12288).rearrange("p (k n) -> p k n", k=8)
        w_os_s = av(16384, 4096).rearrange("p (k n) -> p k n", k=4)
        sgb = av(20480, 512).rearrange("p (k n) -> p k n", k=4)
        sgs = av(20992, 512).rearrange("p (k n) -> p k n", k=4)
        sms = av(21504, 1024).rearrange("p (k n) -> p k n", k=8)
        sgt = av(22528, 512).rearrange("p (k n) -> p k n", k=4)
        bst = av(23040, 1024).rearrange("p (k n) -> p k n", k=8)

        pA = ps("pA", [128, 1536])
        pT = ps("pT", [128, 8, 128], BF16)
        pS = ps("pS", [128, 1024])
        pO = ps("pO", [128, 2, 512])

        def bfv(ap_):
            return ap_.bitcast(BF16).rearrange("p (k t) -> p k t", t=128)

        TA = [pT[:, 0:4, :], bfv(pO[:, 0, :])]
        TAk = ["pT", "pO0"]
        TB = [bfv(pA[:, 1024:1536]), bfv(pO[:, 1, :])]
        TBk = ["pA2", "pO1"]
        UB = [pA[:, 0:512], pA[:, 512:1024]]
        UBk = ["pA0", "pA1"]

        def dma(eng, out, in_, reads, writes, **kw):
            P.add(eng, "dma_start", reads=reads, writes=writes, dma=True, out=out, in_=in_, **kw)

        def wload(dst, src, key):
            dma("pool", dst, src.rearrange("(kt p) n -> p kt n", p=128), [], [key])

        def tt(eng, out, in0, in1, op, reads, writes, **kw):
            P.add(eng, "tensor_tensor", reads=reads, writes=writes, out=out, in0=in0, in1=in1, op=op, **kw)

        def ts(eng, out, in0, s1, s2, op0, op1, reads, writes, **kw):
            if s2 is None:
                P.add(eng, "tensor_scalar", reads=reads, writes=writes, out=out, in0=in0, scalar1=s1, scalar2=None,
                      op0=op0, **kw)
            else:
                P.add(eng, "tensor_scalar", reads=reads, writes=writes, out=out, in0=in0, scalar1=s1, scalar2=s2,
                      op0=op0, op1=op1, **kw)

        def cp(eng, out, in_, reads, writes, **kw):
            P.add(eng, "tensor_copy", reads=reads, writes=writes, out=out, in_=in_, **kw)

        P.add("pool", "memset", reads=[], writes=["epsc"], nophase=True, ap=epsc[:, :], constant=EPS)
        dma("pool", ident[:, :], ident_d[:, :], [], ["ident"])
        dma("sp", identf[:, :], ident_d[:, :], [], ["identf"])
        dma("sp", selT[:, :], sel[:, :], [], ["selT"])
        dma("sp", hb[:, :], hbias[:, :], [], ["hb"])
        dma("sp", cmask[:, :], cmask_d[:, :], [], ["cmask"])
        dma("sp", selm[:, :], selm_d[:, :], [], ["selm"])
        dma("sp", pmask[:, :], pmask_d[:, :], [], ["pmask"])
        dma("sp", stg[0:24, :], c3.rearrange("b (kt p) -> (b kt) p", p=128), [], ["stg"])
        dma("sp", stg[24:48, :], b_ada.rearrange("(ct p) -> ct p", p=128), [], ["stg1"])
        dma("sp", stg[48:56, :], norm_g.rearrange("(kt p) -> kt p", p=128), [], ["stg2"])
        dma("sp", stg[56:60, :], d_skip.rearrange("(kt p) -> kt p", p=128), [], ["stg3"])
        dma("sp", stg[60:68, :], b_glu.rearrange("(kt p) -> kt p", p=128), [], ["stg4"])
        P.add("pe", "transpose", reads=["stg", "stg1", "stg2", "stg3", "stg4", "identf"], writes=["pS"], out=pS[:, 0:68],
              in_=stg[0:68, :], identity=identf[0:68, 0:68])
        cp("dve", smalls[:, :], pS[:, 0:68], ["pS"], ["cT", "bada", "ng"])
        bgate = sq[0:3, :]
        gate_tok = qk_sb[0:3, :]
        dma("pool", gqk[:, 0:512], bass.AP(tensor=q_norm_g.tensor, offset=0, ap=[[0, 128], [0, 8], [1, 64]]),
            [], ["gqk_q"])
        dma("pool", gqk[:, 512:1024], bass.AP(tensor=k_norm_g.tensor, offset=0, ap=[[0, 128], [0, 8], [1, 64]]),
            [], ["gqk_k"])
        tt("dve", gqg[:, :], gqk[:, 0:512], gqk[:, 512:1024], ALU.mult, ["gqk_q", "gqk_k"], ["gqg"])
        dma("sp", e_s[:, 129:385], rel_bias[:, 1:257], [], ["xin1"])
        cp("dve", e_s[:, 0:129], e_s[:, 384:385].to_broadcast([8, 129]), ["xin1"], ["xin1"])
        cp("dve", e_s[:, 385:768], e_s[:, 384:385].to_broadcast([8, 383]), ["xin1"], ["xin1"])
        dma("sp", e_d[:, :], e_s[:, :], ["xin1"], ["e_d"])
        dma("sp", m_d.rearrange("h (r e) -> h r e", e=768),
            bass.AP(tensor=e_d.tensor, offset=0, ap=[[768, 8], [0, 130], [1, 768]]), ["e_d"], ["m_d"])

        TI = {"n": 0, "pend": None}

        def rms_front(row0, nt, src=None):
            src = x_all if src is None else src
            i = TI["n"] % 2
            TI["n"] += 1
            xt, xk = xin[i], "xin%d" % i
            xn, xnk = xn2[i], "xn%d" % i
            sst, ssk = ss2[i], "ss%d" % i
            dma("sp", xt[0:nt, :], src[row0:row0 + nt, :], [], [xk])
            P.add("act", "activation", reads=[xk], writes=[xnk, ssk], out=xn[0:nt, :], in_=xt[0:nt, :],
                  func=AF.Square, accum_out=sst[0:nt, 0:1])
            P.add("act", "activation", reads=[ssk, "epsc"], writes=[ssk], out=sst[0:nt, 2:3], in_=sst[0:nt, 0:1], func=AF.Ln,
                  scale=1.0 / D, bias=epsc[0:nt, 0:1])
            P.add("act", "activation", reads=[ssk], writes=[ssk], out=sst[0:nt, 3:4], in_=sst[0:nt, 2:3], func=AF.Exp, scale=-0.5)
            P.add("act", "activation", reads=[xk, ssk], writes=[xnk], out=xn[0:nt, :], in_=xt[0:nt, :],
                  func=AF.Copy, scale=sst[0:nt, 3:4])
            return i

        def rms_back(i, nt, mods):
            CUR["i"] = i
            xn, xnk = xn2[i], "xn%d" % i
            hT, hk = HT(), HK()
            for k in range(8):
                P.add("pe", "transpose", reads=[xnk, "ident"], writes=["pT"], out=pT[:, k, 0:nt],
                      in_=xn[0:nt, k * 128:(k + 1) * 128], identity=ident[0:nt, 0:nt])
            for k in range(8):
                for (c0, c1, b) in mods:
                    if k < 4:
                        ts("dve", hT[:, k, c0:c1], pT[:, k, c0:c1], Amod[:, k, b:b + 1], modT[:, k, b:b + 1],
                           ALU.mult, ALU.add, ["pT", "Amod", "modT"], [hk])
                    else:
                        P.add("act", "activation", reads=["pT", "Amod", "modT"], writes=[hk], out=hT[:, k, c0:c1],
                              in_=pT[:, k, c0:c1], func=AF.Identity, scale=Amod[:, k, b:b + 1], bias=modT[:, k, b:b + 1])

        def run_tiles(tiles, body, src=None):
            nxt = rms_front(tiles[0][0], tiles[0][1], src)
            for n, tl in enumerate(tiles):
                cur = nxt
                rms_back(cur, tl[1], tl[2])
                if n + 1 < len(tiles):
                    nxt = rms_front(tiles[n + 1][0], tiles[n + 1][1], src)
                CUR["i"] = cur
                body(n, tl, cur)

        CUR = {"i": 0}

        def HT():
            return hT2[CUR["i"]]

        def HK():
            return "hT%d" % CUR["i"]

        def qkv_mm(nt, with_q):
            for cb in range(0 if with_q else 1, 3):
                for k in range(8):
                    P.add("pe", "matmul", reads=[HK(), "w_qkv"], writes=["pA"], out=pA[0:nt, cb * 512:(cb + 1) * 512],
                          lhsT=HT()[:, k, 0:nt], rhs=w_qkv[:, k, cb * 512:(cb + 1) * 512], start=(k == 0), stop=(k == 7))

        def qkv_tile(nt, slot, with_q, kdst, vdst, out_rows, gm=False, fold=True, pre_mm=False):
            c_lo = 0 if with_q else 512
            if not pre_mm:
                qkv_mm(nt, with_q)
            if gm:
                for ct in range(12):
                    for k in range(8):
                        if ct < 4:
                            o_, ok_ = pO[:, 0, ct * 128:ct * 128 + nt], "pO"
                        else:
                            o_, ok_ = pS[:, (ct - 4) * 128:(ct - 4) * 128 + nt], "pS"
                        P.add("pe", "matmul", reads=[HK(), "w_gm", "w_gm2"], writes=[ok_], out=o_,
                              lhsT=w_gm[:, k, ct * 128:(ct + 1) * 128], rhs=HT()[:, k, 0:nt], start=(k == 0), stop=(k == 7))
            nh = 16 if with_q else 8
            h0 = 0 if with_q else 8
            P.add("act", "activation", reads=["pA"], writes=["sq"], out=sq[0:nt, c_lo:1024], in_=pA[0:nt, c_lo:1024],
                  func=AF.Square)
            P.add("dve", "tensor_reduce", reads=["sq"], writes=["ss16"], out=ss16[0:nt, h0:16],
                  in_=sq[0:nt, c_lo:1024].rearrange("p (h d) -> p h d", d=64), axis=AX.X, op=ALU.add)
            P.add("act", "activation", reads=["ss16", "epsc"], writes=["ss16"], out=ss16[0:nt, h0:16], in_=ss16[0:nt, h0:16],
                  func=AF.Ln, scale=1.0 / 64, bias=epsc[0:nt, 0:1])
            P.add("act", "activation", reads=["ss16"], writes=["ss16"], out=ss16[0:nt, h0:16], in_=ss16[0:nt, h0:16],
                  func=AF.Exp, scale=-0.5)
            rk = ss16[0:nt, 8:16].rearrange("p (h o) -> p h o", o=1).to_broadcast([nt, 8, 64])
            rq = ss16[0:nt, 0:8].rearrange("p (h o) -> p h o", o=1).to_broadcast([nt, 8, 64])
            pAk = pA[0:nt, 512:1024].rearrange("p (h d) -> p h d", d=64)
            pAq = pA[0:nt, 0:512].rearrange("p (h d) -> p h d", d=64)
            if fold:
                tt("dve", qkb[0:nt, 512:1024].rearrange("p (h d) -> p h d", d=64), pAk, rk, ALU.mult, ["pA", "ss16"], ["qkb"])
            else:
                tt("dve", sq[0:nt, 512:1024].rearrange("p (h d) -> p h d", d=64), pAk, rk, ALU.mult, ["pA", "ss16"], ["sq"])
                tt("dve", qkb[0:nt, 512:1024], sq[0:nt, 512:1024], gqk[0:nt, 512:1024], ALU.mult, ["sq", "gqk_k"], ["qkb"])
            if with_q:
                tt("dve", sq[0:nt, 0:512].rearrange("p (h d) -> p h d", d=64), pAq, rq, ALU.mult, ["pA", "ss16"], ["sq"])
                tt("dve", qkb[0:nt, 0:512], sq[0:nt, 0:512], gqg[0:nt, :] if fold else gqk[0:nt, 0:512], ALU.mult,
                   ["sq", "gqk_q", "gqg"], ["qkb"])
            P.add("act", "activation", reads=["pA", "Vones"], writes=[("V", slot)], out=V[0:nt, slot, :, 0:64],
                  in_=pA[0:nt, 1024:1536].rearrange("p (h d) -> p h d", d=64), func=AF.Copy)
            if gm:
                P.add("act", "activation", reads=["pS"], writes=["sma"], out=sma[:, :, 0:nt],
                      in_=pS[:, :].rearrange("p (c t) -> p c t", t=128)[:, :, 0:nt], func=AF.Sigmoid)
                P.add("act", "activation", reads=["pO"], writes=["sga"], out=sga[:, :, 0:nt],
                      in_=pO[:, 0, :].rearrange("p (c t) -> p c t", t=128)[:, :, 0:nt], func=AF.Silu)
            if kdst is not None:
                tt("dve", kout[0:nt, :].rearrange("p (h d) -> p h d", d=64), pAk, rk, ALU.mult, ["pA", "ss16"], ["kout"])
                tt("dve", kout[0:nt, :], kout[0:nt, :], gqk[0:nt, 512:1024], ALU.mult, ["kout", "gqk_k"], ["kout"])
                dma("sp", kdst[out_rows:out_rows + nt, :], kout[0:nt, :], ["kout"], [])
                cp("dve", vout[0:nt, :], pA[0:nt, 1024:1536], ["pA"], ["vout"])
                dma("sp", vdst[out_rows:out_rows + nt, :], vout[0:nt, :], ["vout"], [])
            for i in range(0 if with_q else 4, 8):
                P.add("pe", "transpose", reads=["qkb", "ident"], writes=["pT"], out=pT[:, i, 0:nt],
                      in_=qkb[0:nt, i * 128:(i + 1) * 128], identity=ident[0:nt, 0:nt])
            if with_q:
                cp("dve", qT[:, :, 0:nt], pT[:, 0:4, 0:nt], ["pT"], ["qT"])
            P.add("act", "activation", reads=["pT"], writes=[("KT", slot)], out=KT[:, :, slot * 128:slot * 128 + nt],
                  in_=pT[:, 4:8, 0:nt], func=AF.Copy)

        def attention(nq, ktiles, mtile):
            nkt = len(ktiles)
            nk_last = ktiles[4][1]
            for h in range(8):
                hp, h2 = h // 2, h % 2
                pr = slice(64 * h2, 64 * h2 + 64)
                for j, (slot, nk, tp_, halo) in enumerate(ktiles):
                    P.add("pe", "matmul", reads=[("KT", slot), "qT"], writes=["pS"], out=pS[0:nk, (4 - j) * 128:(4 - j) * 128 + nq],
                          lhsT=KT[pr, hp, slot * 128:slot * 128 + nk], rhs=qT[pr, hp, 0:nq], start=True, stop=True)
                pSv = pS[:, 0:640].rearrange("p (t q) -> p t q", q=128)
                P.add("dve", "scalar_tensor_tensor", reads=["pS", "BM"], writes=["stmp"], out=stmp[:, 1:5, 0:nq],
                      in0=pSv[:, 1:5, 0:nq], scalar=0.125, in1=BM[:, h, 1:5, 0:nq], op0=ALU.mult, op1=ALU.add)
                P.add("dve", "scalar_tensor_tensor", reads=["pS", "BM"], writes=["stmp"], out=stmp[0:nk_last, 0, 0:nq],
                      in0=pSv[0:nk_last, 0, 0:nq], scalar=0.125, in1=BM[0:nk_last, h, 0, 0:nq], op0=ALU.mult, op1=ALU.add)
                P.add("act", "activation", reads=["stmp"], writes=["PT"], out=PT[:, 1:5, 0:nq], in_=stmp[:, 1:5, 0:nq], func=AF.Exp)
                P.add("act", "activation", reads=["stmp"], writes=["PT"], out=PT[0:nk_last, 0, 0:nq], in_=stmp[0:nk_last, 0, 0:nq],
                      func=AF.Exp)
                for j, (slot, nk, tp_, halo) in enumerate(ktiles):
                    P.add("pe", "matmul", reads=["PT", ("V", slot)], writes=["pO"],
                          out=pO[0:nq, h // 4, (h % 4) * 65:(h % 4) * 65 + 65],
                          lhsT=PT[0:nk, 4 - j, 0:nq], rhs=V[0:nk, slot, h, :], start=(j == 0), stop=(j == nkt - 1))
            attention_tail(nq, mtile)

        def attention_tail(nq, mtile, gm_done=False):
            pOv = pO[0:nq, :, 0:260].rearrange("p a (h e) -> p a h e", e=65)
            P.add("dve", "reciprocal", reads=["pO"], writes=["rden"],
                  out=rden[0:nq, :].rearrange("p (a h o) -> p a h o", a=2, o=1), in_=pOv[:, :, :, 64:65])
            tt("dve", AO[0:nq, :, :].rearrange("p (a h) d -> p a h d", a=2), pOv[:, :, :, 0:64],
               rden[0:nq, :].rearrange("p (a h o) -> p a h o", a=2, o=1).to_broadcast([nq, 2, 4, 64]), ALU.mult,
               ["pO", "rden"], ["AO"])
            AOf = AO[:, :, :].rearrange("p h d -> p (h d)")
            for i in range(4):
                P.add("pe", "transpose", reads=["AO", "ident"], writes=["pT"], out=pT[:, i, 0:nq],
                      in_=AOf[0:nq, i * 128:(i + 1) * 128], identity=ident[0:nq, 0:nq])
            if not gm_done:
                for ct in range(12):
                    for k in range(8):
                        P.add("pe", "matmul", reads=[HK(), "w_gm", "w_gm2"], writes=["pA"], out=pA[:, ct * 128:ct * 128 + nq],
                              lhsT=w_gm[:, k, ct * 128:(ct + 1) * 128], rhs=HT()[:, k, 0:nq], start=(k == 0), stop=(k == 7))
                pAv = pA[:, :].rearrange("p (c t) -> p c t", t=128)
                P.add("act", "activation", reads=["pA"], writes=["sga"], out=sga[:, :, 0:nq], in_=pAv[:, 0:4, 0:nq], func=AF.Silu)
                P.add("act", "activation", reads=["pA"], writes=["sma"], out=sma[:, :, 0:nq], in_=pAv[:, 4:12, 0:nq], func=AF.Sigmoid)
            tt("dve", AOgT[:, :, 0:nq], pT[:, 0:4, 0:nq], sga[:, :, 0:nq], ALU.mult, ["pT", "sga"], ["AOgT"])
            for ct in range(8):
                for k in range(4):
                    P.add("pe", "matmul", reads=["AOgT", "w_oa"], writes=["pS"], out=pS[:, ct * 128:ct * 128 + nq],
                          lhsT=w_oa_s[:, k, ct * 128:(ct + 1) * 128], rhs=AOgT[:, k, 0:nq], start=(k == 0), stop=(k == 3))
            tt("dve", mgt[:, :, 0:nq], pS[:, :].rearrange("p (c t) -> p c t", t=128)[:, :, 0:nq], sma[:, :, 0:nq], ALU.mult,
               ["pS", "sma"], ["mgt"])
            dma("sp", mg_d[mtile[0], :, :].rearrange("p (k t) -> p k t", k=8)[:, :, mtile[1]:mtile[1] + nq],
                mgt[:, :, 0:nq], ["mgt"], [("mg_d", mtile[0])])

        def attention_fast(ktiles, mtile, nh, before_tail=None):
            nq = 128

            def qk(h):
                hp, h2 = h // 2, h % 2
                pr = slice(64 * h2, 64 * h2 + 64)
                Sb, skey = (pS, "pS") if h % 2 == 0 else (pA, "pA")
                for j, (slot, nk, tp_, halo) in enumerate(ktiles):
                    P.add("pe", "matmul", reads=[("KT", slot), "qT"], writes=[skey], out=Sb[:, (4 - j) * 128:(5 - j) * 128],
                          lhsT=KT[pr, hp, slot * 128:slot * 128 + 128], rhs=qT[pr, hp, 0:nq], start=True, stop=True)

            def softmax(h):
                Sb, skey = (pS, "pS") if h % 2 == 0 else (pA, "pA")
                st_, stk = stmp2[h % 2], "stmp%d" % (h % 2)
                PTb, ptk = PT2[h % 2], "PT%d" % (h % 2)
                P.add("dve", "scalar_tensor_tensor", reads=[skey, "BM"], writes=[stk], out=st_[:, :, :].rearrange("p t q -> p (t q)"),
                      in0=Sb[:, 0:640], scalar=0.125, in1=BM[:, h, :, :].rearrange("p t q -> p (t q)"),
                      op0=ALU.mult, op1=ALU.add)
                c_h = (5 - nh) * 128
                stf = st_[:, :, :].rearrange("p t q -> p (t q)")
                ptf = PTb[:, :, :].rearrange("p t q -> p (t q)")
                if nh < 5:
                    P.add("act", "activation", reads=[stk], writes=[ptk], out=ptf[:, 0:c_h], in_=stf[:, 0:c_h], func=AF.Exp)
                if nh > 0:
                    P.add("act", "activation", reads=[stk, "hb"], writes=[ptk], out=ptf[:, c_h:640], in_=stf[:, c_h:640],
                          func=AF.Exp, bias=hb[:, 0:1])

            def pv(h):
                PTb, ptk = PT2[h % 2], "PT%d" % (h % 2)
                for j, (slot, nk, tp_, halo) in enumerate(ktiles):
                    P.add("pe", "matmul", reads=[ptk, ("V", slot)], writes=["pO"],
                          out=pO[0:nq, h // 4, (h % 4) * 65:(h % 4) * 65 + 65],
                          lhsT=PTb[:, 4 - j, :], rhs=V[:, slot, h, :], start=(j == 0), stop=(j == 4))

            qk(0)
            for h in range(8):
                softmax(h)
                if h + 1 < 8:
                    qk(h + 1)
                pv(h)
            if before_tail is not None:
                before_tail()
            attention_tail(nq, mtile, True)


        own_tiles = [(NHALO + i * 128, 128, [(0, 128, 0)], i * 128) for i in range(16)]
        samp_tiles = [(NHALO + NOWN + s * 16, 16, [(0, 16, 1 + s)], NOWN + s * 16) for s in range(2)]

        wload(w_u, w_in[:, 2048:2560], "w_u")
        PRM = {}

        def pt(name):
            if name not in PRM:
                PRM[name] = len(PRM)
                assert len(PRM) <= 34
            return prm[:, PRM[name], :]

        k_ = ["prm"]
        dma("sp", xin[0][0:32, 0:64], lam_re[:, :], [], ["xin0"])
        dma("sp", xin[0][0:32, 64:128], lam_im[:, :], [], ["xin0b"])
        P.add("pe", "transpose", reads=["xin0", "xin0b", "identf"], writes=["pS"], out=pS[:, 0:32], in_=xin[0][0:32, 0:128],
              identity=identf[0:32, 0:32])
        cp("dve", pt("lam"), pS[:, 0:32], ["pS"], k_)
        cp("dve", pt("lr")[0:64, :], pt("lam")[0:64, :], k_, k_)
        cp("dve", pt("lr")[64:128, :], pt("lam")[0:64, :], k_, k_)
        cp("dve", pt("li")[0:64, :], pt("lam")[64:128, :], k_, k_)
        cp("dve", pt("li")[64:128, :], pt("lam")[64:128, :], k_, k_)
        dma("sp", pt("dt"), bass.AP(tensor=log_dt.tensor, offset=0, ap=[[0, 128], [1, 32]]), k_, k_)
        P.add("act", "activation", reads=k_, writes=k_, out=pt("dt"), in_=pt("dt"), func=AF.Exp)
        tt("dve", pt("x"), pt("lr"), pt("dt"), ALU.mult, k_, k_)
        ts("dve", pt("m"), pt("x"), 0.25, 1.0, ALU.mult, ALU.add, k_, k_)
        for cc in (1.0 / 3, 0.5, 1.0):
            P.add("dve", "scalar_tensor_tensor", reads=k_, writes=k_, out=pt("m"), in0=pt("x"), scalar=cc, in1=pt("m"),
                  op0=ALU.mult, op1=ALU.mult)
            ts("dve", pt("m"), pt("m"), 1.0, None, ALU.add, None, k_, k_)
        tt("dve", pt("ang"), pt("li"), pt("dt"), ALU.mult, k_, k_)
        ts("dve", pt("t0"), pt("ang"), 1.0 / (2 * np.pi), None, ALU.mult, None, k_, k_)
        cp("dve", ki32[:, :], pt("t0"), k_, ["ki32"])
        cp("dve", pt("t0"), ki32[:, :], ["ki32"], k_)
        P.add("dve", "scalar_tensor_tensor", reads=k_, writes=k_, out=pt("ang"), in0=pt("t0"), scalar=-2 * np.pi,
              in1=pt("ang"), op0=ALU.mult, op1=ALU.add)
        ts("dve", pt("psi"), pt("ang"), 1.0 / 32, None, ALU.mult, None, k_, k_)
        tt("dve", pt("p2"), pt("psi"), pt("psi"), ALU.mult, k_, k_)
        ts("dve", pt("s"), pt("p2"), -1.0 / 42, 1.0, ALU.mult, ALU.add, k_, k_)
        for cc in (-1.0 / 20, -1.0 / 6):
            P.add("dve", "scalar_tensor_tensor", reads=k_, writes=k_, out=pt("s"), in0=pt("p2"), scalar=cc, in1=pt("s"),
                  op0=ALU.mult, op1=ALU.mult)
            ts("dve", pt("s"), pt("s"), 1.0, None, ALU.add, None, k_, k_)
        tt("dve", pt("s"), pt("s"), pt("psi"), ALU.mult, k_, k_)
        ts("dve", pt("c"), pt("p2"), -1.0 / 56, 1.0, ALU.mult, ALU.add, k_, k_)
        for cc in (-1.0 / 30, -1.0 / 12, -0.5):
            P.add("dve", "scalar_tensor_tensor", reads=k_, writes=k_, out=pt("c"), in0=pt("p2"), scalar=cc, in1=pt("c"),
                  op0=ALU.mult, op1=ALU.mult)
            ts("dve", pt("c"), pt("c"), 1.0, None, ALU.add, None, k_, k_)

        def csquare(r, i, t1, t2):
            tt("dve", t1, r, r, ALU.mult, k_, k_)
            tt("dve", t2, i, i, ALU.mult, k_, k_)
            P.add("dve", "scalar_tensor_tensor", reads=k_, writes=k_, out=i, in0=r, scalar=2.0, in1=i,
                  op0=ALU.mult, op1=ALU.mult)
            tt("dve", r, t1, t2, ALU.subtract, k_, k_)

        for _ in range(5):
            csquare(pt("c"), pt("s"), pt("t0"), pt("t1"))
        tt("dve", pt("ar"), pt("m"), pt("c"), ALU.mult, k_, k_)
        tt("dve", pt("ai"), pt("m"), pt("s"), ALU.mult, k_, k_)
        tt("dve", pt("den"), pt("lr"), pt("lr"), ALU.mult, k_, k_)
        tt("dve", pt("t0"), pt("li"), pt("li"), ALU.mult, k_, k_)
        tt("dve", pt("den"), pt("den"), pt("t0"), ALU.add, k_, k_)
        P.add("dve", "reciprocal", reads=k_, writes=k_, out=pt("den"), in_=pt("den"))
        ts("dve", pt("nr"), pt("ar"), -1.0, None, ALU.add, None, k_, k_)
        tt("dve", pt("t0"), pt("nr"), pt("lr"), ALU.mult, k_, k_)
        tt("dve", pt("t1"), pt("ai"), pt("li"), ALU.mult, k_, k_)
        tt("dve", pt("cor"), pt("t0"), pt("t1"), ALU.add, k_, k_)
        tt("dve", pt("cor"), pt("cor"), pt("den"), ALU.mult, k_, k_)
        tt("dve", pt("t0"), pt("ai"), pt("lr"), ALU.mult, k_, k_)
        tt("dve", pt("t1"), pt("nr"), pt("li"), ALU.mult, k_, k_)
        tt("dve", pt("coi"), pt("t0"), pt("t1"), ALU.subtract, k_, k_)
        tt("dve", pt("coi"), pt("coi"), pt("den"), ALU.mult, k_, k_)
        ts("dve", pt("cois"), pt("coi"), cmask[:, 0:1], None, ALU.mult, None, k_ + ["cmask"], k_)
        dma("sp", Bstk[0:64, :, :], b_re.rearrange("g n c -> n g c"), [], ["Bstk"])
        dma("sp", Bstk[64:128, :, :], b_im.rearrange("g n c -> n g c"), [], ["Bstk2"])
        cp("dve", Bsw[0:64, :, :], Bstk[64:128, :, :], ["Bstk", "Bstk2"], ["Bsw"])
        cp("dve", Bsw[64:128, :, :], Bstk[0:64, :, :], ["Bstk", "Bstk2"], ["Bsw"])

        def bc_c(name):
            return pt(name).rearrange("p (g o) -> p g o", o=1).to_broadcast([128, 32, 16])

        tt("dve", Bstk[:, :, :], Bstk[:, :, :], bc_c("cor"), ALU.mult, ["Bstk", "Bstk2", "Bsw"] + k_, ["Bstk"])
        tt("dve", Bsw[:, :, :], Bsw[:, :, :], bc_c("cois"), ALU.mult, ["Bsw"] + k_, ["Bsw"])
        tt("dve", Bstk[:, :, :], Bstk[:, :, :], Bsw[:, :, :], ALU.add, ["Bstk", "Bsw"], ["Bstk"])
        cp("dve", Bsw[0:64, :, :], Bstk[64:128, :, :], ["Bstk"], ["Bsw"])
        cp("dve", Bsw[64:128, :, :], Bstk[0:64, :, :], ["Bstk"], ["Bsw"])
        cp("dve", Bbb[:, :, :], Bstk[:, :, :], ["Bstk"], ["Bbb"])
        kp = ["PW"]
        P.add("dve", "memset", reads=[], writes=kp, ap=PW1[:, :, 0:1], constant=1.0)
        P.add("dve", "memset", reads=[], writes=kp, ap=PW2[:, :, 0:1], constant=0.0)
        cp("dve", PW1[:, :, 1:2], pt("ar").rearrange("p (g o) -> p g o", o=1), k_ + kp, kp)
        cp("dve", PW2[:, :, 1:2], pt("ai").rearrange("p (g o) -> p g o", o=1), k_ + kp, kp)
        tw = qk_sb[:, 0:512].rearrange("p (a g t) -> p a g t", a=2, g=32)
        for kk in (1, 2, 4, 8):
            src_r, src_i = PW1[:, :, 1:kk + 1], PW2[:, :, 1:kk + 1]
            kr = PW1[:, :, kk:kk + 1].to_broadcast([128, 32, kk])
            ki = PW2[:, :, kk:kk + 1].to_broadcast([128, 32, kk])
            t_a, t_b = tw[:, 0, :, 0:kk], tw[:, 1, :, 0:kk]
            tt("dve", t_a, src_r, kr, ALU.mult, kp, ["qk_sb"])
            tt("dve", t_b, src_i, ki, ALU.mult, kp, ["qk_sb"])
            tt("dve", PW1[:, :, kk + 1:2 * kk + 1], t_a, t_b, ALU.subtract, ["qk_sb"], kp)
            tt("dve", t_a, src_r, ki, ALU.mult, kp, ["qk_sb"])
            tt("dve", t_b, src_i, kr, ALU.mult, kp, ["qk_sb"])
            tt("dve", PW2[:, :, kk + 1:2 * kk + 1], t_a, t_b, ALU.add, ["qk_sb"], kp)
        PW2s = sq[:, 0:544].rearrange("p (g t) -> p g t", g=32)
        ts("dve", PW2s, PW2[:, :, :], cmask[:, 0:1], None, ALU.mult, None, kp + ["cmask"], ["sq"])
        cp("dve", pt("Y1"), PW1[:, :, 16], kp, k_)
        cp("dve", pt("Y2"), PW2s[:, :, 16], ["sq"], k_)
        for gq in range(8):
            gs_ = slice(gq * 4, gq * 4 + 4)
            t_a = qk_sb[:, 0:1024].rearrange("p (g t c) -> p g t c", g=4, t=16)
            t_b = xin[1][:, 0:1024].rearrange("p (g t c) -> p g t c", g=4, t=16)
            tt("dve", t_a, Bstk[:, gs_, :].rearrange("p g (o c) -> p g o c", o=1).to_broadcast([128, 4, 16, 16]),
               PW1[:, gs_, 0:16].rearrange("p g (t o) -> p g t o", o=1).to_broadcast([128, 4, 16, 16]), ALU.mult,
               ["Bstk"] + kp, ["qk_sb"])
            tt("pool", t_b, Bsw[:, gs_, :].rearrange("p g (o c) -> p g o c", o=1).to_broadcast([128, 4, 16, 16]),
               PW2s[:, gs_, 0:16].rearrange("p g (t o) -> p g t o", o=1).to_broadcast([128, 4, 16, 16]), ALU.mult,
               ["Bsw", "sq"], ["xin1"])
            tt("dve", Pst[:, gq // 2, :, (gq % 2) * 4:(gq % 2) * 4 + 4, :].rearrange("p t g c -> p g t c"), t_a, t_b, ALU.add, ["qk_sb", "xin1"], ["Pst"])
        for q in range(4):
            for half in range(2):
                for tl in range(8):
                    tau = half * 8 + tl
                    P.add("pe", "transpose", reads=["Pst", "ident"], writes=["pT"], out=pT[:, 7 - tl, :],
                          in_=Pst[:, q, tau, :, :].rearrange("p g c -> p (g c)"), identity=ident[:, :])
                j0 = 8 - 8 * half
                ts("dve", W1z[:, q, 0, j0:j0 + 8, :], pT[:, :, :], cmask[:, 2:3], None, ALU.mult, None, ["pT", "cmask"], ["W1z"])
                P.add("act", "activation", reads=["pT", "cmask"], writes=["W1z"], out=W1z[:, q, 1, j0:j0 + 8, :],
                      in_=pT[:, :, :], func=AF.Copy, scale=cmask[:, 3:4])
        def cmul_acc(dst, x, y1, y2, t_sw, t_a, keys_r, keys_w):
            cp("dve", t_sw[0:64], x[64:128], keys_r, ["scan_t"], nophase=True)
            cp("dve", t_sw[64:128], x[0:64], keys_r, ["scan_t"], nophase=True)
            tt("dve", t_a, x, y1, ALU.mult, keys_r + ["prm"], ["scan_t2"], nophase=True)
            tt("dve", t_sw, t_sw, y2, ALU.mult, ["scan_t", "prm"], ["scan_t"], nophase=True)
            tt("dve", dst, dst, t_a, ALU.add, ["scan_t2"] + keys_w, keys_w, nophase=True)
            tt("dve", dst, dst, t_sw, ALU.add, ["scan_t"] + keys_w, keys_w, nophase=True)

        sc_a = prm[:, 34, :]
        sc_b = prm[:, 35, :]
        dma("sp", bgate, bass.AP(tensor=b_ada.tensor, offset=2048, ap=[[0, 3], [1, 1024]]), [], ["sq"])
        P.add("act", "activation", reads=["cT"], writes=["scT"], out=scT[:, :, :], in_=cT, func=AF.Silu)
        for ch in range(6):
            wb = wada[ch % 2]
            wk = "wada%d" % (ch % 2)
            wload(wb, w_ada[:, ch * 512:(ch + 1) * 512], wk)
            for c4 in range(4):
                for k in range(8):
                    P.add("pe", "matmul", reads=[wk, "scT"], writes=["pA"], out=pA[:, c4 * 4:c4 * 4 + 3],
                          lhsT=wb[:, k, c4 * 128:(c4 + 1) * 128], rhs=scT[:, k, :], start=(k == 0), stop=(k == 7))
            tt("dve", modT[:, ch * 4:(ch + 1) * 4, :], pA[:, 0:16].rearrange("p (c b) -> p c b", b=4)[:, :, 0:3],
               bada[:, ch * 4:(ch + 1) * 4].rearrange("p (c o) -> p c o", o=1).to_broadcast([128, 4, 3]), ALU.add,
               ["pA", "bada"], ["modT"])
            if ch >= 4:
                for k in range(8):
                    P.add("pe", "matmul", reads=[wk, "scT"], writes=["pS"], out=pS[0:3, 0:512],
                          lhsT=scT[:, k, :], rhs=wb[:, k, :], start=(k == 0), stop=(k == 7))
                tt("dve", gate_tok[:, (ch - 4) * 512:(ch - 3) * 512], pS[0:3, 0:512],
                   bgate[:, (ch - 4) * 512:(ch - 3) * 512], ALU.add, ["pS", "sq"], ["qk_sb"])
        ts("dve", Amod[:, :, :], modT[:, 8:16, :], 1.0, None, ALU.add, None, ["modT"], ["Amod"])
        tt("dve", Amod[:, :, :], Amod[:, :, :], ng.rearrange("p (k o) -> p k o", o=1).to_broadcast([128, 8, 3]), ALU.mult,
           ["Amod", "ng"], ["Amod"])
        for which, (c0, nt) in enumerate(((0, 128), (128, 16), (144, 16))):
            for half in range(2):
                P.add("pe", "matmul", reads=["selT", "qk_sb"], writes=["pS"], out=pS[0:nt, half * 512:(half + 1) * 512],
                      lhsT=selT[:, c0:c0 + nt], rhs=gate_tok[:, half * 512:(half + 1) * 512], start=True, stop=True)
            cp("dve", gate_bc[0:nt, which, :], pS[0:nt, :], ["pS"], ["gate_bc"])

        P.barrier(dummy[:, 0:1])

        def wstage(dst, src, key):
            dma("pool", dst, src, [], [key], nophase=True)

        wstage(wq_d[:, :], w_in[:, 0:1536], "wq_d")
        wstage(wgm_d[:, 0:512], w_in[:, 1536:2048], "wgm_d")
        wstage(wgm_d[:, 512:1536], w_in[:, 3072:4096], "wgm_d2")
        wstage(woa_d[:, :], w_oa[:, :], "woa_d")
        wstage(wglu_d[:, :], w_glu[:, :], "wglu_d")
        wstage(wgsms_d[:, 0:512], w_in[:, 2560:3072], "wgsms_d")
        wstage(wgsms_d[:, 512:1536], w_in[:, 4096:5120], "wgsms_d2")
        wstage(wos_d[:, :], w_os[:, :], "wos_d")
        wstage(wout_d[:, :], w_out[:, :], "wout_d")

        def wload2(dst, src, rkeys, key):
            dma("sp", dst, src.rearrange("(kt p) n -> p kt n", p=128), rkeys, [key])

        Q1 = scr[:, 0:544].rearrange("p (m g) -> p m g", g=32)
        Q2 = scr[:, 544:1088].rearrange("p (m g) -> p m g", g=32)
        Q2s = scr[:, 1088:1632].rearrange("p (m g) -> p m g", g=32)
        tsw8 = scr[:, 1632:1888].rearrange("p (b g) -> p b g", g=32)
        ta8 = scr[:, 1888:2144].rearrange("p (b g) -> p b g", g=32)
        tb1 = scr[:, 2144:4064].rearrange("p (b i g) -> p b i g", b=4, i=15)
        tb2 = scr[:, 4064:5984].rearrange("p (b i g) -> p b i g", b=4, i=15)
        qa = scr[:, 5984:6240].rearrange("p (m g) -> p m g", g=32)
        qb = scr[:, 6240:6496].rearrange("p (m g) -> p m g", g=32)
        kq = ["Q"]
        P.add("dve", "memset", reads=[], writes=kq, ap=Q1[:, 0, :], constant=1.0)
        P.add("dve", "memset", reads=[], writes=kq, ap=Q2[:, 0, :], constant=0.0)
        cp("dve", Q1[:, 1, :], PW1[:, :, 16], kp + kq, kq)
        cp("dve", Q2[:, 1, :], PW2[:, :, 16], kp + kq, kq)
        for kk in (1, 2, 4, 8):
            src_r, src_i = Q1[:, 1:kk + 1, :], Q2[:, 1:kk + 1, :]
            kr = Q1[:, kk:kk + 1, :].to_broadcast([128, kk, 32])
            ki = Q2[:, kk:kk + 1, :].to_broadcast([128, kk, 32])
            t_a, t_b = qa[:, 0:kk, :], qb[:, 0:kk, :]
            tt("dve", t_a, src_r, kr, ALU.mult, kq, ["qa"])
            tt("dve", t_b, src_i, ki, ALU.mult, kq, ["qb"])
            tt("dve", Q1[:, kk + 1:2 * kk + 1, :], t_a, t_b, ALU.subtract, ["qa", "qb"] + kq, kq)
            tt("dve", t_a, src_r, ki, ALU.mult, kq, ["qa"])
            tt("dve", t_b, src_i, kr, ALU.mult, kq, ["qb"])
            tt("dve", Q2[:, kk + 1:2 * kk + 1, :], t_a, t_b, ALU.add, ["qa", "qb"] + kq, kq)
        ts("dve", Q2s, Q2, cmask[:, 0:1], None, ALU.mult, None, kq + ["cmask"], kq)
        y1b8 = pt("Y1").rearrange("p (o g) -> p o g", o=1).to_broadcast([128, 8, 32])
        y2b8 = pt("Y2").rearrange("p (o g) -> p o g", o=1).to_broadcast([128, 8, 32])
        P.add("pool", "memset", reads=[], writes=["XS", "XSfree"], nophase=True, ap=XS[:, 0, :], constant=0.0)
        for seg in range(4):
            if seg < 3:
                seg_tiles = [(seg * NOWN + i * 128, 128, [(0, 128, 0)], i * 128) for i in range(16)]
            else:
                seg_tiles = own_tiles + samp_tiles
            srcA = x_prev if seg < 3 else x_all
            NA = len(seg_tiles)

            def a_front(n):
                (row0, nt, mods, col0) = seg_tiles[n]
                i = n % 2
                xt, xk = xin[i], "xin%d" % i
                xn, xnk = xn2[i], "xn%d" % i
                sst, ssk = ss2[i], "ss%d" % i
                dma("sp", xt[0:nt, :], srcA[row0:row0 + nt, :], [], [xk])
                P.add("act", "activation", reads=[xk], writes=[xnk, ssk], out=xn[0:nt, :], in_=xt[0:nt, :],
                      func=AF.Square, accum_out=sst[0:nt, 0:1])
                P.add("act", "activation", reads=[ssk, "epsc"], writes=[ssk], out=sst[0:nt, 2:3], in_=sst[0:nt, 0:1], func=AF.Ln,
                      scale=1.0 / D, bias=epsc[0:nt, 0:1])
                P.add("act", "activation", reads=[ssk], writes=[ssk], out=sst[0:nt, 3:4], in_=sst[0:nt, 2:3], func=AF.Exp, scale=-0.5)
                ts("dve", xn[0:nt, :], xt[0:nt, :], sst[0:nt, 3:4], None, ALU.mult, None, [xk, ssk], [xnk])

            def a_T(n):
                (row0, nt, mods, col0) = seg_tiles[n]
                i = n % 2
                xn, xnk = xn2[i], "xn%d" % i
                for k in range(8):
                    tb, tk = (TA[i], TAk[i]) if k < 4 else (TB[i], TBk[i])
                    P.add("pe", "transpose", reads=[xnk, "ident"], writes=[tk], out=tb[:, k % 4, 0:nt],
                          in_=xn[0:nt, k * 128:(k + 1) * 128], identity=ident[0:nt, 0:nt])

            def a_evac(n):
                (row0, nt, mods, col0) = seg_tiles[n]
                i = n % 2
                hT, hk = hT2[i], "hT%d" % i
                for k in range(8):
                    tb, tk = (TA[i], TAk[i]) if k < 4 else (TB[i], TBk[i])
                    for (c0, c1, bsel) in mods:
                        if k < 4:
                            ts("dve", hT[:, k, c0:c1], tb[:, k % 4, c0:c1], Amod[:, k, bsel:bsel + 1], modT[:, k, bsel:bsel + 1],
                               ALU.mult, ALU.add, [tk, "Amod", "modT"], [hk])
                        else:
                            P.add("act", "activation", reads=[tk, "Amod", "modT"], writes=[hk], out=hT[:, k, c0:c1],
                                  in_=tb[:, k % 4, c0:c1], func=AF.Identity, scale=Amod[:, k, bsel:bsel + 1],
                                  bias=modT[:, k, bsel:bsel + 1])

            def a_mm(n):
                (row0, nt, mods, col0) = seg_tiles[n]
                i = n % 2
                hT, hk = hT2[i], "hT%d" % i
                for q in range(4):
                    for k in range(8):
                        P.add("pe", "matmul", reads=[hk, "w_u"], writes=[UBk[i]], out=UB[i][:, q * 128:q * 128 + nt],
                              lhsT=w_u[:, k, q * 128:(q + 1) * 128], rhs=hT[:, k, 0:nt], start=(k == 0), stop=(k == 7))

            def a_uevac(n, seg=seg):
                (row0, nt, mods, col0) = seg_tiles[n]
                i = n % 2
                ts("dve", uT[:, :, col0:col0 + nt], UB[i].rearrange("p (q t) -> p q t", q=4)[:, :, 0:nt],
                   pmask[:, seg:seg + 1], None, ALU.mult, None, [UBk[i], "pmask"], ["uT"])

            a_front(0)
            if NA > 1:
                a_front(1)
            a_T(0)
            a_evac(0)
            for n in range(NA):
                if n + 1 < NA:
                    a_T(n + 1)
                a_mm(n)
                if n + 1 < NA:
                    a_evac(n + 1)
                if n + 2 < NA:
                    a_front(n + 2)
                a_uevac(n)
            nch = 128 if seg < 3 else NCH
            wbanks = [(pA, 0, "pA0"), (pA, 512, "pA1"), (pS, 0, "pS0"), (pS, 512, "pS1")]
            for q in range(4):
                for par in range(2):
                    for j in range(16):
                        for s_ in range(4):
                            bk, off, bkey = wbanks[s_]
                            P.add("pe", "matmul", reads=["uT", "W1z"], writes=[bkey], out=bk[:, off:off + nch],
                                  lhsT=W1z[32 * s_:32 * s_ + 32, q, par, j, :],
                                  rhs=uT[32 * s_:32 * s_ + 32, q, 0:nch * 16].rearrange("p (k i) -> p k i", i=16)[:, :, j],
                                  start=(j == 0), stop=(j == 15), tile_position=(32 * s_, 0))
                    for s_ in range(4):
                        bk, off, bkey = wbanks[s_]
                        g = q * 8 + s_ * 2 + par
                        if s_ % 2 == 0:
                            cp("dve", XS[:, 1:129, g], bk[:, off:off + 128], [bkey, "XSfree"], [("XSg", g)], nophase=True)
                        else:
                            P.add("act", "activation", reads=[bkey, "XSfree"], writes=[("XSg", g)], nophase=True,
                                  out=XS[:, 1:129, g], in_=bk[:, off:off + 128], func=AF.Copy)
                        if seg == 3:
                            cp("dve", WSs[:, :, g], bk[:, off + 128:off + 130], [bkey], ["WSs"])
            P.add("dve", "memset", reads=[("XSg", g_) for g_ in range(32)], writes=["XS"], nophase=True, ap=dummy[:, 1:2],
                  constant=0.0)
            XSb = XS[:, 1:129, :].rearrange("p (b i) g -> p b i g", i=16)
            for i in range(1, 16):
                cmul_acc(XSb[:, :, i, :], XSb[:, :, i - 1, :], y1b8, y2b8, tsw8, ta8, ["XS"], ["XS"])
            for bb in range(8):
                cmul_acc(XS[:, 16 * (bb + 1), :], XS[:, 16 * bb, :], Q1[:, 16, :], Q2s[:, 16, :], sc_a, sc_b, ["XS", "Q"], ["XS"])
            Cst = XS[:, 0:128, :].rearrange("p (b i) g -> p b i g", i=16)
            for b0 in ((0, 4) if seg == 3 else ()):
                Ch = Cst[:, b0:b0 + 4, 0, :]
                cp("dve", tsw8[0:64, 0:4, :], Ch[64:128], ["XS"], ["scan_t"], nophase=True)
                cp("dve", tsw8[64:128, 0:4, :], Ch[0:64], ["XS"], ["scan_t"], nophase=True)
                tt("dve", tb1, Ch.rearrange("p b (o g) -> p b o g", o=1).to_broadcast([128, 4, 15, 32]),
                   Q1[:, 1:16, :].rearrange("p (o m) g -> p o m g", o=1).to_broadcast([128, 4, 15, 32]), ALU.mult,
                   ["XS", "Q"], ["tb1"], nophase=True)
                tt("dve", tb2, tsw8[:, 0:4, :].rearrange("p b (o g) -> p b o g", o=1).to_broadcast([128, 4, 15, 32]),
                   Q2s[:, 1:16, :].rearrange("p (o m) g -> p o m g", o=1).to_broadcast([128, 4, 15, 32]), ALU.mult,
                   ["scan_t", "Q"], ["tb2"], nophase=True)
                tt("dve", XSb[:, b0:b0 + 4, 0:15, :], XSb[:, b0:b0 + 4, 0:15, :], tb1, ALU.add, ["XS", "tb1"], ["XS"], nophase=True)
                tt("dve", XSb[:, b0:b0 + 4, 0:15, :], XSb[:, b0:b0 + 4, 0:15, :], tb2, ALU.add, ["XS", "tb2"], ["XS"], nophase=True)
            if seg < 3:
                cp("dve", XS[:, 0, :], XS[:, 128, :], ["XS"], ["XS", "XSfree"], nophase=True)
        P.add("pe", "transpose", reads=["XS", "identf"], writes=["pS"], nophase=True, out=pS[0:32, 0:128], in_=XS[:, 128, :],
              identity=identf[:, :])
        cp("dve", kout[0:32, 0:128], pS[0:32, 0:128], ["pS"], ["kout"], nophase=True)
        dma("sp", ssm_p[:, :], kout[0:32, 0:128], ["kout"], [], nophase=True)
        dma("sp", xin[0][0:64, 0:64], st_re[:, :], [], ["xin0"])
        dma("sp", xin[0][0:64, 64:128], st_im[:, :], [], ["xin0b"])
        P.add("pe", "transpose", reads=["xin0", "xin0b", "identf"], writes=["pS"], out=pS[:, 0:64], in_=xin[0][0:64, 0:128],
              identity=identf[0:64, 0:64])
        cp("dve", xs0[:, :, :], pS[:, 0:64].rearrange("p (s g) -> p s g", s=2), ["pS"], ["xs0"])
        for s_ in range(2):
            cmul_acc(WSs[:, s_, :], xs0[:, s_, :], pt("Y1"), pt("Y2"), sc_a, sc_b, ["xs0", "WSs"], ["WSs"])
        P.add("pe", "transpose", reads=["WSs", "identf"], writes=["pS"], out=pS[0:64, 0:128],
              in_=WSs[:, :, :].rearrange("p s g -> p (s g)"), identity=identf[:, :])
        cp("dve", vout[0:64, 0:128], pS[0:64, 0:128], ["pS"], ["vout"])
        dma("sp", ssm_s[:, :], vout[0:64, 0:128], ["vout"], [])

        P.barrier(dummy[:, 0:1])
        for h in range(8):
            dma("sp", BM[:, h, :, :],
                bass.AP(tensor=m_d.tensor, offset=h * 130 * 768 + 256, ap=[[767, 128], [128, 5], [1, 128]]),
                ["m_d"], ["BM"])
        P.add("pool", "memset", reads=[], writes=["BM"], ap=BM[0:64, :, 4, 64:128], constant=-30000.0)
        P.add("pool", "memset", reads=[], writes=["BM"], ap=BM[64:128, :, 0, 0:64], constant=-30000.0)
        P.add("pool", "memset", reads=[], writes=["Vones"], ap=V[:, :, :, 64:65], constant=1.0)
        wload2(w_qkv[:, :, :], wq_d[:, :], ["wq_d"], "w_qkv")
        wload2(w_gm[:, :, :], wgm_d[:, :], ["wgm_d", "wgm_d2"], "w_gm")
        wload2(w_oa_s[:, :, :], woa_d[:, :], ["woa_d"], "w_oa")
        tilesB = [(t * 128, 128, [(0, 128, 0)], 0) for t in range(20)]
        stB = {"nxt": rms_front(tilesB[0][0], 128), "pre": False}

        def b_prepare(n):
            cur = stB["nxt"]
            rms_back(cur, 128, tilesB[n][2])
            if n + 1 < 20:
                stB["nxt"] = rms_front(tilesB[n + 1][0], 128)
            CUR["i"] = cur
            qkv_mm(128, n >= 4)
            stB["cur"] = cur

        b_prepare(0)
        for t in range(20):
            slot = t % 6
            CUR["i"] = stB["cur"]
            if t < 4:
                qkv_tile(128, slot, False, None, None, 0, pre_mm=True)
                if t + 1 < 20:
                    b_prepare(t + 1)
            else:
                own = t - 4
                last = own >= 12
                qkv_tile(128, slot, True, k_last if last else None, v_last if last else None, (own - 12) * 128, gm=True,
                         pre_mm=True)
                ktiles = [((t - 4 + j) % 6, 128, 4 - j, (t - 4 + j) < 4) for j in range(5)]
                mycur = stB["cur"]

                def hook(t=t):
                    if t + 1 < 20:
                        b_prepare(t + 1)

                attention_fast(ktiles, (own, 0), max(0, 4 - own), before_tail=hook)
        kc_slots = [(qkb[:, 0:512], "qkb"), (AOgT[:, :, :].rearrange("p k t -> p (k t)"), "AOgT"),
                    (mgt[:, 0:4, :].rearrange("p k t -> p (k t)"), "mgt"), (AO[:, :, :].rearrange("p h d -> p (h d)"), "AO")]
        for s_ in range(2):
            for tk in range(4):
                kc, kck = kc_slots[tk]
                dma("pool", kc, cache_k[s_, tk * 128:(tk + 1) * 128, :], [], [kck])
                dma("pool", V[:, tk, :, 0:64], cache_v[s_, tk * 128:(tk + 1) * 128, :].rearrange("p (h d) -> p h d", d=64),
                    ["Vones"], [("V", tk)])
            for tk in range(4):
                kc, kck = kc_slots[tk]
                for i in range(4):
                    P.add("pe", "transpose", reads=[kck, "ident"], writes=["pT"], out=pT[:, 4 + i, :],
                          in_=kc[:, i * 128:(i + 1) * 128], identity=ident[:, :])
                P.add("act", "activation", reads=["pT"], writes=[("KT", tk)], out=KT[:, :, tk * 128:(tk + 1) * 128],
                      in_=pT[:, 4:8, :], func=AF.Copy)
            (row0, nt, mods, col0) = samp_tiles[s_]
            rms_back(rms_front(row0, nt), nt, mods)
            qkv_tile(16, 4, True, k_samp, v_samp, s_ * 16, fold=False)
            ktiles = [(0, 128, 4, False), (1, 128, 3, False), (2, 128, 2, False), (3, 128, 1, False), (4, 16, 0, False)]
            attention(16, ktiles, (16, s_ * 16))

        P.barrier(dummy[:, 0:1])
        P.add("pool", "memset", reads=[], writes=["CAz"], ap=arena[:, 0:8704], constant=0.0)
        P.add("dve", "memset", reads=[], writes=["CAz2"], ap=arena[:, 8704:17408], constant=0.0)
        for q in range(4):
            dma("sp", Cin[:, q, 0:64], c_re[q * 128:(q + 1) * 128, :], [], ["Cin"])
            dma("sp", Cin[:, q, 64:128], c_im[q * 128:(q + 1) * 128, :], [], ["Cin2"])
        for q in range(4):
            P.add("pe", "transpose", reads=["Cin", "Cin2", "identf"], writes=["pS"], out=pS[:, q * 128:(q + 1) * 128],
                  in_=Cin[:, q, :], identity=identf[:, :])
        cp("dve", Cstk[:, :], pS[:, 0:512], ["pS"], ["Cstk"])
        cp("dve", Csw[0:64, :], Cstk[64:128, :], ["Cstk"], ["Csw"])
        cp("dve", Csw[64:128, :], Cstk[0:64, :], ["Cstk"], ["Csw"])
        AA1 = sq[:, 0:544].rearrange("p (g t) -> p g t", g=32)
        AA2 = xin[1][:, 0:544].rearrange("p (g t) -> p g t", g=32)
        ts("dve", AA1, PW1[:, :, :], cmask[:, 1:2], None, ALU.mult, None, ["PW", "cmask"], ["sq"])
        ts("dve", AA2, PW2[:, :, :], -1.0, None, ALU.mult, None, ["PW"], ["xin1"])
        Cst4 = Cstk[:, :].rearrange("p (s e c) -> p s e c", s=16, e=2)
        Csw4 = Csw[:, :].rearrange("p (s e c) -> p s e c", s=16, e=2)
        AA1v = AA1.rearrange("p (s e) t -> p s e t", e=2)
        AA2v = AA2.rearrange("p (s e) t -> p s e t", e=2)
        CAzv = CAz.rearrange("p (s e) t r c -> p s e t r c", e=2)
        for s0 in range(0, 16, 3):
            ns = min(3, 16 - s0)
            for e in range(2):
                t_a = qk_sb[:, 0:ns * 272].rearrange("p (s t c) -> p s t c", s=ns, t=17)
                t_b = xin[0][:, 0:ns * 272].rearrange("p (s t c) -> p s t c", s=ns, t=17)
                tt("dve", t_a, Cst4[:, s0:s0 + ns, e, :].rearrange("p s (o c) -> p s o c", o=1).to_broadcast([128, ns, 17, 16]),
                   AA1v[:, s0:s0 + ns, e, :].rearrange("p s (t o) -> p s t o", o=1).to_broadcast([128, ns, 17, 16]), ALU.mult,
                   ["Cstk", "sq"], ["qk_sb"])
                tt("pool", t_b, Csw4[:, s0:s0 + ns, e, :].rearrange("p s (o c) -> p s o c", o=1).to_broadcast([128, ns, 17, 16]),
                   AA2v[:, s0:s0 + ns, e, :].rearrange("p s (t o) -> p s t o", o=1).to_broadcast([128, ns, 17, 16]), ALU.mult,
                   ["Csw", "xin1"], ["xin0"])
                tt("dve", CAzv[:, s0:s0 + ns, e, :, e, :], t_a, t_b, ALU.add, ["qk_sb", "xin0", "CAz", "CAz2"], ["CAz"])
        for g in range(32):
            q, g8 = g // 8, g % 8
            pk = "pS%d" % (g % 2)
            P.add("pe", "matmul", reads=["Bbb", "CAz"], writes=[pk], out=pS[:, (g % 2) * 512:(g % 2) * 512 + 272],
                  lhsT=Bbb[:, :, :].rearrange("p g c -> p (g c)")[:, q * 128:(q + 1) * 128], rhs=CAz[:, g, :, g % 2, :], start=True, stop=True)
            if g % 2 == 0:
                ts("dve", Kblk[:, q, :, g8 * 16:(g8 + 1) * 16],
                   pS[:, 0:256].rearrange("p (t c) -> p t c", t=16), cmask[:, 4 + g8:5 + g8], None,
                   ALU.mult, None, [pk, "cmask"], [("Kb", g)])
            else:
                P.add("act", "activation", reads=[pk, "cmask"], writes=[("Kb", g)], out=Kblk[:, q, :, g8 * 16:(g8 + 1) * 16],
                      in_=pS[:, 512:768].rearrange("p (t c) -> p t c", t=16), func=AF.Copy, scale=cmask[:, 4 + g8:5 + g8])
        P.add("dve", "memset", reads=[("Kb", g_) for g_ in range(32)], writes=["Kblk"], ap=dummy[:, 1:2], constant=0.0)
        for q in range(4):
            P.add("dve", "scalar_tensor_tensor", reads=["Kblk", "identf", "bada"], writes=["Kblk"], out=Kblk[:, q, 0, :],
                  in0=identf[:, :], scalar=dq[:, q:q + 1], in1=Kblk[:, q, 0, :], op0=ALU.mult, op1=ALU.add)
        cp("dve", Xb[:, :, 0:128], XS[:, 0:128, :].rearrange("p k g -> p g k"), ["XS"], ["Xb"])
        cp("dve", Xb[:, :, 128:130], xs0[:, :, :].rearrange("p s g -> p g s"), ["xs0"], ["Xb"])
        blocks = [(b0 * 512, 512) for b0 in range(4)] + [(2048, 32)]
        for bi, (t0, ntk) in enumerate(blocks):
            nk = ntk // 16
            k0 = t0 // 16
            for q in range(4):
                bank = pA if q % 2 == 0 else pS
                bkey = "pA" if q % 2 == 0 else "pS"
                for tau in range(16):
                    ov = bank[:, 0:ntk].rearrange("p (k i) -> p k i", i=16)[:, :, tau:16]
                    uv = uT[:, q, t0:t0 + ntk].rearrange("p (k i) -> p k i", i=16)[:, :, 0:16 - tau]
                    P.add("pe", "matmul", reads=["uT", "Kblk"], writes=[bkey], out=ov, lhsT=Kblk[:, q, tau, :], rhs=uv,
                          start=(tau == 0), stop=False)
                for s_ in range(4):
                    for i in range(16):
                        for e in range(2):
                            g = q * 8 + s_ * 2 + e
                            last = (s_ == 3 and i == 15 and e == 1)
                            P.add("pe", "matmul", reads=["Xb", "CAz"], writes=[bkey],
                                  out=bank[32 * s_:32 * s_ + 32, 0:ntk].rearrange("p (k i) -> p k i", i=16)[:, :, i],
                                  lhsT=CAz[:, g, i + 1, :, :].rearrange("p r c -> p (r c)"), rhs=Xb[:, g, k0:k0 + nk], start=False, stop=last,
                                  tile_position=(0, 32 * s_), skip_group_check=True)
                P.add("act", "activation", reads=[bkey], writes=["yT"], out=yT[:, q, t0:t0 + ntk], in_=bank[:, 0:ntk],
                      func=AF.Copy)

        P.barrier(dummy[:, 0:1])
        wload2(w_glu_s, wglu_d[:, :], ["wglu_d"], "w_glu")
        wload2(w_gsms[:, :, :], wgsms_d[:, :], ["wgsms_d", "wgsms_d2"], "w_gs")
        wload2(w_os_s, wos_d[:, :], ["wos_d"], "w_os")
        dma("sp", w_out_s[:, :, :], wout_d.rearrange("(kt p) n -> p kt n", p=128), ["wout_d"], ["wada0", "wada1"])
        tilesC = own_tiles + samp_tiles
        NC2 = len(tilesC)
        xin3 = [xin[0], xin[1], xin2c]
        sgb2 = [sgb, av(32256, 512).rearrange("p (k n) -> p k n", k=4)]
        sgs2 = [sgs, av(32768, 512).rearrange("p (k n) -> p k n", k=4)]
        sms2 = [sms, av(33280, 1024).rearrange("p (k n) -> p k n", k=8)]
        sgt2 = [sgt, av(34304, 512).rearrange("p (k n) -> p k n", k=4)]
        mgt2 = [mgt, av(34816, 1024).rearrange("p (k n) -> p k n", k=8)]

        def c2_rms_front(n):
            (row0, nt, mods, col0) = tilesC[n]
            i3, i2 = n % 3, n % 2
            xt, xk = xin3[i3], "xinC%d" % i3
            xn, xnk = xn2[i2], "xn%d" % i2
            sst, ssk = ss2[i2], "ss%d" % i2
            dma("sp", xt[0:nt, :], x_all[row0:row0 + nt, :], [], [xk])
            P.add("act", "activation", reads=[xk], writes=[xnk, ssk], out=xn[0:nt, :], in_=xt[0:nt, :],
                  func=AF.Square, accum_out=sst[0:nt, 0:1])
            P.add("act", "activation", reads=[ssk, "epsc"], writes=[ssk], out=sst[0:nt, 2:3], in_=sst[0:nt, 0:1], func=AF.Ln,
                  scale=1.0 / D, bias=epsc[0:nt, 0:1])
            P.add("act", "activation", reads=[ssk], writes=[ssk], out=sst[0:nt, 3:4], in_=sst[0:nt, 2:3], func=AF.Exp, scale=-0.5)
            P.add("act", "activation", reads=[xk, ssk], writes=[xnk], out=xn[0:nt, :], in_=xt[0:nt, :],
                  func=AF.Copy, scale=sst[0:nt, 3:4])

        def c2_front_a(ti):
            (row0, nt, mods, col0) = tilesC[ti]
            c2_rms_front(ti)
            rms_back(ti % 2, nt, mods)

        def c2_front(ti):
            (row0, nt, mods, col0) = tilesC[ti]
            p = ti % 2
            CUR["i"] = p
            mti = (ti, 0) if ti < 16 else (16, (ti - 16) * 16)
            dma("sp", mgt2[p][:, :, 0:nt], mg_d[mti[0], :, :].rearrange("p (k t) -> p k t", k=8)[:, :, mti[1]:mti[1] + nt],
                [("mg_d", mti[0])], ["mgt%d" % p])
            for ct in range(8):
                for k in range(4):
                    P.add("pe", "matmul", reads=["yT", "w_glu"], writes=["pS"], out=pS[:, ct * 128:ct * 128 + nt],
                          lhsT=w_glu_s[:, k, ct * 128:(ct + 1) * 128], rhs=yT[:, k, col0:col0 + nt], start=(k == 0), stop=(k == 3))
            pSv = pS[:, :].rearrange("p (c t) -> p c t", t=128)
            for ct in range(4):
                P.add("act", "activation", reads=["pS", "bada"], writes=["sgb%d" % p], out=sgb2[p][:, ct, 0:nt], in_=pSv[:, 4 + ct, 0:nt],
                      func=AF.Sigmoid, bias=bglu[:, 4 + ct:5 + ct])
            for ct in range(12):
                for k in range(8):
                    P.add("pe", "matmul", reads=[HK(), "w_gs", "w_ms"], writes=["pA"], out=pA[:, ct * 128:ct * 128 + nt],
                          lhsT=w_gsms[:, k, ct * 128:(ct + 1) * 128], rhs=HT()[:, k, 0:nt], start=(k == 0), stop=(k == 7))
            pAv = pA[:, :].rearrange("p (c t) -> p c t", t=128)
            P.add("act", "activation", reads=["pA"], writes=["sgs%d" % p], out=sgs2[p][:, :, 0:nt], in_=pAv[:, 0:4, 0:nt], func=AF.Silu)
            P.add("act", "activation", reads=["pA"], writes=["sms%d" % p], out=sms2[p][:, :, 0:nt], in_=pAv[:, 4:12, 0:nt], func=AF.Sigmoid)
            for ct in range(4):
                P.add("dve", "scalar_tensor_tensor", reads=["pS", "sgb%d" % p, "bada"], writes=["sgt%d" % p], out=sgt2[p][:, ct, 0:nt],
                      in0=pSv[:, ct, 0:nt], scalar=bglu[:, ct:ct + 1], in1=sgb2[p][:, ct, 0:nt], op0=ALU.add, op1=ALU.mult)
            tt("dve", sgt2[p][:, :, 0:nt], sgt2[p][:, :, 0:nt], sgs2[p][:, :, 0:nt], ALU.mult, ["sgt%d" % p, "sgs%d" % p], ["sgt%d" % p])

        def c2_back(ti):
            (row0, nt, mods, col0) = tilesC[ti]
            p = ti % 2
            xt, xk = xin3[ti % 3], "xinC%d" % (ti % 3)
            for ct in range(8):
                for k in range(4):
                    P.add("pe", "matmul", reads=["sgt%d" % p, "w_os"], writes=["pO"], out=pO[:, ct // 4, (ct % 4) * 128:(ct % 4) * 128 + nt],
                          lhsT=w_os_s[:, k, ct * 128:(ct + 1) * 128], rhs=sgt2[p][:, k, 0:nt], start=(k == 0), stop=(k == 3))
            tt("dve", bst[:, :, 0:nt], pO[:, :, :].rearrange("p a (c t) -> p (a c) t", t=128)[:, :, 0:nt], sms2[p][:, :, 0:nt],
               ALU.mult, ["pO", "sms%d" % p], ["bst"])
            tt("dve", bst[:, :, 0:nt], bst[:, :, 0:nt], mgt2[p][:, :, 0:nt], ALU.add, ["bst", "mgt%d" % p], ["bst"])

        def c2_back_b(ti):
            (row0, nt, mods, col0) = tilesC[ti]
            p = ti % 2
            xt, xk = xin3[ti % 3], "xinC%d" % (ti % 3)
            for half in range(2):
                for k in range(8):
                    P.add("pe", "matmul", reads=["bst", "wada0", "wada1"], writes=["pO"],
                          out=pO[0:nt, half, :], lhsT=bst[:, k, 0:nt],
                          rhs=w_out_s[:, k, half * 512:(half + 1) * 512], start=(k == 0), stop=(k == 7))
            tt("dve", qk_sb[0:nt, :], pO[0:nt, :, :].rearrange("p a c -> p (a c)"), gate_bc[0:nt, 0 if ti < 16 else ti - 15, :], ALU.mult,
               ["pO", "gate_bc"], ["qk_sb"])
            tt("dve", sq[0:nt, :], qk_sb[0:nt, :], xt[0:nt, :], ALU.add, ["qk_sb", xk], ["sq"])
            dma("sp", y_out[col0:col0 + nt, :], sq[0:nt, :], ["sq"], [])

        c2_front_a(0)
        c2_front(0)
        for ti in range(NC2):
            if ti + 1 < NC2:
                c2_front_a(ti + 1)
            c2_back(ti)
            if ti + 1 < NC2:
                c2_front(ti + 1)
            c2_back_b(ti)

        P.emit(nc)
    return nc


_NC = None


def kernel(x_prompt, x_sample, c_prompt, c_sample, cache_k, cache_v, state_ssm_re, state_ssm_im,
           norm_g, w_ada, b_ada, w_in, q_norm_g, k_norm_g, rel_bias, lambda_re, lambda_im, log_dt,
           b_re, b_im, c_re, c_im, d_skip, w_glu, b_glu, w_oa, w_os, w_out):
    global _NC
    f = lambda a: np.ascontiguousarray(np.asarray(a, dtype=np.float32))
    x_prompt, x_sample = f(x_prompt), f(x_sample)
    if _NC is None:
        _NC = build()
    nc = _NC
    ident = np.eye(128, dtype=np.float32)
    sel = np.zeros((3, 256), np.float32)
    sel[0, 0:128] = 1.0
    sel[1, 128:144] = 1.0
    sel[2, 144:160] = 1.0
    pidx = np.arange(128)
    cmask = np.zeros((128, 16), np.float32)
    cmask[:, 0] = np.where(pidx < 64, -1.0, 1.0)
    cmask[:, 1] = -cmask[:, 0]
    for par in range(2):
        cmask[:, 2 + par] = ((pidx // 16) % 2 == par)
    for g8 in range(8):
        cmask[:, 4 + g8] = (pidx // 16 == g8)
    in_maps = []
    for c in range(8):
        b, j = c // 4, c % 4
        t0 = j * NOWN
        halo = np.zeros((NHALO, D), np.float32) if j == 0 else x_prompt[b, t0 - NHALO:t0]
        xs = x_sample[2 * c:2 * c + 2].reshape(NSAMP, D)
        x_all = np.concatenate([halo, x_prompt[b, t0:t0 + NOWN], xs], axis=0)
        c3 = np.stack([f(c_prompt)[b], f(c_sample)[2 * c], f(c_sample)[2 * c + 1]])
        hbias = np.full((128, 1), -30000.0 if j == 0 else 0.0, np.float32)
        x_prev = np.zeros((3 * NOWN, D), np.float32)
        pmask = np.ones((128, 4), np.float32)
        for i in range(3):
            js = j - 3 + i
            if js >= 0:
                x_prev[i * NOWN:(i + 1) * NOWN] = x_prompt[b, js * NOWN:(js + 1) * NOWN]
            else:
                pmask[:, i] = 0.0
        selm = np.zeros((128, 24), np.float32)
        for jr in range(j):
            selm[:, (b * 4 + jr) * 3 + (j - 1 - jr)] = 1.0
        in_maps.append({
            "x_all": np.ascontiguousarray(x_all), "c3": np.ascontiguousarray(c3),
            "cache_k": f(cache_k)[0, 2 * c:2 * c + 2].reshape(2, 512, 512),
            "cache_v": f(cache_v)[0, 2 * c:2 * c + 2].reshape(2, 512, 512),
            "hbias": hbias, "sel": sel, "ident": ident, "cmask": cmask, "selm": selm, "x_prev": x_prev, "pmask": pmask,
            "st_re": f(state_ssm_re)[0, 2 * c:2 * c + 2].reshape(64, 64),
            "st_im": f(state_ssm_im)[0, 2 * c:2 * c + 2].reshape(64, 64),
            "lambda_re": f(lambda_re)[0], "lambda_im": f(lambda_im)[0], "log_dt": f(log_dt)[0],
            "b_re": f(b_re)[0], "b_im": f(b_im)[0], "c_re": f(c_re)[0].reshape(512, 64), "c_im": f(c_im)[0].reshape(512, 64),
            "d_skip": f(d_skip)[0], "w_glu": f(w_glu)[0], "b_glu": f(b_glu)[0], "w_os": f(w_os)[0],
            "norm_g": f(norm_g)[0], "w_ada": f(w_ada)[0], "b_ada": f(b_ada)[0], "w_in": f(w_in)[0],
            "q_norm_g": f(q_norm_g)[0], "k_norm_g": f(k_norm_g)[0], "rel_bias": f(rel_bias)[0],
            "w_oa": f(w_oa)[0], "w_out": f(w_out)[0],
        })
    res = run_bass_kernel_spmd(nc, in_maps, core_ids=list(range(8)))
    R = res.results
    y_prompt = np.zeros((2, 8192, D), np.float32)
    y_sample = np.zeros((16, 16, D), np.float32)
    nk_p = np.zeros((1, 2, 512, 8, 64), np.float32)
    nv_p = np.zeros((1, 2, 512, 8, 64), np.float32)
    sr_p = np.zeros((1, 2, 32, 64), np.float32)
    si_p = np.zeros((1, 2, 32, 64), np.float32)
    nk_s = np.zeros((1, 16, 16, 8, 64), np.float32)
    nv_s = np.zeros((1, 16, 16, 8, 64), np.float32)
    sr_s = np.zeros((1, 16, 32, 64), np.float32)
    si_s = np.zeros((1, 16, 32, 64), np.float32)
    for c in range(8):
        b, j = c // 4, c % 4
        r = R[c]
        y_prompt[b, j * NOWN:(j + 1) * NOWN] = r["y_out"][0:NOWN]
        y_sample[2 * c:2 * c + 2] = r["y_out"][NOWN:].reshape(2, 16, D)
        if j == 3:
            nk_p[0, b] = r["k_last"].reshape(512, 8, 64)
            nv_p[0, b] = r["v_last"].reshape(512, 8, 64)
        nk_s[0, 2 * c:2 * c + 2] = r["k_samp"].reshape(2, 16, 8, 64)
        if j == 3:
            sr_p[0, b] = r["ssm_p"][:, 0:64]
            si_p[0, b] = r["ssm_p"][:, 64:128]
        sr_s[0, 2 * c:2 * c + 2] = r["ssm_s"][:, 0:64].reshape(2, 32, 64)
        si_s[0, 2 * c:2 * c + 2] = r["ssm_s"][:, 64:128].reshape(2, 32, 64)
        nv_s[0, 2 * c:2 * c + 2] = r["v_samp"].reshape(2, 16, 8, 64)
    return (y_prompt, y_sample, nk_p, nv_p, sr_p, si_p, nk_s, nv_s, sr_s, si_s)
```
